# Optimizing a Trainium2 kernel written in Bass

```python
import jax, jax.numpy as jnp
from jax import lax
import numpy as np

D_MODEL = 2048
BATCH = 4
SEQ = 2048
DEPTH = 4
DEC_BATCH = 16
DEC_SEQ = 2048
PAST_LEN = 128

PLE_DIM = 256
HEAD_DIM = 128
A_HEADS = 6
A_KV_HEADS = 2
A_GROUP = A_HEADS // A_KV_HEADS
A_WINDOW = 128
A_BLOCK = 128
B_HEADS = 6
B_Q_LORA = 512
B_KV_LORA = 512
B_NOPE = 128
B_ROPE = 64
B_V = 128
B_QK = B_NOPE + B_ROPE
B_BLOCK = 128
ROPE_THETA = 10000.0
C_HEADS = 4
GRID_W = 64
NA_ROWS_MAX = 8
NA_COLS = 16
A_WIDTH = A_HEADS * HEAD_DIM
A_KV_WIDTH = A_KV_HEADS * HEAD_DIM
B_WIDTH = B_HEADS * B_V
C_WIDTH = C_HEADS * HEAD_DIM
MIX_WIDTH = A_WIDTH + B_WIDTH + C_WIDTH
IN_SPLITS = (A_WIDTH, A_KV_WIDTH, A_KV_WIDTH, A_WIDTH,
             B_Q_LORA, B_KV_LORA, B_ROPE, B_WIDTH,
             C_WIDTH, C_WIDTH, C_WIDTH, C_WIDTH)
IN_WIDTH = (2 * A_WIDTH + 2 * A_KV_WIDTH + B_Q_LORA + B_KV_LORA + B_ROPE + B_WIDTH + 4 * C_WIDTH)
EPS = 1e-6
NEG_INF = -1e30

kernel_name = "hymba_style_hybrid_encoder"


def _rmsnorm(x, g):
    xf = x.astype(jnp.float32)
    xf = xf * lax.rsqrt(jnp.mean(xf * xf, axis=-1, keepdims=True) + EPS)
    return (xf * g.astype(jnp.float32)).astype(x.dtype)


def _split(x, sizes):
    outs, off = [], 0
    for s in sizes:
        outs.append(x[..., off:off + s])
        off += s
    return outs


def _alibi_slopes(n):
    return jnp.exp2(-8.0 * jnp.arange(1, n + 1, dtype=jnp.float32) / n)


def _rope(x, pos):
    half = x.shape[-1] // 2
    inv = ROPE_THETA ** (-jnp.arange(half, dtype=jnp.float32) / half)
    ang = pos.astype(jnp.float32)[:, None] * inv[None, :]
    cos = jnp.cos(ang)[:, None, :].astype(x.dtype)
    sin = jnp.sin(ang)[:, None, :].astype(x.dtype)
    x1, x2 = x[..., :half], x[..., half:]
    return jnp.concatenate([x1 * cos - x2 * sin, x1 * sin + x2 * cos], axis=-1)


def _window_gqa(q, k, v, g_q, g_k, sink):
    B, L, _ = q.shape
    nb = L // A_BLOCK
    q = _rmsnorm(q.reshape(B, L, A_HEADS, HEAD_DIM), g_q)
    k = _rmsnorm(k.reshape(B, L, A_KV_HEADS, HEAD_DIM), g_k)
    v = v.reshape(B, L, A_KV_HEADS, HEAD_DIM)
    q = q.transpose(0, 2, 1, 3).reshape(B, A_KV_HEADS, A_GROUP, nb, A_BLOCK, HEAD_DIM)

    def band(t):
        t = jnp.pad(t.transpose(0, 2, 1, 3), ((0, 0), (0, 0), (A_BLOCK, A_BLOCK), (0, 0)))
        t = t.reshape(B, A_KV_HEADS, nb + 2, A_BLOCK, HEAD_DIM)
        return jnp.concatenate([t[:, :, :-2], t[:, :, 1:-1], t[:, :, 2:]], axis=3)

    kb, vb = band(k), band(v)
    logits = jnp.einsum('bkgnqd,bknsd->bkgnqs', q, kb).astype(jnp.float32) * (HEAD_DIM ** -0.5)
    qi = jnp.arange(A_BLOCK)
    si = jnp.arange(3 * A_BLOCK)
    delta = si[None, :] - A_BLOCK - qi[:, None]
    tk = jnp.arange(nb)[:, None] * A_BLOCK - A_BLOCK + si[None, :]
    valid = (jnp.abs(delta) <= A_WINDOW)[None] & ((tk >= 0) & (tk < L))[:, None, :]
    slopes = _alibi_slopes(A_HEADS).reshape(A_KV_HEADS, A_GROUP)
    logits = logits - slopes[:, :, None, None, None] * jnp.abs(delta).astype(jnp.float32)
    logits = jnp.where(valid, logits, NEG_INF)
    sink_f = sink.astype(jnp.float32).reshape(A_KV_HEADS, A_GROUP, 1, 1, 1)
    m = jnp.maximum(jnp.max(logits, axis=-1, keepdims=True), sink_f)
    e = jnp.exp(logits - m)
    probs = e / (jnp.sum(e, axis=-1, keepdims=True) + jnp.exp(sink_f - m))
    out = jnp.einsum('bkgnqs,bknsd->bkgnqd', probs.astype(vb.dtype), vb)
    return out.reshape(B, A_HEADS, L, HEAD_DIM).transpose(0, 2, 1, 3).reshape(B, L, A_WIDTH)


def _mla(cq, ckv, kr, g_cq, g_ckv, w_uq, w_ukv, g_q, g_k):
    B, L, _ = cq.shape
    nb = L // B_BLOCK
    pos = jnp.arange(L)
    q = (_rmsnorm(cq, g_cq) @ w_uq).reshape(B, L, B_HEADS, B_QK)
    kv = (_rmsnorm(ckv, g_ckv) @ w_ukv).reshape(B, L, B_HEADS, B_NOPE + B_V)
    k_nope, v = kv[..., :B_NOPE], kv[..., B_NOPE:]
    k = jnp.concatenate([k_nope, jnp.broadcast_to(kr[:, :, None, :], (B, L, B_HEADS, B_ROPE))], axis=-1)
    q = _rmsnorm(q, g_q)
    k = _rmsnorm(k, g_k)
    q = jnp.concatenate([q[..., :B_NOPE], _rope(q[..., B_NOPE:], pos)], axis=-1)
    k = jnp.concatenate([k[..., :B_NOPE], _rope(k[..., B_NOPE:], pos)], axis=-1)
    qb = q.transpose(0, 2, 1, 3).reshape(B, B_HEADS, nb, B_BLOCK, B_QK).transpose(2, 0, 1, 3, 4)
    k = k.transpose(0, 2, 1, 3)
    v = v.transpose(0, 2, 1, 3)
    scale = B_QK ** -0.5

    def block(qblk):
        s = jnp.einsum('bhqd,bhkd->bhqk', qblk, k).astype(jnp.float32) * scale
        p = jax.nn.softmax(s, axis=-1)
        return jnp.einsum('bhqk,bhkd->bhqd', p.astype(v.dtype), v)

    out = lax.map(block, qb)
    return out.transpose(1, 0, 3, 2, 4).reshape(B, L, B_WIDTH)


def _neighbourhood(q, k, v, g_q, g_k, rpb):
    B, L, _ = q.shape
    rows = L // GRID_W
    wr = min(NA_ROWS_MAX, rows)

    def grid(t):
        return t.reshape(B, rows, GRID_W, C_HEADS, HEAD_DIM).transpose(0, 3, 1, 2, 4)

    qg = grid(_rmsnorm(q.reshape(B, L, C_HEADS, HEAD_DIM), g_q))
    kg = grid(_rmsnorm(k.reshape(B, L, C_HEADS, HEAD_DIM), g_k))
    vg = grid(v)
    r = jnp.arange(rows)
    c = jnp.arange(GRID_W)
    rs = jnp.clip(r - wr // 2, 0, rows - wr)
    row_idx = rs[:, None] + jnp.arange(wr)[None, :]
    cs = jnp.clip(c - NA_COLS // 2, 0, GRID_W - NA_COLS)
    col_valid = (c[None, :] >= cs[:, None]) & (c[None, :] < cs[:, None] + NA_COLS)
    k_rows = kg[:, :, row_idx]
    v_rows = vg[:, :, row_idx]
    logits = jnp.einsum('bhrqd,bhrwcd->bhrqwc', qg, k_rows).astype(jnp.float32) * (HEAD_DIM ** -0.5)
    dr = row_idx - r[:, None]
    dc = c[None, :] - c[:, None]
    bias = rpb.astype(jnp.float32)[:, (dr + NA_ROWS_MAX - 1)[:, None, :, None],
                                   jnp.clip(dc + NA_COLS - 1, 0, 2 * NA_COLS - 2)[None, :, None, :]]
    logits = jnp.where(col_valid[:, None, :], logits + bias, NEG_INF)
    sh = logits.shape
    probs = jax.nn.softmax(logits.reshape(sh[:4] + (wr * GRID_W,)), axis=-1).reshape(sh)
    out = jnp.einsum('bhrqwc,bhrwcd->bhrqd', probs.astype(v_rows.dtype), v_rows)
    return out.transpose(0, 2, 3, 1, 4).reshape(B, L, C_WIDTH)


def _trunk(x, p, W):
    h = x
    for l in range(DEPTH):
        u = _rmsnorm(h, W['norm_in'][l])
        (aq, ak, av, az, bcq, bckv, bkr, bz, cq, ck, cv, cz) = _split(u @ W['w_in'][l], IN_SPLITS)
        ya = _window_gqa(aq, ak, av, W['a_q_norm'][l], W['a_k_norm'][l], W['a_sink'][l]) * jax.nn.silu(az)
        yb = _mla(bcq, bckv, bkr, W['b_cq_norm'][l], W['b_ckv_norm'][l], W['b_w_uq'][l], W['b_w_ukv'][l],
                  W['b_q_norm'][l], W['b_k_norm'][l]) * jax.nn.silu(bz)
        yc = _neighbourhood(cq, ck, cv, W['c_q_norm'][l], W['c_k_norm'][l], W['c_rpb'][l]) * jax.nn.silu(cz)
        h = h + jnp.concatenate([ya, yb, yc], axis=-1) @ W['w_out'][l]
        gate = jax.nn.sigmoid(_rmsnorm(h, W['ple_norm'][l]) @ W['w_ple_gate'][l])
        h = h + gate * _rmsnorm(p[l] @ W['w_ple_proj'][l], W['ple_post_norm'][l])
    return h


def setup_inputs(seed: int = 0) -> dict:
    key = jax.random.key(seed)
    ks = jax.random.split(key, 24)
    f32 = jnp.float32

    def nrm(k, shape, scale):
        return jax.random.normal(k, shape, f32) * scale

    def gain(k, shape):
        return 1.0 + 0.05 * jax.random.normal(k, shape, f32)

    return {
        'x_prompt': nrm(ks[0], (BATCH, SEQ, D_MODEL), 1.0),
        'x_sample': nrm(ks[1], (DEC_BATCH, DEC_SEQ, D_MODEL), 1.0),
        'p_prompt': nrm(ks[2], (DEPTH, BATCH, SEQ, PLE_DIM), 1.0),
        'p_sample': nrm(ks[3], (DEPTH, DEC_BATCH, DEC_SEQ, PLE_DIM), 1.0),
        'norm_in': gain(ks[4], (DEPTH, D_MODEL)),
        'w_in': nrm(ks[5], (DEPTH, D_MODEL, IN_WIDTH), D_MODEL ** -0.5),
        'a_q_norm': gain(ks[6], (DEPTH, HEAD_DIM)),
        'a_k_norm': gain(ks[7], (DEPTH, HEAD_DIM)),
        'a_sink': nrm(ks[8], (DEPTH, A_HEADS), 0.5),
        'b_cq_norm': gain(ks[9], (DEPTH, B_Q_LORA)),
        'b_ckv_norm': gain(ks[10], (DEPTH, B_KV_LORA)),
        'b_w_uq': nrm(ks[11], (DEPTH, B_Q_LORA, B_HEADS * B_QK), B_Q_LORA ** -0.5),
        'b_w_ukv': nrm(ks[12], (DEPTH, B_KV_LORA, B_HEADS * (B_NOPE + B_V)), B_KV_LORA ** -0.5),
        'b_q_norm': gain(ks[13], (DEPTH, B_QK)),
        'b_k_norm': gain(ks[14], (DEPTH, B_QK)),
        'c_q_norm': gain(ks[15], (DEPTH, HEAD_DIM)),
        'c_k_norm': gain(ks[16], (DEPTH, HEAD_DIM)),
        'c_rpb': nrm(ks[17], (DEPTH, C_HEADS, 2 * NA_ROWS_MAX - 1, 2 * NA_COLS - 1), 0.1),
        'w_out': nrm(ks[18], (DEPTH, MIX_WIDTH, D_MODEL), MIX_WIDTH ** -0.5),
        'ple_norm': gain(ks[19], (DEPTH, D_MODEL)),
        'w_ple_gate': nrm(ks[20], (DEPTH, D_MODEL, D_MODEL), D_MODEL ** -0.5),
        'w_ple_proj': nrm(ks[21], (DEPTH, PLE_DIM, D_MODEL), PLE_DIM ** -0.5),
        'ple_post_norm': gain(ks[22], (DEPTH, D_MODEL)),
    }


def reference(x_prompt, x_sample, p_prompt, p_sample, norm_in, w_in, a_q_norm, a_k_norm, a_sink,
              b_cq_norm, b_ckv_norm, b_w_uq, b_w_ukv, b_q_norm, b_k_norm, c_q_norm, c_k_norm, c_rpb,
              w_out, ple_norm, w_ple_gate, w_ple_proj, ple_post_norm):
    W = {
        'norm_in': norm_in, 'w_in': w_in,
        'a_q_norm': a_q_norm, 'a_k_norm': a_k_norm, 'a_sink': a_sink,
        'b_cq_norm': b_cq_norm, 'b_ckv_norm': b_ckv_norm, 'b_w_uq': b_w_uq, 'b_w_ukv': b_w_ukv,
        'b_q_norm': b_q_norm, 'b_k_norm': b_k_norm,
        'c_q_norm': c_q_norm, 'c_k_norm': c_k_norm, 'c_rpb': c_rpb,
        'w_out': w_out, 'ple_norm': ple_norm, 'w_ple_gate': w_ple_gate,
        'w_ple_proj': w_ple_proj, 'ple_post_norm': ple_post_norm,
    }
    y_prompt = _trunk(x_prompt, p_prompt, W)
    y_sample = _trunk(x_sample, p_sample, W)
    return (y_prompt, y_sample)
```

```python
import types
import numpy as np
from contextlib import ExitStack
import ml_dtypes
import concourse.bass as bass
import concourse.mybir as mybir
from concourse.bass_utils import run_bass_kernel_spmd

F32 = mybir.dt.float32
BF16 = mybir.dt.bfloat16
AF = mybir.ActivationFunctionType
ALU = mybir.AluOpType

L = 2048
D = 2048
NT = 16
DEPTH = 4
NSEQ = 3
EPS = 1e-6
IN_W = 5952

EPOCH = 30000


def _freeze(fn):
    if fn.__closure__ is None:
        return fn
    cells = tuple(types.CellType(c.cell_contents) for c in fn.__closure__)
    return types.FunctionType(fn.__code__, fn.__globals__, fn.__name__, fn.__defaults__, cells)


class Tk:
    __slots__ = ("sem", "val", "q", "dma")

    def __init__(self, sem, val, q, dma=False):
        self.sem = sem
        self.val = val
        self.q = q
        self.dma = dma


class Buf:
    __slots__ = ("name", "w", "r")

    def __init__(self, name=""):
        self.name = name
        self.w = {}
        self.r = {}


class DSem:
    __slots__ = ("h", "count")

    def __init__(self, h):
        self.h = h
        self.count = 0


class Sched:
    QS = ("pe", "act", "dve", "pool", "sp")

    def __init__(self, nc, es):
        self.nc = nc
        self.es = es
        self.qs = {k: [] for k in self.QS}
        self.esem = {}
        self.ecount = {}
        self.waited = {k: {} for k in self.QS}
        self.nsem = 0
        self.dsems = []
        self.dpool = []
        self.pool_i = 0
        for k in self.QS:
            self._new_epoch(k)

    def _alloc_sem(self, name):
        self.nsem += 1
        return self.es.enter_context(self.nc.semaphore(name))

    def _new_epoch(self, k):
        self.esem[k] = self._alloc_sem(f"e{k}{self.nsem}")
        self.ecount[k] = 0

    def dsem(self, name="d", persistent=False):
        if persistent:
            d = DSem(self._alloc_sem(f"{name}{self.nsem}"))
            self.dsems.append(d)
            return d
        if self.pool_i >= len(self.dpool):
            d = DSem(self._alloc_sem(f"dp{self.nsem}"))
            self.dsems.append(d)
            self.dpool.append(d)
        d = self.dpool[self.pool_i]
        self.pool_i += 1
        return d

    def op(self, q, fn, reads=(), writes=(), wacc=(), dsem=None):
        raw = {}
        oth = {}

        def add(d, t):
            k = id(t.sem)
            if k not in d or d[k].val < t.val:
                d[k] = t

        for b in reads:
            for t in b.w.values():
                add(raw, t)
        for b in writes:
            for t in b.w.values():
                add(oth, t)
            for t in b.r.values():
                add(oth, t)
        for b in wacc:
            for t in b.r.values():
                add(oth, t)
        waits = []
        wd = self.waited[q]
        for d, is_raw in ((raw, True), (oth, False)):
            for k, t in d.items():
                if t.q == q and (not t.dma) and dsem is None:
                    if not is_raw or q == "pe":
                        continue
                if wd.get(k, 0) >= t.val:
                    continue
                wd[k] = t.val
                waits.append((t.sem, t.val))
        if dsem is not None:
            dsem.count += 16
            tk = Tk(dsem.h, dsem.count, q, True)
            inc = (dsem.h, 16)
        else:
            if self.ecount[q] >= EPOCH:
                self._new_epoch(q)
            self.ecount[q] += 1
            tk = Tk(self.esem[q], self.ecount[q], q)
            inc = (self.esem[q], 1)
        self.qs[q].append((waits, _freeze(fn), inc))
        k = id(tk.sem)
        for b in writes:
            b.w = {k: tk}
            b.r = {}
        for b in wacc:
            b.w[k] = tk
        for b in reads:
            if k not in b.r or b.r[k].val < tk.val:
                b.r[k] = tk
        return tk

    def barrier(self):
        self.pool_i = 0
        tks = []
        for q in self.QS:
            if self.ecount[q] > 0:
                tks.append((self.esem[q], self.ecount[q], q))
        for d in self.dsems:
            if d.count > 0:
                tks.append((d.h, d.count, None))
        for q in self.QS:
            waits = []
            wd = self.waited[q]
            for (h, v, src) in tks:
                if src == q:
                    continue
                if wd.get(id(h), 0) >= v:
                    continue
                wd[id(h)] = v
                waits.append((h, v))
            if waits:
                self.qs[q].append((waits, None, None))

    def emit(self):
        nc = self.nc
        self.barrier()
        qs = self.qs

        def run(eng, lst):
            for waits, fn, inc in lst:
                for (s, v) in waits:
                    eng.wait_ge(s, v)
                if fn is not None:
                    fn(eng).then_inc(inc[0], inc[1])

        with nc.Block() as block:
            @block.tensor
            def _(e):
                run(e, qs["pe"])

            @block.scalar
            def _(e):
                run(e, qs["act"])

            @block.vector
            def _(e):
                run(e, qs["dve"])

            @block.gpsimd
            def _(e):
                run(e, qs["pool"])

            @block.sync
            def _(e):
                run(e, qs["sp"])


class Ring:
    def __init__(self, S, aps, name, with_dsem=True):
        self.aps = aps
        self.bufs = [Buf(f"{name}{i}") for i in range(len(aps))]
        self.ds = [S.dsem(name) for _ in aps] if with_dsem else [None] * len(aps)
        self.i = 0

    def next(self):
        k = self.i % len(self.aps)
        self.i += 1
        return self.aps[k], self.bufs[k], self.ds[k]


_O = dict(aq=0, ak=768, av=1024, az=1280, bcq=2048, bckv=2560, bkr=3072, bz=3136,
          cq=3904, ck=4416, cv=4928, cz=5440)


def _in_blocks():
    r = lambda a, n: list(range(a, a + n))
    blocks = [
        ("QK", r(_O["aq"], 512)),
        ("QK", r(_O["aq"] + 512, 256) + r(_O["ak"], 256)),
        ("QK", r(_O["cq"], 512)),
        ("QK", r(_O["ck"], 512)),
        ("LAT", r(_O["bcq"], 512)),
        ("LAT", r(_O["bckv"], 512)),
        ("VKR", r(_O["av"], 256) + r(_O["bkr"], 64)),
        ("V", r(_O["cv"], 512)),
        ("Z", r(_O["az"], 512)),
        ("Z", r(_O["az"] + 512, 256) + r(_O["bz"], 256)),
        ("Z", r(_O["bz"] + 256, 512)),
        ("Z", r(_O["cz"], 512)),
    ]
    return blocks


IN_BLOCKS = _in_blocks()
IN_PERM = np.concatenate([np.array(c) for _, c in IN_BLOCKS])
UKV_PERM = np.concatenate([np.arange(h * 256, h * 256 + 128) for h in range(6)] +
                          [np.arange(h * 256 + 128, h * 256 + 256) for h in range(6)])


def _const_tables():
    slopes = 2.0 ** (-8.0 * np.arange(1, 7, dtype=np.float64) / 6)
    kt = np.arange(128)[:, None]
    qo = np.arange(384)[None, :]
    delta = 128 + kt - qo
    tabA = np.zeros((6, 128, 384), np.float32)
    for h in range(6):
        tabA[h] = (np.exp(-slopes[h] * np.abs(delta)) * (np.abs(delta) <= 128)).astype(np.float32)
    half = 32
    inv = 10000.0 ** (-np.arange(half, dtype=np.float32) / half)
    ang = np.arange(L, dtype=np.float32)[:, None] * inv[None, :]
    cos = np.cos(ang).astype(np.float32)
    sin = np.sin(ang).astype(np.float32)
    kc = np.arange(64)[:, None]
    qc = np.arange(64)[None, :]
    cs = np.clip(qc - 8, 0, 48)
    valid = (kc >= cs) & (kc < cs + 16)
    oh = np.zeros((32, 64, 64), np.float32)
    b = 15 + kc - qc
    for bb in range(31):
        oh[bb] = ((b == bb) & valid).astype(np.float32)
    oh[31] = np.where(valid, 0.0, -30000.0)
    return tabA, cos, sin, oh.reshape(32, 4096)


def build(nseq=NSEQ, nlayers=DEPTH, debug=False):
    nc = bass.Bass("TRN2", target_bir_lowering=False)
    es = ExitStack()

    def din(name, shape, dt=F32):
        return nc.dram_tensor(name, list(shape), dt, kind="ExternalInput").ap()

    dbg_kind = "ExternalOutput" if debug else "Internal"

    def dscr(name, shape, dt=F32):
        return nc.dram_tensor(name, list(shape), dt, kind=dbg_kind).ap()

    x_d = din("x", [nseq, L, D])
    p_d = din("p", [DEPTH, nseq, L, 256])
    y_d = nc.dram_tensor("y", [nseq, L, D], F32, kind="ExternalOutput").ap()
    w_in_d = din("w_in", [DEPTH, D, IN_W])
    w_out_d = din("w_out", [DEPTH, D, D])
    w_gate_d = din("w_gate", [DEPTH, D, D])
    w_pp_d = din("w_pp", [DEPTH, 256, D])
    w_uq_d = din("w_uq", [DEPTH, 512, 1152])
    w_ukv_d = din("w_ukv", [DEPTH, 512, 1536])
    g_in_d = din("g_in", [DEPTH, D])
    g_ple_d = din("g_ple", [DEPTH, D])
    g_post_d = din("g_post", [DEPTH, D])
    g_aq_d = din("g_aq", [DEPTH, 128])
    g_ak_d = din("g_ak", [DEPTH, 128])
    g_cq_d = din("g_cq", [DEPTH, 128])
    g_ck_d = din("g_ck", [DEPTH, 128])
    g_bcq_d = din("g_bcq", [DEPTH, 512])
    g_bckv_d = din("g_bckv", [DEPTH, 512])
    g_bq_d = din("g_bq", [DEPTH, 192])
    g_bk_d = din("g_bk", [DEPTH, 192])
    sink_d = din("sink", [DEPTH, 6])
    rpbT_d = din("rpbT", [DEPTH, 4, 31, 15])
    ident_d = din("ident", [128, 128], BF16)
    tabA_d = din("tabA", [6, 128, 384])
    cos_d = din("cosT", [L, 32])
    sin_d = din("sinT", [L, 32])
    oh_d = din("onehot", [32, 4096])

    hscr_d = dscr("hscr", [nseq, L, D])
    featA_d = dscr("featA", [8, 128, L], BF16)
    featC_d = dscr("featC", [8, 128, L], BF16)
    qn_d = dscr("qn", [6, 128, L], BF16)
    qr_d = dscr("qr", [6, 128, L], BF16)
    kn_d = dscr("kn", [6, 128, L], BF16)
    vA_d = dscr("vA", [L, 256], BF16)
    vB_d = dscr("vB", [L, 768], BF16)
    vC_d = dscr("vC", [L, 512], BF16)
    sz_d = dscr("sz", [L, D])
    xd_d = dscr("xd", [4, 15, 4096])
    ctab_d = dscr("ctab", [4, 128, 26 * 64])

    S = Sched(nc, es)

    def sbt(name, shape, dt):
        return nc.alloc_sbuf_tensor(name, list(shape), dt)

    ident = sbt("ident_s", [128, 128], BF16)
    cosT = sbt("cosT_s", [128, NT, 32], F32)
    sinT = sbt("sinT_s", [128, NT, 32], F32)
    G1 = sbt("G1", [128, D], F32)
    G2 = sbt("G2", [128, D], F32)
    g_aq = sbt("g_aq_s", [128, 128], F32)
    g_ak = sbt("g_ak_s", [128, 128], F32)
    g_cq = sbt("g_cq_s", [128, 128], F32)
    g_ck = sbt("g_ck_s", [128, 128], F32)
    g_bq = sbt("g_bq_s", [128, 192], F32)
    g_bk = sbt("g_bk_s", [128, 192], F32)
    g_bcq = sbt("g_bcq_s", [128, 512], F32)
    g_bckv = sbt("g_bckv_s", [128, 512], F32)
    esink = sbt("esink", [128, 8], F32)
    kscale = sbt("kscale", [128, NT, 6], F32)
    ss_kr = sbt("ss_kr", [128, NT], F32)
    rstd_p = sbt("rstd_p", [128, NT], F32)
    stat = sbt("stat", [128, 512], F32)
    vaug = [sbt(f"vaug{i}", [128, NT, 129], BF16) for i in range(2)]
    ARENA_ACT = sbt("arena_act", [128, 16 * L], BF16)
    ARENA_W = sbt("arena_w", [128, 16 * 1024], BF16)
    XBYTES = 66 * 1024
    ARENA_X = sbt("arena_x", [128, XBYTES // 4], F32)

    actT = ARENA_ACT[:, :].rearrange("p (c n) -> p c n", c=16)
    PS = nc.alloc_psum_tensor("ps", [128, 3072], F32)
    PT = nc.alloc_psum_tensor("pt", [128, 2048], BF16)
    bank = [PS[:, i * 512:(i + 1) * 512] for i in range(6)]
    Bbank = [Buf(f"bank{i}") for i in range(6)]
    tbank = [PT[:, i * 1024:(i + 1) * 1024] for i in range(2)]
    Btb = [Buf(f"tb{i}") for i in range(2)]

    class Carver:
        def __init__(self, arena_f32, nbytes):
            self.a = arena_f32
            self.n = nbytes
            self.off = 0

        def reset(self):
            self.off = 0

        def take(self, shape, dt):
            esz = 4 if dt == F32 else 2
            nel = int(np.prod(shape[1:]))
            nb = nel * esz
            nb_al = (nb + 31) // 32 * 32
            assert self.off + nb_al <= self.n, ("arena overflow", self.off, nb_al, self.n)
            v = self.a[:, self.off // 4:(self.off + nb_al) // 4]
            if dt != F32:
                v = v.bitcast(dt)
            v = v[0:shape[0], 0:nel]
            if len(shape) == 3:
                v = v.rearrange("p (a b) -> p a b", a=shape[1])
            self.off += nb_al
            return v

    CX = Carver(ARENA_X, XBYTES)
    ARENA_W32 = ARENA_W[:, :].bitcast(F32)
    CW = Carver(ARENA_W32, 32 * 1024)

    Bconst = Buf("const")
    d_const = S.dsem("const", persistent=True)

    def dma(q, out, in_, reads=(), writes=(), wacc=(), dsem=None):
        assert dsem is not None
        return S.op(q, lambda e: e.dma_start(out=out, in_=in_), reads=reads, writes=writes,
                    wacc=wacc, dsem=dsem)

    dma("sp", ident[:], ident_d, wacc=[Bconst], dsem=d_const)
    dma("sp", cosT[:], cos_d.rearrange("(t p) i -> p t i", p=128), wacc=[Bconst], dsem=d_const)
    dma("sp", sinT[:], sin_d.rearrange("(t p) i -> p t i", p=128), wacc=[Bconst], dsem=d_const)
    for i in range(2):
        S.op("dve", (lambda i: lambda e: e.memset(vaug[i][:, :, 128:129], 1.0))(i), wacc=[Bconst])
    S.barrier()

    stat_i = [0]

    def stat_take(n):
        if stat_i[0] + n > 512:
            stat_i[0] = 0
        a = stat[:, stat_i[0]:stat_i[0] + n]
        stat_i[0] += n
        return a

    def rstd_from_ss(ss_ap, n, count, eps=EPS, extra_scale=None, out=None):
        r = out if out is not None else stat_take(n)
        b = Buf("rstd")
        return r, b

    bank_i = [0]

    def next_bank():
        k = bank_i[0] % 6
        bank_i[0] += 1
        return bank[k], Bbank[k]

    tb_i = [0]

    def next_tb():
        k = tb_i[0] % 2
        tb_i[0] += 1
        return tbank[k], Btb[k]

    DQ = []

    def run_deferred():
        while DQ:
            DQ.pop(0)()

    _orig_barrier = S.barrier

    def _checked_barrier():
        assert not DQ, "deferred work pending at barrier"
        _orig_barrier()
    S.barrier = _checked_barrier

    def bcast_load(dst, src_row, n, buf, ds):
        dma("sp", dst, src_row.partition_broadcast(128), writes=[buf], dsem=ds)

    Bg1, Bg2, Bgs = Buf("G1"), Buf("G2"), Buf("gsm")
    d_g1, d_g2, d_gs = S.dsem("g1", True), S.dsem("g2", True), S.dsem("gs", True)

    def rope(xr, out_bf, t, tmp, Bx, Bout, Btmp):
        c = cosT[:, t, :]
        s = sinT[:, t, :]
        x1 = xr[:, 0:32]
        x2 = xr[:, 32:64]
        S.op("dve", lambda e: e.tensor_tensor(out=tmp[:, 0, :], in0=x1, in1=c, op=ALU.mult), reads=[Bx, Bconst], writes=[Btmp])
        S.op("dve", lambda e: e.tensor_tensor(out=tmp[:, 1, :], in0=x2, in1=s, op=ALU.mult), reads=[Bx], writes=[Btmp])
        S.op("dve", lambda e: e.tensor_tensor(out=tmp[:, 2, :], in0=x1, in1=s, op=ALU.mult), reads=[Bx], writes=[Btmp])
        S.op("dve", lambda e: e.tensor_tensor(out=tmp[:, 3, :], in0=x2, in1=c, op=ALU.mult), reads=[Bx], writes=[Btmp])
        S.op("dve", lambda e: e.tensor_tensor(out=out_bf[:, 0:32], in0=tmp[:, 0, :], in1=tmp[:, 1, :], op=ALU.subtract), reads=[Btmp], writes=[Bout])
        S.op("dve", lambda e: e.tensor_tensor(out=out_bf[:, 32:64], in0=tmp[:, 2, :], in1=tmp[:, 3, :], op=ALU.add), reads=[Btmp], writes=[Bout])

    def sumsq(ps_ap, junk_ap, ss_ap, Bps, Bjunk, Bss):
        S.op("act", lambda e: e.activation(out=junk_ap, in_=ps_ap, func=AF.Square, accum_out=ss_ap),
             writes=[Bps, Bjunk, Bss])

    def rstd(ss_ap, r_ap, count, Bss, Br, mul=None):
        m2 = 1.0 if mul is None else float(mul) ** 2
        S.op("act", lambda e: e.activation(out=r_ap, in_=ss_ap, func=AF.Sqrt, scale=1.0 / (count * m2), bias=EPS / m2),
             reads=[Bss], writes=[Br])
        S.op("dve", lambda e: e.reciprocal(out=r_ap, in_=r_ap), reads=[Br], writes=[Br])

    for l in range(nlayers):
        S.barrier()
        for (dst, src, n) in ((g_aq, g_aq_d, 128), (g_ak, g_ak_d, 128), (g_cq, g_cq_d, 128), (g_ck, g_ck_d, 128),
                              (g_bq, g_bq_d, 192), (g_bk, g_bk_d, 192), (g_bcq, g_bcq_d, 512), (g_bckv, g_bckv_d, 512)):
            dma("sp", dst[:], src[l, :].partition_broadcast(128), wacc=[Bgs], dsem=d_gs)
        dma("sp", esink[:, 0:6], sink_d[l, :].partition_broadcast(128), wacc=[Bgs], dsem=d_gs)
        S.barrier()
        S.op("act", lambda e: e.activation(out=esink[:, 0:6], in_=esink[:, 0:6], func=AF.Exp), writes=[Bgs])
        CX.reset()
        CW.reset()
        oh_s = CX.take([32, 4096], F32)
        xa_s = CX.take([15, 4096], F32)
        rt_s = CX.take([32, 16], F32)
        xtr = CW.take([128, 26, 64], F32)
        Boh, Bxa, Brt, Bxtr, Bxd = Buf(), Buf(), Buf(), Buf(), Buf()
        d_t1, d_t2 = d_g1, d_g2
        dma("sp", oh_s, oh_d, writes=[Boh], dsem=d_t1)
        for h in range(4):
            S.op("dve", lambda e: e.memset(rt_s[:, 0:15], 1.0), writes=[Brt])
            dma("sp", rt_s[0:31, 0:15], rpbT_d[l, h], writes=[Brt], dsem=d_t2)
            for c in range(8):
                bk, Bb = next_bank()
                S.op("pe", (lambda bk, c: lambda e: e.matmul(bk[0:15, :], lhsT=rt_s[:, 0:15], rhs=oh_s[:, c * 512:(c + 1) * 512],
                                                              start=True, stop=True))(bk, c), reads=[Brt, Boh], writes=[Bb])
                S.op("act", (lambda bk, c: lambda e: e.activation(out=xa_s[:, c * 512:(c + 1) * 512], in_=bk[0:15, :], func=AF.Exp))(bk, c),
                     writes=[Bb, Bxa])
            dma("sp", xd_d[h], xa_s, reads=[Bxa], writes=[Bxd], dsem=d_t1)
            S.op("dve", lambda e: e.memset(xtr, 0.0), writes=[Bxtr])
            src = xd_d[h].rearrange("a (k q) -> k a q", k=64)
            dma("sp", xtr[0:64, 0:15, :], src, reads=[Bxd], writes=[Bxtr], dsem=d_t2)
            dma("sp", xtr[64:128, 1:16, :], src, reads=[Bxd], writes=[Bxtr], dsem=d_t2)
            S.op("dve", lambda e: e.tensor_copy(out=xtr[:, 16:26, :], in_=xtr[:, 3:13, :]), reads=[Bxtr], writes=[Bxtr])
            S.op("dve", lambda e: e.memset(xtr[:, 16, :], 0.0), writes=[Bxtr])
            S.op("dve", lambda e: e.memset(xtr[64:128, 17, :], 0.0), writes=[Bxtr])
            S.op("dve", lambda e: e.memset(xtr[0:64, 25, :], 0.0), writes=[Bxtr])
            dma("sp", ctab_d[h], xtr.rearrange("p a b -> p (a b)"), reads=[Bxtr], writes=[Bxd], dsem=d_t1)
        S.barrier()

        for s in range(nseq):
            h_src = x_d[s] if l == 0 else hscr_d[s]
            h_dst = y_d[s] if l == nlayers - 1 else hscr_d[s]

            S.barrier()
            CX.reset()
            hb = [CX.take([128, D], F32) for _ in range(2)]
            ub = [CX.take([128, D], BF16) for _ in range(2)]
            junk = CX.take([128, D], BF16)
            Bjunk = Buf("junk")
            Rhb = Ring(S, hb, "hb")
            Rub = Ring(S, ub, "ub", with_dsem=False)
            bcast_load(G1[:], g_in_d[l, :], D, Bg1, d_g1)
            Bact = [Buf(f"act{t}") for t in range(NT)]

            def norm_transpose(src_rows, gtile, Bgt, t, pre=None):
                hbt, Bh, dh = Rhb.next()
                dma("sp", hbt, src_rows, writes=[Bh], dsem=dh)
                ss = stat_take(1)
                rr = stat_take(1)
                Bss, Br = Buf(), Buf()
                S.op("act", lambda e: e.activation(out=junk, in_=hbt, func=AF.Square, accum_out=ss),
                     reads=[Bh], writes=[Bjunk, Bss])
                rstd(ss, rr, D, Bss, Br)
                ubt, Bu, _ = Rub.next()
                S.op("dve", lambda e: e.scalar_tensor_tensor(out=ubt, in0=hbt, scalar=rr, in1=gtile[:], op0=ALU.mult, op1=ALU.mult),
                     reads=[Bh, Br, Bgt], writes=[Bu])
                run_deferred()

                def tr_part():
                    for half in range(2):
                        tb, Bt = next_tb()
                        for k in range(8):
                            kc = half * 8 + k
                            S.op("pe", (lambda tb, k, kc: lambda e: e.transpose(out=tb[:, k * 128:(k + 1) * 128], in_=ubt[:, kc * 128:(kc + 1) * 128],
                                                                                identity=ident[:]))(tb, k, kc), reads=[Bu, Bconst], writes=[Bt])
                        dst = actT[:, half * 8:(half + 1) * 8, t * 128:(t + 1) * 128]
                        S.op("act", (lambda tb, dst: lambda e: e.copy(out=dst, in_=tb.rearrange("p (a b) -> p a b", a=8)))(tb, dst),
                             writes=[Bt, Bact[t]])
                DQ.append(tr_part)

            for t in range(NT):
                norm_transpose(h_src[t * 128:(t + 1) * 128, :], G1, Bg1, t)
            run_deferred()

            S.barrier()
            CX.reset()
            CW.reset()
            wbuf = [CW.take([128, 16, 512], BF16) for _ in range(2)]
            Rw = Ring(S, wbuf, "w")
            stageT = CX.take([128, 4, L], BF16)
            Bstage = Buf("stageT")
            d_stage = S.dsem("stg")
            latT = [CX.take([128, 4, L], BF16) for _ in range(2)]
            Blat = [Buf("lat0"), Buf("lat1")]
            krT = CX.take([128, L], BF16)
            Bkr = Buf("krT")
            xn = [CX.take([128, 512], BF16) for _ in range(3)]
            Rxn = Ring(S, xn, "xn", with_dsem=False)
            zst = [CX.take([128, 512], F32) for _ in range(2)]
            Rz = Ring(S, zst, "z")
            vst = [CX.take([128, 512], BF16) for _ in range(2)]
            Rv = Ring(S, vst, "v")
            xr = CX.take([128, 128], F32)
            ropet = CX.take([128, 4, 32], F32)
            xrr4 = CX.take([128, 4, 128], BF16)
            junk = CX.take([128, 512], BF16)
            Bxr, Bropet = Buf(), Buf()
            Rxrr = Ring(S, [xrr4[:, i, :] for i in range(4)], "xrr", with_dsem=False)
            S.op("dve", lambda e: e.memset(xrr4[:, :, 64:128], 0.0), writes=Rxrr.bufs)
            Bsz, BvA, BvB, BvC, BfA, BfC = Buf("sz"), Buf("vA"), Buf("vB"), Buf("vC"), Buf("fA"), Buf("fC")
            Bqn, Bqr, Bkn = Buf("qn"), Buf("qr"), Buf("kn")
            Bksc = Buf("kscale")

            def load_w(src_ap, ncols, nk):
                wb, Bw, dw = Rw.next()
                dst = wb[:, 0:nk, 0:ncols]
                dma("pool", dst, src_ap, writes=[Bw], dsem=dw)
                return wb, Bw

            def proj_mm(bk, Bb, lhs_of_kc, Blhs, wb, Bw, ncols, nk):
                for kc in range(nk):
                    lhs = lhs_of_kc(kc)
                    S.op("pe", lambda e: e.matmul(bk[:, 0:ncols], lhsT=lhs, rhs=wb[:, kc, 0:ncols],
                                                  start=(kc == 0), stop=(kc == nk - 1)),
                         reads=[Blhs, Bw], writes=[Bb])

            def transposes_to(srcs, dst_ap, Bsrc, Bdst, wacc=False, defer=True):
                if defer:
                    DQ.append(lambda: transposes_to(srcs, dst_ap, Bsrc, Bdst, wacc=wacc, defer=False))
                    return
                tb, Bt = next_tb()
                n = len(srcs)
                for k, sap in enumerate(srcs):
                    S.op("pe", (lambda k, sap: lambda e: e.transpose(out=tb[:, k * 128:(k + 1) * 128], in_=sap, identity=ident[:]))(k, sap),
                         reads=[Bsrc, Bconst], writes=[Bt])
                if n == 1:
                    src = tb[:, 0:128]
                else:
                    src = tb[:, 0:n * 128].rearrange("p (a b) -> p a b", a=n)
                if wacc:
                    S.op("act", lambda e: e.copy(out=dst_ap, in_=src), writes=[Bt], wacc=[Bdst])
                else:
                    S.op("act", lambda e: e.copy(out=dst_ap, in_=src), writes=[Bt, Bdst])

            in_w_l = w_in_d[l].rearrange("(kc p) n -> p kc n", p=128)
            col0 = 0
            next_w = load_w(in_w_l[:, :, 0:len(IN_BLOCKS[0][1])], len(IN_BLOCKS[0][1]), 16)
            for bi, (btype, cols) in enumerate(IN_BLOCKS):
                ncols = len(cols)
                wb, Bw = next_w
                if bi + 1 < len(IN_BLOCKS):
                    nn = len(IN_BLOCKS[bi + 1][1])
                    next_w = load_w(in_w_l[:, :, col0 + ncols:col0 + ncols + nn], nn, 16)
                if btype == "QK":
                    if bi == 0:
                        gl = [g_aq] * 4
                    elif bi == 1:
                        gl = [g_aq, g_aq, g_ak, g_ak]
                    elif bi == 2:
                        gl = [g_cq] * 4
                    else:
                        gl = [g_ck] * 4
                if btype == "VKR":
                    S.op("dve", lambda e: e.memset(krT[64:128, :], 0.0), writes=[Bkr])
                def p2_tile(t, bi=bi, btype=btype, wb=wb, Bw=Bw, ncols=ncols, gl=(gl if btype == "QK" else None)):
                    bk, Bb = next_bank()
                    proj_mm(bk, Bb, lambda kc: actT[:, kc, t * 128:(t + 1) * 128], Bact[t], wb, Bw, ncols, 16)
                    run_deferred()
                    tsl = slice(t * 128, (t + 1) * 128)
                    if btype == "QK":
                        ss = stat_take(4)
                        rr = stat_take(4)
                        Bss, Br = Buf(), Buf()
                        for u in range(4):
                            S.op("act", (lambda u: lambda e: e.activation(out=junk[:, 0:128], in_=bk[:, u * 128:(u + 1) * 128], func=AF.Square,
                                                                         accum_out=ss[:, u:u + 1]))(u), writes=[Bb, Bss, Bjunk])
                        rstd(ss, rr, 128, Bss, Br)
                        xnt, Bxn, _ = Rxn.next()
                        for u in range(4):
                            S.op("dve", (lambda u: lambda e: e.scalar_tensor_tensor(out=xnt[:, u * 128:(u + 1) * 128], in0=bk[:, u * 128:(u + 1) * 128],
                                                                                   scalar=rr[:, u:u + 1], in1=gl[u][:], op0=ALU.mult, op1=ALU.mult))(u),
                                 reads=[Br, Bgs], writes=[Bb, Bxn])
                        transposes_to([xnt[:, u * 128:(u + 1) * 128] for u in range(4)], stageT[:, :, tsl], Bxn, Bstage, wacc=True)
                    elif btype == "LAT":
                        which = bi - 4
                        gt = g_bcq if which == 0 else g_bckv
                        ss = stat_take(1)
                        rr = stat_take(1)
                        Bss, Br = Buf(), Buf()
                        S.op("act", lambda e: e.activation(out=junk, in_=bk, func=AF.Square, accum_out=ss), writes=[Bb, Bss, Bjunk])
                        rstd(ss, rr, 512, Bss, Br)
                        xnt, Bxn, _ = Rxn.next()
                        S.op("dve", lambda e: e.scalar_tensor_tensor(out=xnt, in0=bk, scalar=rr, in1=gt[:], op0=ALU.mult, op1=ALU.mult),
                             reads=[Br, Bgs], writes=[Bb, Bxn])
                        transposes_to([xnt[:, u * 128:(u + 1) * 128] for u in range(4)], latT[which][:, :, tsl], Bxn, Blat[which], wacc=True)
                    elif btype == "VKR":
                        vt, Bv, dv = Rv.next()
                        S.op("act", lambda e: e.copy(out=vt[:, 0:256], in_=bk[:, 0:256]), writes=[Bb, Bv])
                        dma("sp", vA_d[tsl, :], vt[:, 0:256], reads=[Bv], wacc=[BvA], dsem=dv)
                        S.op("act", lambda e: e.activation(out=junk[:, 0:64], in_=bk[:, 256:320], func=AF.Square, accum_out=ss_kr[:, t:t + 1]),
                             writes=[Bb, Bjunk], wacc=[Bkr])
                        S.op("dve", lambda e: e.tensor_tensor(out=xr[:, 0:64], in0=bk[:, 256:320], in1=g_bk[:, 128:192], op=ALU.mult),
                             reads=[Bgs], writes=[Bb, Bxr])
                        xrr, Bxrr, _ = Rxrr.next()
                        rope(xr, xrr, t, ropet, Bxr, Bxrr, Bropet)
                        transposes_to([xrr], krT[:, tsl], Bxrr, Bkr, wacc=True)
                    elif btype == "V":
                        vt, Bv, dv = Rv.next()
                        S.op("act", lambda e: e.copy(out=vt, in_=bk), writes=[Bb, Bv])
                        dma("sp", vC_d[tsl, :], vt, reads=[Bv], wacc=[BvC], dsem=dv)
                    elif btype == "Z":
                        zt, Bz, dz = Rz.next()
                        S.op("act", lambda e: e.activation(out=zt, in_=bk, func=AF.Silu), writes=[Bb, Bz])
                        zc0 = (bi - 8) * 512
                        dma("sp", sz_d[tsl, zc0:zc0 + 512], zt, reads=[Bz], wacc=[Bsz], dsem=dz)
                for t in range(NT):
                    p2_tile(t)
                if btype == "QK":
                    if bi == 0:
                        dst, Bd = featA_d[0:4], BfA
                    elif bi == 1:
                        dst, Bd = featA_d[4:8], BfA
                    elif bi == 2:
                        dst, Bd = featC_d[0:4], BfC
                    else:
                        dst, Bd = featC_d[4:8], BfC
                    DQ.append((lambda dst, Bd: lambda: dma("sp", dst.rearrange("u p n -> p u n"), stageT, reads=[Bstage], wacc=[Bd], dsem=d_stage))(dst, Bd))
                col0 += ncols

            uq_l = w_uq_d[l].rearrange("(kc p) n -> p kc n", p=128)
            ukv_l = w_ukv_d[l].rearrange("(kc p) n -> p kc n", p=128)
            stq_n = stageT[:, 0:2, :]
            stq_r = stageT[:, 2:4, :]
            stk = stageT[:, 0:3, :]
            p2b_blocks = ([("q", qb, uq_l[:, :, qb * 384:(qb + 1) * 384]) for qb in range(3)] +
                          [("k", kb, ukv_l[:, :, kb * 384:(kb + 1) * 384]) for kb in range(2)] +
                          [("v", vb, ukv_l[:, :, 768 + vb * 384:768 + (vb + 1) * 384]) for vb in range(2)])
            next_w = load_w(p2b_blocks[0][2], 384, 4)

            def p2b_q_tile(qb, t, wb, Bw):
                tsl = slice(t * 128, (t + 1) * 128)
                bk, Bb = next_bank()
                proj_mm(bk, Bb, lambda kc: latT[0][:, kc, tsl], Blat[0], wb, Bw, 384, 4)
                run_deferred()
                ss = stat_take(2)
                rr = stat_take(2)
                Bss, Br = Buf(), Buf()
                for hh in range(2):
                    S.op("act", (lambda hh: lambda e: e.activation(out=junk[:, 0:192], in_=bk[:, hh * 192:(hh + 1) * 192], func=AF.Square,
                                                                  accum_out=ss[:, hh:hh + 1]))(hh), writes=[Bb, Bss, Bjunk])
                rstd(ss, rr, 192, Bss, Br)
                xnt, Bxn, _ = Rxn.next()
                for hh in range(2):
                    c0 = hh * 192
                    S.op("dve", (lambda hh, c0: lambda e: e.scalar_tensor_tensor(out=xnt[:, hh * 128:(hh + 1) * 128], in0=bk[:, c0:c0 + 128],
                                                                                scalar=rr[:, hh:hh + 1], in1=g_bq[:, 0:128], op0=ALU.mult, op1=ALU.mult))(hh, c0),
                         reads=[Br, Bgs], writes=[Bb, Bxn])
                transposes_to([xnt[:, hh * 128:(hh + 1) * 128] for hh in range(2)], stq_n[:, :, tsl], Bxn, Bstage, wacc=True)
                xrrs = []
                for hh in range(2):
                    c0 = hh * 192 + 128
                    S.op("dve", (lambda hh, c0: lambda e: e.scalar_tensor_tensor(out=xr[:, 0:64], in0=bk[:, c0:c0 + 64],
                                                                                scalar=rr[:, hh:hh + 1], in1=g_bq[:, 128:192], op0=ALU.mult, op1=ALU.mult))(hh, c0),
                         reads=[Br, Bgs], writes=[Bb, Bxr])
                    xrr, Bxrr, _ = Rxrr.next()
                    rope(xr, xrr, t, ropet, Bxr, Bxrr, Bropet)
                    xrrs.append((xrr, Bxrr))

                def rope_tr(xrrs=xrrs):
                    tb, Bt = next_tb()
                    for k_, (xrr, Bxrr) in enumerate(xrrs):
                        S.op("pe", (lambda k_, xrr: lambda e: e.transpose(out=tb[:, k_ * 128:(k_ + 1) * 128], in_=xrr, identity=ident[:]))(k_, xrr),
                             reads=[Bxrr, Bconst], writes=[Bt])
                    S.op("act", lambda e: e.copy(out=stq_r[:, :, tsl], in_=tb[:, 0:256].rearrange("p (a b) -> p a b", a=2)), writes=[Bt], wacc=[Bstage])
                DQ.append(rope_tr)

            def p2b_k_tile(kb, t, wb, Bw):
                tsl = slice(t * 128, (t + 1) * 128)
                bk, Bb = next_bank()
                proj_mm(bk, Bb, lambda kc: latT[1][:, kc, tsl], Blat[1], wb, Bw, 384, 4)
                run_deferred()
                ss = stat_take(3)
                Bss = Buf()
                for hh in range(3):
                    S.op("act", (lambda hh: lambda e: e.activation(out=junk[:, 0:128], in_=bk[:, hh * 128:(hh + 1) * 128], func=AF.Square,
                                                                  accum_out=ss[:, hh:hh + 1]))(hh), writes=[Bb, Bss, Bjunk])
                S.op("dve", lambda e: e.tensor_scalar(out=ss, in0=ss, scalar1=ss_kr[:, t:t + 1], scalar2=None, op0=ALU.add),
                     reads=[Bss, Bkr], writes=[Bss])
                rstd(ss, kscale[:, t, kb * 3:kb * 3 + 3], 192, Bss, Bksc, mul=192.0 ** -0.5)
                xnt, Bxn, _ = Rxn.next()
                S.op("dve", lambda e: e.tensor_tensor(out=xnt[:, 0:384].rearrange("p (a b) -> p a b", a=3), in0=bk[:, 0:384].rearrange("p (a b) -> p a b", a=3),
                                                      in1=g_bk[:, 0:128].unsqueeze(1).to_broadcast([128, 3, 128]), op=ALU.mult),
                     reads=[Bgs], writes=[Bb, Bxn])
                transposes_to([xnt[:, hh * 128:(hh + 1) * 128] for hh in range(3)], stk[:, :, tsl], Bxn, Bstage, wacc=True)

            def p2b_v_tile(vb, t, wb, Bw):
                tsl = slice(t * 128, (t + 1) * 128)
                bk, Bb = next_bank()
                proj_mm(bk, Bb, lambda kc: latT[1][:, kc, tsl], Blat[1], wb, Bw, 384, 4)
                run_deferred()
                vt, Bv, dv = Rv.next()
                S.op("act", lambda e: e.copy(out=vt[:, 0:384], in_=bk[:, 0:384]), writes=[Bb, Bv])
                dma("sp", vB_d[tsl, vb * 384:(vb + 1) * 384], vt[:, 0:384], reads=[Bv], wacc=[BvB], dsem=dv)

            for pi, (kind, idx, _src) in enumerate(p2b_blocks):
                wb, Bw = next_w
                if pi + 1 < len(p2b_blocks):
                    next_w = load_w(p2b_blocks[pi + 1][2], 384, 4)
                for t in range(NT):
                    if kind == "q":
                        p2b_q_tile(idx, t, wb, Bw)
                    elif kind == "k":
                        p2b_k_tile(idx, t, wb, Bw)
                    else:
                        p2b_v_tile(idx, t, wb, Bw)
                if kind == "q":
                    DQ.append((lambda idx: lambda: (dma("sp", qn_d[idx * 2:idx * 2 + 2].rearrange("u p n -> p u n"), stq_n, reads=[Bstage], wacc=[Bqn], dsem=d_stage),
                                                   dma("sp", qr_d[idx * 2:idx * 2 + 2].rearrange("u p n -> p u n"), stq_r, reads=[Bstage], wacc=[Bqr], dsem=d_stage)))(idx))
                elif kind == "k":
                    DQ.append((lambda idx: lambda: dma("sp", kn_d[idx * 3:idx * 3 + 3].rearrange("u p n -> p u n"), stk, reads=[Bstage], wacc=[Bkn], dsem=d_stage))(idx))
            run_deferred()

            S.barrier()
            CX.reset()
            _ = CX.take([128, 4, L], BF16)
            _ = [CX.take([128, 4, L], BF16) for _ in range(2)]
            krT2 = CX.take([128, L], BF16)
            CX.reset()
            opnd = [[CX.take([128, L], BF16) for _ in range(3)] for _ in range(2)]
            szt = [CX.take([128, NT, 128], F32) for _ in range(2)]
            et = [CX.take([128, 640], F32) for _ in range(3)]
            assert CX.off <= 48 * 1024
            CX.off = 48 * 1024 + 4 * 1024
            ptile = [CX.take([128, 640], BF16) for _ in range(6)]
            yg = [CX.take([128, 128], BF16) for _ in range(4)]
            Ret = Ring(S, et, "et", with_dsem=False)
            Rpt = Ring(S, ptile, "pt", with_dsem=False)
            Ryg = Ring(S, yg, "yg", with_dsem=False)
            CW.reset()
            tabA_s = CW.take([128, 6, 384], F32)
            ctab_s = [CW.take([128, 26, 64], F32) for _ in range(2)]
            Btab = Buf("tabA")
            d_tab = S.dsem("tab")
            dma("sp", tabA_s, tabA_d.rearrange("h p n -> p h n"), writes=[Btab], dsem=d_tab)
            Bmix = [Buf(f"mix{i}") for i in range(NT)]
            Rq = Ring(S, [opnd[0][0], opnd[1][0]], "q")
            Rk = Ring(S, [opnd[0][1], opnd[1][1]], "k")
            Rqr = Ring(S, [opnd[0][2], opnd[1][2]], "qr")
            Rsz = Ring(S, szt, "sz")
            Rva = Ring(S, [vaug[0], vaug[1]], "va")
            Rct = Ring(S, ctab_s, "ct")
            mixT = actT
            acc_i = [0]

            def fin_dve(acc_ap, Bacc, szslice, Bszt, sink_col=None):
                r = stat_take(1)
                Br = Buf()
                if sink_col is not None:
                    S.op("dve", lambda e: e.tensor_scalar(out=r, in0=acc_ap[:, 128:129], scalar1=esink[:, sink_col:sink_col + 1], scalar2=None, op0=ALU.add),
                         reads=[Bgs], writes=[Bacc, Br])
                    S.op("dve", lambda e: e.reciprocal(out=r, in_=r), reads=[Br], writes=[Br])
                else:
                    S.op("dve", lambda e: e.reciprocal(out=r, in_=acc_ap[:, 128:129]), writes=[Bacc, Br])
                ygt, Byg, _ = Ryg.next()
                S.op("dve", lambda e: e.scalar_tensor_tensor(out=ygt, in0=acc_ap[:, 0:128], scalar=r, in1=szslice, op0=ALU.mult, op1=ALU.mult),
                     reads=[Br, Bszt], writes=[Bacc, Byg])
                return ygt, Byg

            def fin_tr(ygs, chunk, i0):
                tb, Bt = next_tb()
                n = len(ygs)
                for k_, (ygt, Byg) in enumerate(ygs):
                    S.op("pe", (lambda k_, ygt: lambda e: e.transpose(out=tb[:, k_ * 128:(k_ + 1) * 128], in_=ygt, identity=ident[:]))(k_, ygt),
                         reads=[Byg, Bconst], writes=[Bt])
                S.op("act", lambda e: e.copy(out=mixT[:, chunk, i0 * 128:(i0 + n) * 128], in_=tb[:, 0:n * 128]), writes=[Bt], wacc=[Bmix[i0 + k] for k in range(n)])

            def load_head(ring, src, extra_reads=()):
                ap, B, d = ring.next()
                dma("sp", ap, src, reads=list(extra_reads), writes=[B], dsem=d)
                return ap, B

            def load_v(src_cols, Bsrc):
                ap, B, d = Rva.next()
                dma("sp", ap[:, :, 0:128], src_cols.rearrange("(t p) c -> p t c", p=128), reads=[Bsrc], writes=[B], dsem=d)
                return ap, B

            def load_sz(c0):
                ap, B, d = Rsz.next()
                dma("sp", ap, sz_d[:, c0:c0 + 128].rearrange("(t p) c -> p t c", p=128), reads=[Bsz], writes=[B], dsem=d)
                return ap, B

            SC_A = 128.0 ** -0.5
            A_ops = {}

            def a_load(h):
                kvh = h // 3
                if h % 3 == 0:
                    A_ops[("k", kvh)] = load_head(Rk, featA_d[6 + kvh], [BfA])
                    A_ops[("v", kvh)] = load_v(vA_d[:, kvh * 128:(kvh + 1) * 128], BvA)
                A_ops[("q", h)] = load_head(Rq, featA_d[h], [BfA])
                A_ops[("sz", h)] = load_sz(h * 128)

            A_pts = {}

            def a_sA(h, j):
                kT, Bk = A_ops[("k", h // 3)]
                qT, Bq = A_ops[("q", h)]
                qlo, qhi = max(j - 1, 0), min(j + 1, NT - 1)
                nq = (qhi - qlo + 1) * 128
                tc0 = (qlo - (j - 1)) * 128
                sb_, Bsb = bank[j % 2], Bbank[j % 2]
                S.op("pe", lambda e: e.matmul(sb_[:, 0:nq], lhsT=kT[:, j * 128:(j + 1) * 128], rhs=qT[:, qlo * 128:qlo * 128 + nq], start=True, stop=True),
                     reads=[Bk, Bq], writes=[Bsb])
                e_t, Be, _ = Ret.next()
                S.op("act", lambda e: e.activation(out=e_t[:, 0:nq], in_=sb_[:, 0:nq], func=AF.Exp, scale=SC_A), writes=[Bsb, Be])
                p_t, Bp, _ = Rpt.next()
                S.op("dve", lambda e: e.tensor_tensor(out=p_t[:, 0:nq], in0=e_t[:, 0:nq], in1=tabA_s[:, h, tc0:tc0 + nq], op=ALU.mult),
                     reads=[Be, Btab], writes=[Bp])
                A_pts[(h, j)] = (p_t, Bp, qlo)

            A_yg = {}

            def a_sB(h, i):
                va, Bva = A_ops[("v", h // 3)]
                szh, Bszh = A_ops[("sz", h)]
                ab, Bab = bank[4 + acc_i[0] % 2], Bbank[4 + acc_i[0] % 2]
                acc_i[0] += 1
                js = [jj for jj in (i - 1, i, i + 1) if 0 <= jj < NT]
                for n_, jj in enumerate(js):
                    p_t, Bp, qlo = A_pts[(h, jj)]
                    co = (i - qlo) * 128
                    S.op("pe", (lambda p_t, co, jj, n_: lambda e: e.matmul(ab[:, 0:129], lhsT=p_t[:, co:co + 128], rhs=va[:, jj, :],
                                                                          start=(n_ == 0), stop=(n_ == len(js) - 1)))(p_t, co, jj, n_),
                         reads=[Bp, Bva], writes=[Bab])
                A_yg[(h, i)] = fin_dve(ab, Bab, szh[:, i, :], Bszh, sink_col=h)
                A_pts.pop((h, i - 1), None)

            def a_sD(h, i):
                fin_tr([A_yg.pop((h, i))], h, i)

            nA = 6 * NT
            a_load(0)
            for st in range(nA + 4):
                if st < nA:
                    h, j = divmod(st, NT)
                    if j == 4 and h + 1 < 6:
                        a_load(h + 1)
                    a_sA(h, j)
                if 0 <= st - 2 < nA:
                    a_sB(*divmod(st - 2, NT))
                if 0 <= st - 4 < nA:
                    a_sD(*divmod(st - 4, NT))

            B_ops = {}

            def b_load(h):
                B_ops[("q", h)] = load_head(Rq, qn_d[h], [Bqn])
                B_ops[("qr", h)] = load_head(Rqr, qr_d[h], [Bqr])
                B_ops[("k", h)] = load_head(Rk, kn_d[h], [Bkn])
                B_ops[("v", h)] = load_v(vB_d[:, h * 128:(h + 1) * 128], BvB)
                B_ops[("sz", h)] = load_sz(768 + h * 128)

            B_pts = {}

            def b_sA(h, qt, j):
                qT, Bq = B_ops[("q", h)]
                qrT, Bqr_ = B_ops[("qr", h)]
                kT, Bk = B_ops[("k", h)]
                qs = slice(qt * 512, (qt + 1) * 512)
                ks = slice(j * 128, (j + 1) * 128)
                sb_, Bsb = bank[j % 2], Bbank[j % 2]
                S.op("pe", lambda e: e.matmul(sb_, lhsT=kT[:, ks], rhs=qT[:, qs], start=True, stop=False), reads=[Bk, Bq], writes=[Bsb])
                S.op("pe", lambda e: e.matmul(sb_, lhsT=krT2[:, ks], rhs=qrT[:, qs], start=False, stop=True), reads=[Bkr, Bqr_], writes=[Bsb])
                p_t, Bp, _ = Rpt.next()
                S.op("act", lambda e: e.activation(out=p_t[:, 0:512], in_=sb_, func=AF.Exp, scale=kscale[:, j, h:h + 1]),
                     reads=[Bksc], writes=[Bsb, Bp])
                B_pts[(h, qt, j)] = (p_t, Bp)

            B_yg = {}

            def b_sB(h, qt, j):
                va, Bva = B_ops[("v", h)]
                szh, Bszh = B_ops[("sz", h)]
                p_t, Bp = B_pts.pop((h, qt, j))
                a0 = 2 + 2 * (qt % 2)
                for qb in range(4):
                    ab, Bab = bank[a0 + qb // 2], Bbank[a0 + qb // 2]
                    co = (qb % 2) * 130
                    S.op("pe", (lambda ab, qb, co: lambda e: e.matmul(ab[:, co:co + 129], lhsT=p_t[:, qb * 128:(qb + 1) * 128], rhs=va[:, j, :],
                                                                     start=(j == 0 and qb % 2 == 0), stop=(j == NT - 1), skip_group_check=True))(ab, qb, co),
                         reads=[Bp, Bva], writes=[Bab])
                if j == NT - 1:
                    ygs = []
                    for qb in range(4):
                        ab, Bab = bank[a0 + qb // 2], Bbank[a0 + qb // 2]
                        co = (qb % 2) * 130
                        ygs.append(fin_dve(ab[:, co:co + 129], Bab, szh[:, qt * 4 + qb, :], Bszh))
                    B_yg[(h, qt)] = ygs

            def b_sD(h, qt):
                fin_tr(B_yg.pop((h, qt)), 6 + h, qt * 4)

            nB = 6 * 4 * NT
            b_load(0)
            for st in range(nB + 4):
                if st < nB:
                    h, rem = divmod(st, 4 * NT)
                    qt, j = divmod(rem, NT)
                    if rem == 8 and h + 1 < 6:
                        b_load(h + 1)
                    b_sA(h, qt, j)
                if 0 <= st - 1 < nB:
                    h, rem = divmod(st - 1, 4 * NT)
                    b_sB(h, *divmod(rem, NT))
                if 0 <= st - 4 < nB:
                    h, rem = divmod(st - 4, 4 * NT)
                    qt, j = divmod(rem, NT)
                    if j == NT - 1:
                        b_sD(h, qt)

            SC_C = 128.0 ** -0.5
            C_ops = {}

            def c_load(h):
                C_ops[("q", h)] = load_head(Rq, featC_d[h], [BfC])
                C_ops[("k", h)] = load_head(Rk, featC_d[4 + h], [BfC])
                C_ops[("v", h)] = load_v(vC_d[:, h * 128:(h + 1) * 128], BvC)
                C_ops[("sz", h)] = load_sz(1536 + h * 128)
                ct, Bct, dct = Rct.next()
                dma("sp", ct.rearrange("p a b -> p (a b)"), ctab_d[h], writes=[Bct], dsem=dct)
                C_ops[("ct", h)] = (ct, Bct)

            def c_js(i):
                if i <= 1:
                    return [3, 2, 1, 0]
                if i >= NT - 2:
                    return [15, 14, 13, 12]
                return [i + 2, i + 1, i, i - 1, i - 2]

            C_pts = {}

            def c_sA(h, i):
                qT, Bq = C_ops[("q", h)]
                kT, Bk = C_ops[("k", h)]
                ct, Bct = C_ops[("ct", h)]
                js = c_js(i)
                nj = len(js)
                if 2 <= i <= NT - 3:
                    tsl_ = ct[:, 16:26, :]
                else:
                    b0 = 7 - 2 * (js[0] - i)
                    tsl_ = ct[:, b0:b0 + 2 * nj, :]
                sbase = (i % 2) * 1024
                sb_ = PS[:, sbase:sbase + nj * 128]
                Bs_ = [Bbank[(i % 2) * 2], Bbank[(i % 2) * 2 + 1]]
                for n_, jj in enumerate(js):
                    S.op("pe", (lambda n_, jj: lambda e: e.matmul(PS[:, sbase + n_ * 128:sbase + (n_ + 1) * 128], lhsT=kT[:, jj * 128:(jj + 1) * 128],
                                                                 rhs=qT[:, i * 128:(i + 1) * 128], start=True, stop=True))(n_, jj),
                         reads=[Bk, Bq], writes=Bs_)
                e_t, Be, _ = Ret.next()
                S.op("act", lambda e: e.activation(out=e_t[:, 0:nj * 128], in_=sb_, func=AF.Exp, scale=SC_C), writes=Bs_ + [Be])
                p_t, Bp, _ = Rpt.next()
                S.op("dve", lambda e: e.tensor_tensor(out=p_t[:, 0:nj * 128], in0=e_t[:, 0:nj * 128], in1=tsl_.rearrange("p a b -> p (a b)"), op=ALU.mult),
                     reads=[Be, Bct], writes=[Bp])
                C_pts[(h, i)] = (p_t, Bp)

            C_yg = {}

            def c_sB(h, i):
                va, Bva = C_ops[("v", h)]
                szh, Bszh = C_ops[("sz", h)]
                p_t, Bp = C_pts.pop((h, i))
                js = c_js(i)
                nj = len(js)
                ab, Bab = bank[4 + acc_i[0] % 2], Bbank[4 + acc_i[0] % 2]
                acc_i[0] += 1
                for n_, jj in enumerate(js):
                    S.op("pe", (lambda n_, jj: lambda e: e.matmul(ab[:, 0:129], lhsT=p_t[:, n_ * 128:(n_ + 1) * 128], rhs=va[:, jj, :],
                                                                 start=(n_ == 0), stop=(n_ == nj - 1)))(n_, jj),
                         reads=[Bp, Bva], writes=[Bab])
                C_yg[(h, i)] = fin_dve(ab, Bab, szh[:, i, :], Bszh)

            def c_sD(h, i):
                fin_tr([C_yg.pop((h, i))], 12 + h, i)

            nC = 4 * NT
            c_load(0)
            for st in range(nC + 4):
                if st < nC:
                    h, i = divmod(st, NT)
                    if i == 4 and h + 1 < 4:
                        c_load(h + 1)
                    c_sA(h, i)
                if 0 <= st - 1 < nC:
                    c_sB(*divmod(st - 1, NT))
                if 0 <= st - 3 < nC:
                    c_sD(*divmod(st - 3, NT))

            S.barrier()
            CX.reset()
            CW.reset()
            wbuf = [CW.take([128, 16, 512], BF16) for _ in range(2)]
            Rw = Ring(S, wbuf, "w")
            hsl = [CX.take([128, 512], F32) for _ in range(4)]
            Rhs = Ring(S, hsl, "hs")
            ost = [CX.take([128, 512], F32) for _ in range(3)]
            Ros = Ring(S, ost, "os")
            Bh = Buf("hscr")
            wo_l = w_out_d[l].rearrange("(kc p) n -> p kc n", p=128)
            next_w = load_w(wo_l[:, :, 0:512], 512, 16)
            for c in range(4):
                wb, Bw = next_w
                if c < 3:
                    next_w = load_w(wo_l[:, :, (c + 1) * 512:(c + 2) * 512], 512, 16)
                for t in range(NT):
                    tsl = slice(t * 128, (t + 1) * 128)
                    csl = slice(c * 512, (c + 1) * 512)
                    hs, Bhs, dhs = Rhs.next()
                    dma("sp", hs, h_src[tsl, csl], writes=[Bhs], dsem=dhs)
                    bk, Bb = next_bank()
                    proj_mm(bk, Bb, lambda kc: mixT[:, kc, tsl], Bmix[t], wb, Bw, 512, 16)
                    o, Bo, do = Ros.next()
                    S.op("dve", lambda e: e.tensor_tensor(out=o, in0=bk, in1=hs, op=ALU.add), reads=[Bhs], writes=[Bb, Bo])
                    dma("sp", hscr_d[s][tsl, csl], o, reads=[Bo], wacc=[Bh], dsem=do)

            S.barrier()
            CX.reset()
            CW.reset()
            hb = [CX.take([128, D], F32) for _ in range(2)]
            ub = [CX.take([128, D], BF16) for _ in range(2)]
            junk = CX.take([128, D], BF16)
            pT = CX.take([128, 2, L], BF16)
            wpp = CX.take([128, 2, D], BF16)
            pf = [CX.take([128, 256], F32) for _ in range(2)]
            pb = [CX.take([128, 256], BF16) for _ in range(2)]
            Rhb = Ring(S, hb, "hb")
            Rub = Ring(S, ub, "ub", with_dsem=False)
            Rpf = Ring(S, pf, "pf")
            Rpb = Ring(S, pb, "pb", with_dsem=False)
            Bjunk = Buf("junk")
            BpT, Bwpp = Buf("pT"), Buf("wpp")
            d_wpp = S.dsem("wpp")
            dma("pool", wpp, w_pp_d[l].rearrange("(kc p) n -> p kc n", p=128), writes=[Bwpp], dsem=d_wpp)
            bcast_load(G2[:], g_ple_d[l, :], D, Bg2, d_g2)
            bcast_load(G1[:], g_post_d[l, :], D, Bg1, d_g1)
            Bact = [Buf(f"act{t}") for t in range(NT)]
            Brp = Buf("rstd_p")
            for t in range(NT):
                tsl = slice(t * 128, (t + 1) * 128)
                norm_transpose(hscr_d[s][tsl, :], G2, Bg2, t)
                pft, Bpf, dpf = Rpf.next()
                dma("sp", pft, p_d[l, s, tsl, :], writes=[Bpf], dsem=dpf)
                pbt, Bpb, _ = Rpb.next()
                S.op("dve", lambda e: e.tensor_copy(out=pbt, in_=pft), reads=[Bpf], writes=[Bpb])
                transposes_to([pbt[:, 0:128], pbt[:, 128:256]], pT[:, :, tsl], Bpb, BpT, wacc=True, defer=False)
                ss = stat_take(4)
                Bss = Buf()
                for c in range(4):
                    bk, Bb = next_bank()
                    for kc in range(2):
                        S.op("pe", (lambda bk, kc, c: lambda e: e.matmul(bk, lhsT=pT[:, kc, tsl], rhs=wpp[:, kc, c * 512:(c + 1) * 512],
                                                                         start=(kc == 0), stop=(kc == 1)))(bk, kc, c), reads=[BpT, Bwpp], writes=[Bb])
                    S.op("act", (lambda bk, c: lambda e: e.activation(out=junk[:, 0:512], in_=bk, func=AF.Square, accum_out=ss[:, c:c + 1]))(bk, c),
                         writes=[Bb, Bjunk, Bss])
                sst = stat_take(1)
                S.op("dve", lambda e: e.tensor_reduce(out=sst, in_=ss, axis=mybir.AxisListType.X, op=ALU.add), reads=[Bss], writes=[Bss])
                rstd(sst, rstd_p[:, t:t + 1], D, Bss, Brp)
            run_deferred()

            S.barrier()
            CX.reset()
            _ = [CX.take([128, D], F32) for _ in range(2)]
            _ = [CX.take([128, D], BF16) for _ in range(2)]
            _ = CX.take([128, D], BF16)
            pT = CX.take([128, 2, L], BF16)
            wpp = CX.take([128, 2, D], BF16)
            keep = CX.off
            CX.reset()
            hsl = [CX.take([128, 512], F32) for _ in range(4)]
            gat = [CX.take([128, 512], F32) for _ in range(2)]
            pn = [CX.take([128, 512], F32) for _ in range(2)]
            ost = [CX.take([128, 512], F32) for _ in range(3)]
            assert CX.off <= 28 * 1024
            Rhs = Ring(S, hsl, "hs")
            Rg = Ring(S, gat, "g", with_dsem=False)
            Rpn = Ring(S, pn, "pn", with_dsem=False)
            Ros = Ring(S, ost, "os")
            wbuf = [CW.take([128, 16, 512], BF16) for _ in range(2)]
            Rw = Ring(S, wbuf, "w")
            wg_l = w_gate_d[l].rearrange("(kc p) n -> p kc n", p=128)
            next_w = load_w(wg_l[:, :, 0:512], 512, 16)
            Bout = Buf("hout")
            for c in range(4):
                wb, Bw = next_w
                if c < 3:
                    next_w = load_w(wg_l[:, :, (c + 1) * 512:(c + 2) * 512], 512, 16)
                csl = slice(c * 512, (c + 1) * 512)
                for t in range(NT):
                    tsl = slice(t * 128, (t + 1) * 128)
                    hs, Bhs, dhs = Rhs.next()
                    dma("sp", hs, hscr_d[s][tsl, csl], reads=[Bh], writes=[Bhs], dsem=dhs)
                    bk, Bb = next_bank()
                    proj_mm(bk, Bb, lambda kc: actT[:, kc, tsl], Bact[t], wb, Bw, 512, 16)
                    bk2, Bb2 = next_bank()
                    for kc in range(2):
                        S.op("pe", (lambda bk2, kc: lambda e: e.matmul(bk2, lhsT=pT[:, kc, tsl], rhs=wpp[:, kc, csl], start=(kc == 0), stop=(kc == 1)))(bk2, kc),
                             reads=[BpT, Bwpp], writes=[Bb2])
                    gt_, Bgt_, _ = Rg.next()
                    S.op("act", lambda e: e.activation(out=gt_, in_=bk, func=AF.Sigmoid), writes=[Bb, Bgt_])
                    pn_, Bpn_, _ = Rpn.next()
                    S.op("dve", lambda e: e.scalar_tensor_tensor(out=pn_, in0=bk2, scalar=rstd_p[:, t:t + 1], in1=G1[:, csl], op0=ALU.mult, op1=ALU.mult),
                         reads=[Brp, Bg1], writes=[Bb2, Bpn_])
                    S.op("dve", lambda e: e.tensor_tensor(out=pn_, in0=pn_, in1=gt_, op=ALU.mult), reads=[Bgt_], writes=[Bpn_])
                    o, Bo, do = Ros.next()
                    S.op("dve", lambda e: e.tensor_tensor(out=o, in0=pn_, in1=hs, op=ALU.add), reads=[Bpn_, Bhs], writes=[Bo])
                    dma("sp", h_dst[tsl, csl], o, reads=[Bo], wacc=[Bout], dsem=do)

    S.emit()
    return nc, S


def _prep_shared(inp, tables):
    tabA, cos, sin, oh = tables
    f = lambda a: np.ascontiguousarray(np.asarray(a, dtype=np.float32))
    rpb = np.asarray(inp["c_rpb"], np.float32)
    rpbT = np.ascontiguousarray(rpb[:, :, ::-1, :].transpose(0, 1, 3, 2))
    return {
        "w_in": np.ascontiguousarray(np.asarray(inp["w_in"], np.float32)[:, :, IN_PERM]),
        "w_out": f(inp["w_out"]), "w_gate": f(inp["w_ple_gate"]), "w_pp": f(inp["w_ple_proj"]),
        "w_uq": f(inp["b_w_uq"]),
        "w_ukv": np.ascontiguousarray(np.asarray(inp["b_w_ukv"], np.float32)[:, :, UKV_PERM]),
        "g_in": f(inp["norm_in"]), "g_ple": f(inp["ple_norm"]), "g_post": f(inp["ple_post_norm"]),
        "g_aq": f(inp["a_q_norm"]), "g_ak": f(inp["a_k_norm"]), "g_cq": f(inp["c_q_norm"]), "g_ck": f(inp["c_k_norm"]),
        "g_bcq": f(inp["b_cq_norm"]), "g_bckv": f(inp["b_ckv_norm"]), "g_bq": f(inp["b_q_norm"]), "g_bk": f(inp["b_k_norm"]),
        "sink": f(inp["a_sink"]), "rpbT": rpbT,
        "ident": np.eye(128).astype(ml_dtypes.bfloat16),
        "tabA": tabA, "cosT": cos, "sinT": sin, "onehot": oh,
    }


_CACHE = {}


def kernel(**inputs):
    xp = np.asarray(inputs["x_prompt"], np.float32)
    xs = np.asarray(inputs["x_sample"], np.float32)
    pp = np.asarray(inputs["p_prompt"], np.float32)
    psm = np.asarray(inputs["p_sample"], np.float32)
    nB, nS = xp.shape[0], xs.shape[0]
    x_all = np.concatenate([xp, xs], axis=0)
    p_all = np.concatenate([pp, psm], axis=1)
    ntot = nB + nS
    ncores = 8
    slots = [[(c + 8 * k) % ntot if (c + 8 * k) < ntot else (c + 8 * k) % ntot for k in range(NSEQ)] for c in range(ncores)]
    if "nc" not in _CACHE:
        _CACHE["nc"] = build()[0]
        _CACHE["tables"] = _const_tables()
    nc = _CACHE["nc"]
    shared = _prep_shared(inputs, _CACHE["tables"])
    in_maps = []
    for c in range(ncores):
        m = dict(shared)
        m["x"] = np.ascontiguousarray(x_all[slots[c]])
        m["p"] = np.ascontiguousarray(p_all[:, slots[c]])
        in_maps.append(m)
    res = run_bass_kernel_spmd(nc, in_maps, core_ids=list(range(ncores)))
    y_all = np.zeros_like(x_all)
    done = set()
    for c in range(ncores):
        yc = res.results[c]["y"]
        for k, sidx in enumerate(slots[c]):
            if (c + 8 * k) < ntot and sidx not in done:
                y_all[sidx] = yc[k]
                done.add(sidx)
    return (y_all[:nB], y_all[nB:])
```

```python
import types
import numpy as np
from contextlib import ExitStack
import ml_dtypes
import concourse.bass as bass
import concourse.mybir as mybir
from concourse.bass_utils import run_bass_kernel_spmd

F32 = mybir.dt.float32
BF16 = mybir.dt.bfloat16
AF = mybir.ActivationFunctionType
ALU = mybir.AluOpType

L = 2048
D = 2048
NT = 16
DEPTH = 4
NSEQ = 3
EPS = 1e-6
IN_W = 5952

EPOCH = 30000


def _freeze(fn):
    if fn.__closure__ is None:
        return fn
    cells = tuple(types.CellType(c.cell_contents) for c in fn.__closure__)
    return types.FunctionType(fn.__code__, fn.__globals__, fn.__name__, fn.__defaults__, cells)


class Tk:
    __slots__ = ("sem", "val", "q", "dma")

    def __init__(self, sem, val, q, dma=False):
        self.sem = sem
        self.val = val
        self.q = q
        self.dma = dma


class Buf:
    __slots__ = ("name", "w", "r")

    def __init__(self, name=""):
        self.name = name
        self.w = {}
        self.r = {}


class DSem:
    __slots__ = ("h", "count")

    def __init__(self, h):
        self.h = h
        self.count = 0


class Sched:
    QS = ("pe", "act", "dve", "pool", "sp")

    def __init__(self, nc, es):
        self.nc = nc
        self.es = es
        self.qs = {k: [] for k in self.QS}
        self.esem = {}
        self.ecount = {}
        self.waited = {k: {} for k in self.QS}
        self.nsem = 0
        self.dsems = []
        self.dpool = []
        self.pool_i = 0
        for k in self.QS:
            self._new_epoch(k)

    def _alloc_sem(self, name):
        self.nsem += 1
        return self.es.enter_context(self.nc.semaphore(name))

    def _new_epoch(self, k):
        self.esem[k] = self._alloc_sem(f"e{k}{self.nsem}")
        self.ecount[k] = 0

    def dsem(self, name="d", persistent=False):
        if persistent:
            d = DSem(self._alloc_sem(f"{name}{self.nsem}"))
            self.dsems.append(d)
            return d
        if self.pool_i >= len(self.dpool):
            d = DSem(self._alloc_sem(f"dp{self.nsem}"))
            self.dsems.append(d)
            self.dpool.append(d)
        d = self.dpool[self.pool_i]
        self.pool_i += 1
        return d

    def op(self, q, fn, reads=(), writes=(), wacc=(), dsem=None):
        raw = {}
        oth = {}

        def add(d, t):
            k = id(t.sem)
            if k not in d or d[k].val < t.val:
                d[k] = t

        for b in reads:
            for t in b.w.values():
                add(raw, t)
        for b in writes:
            for t in b.w.values():
                add(oth, t)
            for t in b.r.values():
                add(oth, t)
        for b in wacc:
            for t in b.r.values():
                add(oth, t)
        waits = []
        wd = self.waited[q]
        for d, is_raw in ((raw, True), (oth, False)):
            for k, t in d.items():
                if t.q == q and (not t.dma) and dsem is None:
                    if not is_raw or q == "pe":
                        continue
                if wd.get(k, 0) >= t.val:
                    continue
                wd[k] = t.val
                waits.append((t.sem, t.val))
        if dsem is not None:
            dsem.count += 16
            tk = Tk(dsem.h, dsem.count, q, True)
            inc = (dsem.h, 16)
        else:
            if self.ecount[q] >= EPOCH:
                self._new_epoch(q)
            self.ecount[q] += 1
            tk = Tk(self.esem[q], self.ecount[q], q)
            inc = (self.esem[q], 1)
        self.qs[q].append((waits, _freeze(fn), inc))
        k = id(tk.sem)
        for b in writes:
            b.w = {k: tk}
            b.r = {}
        for b in wacc:
            b.w[k] = tk
        for b in reads:
            if k not in b.r or b.r[k].val < tk.val:
                b.r[k] = tk
        return tk

    def barrier(self):
        self.pool_i = 0
        tks = []
        for q in self.QS:
            if self.ecount[q] > 0:
                tks.append((self.esem[q], self.ecount[q], q))
        for d in self.dsems:
            if d.count > 0:
                tks.append((d.h, d.count, None))
        for q in self.QS:
            waits = []
            wd = self.waited[q]
            for (h, v, src) in tks:
                if src == q:
                    continue
                if wd.get(id(h), 0) >= v:
                    continue
                wd[id(h)] = v
                waits.append((h, v))
            if waits:
                self.qs[q].append((waits, None, None))

    def emit(self):
        nc = self.nc
        self.barrier()
        qs = self.qs

        def run(eng, lst):
            for waits, fn, inc in lst:
                for (s, v) in waits:
                    eng.wait_ge(s, v)
                if fn is not None:
                    fn(eng).then_inc(inc[0], inc[1])

        with nc.Block() as block:
            @block.tensor
            def _(e):
                run(e, qs["pe"])

            @block.scalar
            def _(e):
                run(e, qs["act"])

            @block.vector
            def _(e):
                run(e, qs["dve"])

            @block.gpsimd
            def _(e):
                run(e, qs["pool"])

            @block.sync
            def _(e):
                run(e, qs["sp"])


class Ring:
    def __init__(self, S, aps, name, with_dsem=True):
        self.aps = aps
        self.bufs = [Buf(f"{name}{i}") for i in range(len(aps))]
        self.ds = [S.dsem(name) for _ in aps] if with_dsem else [None] * len(aps)
        self.i = 0

    def next(self):
        k = self.i % len(self.aps)
        self.i += 1
        return self.aps[k], self.bufs[k], self.ds[k]


_O = dict(aq=0, ak=768, av=1024, az=1280, bcq=2048, bckv=2560, bkr=3072, bz=3136,
          cq=3904, ck=4416, cv=4928, cz=5440)


def _in_blocks():
    r = lambda a, n: list(range(a, a + n))
    blocks = [
        ("QK", r(_O["aq"], 512)),
        ("QK", r(_O["aq"] + 512, 256) + r(_O["ak"], 256)),
        ("QK", r(_O["cq"], 512)),
        ("QK", r(_O["ck"], 512)),
        ("LAT", r(_O["bcq"], 512)),
        ("LAT", r(_O["bckv"], 512)),
        ("VKR", r(_O["av"], 256) + r(_O["bkr"], 64)),
        ("V", r(_O["cv"], 512)),
        ("Z", r(_O["az"], 512)),
        ("Z", r(_O["az"] + 512, 256) + r(_O["bz"], 256)),
        ("Z", r(_O["bz"] + 256, 512)),
        ("Z", r(_O["cz"], 512)),
    ]
    return blocks


IN_BLOCKS = _in_blocks()
IN_PERM = np.concatenate([np.array(c) for _, c in IN_BLOCKS])
UKV_PERM = np.concatenate([np.arange(h * 256, h * 256 + 128) for h in range(6)] +
                          [np.arange(h * 256 + 128, h * 256 + 256) for h in range(6)])


def _const_tables():
    slopes = 2.0 ** (-8.0 * np.arange(1, 7, dtype=np.float64) / 6)
    kt = np.arange(128)[:, None]
    qo = np.arange(384)[None, :]
    delta = 128 + kt - qo
    tabA = np.zeros((6, 128, 384), np.float32)
    for h in range(6):
        tabA[h] = (np.exp(-slopes[h] * np.abs(delta)) * (np.abs(delta) <= 128)).astype(np.float32)
    half = 32
    inv = 10000.0 ** (-np.arange(half, dtype=np.float32) / half)
    ang = np.arange(L, dtype=np.float32)[:, None] * inv[None, :]
    cos = np.cos(ang).astype(np.float32)
    sin = np.sin(ang).astype(np.float32)
    kc = np.arange(64)[:, None]
    qc = np.arange(64)[None, :]
    cs = np.clip(qc - 8, 0, 48)
    valid = (kc >= cs) & (kc < cs + 16)
    oh = np.zeros((32, 64, 64), np.float32)
    b = 15 + kc - qc
    for bb in range(31):
        oh[bb] = ((b == bb) & valid).astype(np.float32)
    oh[31] = np.where(valid, 0.0, -30000.0)
    return tabA, cos, sin, oh.reshape(32, 4096)


def build(nseq=NSEQ, nlayers=DEPTH, debug=False):
    nc = bass.Bass("TRN2", target_bir_lowering=False)
    es = ExitStack()

    def din(name, shape, dt=F32):
        return nc.dram_tensor(name, list(shape), dt, kind="ExternalInput").ap()

    dbg_kind = "ExternalOutput" if debug else "Internal"

    def dscr(name, shape, dt=F32):
        return nc.dram_tensor(name, list(shape), dt, kind=dbg_kind).ap()

    x_d = din("x", [nseq, L, D])
    p_d = din("p", [DEPTH, nseq, L, 256])
    y_d = nc.dram_tensor("y", [nseq, L, D], F32, kind="ExternalOutput").ap()
    w_in_d = din("w_in", [DEPTH, D, IN_W])
    w_out_d = din("w_out", [DEPTH, D, D])
    w_gate_d = din("w_gate", [DEPTH, D, D])
    w_pp_d = din("w_pp", [DEPTH, 256, D])
    w_uq_d = din("w_uq", [DEPTH, 512, 1152])
    w_ukv_d = din("w_ukv", [DEPTH, 512, 1536])
    g_in_d = din("g_in", [DEPTH, D])
    g_ple_d = din("g_ple", [DEPTH, D])
    g_post_d = din("g_post", [DEPTH, D])
    g_aq_d = din("g_aq", [DEPTH, 128])
    g_ak_d = din("g_ak", [DEPTH, 128])
    g_cq_d = din("g_cq", [DEPTH, 128])
    g_ck_d = din("g_ck", [DEPTH, 128])
    g_bcq_d = din("g_bcq", [DEPTH, 512])
    g_bckv_d = din("g_bckv", [DEPTH, 512])
    g_bq_d = din("g_bq", [DEPTH, 192])
    g_bk_d = din("g_bk", [DEPTH, 192])
    sink_d = din("sink", [DEPTH, 6])
    rpbT_d = din("rpbT", [DEPTH, 4, 31, 15])
    ident_d = din("ident", [128, 128], BF16)
    tabA_d = din("tabA", [6, 128, 384])
    cos_d = din("cosT", [L, 32])
    sin_d = din("sinT", [L, 32])
    oh_d = din("onehot", [32, 4096])

    hscr_d = dscr("hscr", [nseq, L, D])
    featA_d = dscr("featA", [8, 128, L], BF16)
    featC_d = dscr("featC", [8, 128, L], BF16)
    qn_d = dscr("qn", [6, 128, L], BF16)
    qr_d = dscr("qr", [6, 128, L], BF16)
    kn_d = dscr("kn", [6, 128, L], BF16)
    vA_d = dscr("vA", [L, 256], BF16)
    vB_d = dscr("vB", [L, 768], BF16)
    vC_d = dscr("vC", [L, 512], BF16)
    sz_d = dscr("sz", [L, D])
    xd_d = dscr("xd", [4, 15, 4096])
    ctab_d = dscr("ctab", [4, 128, 26 * 64])

    S = Sched(nc, es)

    def sbt(name, shape, dt):
        return nc.alloc_sbuf_tensor(name, list(shape), dt)

    ident = sbt("ident_s", [128, 128], BF16)
    cosT = sbt("cosT_s", [128, NT, 32], F32)
    sinT = sbt("sinT_s", [128, NT, 32], F32)
    G1 = sbt("G1", [128, D], F32)
    G2 = sbt("G2", [128, D], F32)
    g_aq = sbt("g_aq_s", [128, 128], F32)
    g_ak = sbt("g_ak_s", [128, 128], F32)
    g_cq = sbt("g_cq_s", [128, 128], F32)
    g_ck = sbt("g_ck_s", [128, 128], F32)
    g_bq = sbt("g_bq_s", [128, 192], F32)
    g_bk = sbt("g_bk_s", [128, 192], F32)
    g_bcq = sbt("g_bcq_s", [128, 512], F32)
    g_bckv = sbt("g_bckv_s", [128, 512], F32)
    esink = sbt("esink", [128, 8], F32)
    kscale = sbt("kscale", [128, NT, 6], F32)
    ss_kr = sbt("ss_kr", [128, NT], F32)
    rstd_p = sbt("rstd_p", [128, NT], F32)
    stat = sbt("stat", [128, 512], F32)
    vaug = [sbt(f"vaug{i}", [128, NT, 129], BF16) for i in range(2)]
    ARENA_ACT = sbt("arena_act", [128, 16 * L], BF16)
    ARENA_W = sbt("arena_w", [128, 16 * 1024], BF16)
    XBYTES = 72 * 1024
    ARENA_X = sbt("arena_x", [128, XBYTES // 4], F32)

    actT = ARENA_ACT[:, :].rearrange("p (c n) -> p c n", c=16)
    PS = nc.alloc_psum_tensor("ps", [128, 3072], F32)
    PT = nc.alloc_psum_tensor("pt", [128, 2048], BF16)
    bank = [PS[:, i * 512:(i + 1) * 512] for i in range(6)]
    Bbank = [Buf(f"bank{i}") for i in range(6)]
    tbank = [PT[:, i * 1024:(i + 1) * 1024] for i in range(2)]
    Btb = [Buf(f"tb{i}") for i in range(2)]

    class Carver:
        def __init__(self, arena_f32, nbytes):
            self.a = arena_f32
            self.n = nbytes
            self.off = 0

        def reset(self):
            self.off = 0

        def take(self, shape, dt):
            esz = 4 if dt == F32 else 2
            nel = int(np.prod(shape[1:]))
            nb = nel * esz
            nb_al = (nb + 31) // 32 * 32
            assert self.off + nb_al <= self.n, ("arena overflow", self.off, nb_al, self.n)
            v = self.a[:, self.off // 4:(self.off + nb_al) // 4]
            if dt != F32:
                v = v.bitcast(dt)
            v = v[0:shape[0], 0:nel]
            if len(shape) == 3:
                v = v.rearrange("p (a b) -> p a b", a=shape[1])
            self.off += nb_al
            return v

    CX = Carver(ARENA_X, XBYTES)
    ARENA_W32 = ARENA_W[:, :].bitcast(F32)
    CW = Carver(ARENA_W32, 32 * 1024)

    Bconst = Buf("const")
    d_const = S.dsem("const", persistent=True)

    def dma(q, out, in_, reads=(), writes=(), wacc=(), dsem=None):
        assert dsem is not None
        return S.op(q, lambda e: e.dma_start(out=out, in_=in_), reads=reads, writes=writes,
                    wacc=wacc, dsem=dsem)

    dma("sp", ident[:], ident_d, wacc=[Bconst], dsem=d_const)
    dma("sp", cosT[:], cos_d.rearrange("(t p) i -> p t i", p=128), wacc=[Bconst], dsem=d_const)
    dma("sp", sinT[:], sin_d.rearrange("(t p) i -> p t i", p=128), wacc=[Bconst], dsem=d_const)
    for i in range(2):
        S.op("dve", (lambda i: lambda e: e.memset(vaug[i][:, :, 128:129], 1.0))(i), wacc=[Bconst])
    S.barrier()

    stat_i = [0]

    def stat_take(n):
        if stat_i[0] + n > 512:
            stat_i[0] = 0
        a = stat[:, stat_i[0]:stat_i[0] + n]
        stat_i[0] += n
        return a

    def rstd_from_ss(ss_ap, n, count, eps=EPS, extra_scale=None, out=None):
        r = out if out is not None else stat_take(n)
        b = Buf("rstd")
        return r, b

    bank_i = [0]

    def next_bank():
        k = bank_i[0] % 6
        bank_i[0] += 1
        return bank[k], Bbank[k]

    tb_i = [0]

    def next_tb():
        k = tb_i[0] % 2
        tb_i[0] += 1
        return tbank[k], Btb[k]

    DQ = []

    def dq_new_group():
        DQ.append([])

    def defer(thunk):
        if not DQ:
            DQ.append([])
        DQ[-1].append(thunk)

    def run_deferred(keep=0):
        while len(DQ) > keep:
            for th in DQ.pop(0):
                th()

    _orig_barrier = S.barrier

    def _checked_barrier():
        assert not DQ, "deferred work pending at barrier"
        _orig_barrier()
    S.barrier = _checked_barrier

    def bcast_load(dst, src_row, n, buf, ds):
        dma("sp", dst, src_row.partition_broadcast(128), writes=[buf], dsem=ds)

    Bg1, Bg2, Bgs = Buf("G1"), Buf("G2"), Buf("gsm")
    d_g1, d_g2, d_gs = S.dsem("g1", True), S.dsem("g2", True), S.dsem("gs", True)

    ROPE_ENG = "pool"

    def rope(xr, out_bf, t, tmp, Bx, Bout, Btmp, n=1):
        c = cosT[:, t, :].unsqueeze(1).to_broadcast([128, n, 32])
        s_ = sinT[:, t, :].unsqueeze(1).to_broadcast([128, n, 32])
        x1 = xr[:, :, 0:32]
        x2 = xr[:, :, 32:64]
        q = ROPE_ENG
        S.op(q, lambda e: e.tensor_tensor(out=tmp[:, 0], in0=x1, in1=c, op=ALU.mult), reads=[Bx, Bconst], writes=[Btmp])
        S.op(q, lambda e: e.tensor_tensor(out=tmp[:, 1], in0=x2, in1=s_, op=ALU.mult), reads=[Bx], writes=[Btmp])
        S.op(q, lambda e: e.tensor_tensor(out=tmp[:, 2], in0=x1, in1=s_, op=ALU.mult), reads=[Bx], writes=[Btmp])
        S.op(q, lambda e: e.tensor_tensor(out=tmp[:, 3], in0=x2, in1=c, op=ALU.mult), reads=[Bx], writes=[Btmp])
        S.op(q, lambda e: e.tensor_tensor(out=out_bf[:, :, 0:32], in0=tmp[:, 0], in1=tmp[:, 1], op=ALU.subtract), reads=[Btmp], writes=[Bout])
        S.op(q, lambda e: e.tensor_tensor(out=out_bf[:, :, 32:64], in0=tmp[:, 2], in1=tmp[:, 3], op=ALU.add), reads=[Btmp], writes=[Bout])

    def sumsq(ps_ap, junk_ap, ss_ap, Bps, Bjunk, Bss):
        S.op("act", lambda e: e.activation(out=junk_ap, in_=ps_ap, func=AF.Square, accum_out=ss_ap),
             writes=[Bps, Bjunk, Bss])

    def rstd(ss_ap, r_ap, count, Bss, Br, mul=None):
        m2 = 1.0 if mul is None else float(mul) ** 2
        S.op("act", lambda e: e.activation(out=r_ap, in_=ss_ap, func=AF.Sqrt, scale=1.0 / (count * m2), bias=EPS / m2),
             reads=[Bss], writes=[Br])
        S.op("dve", lambda e: e.reciprocal(out=r_ap, in_=r_ap), reads=[Br], writes=[Br])

    for l in range(nlayers):
        S.barrier()
        for (dst, src, n) in ((g_aq, g_aq_d, 128), (g_ak, g_ak_d, 128), (g_cq, g_cq_d, 128), (g_ck, g_ck_d, 128),
                              (g_bq, g_bq_d, 192), (g_bk, g_bk_d, 192), (g_bcq, g_bcq_d, 512), (g_bckv, g_bckv_d, 512)):
            dma("sp", dst[:], src[l, :].partition_broadcast(128), wacc=[Bgs], dsem=d_gs)
        dma("sp", esink[:, 0:6], sink_d[l, :].partition_broadcast(128), wacc=[Bgs], dsem=d_gs)
        S.barrier()
        S.op("act", lambda e: e.activation(out=esink[:, 0:6], in_=esink[:, 0:6], func=AF.Exp), writes=[Bgs])
        CX.reset()
        CW.reset()
        oh_s = CX.take([32, 4096], F32)
        xa_s = CX.take([15, 4096], F32)
        rt_s = CX.take([32, 16], F32)
        xtr = CW.take([128, 26, 64], F32)
        Boh, Bxa, Brt, Bxtr, Bxd = Buf(), Buf(), Buf(), Buf(), Buf()
        d_t1, d_t2 = d_g1, d_g2
        dma("sp", oh_s, oh_d, writes=[Boh], dsem=d_t1)
        for h in range(4):
            S.op("dve", lambda e: e.memset(rt_s[:, 0:15], 1.0), writes=[Brt])
            dma("sp", rt_s[0:31, 0:15], rpbT_d[l, h], writes=[Brt], dsem=d_t2)
            for c in range(8):
                bk, Bb = next_bank()
                S.op("pe", (lambda bk, c: lambda e: e.matmul(bk[0:15, :], lhsT=rt_s[:, 0:15], rhs=oh_s[:, c * 512:(c + 1) * 512],
                                                              start=True, stop=True))(bk, c), reads=[Brt, Boh], writes=[Bb])
                S.op("act", (lambda bk, c: lambda e: e.activation(out=xa_s[:, c * 512:(c + 1) * 512], in_=bk[0:15, :], func=AF.Exp))(bk, c),
                     writes=[Bb, Bxa])
            dma("sp", xd_d[h], xa_s, reads=[Bxa], writes=[Bxd], dsem=d_t1)
            S.op("dve", lambda e: e.memset(xtr, 0.0), writes=[Bxtr])
            src = xd_d[h].rearrange("a (k q) -> k a q", k=64)
            dma("sp", xtr[0:64, 0:15, :], src, reads=[Bxd], writes=[Bxtr], dsem=d_t2)
            dma("sp", xtr[64:128, 1:16, :], src, reads=[Bxd], writes=[Bxtr], dsem=d_t2)
            S.op("dve", lambda e: e.tensor_copy(out=xtr[:, 16:26, :], in_=xtr[:, 3:13, :]), reads=[Bxtr], writes=[Bxtr])
            S.op("dve", lambda e: e.memset(xtr[:, 16, :], 0.0), writes=[Bxtr])
            S.op("dve", lambda e: e.memset(xtr[64:128, 17, :], 0.0), writes=[Bxtr])
            S.op("dve", lambda e: e.memset(xtr[0:64, 25, :], 0.0), writes=[Bxtr])
            dma("sp", ctab_d[h], xtr.rearrange("p a b -> p (a b)"), reads=[Bxtr], writes=[Bxd], dsem=d_t1)
        S.barrier()

        for s in range(nseq):
            h_src = x_d[s] if l == 0 else hscr_d[s]
            h_dst = y_d[s] if l == nlayers - 1 else hscr_d[s]

            S.barrier()
            CX.reset()
            hb = [CX.take([128, D], F32) for _ in range(4)]
            ub = [CX.take([128, D], BF16) for _ in range(2)]
            junk = CX.take([128, D], BF16)
            Bjunk = Buf("junk")
            Rhb = Ring(S, hb, "hb")
            Rub = Ring(S, ub, "ub", with_dsem=False)
            bcast_load(G1[:], g_in_d[l, :], D, Bg1, d_g1)
            Bact = [Buf(f"act{t}") for t in range(NT)]

            def norm_transpose(src_rows, gtile, Bgt, t, pre=None):
                hbt, Bh, dh = Rhb.next()
                dma("sp", hbt, src_rows, writes=[Bh], dsem=dh)
                ss = stat_take(1)
                rr = stat_take(1)
                Bss, Br = Buf(), Buf()
                S.op("act", lambda e: e.activation(out=junk, in_=hbt, func=AF.Square, accum_out=ss),
                     reads=[Bh], writes=[Bjunk, Bss])
                rstd(ss, rr, D, Bss, Br)
                ubt, Bu, _ = Rub.next()
                S.op("dve", lambda e: e.scalar_tensor_tensor(out=ubt, in0=hbt, scalar=rr, in1=gtile[:], op0=ALU.mult, op1=ALU.mult),
                     reads=[Bh, Br, Bgt], writes=[Bu])
                run_deferred()

                def tr_part():
                    for half in range(2):
                        tb, Bt = next_tb()
                        for k in range(8):
                            kc = half * 8 + k
                            S.op("pe", (lambda tb, k, kc: lambda e: e.transpose(out=tb[:, k * 128:(k + 1) * 128], in_=ubt[:, kc * 128:(kc + 1) * 128],
                                                                                identity=ident[:]))(tb, k, kc), reads=[Bu, Bconst], writes=[Bt])
                        dst = actT[:, half * 8:(half + 1) * 8, t * 128:(t + 1) * 128]
                        if half == 0:
                            S.op("act", (lambda tb, dst: lambda e: e.copy(out=dst, in_=tb.rearrange("p (a b) -> p a b", a=8)))(tb, dst),
                                 writes=[Bt], wacc=[Bact[t]])
                        else:
                            S.op("dve", (lambda tb, dst: lambda e: e.tensor_copy(out=dst, in_=tb.rearrange("p (a b) -> p a b", a=8)))(tb, dst),
                                 writes=[Bt], wacc=[Bact[t]])
                dq_new_group()
                defer(tr_part)

            for t in range(NT):
                norm_transpose(h_src[t * 128:(t + 1) * 128, :], G1, Bg1, t)
            run_deferred()

            S.barrier()
            CX.reset()
            CW.reset()
            wbuf = [CW.take([128, 16, 512], BF16) for _ in range(2)]
            Rw = Ring(S, wbuf, "w")
            stageT = CX.take([128, 4, L], BF16)
            Bstage = Buf("stageT")
            d_stage = S.dsem("stg")
            latT = [CX.take([128, 4, L], BF16) for _ in range(2)]
            Blat = [Buf("lat0"), Buf("lat1")]
            krT = CX.take([128, L], BF16)
            Bkr = Buf("krT")
            xn = [CX.take([128, 512], BF16) for _ in range(4)]
            Rxn = Ring(S, xn, "xn", with_dsem=False)
            zst = [CX.take([128, 512], F32) for _ in range(2)]
            Rz = Ring(S, zst, "z")
            vst = [CX.take([128, 512], BF16) for _ in range(2)]
            Rv = Ring(S, vst, "v")
            xr2 = [CX.take([128, 2, 64], F32) for _ in range(3)]
            ropet2 = [CX.take([128, 4, 2 * 32], F32).rearrange("p a (n c) -> p a n c", n=2) for _ in range(3)]
            xrr4 = CX.take([128, 8, 128], BF16)
            junk = CX.take([128, 512], BF16)
            Rxr = Ring(S, xr2, "xr", with_dsem=False)
            Rrt = Ring(S, ropet2, "rt", with_dsem=False)
            Rxrr = Ring(S, [xrr4[:, 2 * i:2 * i + 2, :] for i in range(4)], "xrr", with_dsem=False)
            S.op("dve", lambda e: e.memset(xrr4[:, :, 64:128], 0.0), writes=Rxrr.bufs)
            Bsz, BvA, BvB, BvC, BfA, BfC = Buf("sz"), Buf("vA"), Buf("vB"), Buf("vC"), Buf("fA"), Buf("fC")
            Bqn, Bqr, Bkn = Buf("qn"), Buf("qr"), Buf("kn")
            Bksc = Buf("kscale")

            def load_w(src_ap, ncols, nk):
                wb, Bw, dw = Rw.next()
                dst = wb[:, 0:nk, 0:ncols]
                dma("pool", dst, src_ap, writes=[Bw], dsem=dw)
                return wb, Bw

            def proj_mm(bk, Bb, lhs_of_kc, Blhs, wb, Bw, ncols, nk):
                for kc in range(nk):
                    lhs = lhs_of_kc(kc)
                    S.op("pe", lambda e: e.matmul(bk[:, 0:ncols], lhsT=lhs, rhs=wb[:, kc, 0:ncols],
                                                  start=(kc == 0), stop=(kc == nk - 1)),
                         reads=[Blhs, Bw], writes=[Bb])

            def transposes_to(srcs, dst_ap, Bsrc, Bdst, wacc=False, later=True):
                if later:
                    defer(lambda: transposes_to(srcs, dst_ap, Bsrc, Bdst, wacc=wacc, later=False))
                    return
                tb, Bt = next_tb()
                n = len(srcs)
                for k, sap in enumerate(srcs):
                    S.op("pe", (lambda k, sap: lambda e: e.transpose(out=tb[:, k * 128:(k + 1) * 128], in_=sap, identity=ident[:]))(k, sap),
                         reads=[Bsrc, Bconst], writes=[Bt])
                if n == 1:
                    src = tb[:, 0:128]
                else:
                    src = tb[:, 0:n * 128].rearrange("p (a b) -> p a b", a=n)
                if wacc:
                    S.op("act", lambda e: e.copy(out=dst_ap, in_=src), writes=[Bt], wacc=[Bdst])
                else:
                    S.op("act", lambda e: e.copy(out=dst_ap, in_=src), writes=[Bt, Bdst])

            in_w_l = w_in_d[l].rearrange("(kc p) n -> p kc n", p=128)
            col0 = 0
            next_w = load_w(in_w_l[:, :, 0:len(IN_BLOCKS[0][1])], len(IN_BLOCKS[0][1]), 16)
            for bi, (btype, cols) in enumerate(IN_BLOCKS):
                ncols = len(cols)
                wb, Bw = next_w
                if bi + 1 < len(IN_BLOCKS):
                    nn = len(IN_BLOCKS[bi + 1][1])
                    next_w = load_w(in_w_l[:, :, col0 + ncols:col0 + ncols + nn], nn, 16)
                if btype == "QK":
                    if bi == 0:
                        gl = [g_aq] * 4
                    elif bi == 1:
                        gl = [g_aq, g_aq, g_ak, g_ak]
                    elif bi == 2:
                        gl = [g_cq] * 4
                    else:
                        gl = [g_ck] * 4
                if btype == "VKR":
                    S.op("dve", lambda e: e.memset(krT[64:128, :], 0.0), writes=[Bkr])
                def p2_tile(t, bi=bi, btype=btype, wb=wb, Bw=Bw, ncols=ncols, gl=(gl if btype == "QK" else None)):
                    bk, Bb = next_bank()
                    proj_mm(bk, Bb, lambda kc: actT[:, kc, t * 128:(t + 1) * 128], Bact[t], wb, Bw, ncols, 16)
                    run_deferred(keep=1)
                    dq_new_group()
                    tsl = slice(t * 128, (t + 1) * 128)
                    if btype == "QK":
                        ss = stat_take(4)
                        rr = stat_take(4)
                        Bss, Br = Buf(), Buf()
                        for u in range(4):
                            S.op("act", (lambda u: lambda e: e.activation(out=junk[:, 0:128], in_=bk[:, u * 128:(u + 1) * 128], func=AF.Square,
                                                                         accum_out=ss[:, u:u + 1]))(u), writes=[Bb, Bss, Bjunk])
                        rstd(ss, rr, 128, Bss, Br)
                        xnt, Bxn, _ = Rxn.next()
                        for u in range(4):
                            S.op("dve", (lambda u: lambda e: e.scalar_tensor_tensor(out=xnt[:, u * 128:(u + 1) * 128], in0=bk[:, u * 128:(u + 1) * 128],
                                                                                   scalar=rr[:, u:u + 1], in1=gl[u][:], op0=ALU.mult, op1=ALU.mult))(u),
                                 reads=[Br, Bgs], writes=[Bb, Bxn])
                        transposes_to([xnt[:, u * 128:(u + 1) * 128] for u in range(4)], stageT[:, :, tsl], Bxn, Bstage, wacc=True)
                    elif btype == "LAT":
                        which = bi - 4
                        gt = g_bcq if which == 0 else g_bckv
                        ss = stat_take(1)
                        rr = stat_take(1)
                        Bss, Br = Buf(), Buf()
                        S.op("act", lambda e: e.activation(out=junk, in_=bk, func=AF.Square, accum_out=ss), writes=[Bb, Bss, Bjunk])
                        rstd(ss, rr, 512, Bss, Br)
                        xnt, Bxn, _ = Rxn.next()
                        S.op("dve", lambda e: e.scalar_tensor_tensor(out=xnt, in0=bk, scalar=rr, in1=gt[:], op0=ALU.mult, op1=ALU.mult),
                             reads=[Br, Bgs], writes=[Bb, Bxn])
                        transposes_to([xnt[:, u * 128:(u + 1) * 128] for u in range(4)], latT[which][:, :, tsl], Bxn, Blat[which], wacc=True)
                    elif btype == "VKR":
                        vt, Bv, dv = Rv.next()
                        S.op("act", lambda e: e.copy(out=vt[:, 0:256], in_=bk[:, 0:256]), writes=[Bb, Bv])
                        dma("sp", vA_d[tsl, :], vt[:, 0:256], reads=[Bv], wacc=[BvA], dsem=dv)
                        S.op("act", lambda e: e.activation(out=junk[:, 0:64], in_=bk[:, 256:320], func=AF.Square, accum_out=ss_kr[:, t:t + 1]),
                             writes=[Bb, Bjunk], wacc=[Bkr])
                        xr, Bxr, _ = Rxr.next()
                        ropet, Bropet, _ = Rrt.next()
                        S.op("dve", lambda e: e.tensor_tensor(out=xr[:, 0, :], in0=bk[:, 256:320], in1=g_bk[:, 128:192], op=ALU.mult),
                             reads=[Bgs], writes=[Bb, Bxr])
                        xrr, Bxrr, _ = Rxrr.next()
                        rope(xr[:, 0:1, :], xrr[:, 0:1, :], t, ropet[:, :, 0:1, :], Bxr, Bxrr, Bropet, n=1)
                        transposes_to([xrr[:, 0, :]], krT[:, tsl], Bxrr, Bkr, wacc=True)
                    elif btype == "V":
                        vt, Bv, dv = Rv.next()
                        S.op("act", lambda e: e.copy(out=vt, in_=bk), writes=[Bb, Bv])
                        dma("sp", vC_d[tsl, :], vt, reads=[Bv], wacc=[BvC], dsem=dv)
                    elif btype == "Z":
                        zt, Bz, dz = Rz.next()
                        S.op("act", lambda e: e.activation(out=zt, in_=bk, func=AF.Silu), writes=[Bb, Bz])
                        zc0 = (bi - 8) * 512
                        dma("sp", sz_d[tsl, zc0:zc0 + 512], zt, reads=[Bz], wacc=[Bsz], dsem=dz)
                for t in range(NT):
                    p2_tile(t)
                if btype == "QK":
                    if bi == 0:
                        dst, Bd = featA_d[0:4], BfA
                    elif bi == 1:
                        dst, Bd = featA_d[4:8], BfA
                    elif bi == 2:
                        dst, Bd = featC_d[0:4], BfC
                    else:
                        dst, Bd = featC_d[4:8], BfC
                    defer((lambda dst, Bd: lambda: dma("sp", dst.rearrange("u p n -> p u n"), stageT, reads=[Bstage], wacc=[Bd], dsem=d_stage))(dst, Bd))
                col0 += ncols

            uq_l = w_uq_d[l].rearrange("(kc p) n -> p kc n", p=128)
            ukv_l = w_ukv_d[l].rearrange("(kc p) n -> p kc n", p=128)
            stq_n = stageT[:, 0:2, :]
            stq_r = stageT[:, 2:4, :]
            stk = stageT[:, 0:3, :]
            p2b_blocks = ([("q", qb, uq_l[:, :, qb * 384:(qb + 1) * 384]) for qb in range(3)] +
                          [("k", kb, ukv_l[:, :, kb * 384:(kb + 1) * 384]) for kb in range(2)] +
                          [("v", vb, ukv_l[:, :, 768 + vb * 384:768 + (vb + 1) * 384]) for vb in range(2)])
            next_w = load_w(p2b_blocks[0][2], 384, 4)

            def p2b_q_tile(qb, t, wb, Bw):
                tsl = slice(t * 128, (t + 1) * 128)
                bk, Bb = next_bank()
                proj_mm(bk, Bb, lambda kc: latT[0][:, kc, tsl], Blat[0], wb, Bw, 384, 4)
                run_deferred(keep=1)
                dq_new_group()
                ss = stat_take(2)
                rr = stat_take(2)
                Bss, Br = Buf(), Buf()
                for hh in range(2):
                    S.op("act", (lambda hh: lambda e: e.activation(out=junk[:, 0:192], in_=bk[:, hh * 192:(hh + 1) * 192], func=AF.Square,
                                                                  accum_out=ss[:, hh:hh + 1]))(hh), writes=[Bb, Bss, Bjunk])
                rstd(ss, rr, 192, Bss, Br)
                xnt, Bxn, _ = Rxn.next()
                for hh in range(2):
                    c0 = hh * 192
                    S.op("dve", (lambda hh, c0: lambda e: e.scalar_tensor_tensor(out=xnt[:, hh * 128:(hh + 1) * 128], in0=bk[:, c0:c0 + 128],
                                                                                scalar=rr[:, hh:hh + 1], in1=g_bq[:, 0:128], op0=ALU.mult, op1=ALU.mult))(hh, c0),
                         reads=[Br, Bgs], writes=[Bb, Bxn])
                transposes_to([xnt[:, hh * 128:(hh + 1) * 128] for hh in range(2)], stq_n[:, :, tsl], Bxn, Bstage, wacc=True)
                xr, Bxr, _ = Rxr.next()
                ropet, Bropet, _ = Rrt.next()
                for hh in range(2):
                    c0 = hh * 192 + 128
                    S.op("dve", (lambda hh, c0: lambda e: e.scalar_tensor_tensor(out=xr[:, hh, :], in0=bk[:, c0:c0 + 64],
                                                                                scalar=rr[:, hh:hh + 1], in1=g_bq[:, 128:192], op0=ALU.mult, op1=ALU.mult))(hh, c0),
                         reads=[Br, Bgs], writes=[Bb, Bxr])
                xrr, Bxrr, _ = Rxrr.next()
                rope(xr, xrr, t, ropet, Bxr, Bxrr, Bropet, n=2)
                transposes_to([xrr[:, 0, :], xrr[:, 1, :]], stq_r[:, :, tsl], Bxrr, Bstage, wacc=True)

            def p2b_k_tile(kb, t, wb, Bw):
                tsl = slice(t * 128, (t + 1) * 128)
                bk, Bb = next_bank()
                proj_mm(bk, Bb, lambda kc: latT[1][:, kc, tsl], Blat[1], wb, Bw, 384, 4)
                run_deferred(keep=1)
                dq_new_group()
                ss = stat_take(3)
                Bss = Buf()
                for hh in range(3):
                    S.op("act", (lambda hh: lambda e: e.activation(out=junk[:, 0:128], in_=bk[:, hh * 128:(hh + 1) * 128], func=AF.Square,
                                                                  accum_out=ss[:, hh:hh + 1]))(hh), writes=[Bb, Bss, Bjunk])
                S.op("dve", lambda e: e.tensor_scalar(out=ss, in0=ss, scalar1=ss_kr[:, t:t + 1], scalar2=None, op0=ALU.add),
                     reads=[Bss, Bkr], writes=[Bss])
                rstd(ss, kscale[:, t, kb * 3:kb * 3 + 3], 192, Bss, Bksc, mul=192.0 ** -0.5)
                xnt, Bxn, _ = Rxn.next()
                S.op("dve", lambda e: e.tensor_tensor(out=xnt[:, 0:384].rearrange("p (a b) -> p a b", a=3), in0=bk[:, 0:384].rearrange("p (a b) -> p a b", a=3),
                                                      in1=g_bk[:, 0:128].unsqueeze(1).to_broadcast([128, 3, 128]), op=ALU.mult),
                     reads=[Bgs], writes=[Bb, Bxn])
                transposes_to([xnt[:, hh * 128:(hh + 1) * 128] for hh in range(3)], stk[:, :, tsl], Bxn, Bstage, wacc=True)

            def p2b_v_tile(vb, t, wb, Bw):
                tsl = slice(t * 128, (t + 1) * 128)
                bk, Bb = next_bank()
                proj_mm(bk, Bb, lambda kc: latT[1][:, kc, tsl], Blat[1], wb, Bw, 384, 4)
                run_deferred(keep=1)
                dq_new_group()
                vt, Bv, dv = Rv.next()
                S.op("act", lambda e: e.copy(out=vt[:, 0:384], in_=bk[:, 0:384]), writes=[Bb, Bv])
                dma("sp", vB_d[tsl, vb * 384:(vb + 1) * 384], vt[:, 0:384], reads=[Bv], wacc=[BvB], dsem=dv)

            for pi, (kind, idx, _src) in enumerate(p2b_blocks):
                wb, Bw = next_w
                if pi + 1 < len(p2b_blocks):
                    next_w = load_w(p2b_blocks[pi + 1][2], 384, 4)
                for t in range(NT):
                    if kind == "q":
                        p2b_q_tile(idx, t, wb, Bw)
                    elif kind == "k":
                        p2b_k_tile(idx, t, wb, Bw)
                    else:
                        p2b_v_tile(idx, t, wb, Bw)
                if kind == "q":
                    defer((lambda idx: lambda: (dma("sp", qn_d[idx * 2:idx * 2 + 2].rearrange("u p n -> p u n"), stq_n, reads=[Bstage], wacc=[Bqn], dsem=d_stage),
                                                   dma("sp", qr_d[idx * 2:idx * 2 + 2].rearrange("u p n -> p u n"), stq_r, reads=[Bstage], wacc=[Bqr], dsem=d_stage)))(idx))
                elif kind == "k":
                    defer((lambda idx: lambda: dma("sp", kn_d[idx * 3:idx * 3 + 3].rearrange("u p n -> p u n"), stk, reads=[Bstage], wacc=[Bkn], dsem=d_stage))(idx))
            run_deferred()

            S.barrier()
            CX.reset()
            _ = CX.take([128, 4, L], BF16)
            _ = [CX.take([128, 4, L], BF16) for _ in range(2)]
            krT2 = CX.take([128, L], BF16)
            CX.reset()
            opnd = [[CX.take([128, L], BF16) for _ in range(3)] for _ in range(2)]
            szt = [CX.take([128, NT, 128], F32) for _ in range(2)]
            et = [CX.take([128, 640], F32) for _ in range(3)]
            assert CX.off <= 48 * 1024
            CX.off = 48 * 1024 + 4 * 1024
            ptile = [CX.take([128, 640], BF16) for _ in range(6)]
            yg = [CX.take([128, 128], BF16) for _ in range(4)]
            Ret = Ring(S, et, "et", with_dsem=False)
            Rpt = Ring(S, ptile, "pt", with_dsem=False)
            Ryg = Ring(S, yg, "yg", with_dsem=False)
            CW.reset()
            tabA_s = CW.take([128, 6, 384], F32)
            ctab_s = [CW.take([128, 26, 64], F32) for _ in range(2)]
            Btab = Buf("tabA")
            d_tab = S.dsem("tab")
            dma("sp", tabA_s, tabA_d.rearrange("h p n -> p h n"), writes=[Btab], dsem=d_tab)
            Bmix = [Buf(f"mix{i}") for i in range(NT)]
            Rq = Ring(S, [opnd[0][0], opnd[1][0]], "q")
            Rk = Ring(S, [opnd[0][1], opnd[1][1]], "k")
            Rqr = Ring(S, [opnd[0][2], opnd[1][2]], "qr")
            Rsz = Ring(S, szt, "sz")
            Rva = Ring(S, [vaug[0], vaug[1]], "va")
            Rct = Ring(S, ctab_s, "ct")
            mixT = actT
            acc_i = [0]

            def fin_dve(acc_ap, Bacc, szslice, Bszt, sink_col=None):
                r = stat_take(1)
                Br = Buf()
                if sink_col is not None:
                    S.op("dve", lambda e: e.tensor_scalar(out=r, in0=acc_ap[:, 128:129], scalar1=esink[:, sink_col:sink_col + 1], scalar2=None, op0=ALU.add),
                         reads=[Bgs], writes=[Bacc, Br])
                    S.op("dve", lambda e: e.reciprocal(out=r, in_=r), reads=[Br], writes=[Br])
                else:
                    S.op("dve", lambda e: e.reciprocal(out=r, in_=acc_ap[:, 128:129]), writes=[Bacc, Br])
                ygt, Byg, _ = Ryg.next()
                S.op("dve", lambda e: e.scalar_tensor_tensor(out=ygt, in0=acc_ap[:, 0:128], scalar=r, in1=szslice, op0=ALU.mult, op1=ALU.mult),
                     reads=[Br, Bszt], writes=[Bacc, Byg])
                return ygt, Byg

            def fin_tr(ygs, chunk, i0):
                tb, Bt = next_tb()
                n = len(ygs)
                for k_, (ygt, Byg) in enumerate(ygs):
                    S.op("pe", (lambda k_, ygt: lambda e: e.transpose(out=tb[:, k_ * 128:(k_ + 1) * 128], in_=ygt, identity=ident[:]))(k_, ygt),
                         reads=[Byg, Bconst], writes=[Bt])
                S.op("act", lambda e: e.copy(out=mixT[:, chunk, i0 * 128:(i0 + n) * 128], in_=tb[:, 0:n * 128]), writes=[Bt], wacc=[Bmix[i0 + k] for k in range(n)])

            def load_head(ring, src, extra_reads=()):
                ap, B, d = ring.next()
                dma("sp", ap, src, reads=list(extra_reads), writes=[B], dsem=d)
                return ap, B

            def load_v(src_cols, Bsrc):
                ap, B, d = Rva.next()
                dma("sp", ap[:, :, 0:128], src_cols.rearrange("(t p) c -> p t c", p=128), reads=[Bsrc], writes=[B], dsem=d)
                return ap, B

            def load_sz(c0):
                ap, B, d = Rsz.next()
                dma("sp", ap, sz_d[:, c0:c0 + 128].rearrange("(t p) c -> p t c", p=128), reads=[Bsz], writes=[B], dsem=d)
                return ap, B

            SC_A = 128.0 ** -0.5
            A_ops = {}

            def a_load(h):
                kvh = h // 3
                if h % 3 == 0:
                    A_ops[("k", kvh)] = load_head(Rk, featA_d[6 + kvh], [BfA])
                    A_ops[("v", kvh)] = load_v(vA_d[:, kvh * 128:(kvh + 1) * 128], BvA)
                A_ops[("q", h)] = load_head(Rq, featA_d[h], [BfA])
                A_ops[("sz", h)] = load_sz(h * 128)

            A_pts = {}

            def a_sA(h, j):
                kT, Bk = A_ops[("k", h // 3)]
                qT, Bq = A_ops[("q", h)]
                qlo, qhi = max(j - 1, 0), min(j + 1, NT - 1)
                nq = (qhi - qlo + 1) * 128
                tc0 = (qlo - (j - 1)) * 128
                sb_, Bsb = bank[j % 2], Bbank[j % 2]
                S.op("pe", lambda e: e.matmul(sb_[:, 0:nq], lhsT=kT[:, j * 128:(j + 1) * 128], rhs=qT[:, qlo * 128:qlo * 128 + nq], start=True, stop=True),
                     reads=[Bk, Bq], writes=[Bsb])
                e_t, Be, _ = Ret.next()
                S.op("act", lambda e: e.activation(out=e_t[:, 0:nq], in_=sb_[:, 0:nq], func=AF.Exp, scale=SC_A), writes=[Bsb, Be])
                p_t, Bp, _ = Rpt.next()
                S.op("pool" if j % 3 == 2 else "dve", lambda e: e.tensor_tensor(out=p_t[:, 0:nq], in0=e_t[:, 0:nq], in1=tabA_s[:, h, tc0:tc0 + nq], op=ALU.mult),
                     reads=[Be, Btab], writes=[Bp])
                A_pts[(h, j)] = (p_t, Bp, qlo)

            A_yg = {}

            def a_sB(h, i):
                va, Bva = A_ops[("v", h // 3)]
                szh, Bszh = A_ops[("sz", h)]
                ab, Bab = bank[4 + acc_i[0] % 2], Bbank[4 + acc_i[0] % 2]
                acc_i[0] += 1
                js = [jj for jj in (i - 1, i, i + 1) if 0 <= jj < NT]
                for n_, jj in enumerate(js):
                    p_t, Bp, qlo = A_pts[(h, jj)]
                    co = (i - qlo) * 128
                    S.op("pe", (lambda p_t, co, jj, n_: lambda e: e.matmul(ab[:, 0:129], lhsT=p_t[:, co:co + 128], rhs=va[:, jj, :],
                                                                          start=(n_ == 0), stop=(n_ == len(js) - 1)))(p_t, co, jj, n_),
                         reads=[Bp, Bva], writes=[Bab])
                A_yg[(h, i)] = fin_dve(ab, Bab, szh[:, i, :], Bszh, sink_col=h)
                A_pts.pop((h, i - 1), None)

            def a_sD(h, i):
                fin_tr([A_yg.pop((h, i))], h, i)

            nA = 6 * NT
            a_load(0)
            for st in range(nA + 4):
                if st < nA:
                    h, j = divmod(st, NT)
                    if j == 4 and h + 1 < 6:
                        a_load(h + 1)
                    a_sA(h, j)
                if 0 <= st - 2 < nA:
                    a_sB(*divmod(st - 2, NT))
                if 0 <= st - 4 < nA:
                    a_sD(*divmod(st - 4, NT))

            B_ops = {}

            def b_load(h):
                B_ops[("q", h)] = load_head(Rq, qn_d[h], [Bqn])
                B_ops[("qr", h)] = load_head(Rqr, qr_d[h], [Bqr])
                B_ops[("k", h)] = load_head(Rk, kn_d[h], [Bkn])
                B_ops[("v", h)] = load_v(vB_d[:, h * 128:(h + 1) * 128], BvB)
                B_ops[("sz", h)] = load_sz(768 + h * 128)

            B_pts = {}

            def b_sA(h, qt, j):
                qT, Bq = B_ops[("q", h)]
                qrT, Bqr_ = B_ops[("qr", h)]
                kT, Bk = B_ops[("k", h)]
                qs = slice(qt * 512, (qt + 1) * 512)
                ks = slice(j * 128, (j + 1) * 128)
                sb_, Bsb = bank[j % 2], Bbank[j % 2]
                S.op("pe", lambda e: e.matmul(sb_, lhsT=kT[:, ks], rhs=qT[:, qs], start=True, stop=False), reads=[Bk, Bq], writes=[Bsb])
                S.op("pe", lambda e: e.matmul(sb_, lhsT=krT2[:, ks], rhs=qrT[:, qs], start=False, stop=True), reads=[Bkr, Bqr_], writes=[Bsb])
                p_t, Bp, _ = Rpt.next()
                S.op("act", lambda e: e.activation(out=p_t[:, 0:512], in_=sb_, func=AF.Exp, scale=kscale[:, j, h:h + 1]),
                     reads=[Bksc], writes=[Bsb, Bp])
                B_pts[(h, qt, j)] = (p_t, Bp)

            B_yg = {}

            def b_sB(h, qt, j):
                va, Bva = B_ops[("v", h)]
                szh, Bszh = B_ops[("sz", h)]
                p_t, Bp = B_pts.pop((h, qt, j))
                a0 = 2 + 2 * (qt % 2)
                for qb in range(4):
                    ab, Bab = bank[a0 + qb // 2], Bbank[a0 + qb // 2]
                    co = (qb % 2) * 130
                    S.op("pe", (lambda ab, qb, co: lambda e: e.matmul(ab[:, co:co + 129], lhsT=p_t[:, qb * 128:(qb + 1) * 128], rhs=va[:, j, :],
                                                                     start=(j == 0 and qb % 2 == 0), stop=(j == NT - 1), skip_group_check=True))(ab, qb, co),
                         reads=[Bp, Bva], writes=[Bab])
                if j == NT - 1:
                    ygs = []
                    for qb in range(4):
                        ab, Bab = bank[a0 + qb // 2], Bbank[a0 + qb // 2]
                        co = (qb % 2) * 130
                        ygs.append(fin_dve(ab[:, co:co + 129], Bab, szh[:, qt * 4 + qb, :], Bszh))
                    B_yg[(h, qt)] = ygs

            def b_sD(h, qt):
                fin_tr(B_yg.pop((h, qt)), 6 + h, qt * 4)

            nB = 6 * 4 * NT
            b_load(0)
            for st in range(nB + 4):
                if st < nB:
                    h, rem = divmod(st, 4 * NT)
                    qt, j = divmod(rem, NT)
                    if rem == 8 and h + 1 < 6:
                        b_load(h + 1)
                    b_sA(h, qt, j)
                if 0 <= st - 1 < nB:
                    h, rem = divmod(st - 1, 4 * NT)
                    b_sB(h, *divmod(rem, NT))
                if 0 <= st - 4 < nB:
                    h, rem = divmod(st - 4, 4 * NT)
                    qt, j = divmod(rem, NT)
                    if j == NT - 1:
                        b_sD(h, qt)

            SC_C = 128.0 ** -0.5
            C_ops = {}

            def c_load(h):
                C_ops[("q", h)] = load_head(Rq, featC_d[h], [BfC])
                C_ops[("k", h)] = load_head(Rk, featC_d[4 + h], [BfC])
                C_ops[("v", h)] = load_v(vC_d[:, h * 128:(h + 1) * 128], BvC)
                C_ops[("sz", h)] = load_sz(1536 + h * 128)
                ct, Bct, dct = Rct.next()
                dma("sp", ct.rearrange("p a b -> p (a b)"), ctab_d[h], writes=[Bct], dsem=dct)
                C_ops[("ct", h)] = (ct, Bct)

            def c_js(i):
                if i <= 1:
                    return [3, 2, 1, 0]
                if i >= NT - 2:
                    return [15, 14, 13, 12]
                return [i + 2, i + 1, i, i - 1, i - 2]

            C_pts = {}

            def c_sA(h, i):
                qT, Bq = C_ops[("q", h)]
                kT, Bk = C_ops[("k", h)]
                ct, Bct = C_ops[("ct", h)]
                js = c_js(i)
                nj = len(js)
                if 2 <= i <= NT - 3:
                    tsl_ = ct[:, 16:26, :]
                else:
                    b0 = 7 - 2 * (js[0] - i)
                    tsl_ = ct[:, b0:b0 + 2 * nj, :]
                sbase = (i % 2) * 1024
                sb_ = PS[:, sbase:sbase + nj * 128]
                Bs_ = [Bbank[(i % 2) * 2], Bbank[(i % 2) * 2 + 1]]
                for n_, jj in enumerate(js):
                    S.op("pe", (lambda n_, jj: lambda e: e.matmul(PS[:, sbase + n_ * 128:sbase + (n_ + 1) * 128], lhsT=kT[:, jj * 128:(jj + 1) * 128],
                                                                 rhs=qT[:, i * 128:(i + 1) * 128], start=True, stop=True))(n_, jj),
                         reads=[Bk, Bq], writes=Bs_)
                e_t, Be, _ = Ret.next()
                S.op("act", lambda e: e.activation(out=e_t[:, 0:nj * 128], in_=sb_, func=AF.Exp, scale=SC_C), writes=Bs_ + [Be])
                p_t, Bp, _ = Rpt.next()
                S.op("pool" if i % 3 == 2 else "dve", lambda e: e.tensor_tensor(out=p_t[:, 0:nj * 128], in0=e_t[:, 0:nj * 128], in1=tsl_.rearrange("p a b -> p (a b)"), op=ALU.mult),
                     reads=[Be, Bct], writes=[Bp])
                C_pts[(h, i)] = (p_t, Bp)

            C_yg = {}

            def c_sB(h, i):
                va, Bva = C_ops[("v", h)]
                szh, Bszh = C_ops[("sz", h)]
                p_t, Bp = C_pts.pop((h, i))
                js = c_js(i)
                nj = len(js)
                ab, Bab = bank[4 + acc_i[0] % 2], Bbank[4 + acc_i[0] % 2]
                acc_i[0] += 1
                for n_, jj in enumerate(js):
                    S.op("pe", (lambda n_, jj: lambda e: e.matmul(ab[:, 0:129], lhsT=p_t[:, n_ * 128:(n_ + 1) * 128], rhs=va[:, jj, :],
                                                                 start=(n_ == 0), stop=(n_ == nj - 1)))(n_, jj),
                         reads=[Bp, Bva], writes=[Bab])
                C_yg[(h, i)] = fin_dve(ab, Bab, szh[:, i, :], Bszh)

            def c_sD(h, i):
                fin_tr([C_yg.pop((h, i))], 12 + h, i)

            nC = 4 * NT
            c_load(0)
            for st in range(nC + 4):
                if st < nC:
                    h, i = divmod(st, NT)
                    if i == 4 and h + 1 < 4:
                        c_load(h + 1)
                    c_sA(h, i)
                if 0 <= st - 1 < nC:
                    c_sB(*divmod(st - 1, NT))
                if 0 <= st - 3 < nC:
                    c_sD(*divmod(st - 3, NT))

            S.barrier()
            CX.reset()
            CW.reset()
            wbuf = [CW.take([128, 16, 512], BF16) for _ in range(2)]
            Rw = Ring(S, wbuf, "w")
            hsl = [CX.take([128, 4, 512], F32) for _ in range(2)]
            Rhs = Ring(S, hsl, "hs")
            ost = [CX.take([128, 4, 512], F32) for _ in range(2)]
            Ros = Ring(S, ost, "os")
            Bh = Buf("hscr")
            wo_l = w_out_d[l].rearrange("(kc p) n -> p kc n", p=128)
            next_w = load_w(wo_l[:, :, 0:512], 512, 16)

            def p4_group(c, tg, wb, Bw):
                csl = slice(c * 512, (c + 1) * 512)
                rows = slice(tg * 512, (tg + 1) * 512)
                hs, Bhs, dhs = Rhs.next()
                dma("sp", hs, h_src[rows, csl].rearrange("(t p) c -> p t c", p=128), writes=[Bhs], dsem=dhs)
                o, Bo, do = Ros.next()
                for k_ in range(4):
                    t = tg * 4 + k_
                    tsl = slice(t * 128, (t + 1) * 128)
                    bk, Bb = next_bank()
                    proj_mm(bk, Bb, lambda kc: mixT[:, kc, tsl], Bmix[t], wb, Bw, 512, 16)
                    S.op("dve", lambda e: e.tensor_tensor(out=o[:, k_, :], in0=bk, in1=hs[:, k_, :], op=ALU.add), reads=[Bhs], writes=[Bb], wacc=[Bo])
                dma("sp", hscr_d[s][rows, csl].rearrange("(t p) c -> p t c", p=128), o, reads=[Bo], wacc=[Bh], dsem=do)

            for c in range(4):
                wb, Bw = next_w
                if c < 3:
                    next_w = load_w(wo_l[:, :, (c + 1) * 512:(c + 2) * 512], 512, 16)
                for tg in range(4):
                    p4_group(c, tg, wb, Bw)

            S.barrier()
            CX.reset()
            CW.reset()
            hb = [CX.take([128, D], F32) for _ in range(4)]
            ub = [CX.take([128, D], BF16) for _ in range(2)]
            junk = CX.take([128, D], BF16)
            pT = CX.take([128, 2, L], BF16)
            wpp = CX.take([128, 2, D], BF16)
            pf = [CX.take([128, 256], F32) for _ in range(2)]
            pb = [CX.take([128, 256], BF16) for _ in range(2)]
            Rhb = Ring(S, hb, "hb")
            Rub = Ring(S, ub, "ub", with_dsem=False)
            Rpf = Ring(S, pf, "pf")
            Rpb = Ring(S, pb, "pb", with_dsem=False)
            Bjunk = Buf("junk")
            BpT, Bwpp = Buf("pT"), Buf("wpp")
            d_wpp = S.dsem("wpp")
            dma("pool", wpp, w_pp_d[l].rearrange("(kc p) n -> p kc n", p=128), writes=[Bwpp], dsem=d_wpp)
            bcast_load(G2[:], g_ple_d[l, :], D, Bg2, d_g2)
            bcast_load(G1[:], g_post_d[l, :], D, Bg1, d_g1)
            Bact = [Buf(f"act{t}") for t in range(NT)]
            Brp = Buf("rstd_p")
            for t in range(NT):
                tsl = slice(t * 128, (t + 1) * 128)
                norm_transpose(hscr_d[s][tsl, :], G2, Bg2, t)
                pft, Bpf, dpf = Rpf.next()
                dma("sp", pft, p_d[l, s, tsl, :], writes=[Bpf], dsem=dpf)
                pbt, Bpb, _ = Rpb.next()
                S.op("dve", lambda e: e.tensor_copy(out=pbt, in_=pft), reads=[Bpf], writes=[Bpb])
                transposes_to([pbt[:, 0:128], pbt[:, 128:256]], pT[:, :, tsl], Bpb, BpT, wacc=True, later=False)
                ss = stat_take(4)
                Bss = Buf()
                for c in range(4):
                    bk, Bb = next_bank()
                    for kc in range(2):
                        S.op("pe", (lambda bk, kc, c: lambda e: e.matmul(bk, lhsT=pT[:, kc, tsl], rhs=wpp[:, kc, c * 512:(c + 1) * 512],
                                                                         start=(kc == 0), stop=(kc == 1)))(bk, kc, c), reads=[BpT, Bwpp], writes=[Bb])
                    S.op("act", (lambda bk, c: lambda e: e.activation(out=junk[:, 0:512], in_=bk, func=AF.Square, accum_out=ss[:, c:c + 1]))(bk, c),
                         writes=[Bb, Bjunk, Bss])
                sst = stat_take(1)
                S.op("dve", lambda e: e.tensor_reduce(out=sst, in_=ss, axis=mybir.AxisListType.X, op=ALU.add), reads=[Bss], writes=[Bss])
                rstd(sst, rstd_p[:, t:t + 1], D, Bss, Brp)
            run_deferred()

            S.barrier()
            CX.reset()
            _ = [CX.take([128, D], F32) for _ in range(4)]
            _ = [CX.take([128, D], BF16) for _ in range(2)]
            _ = CX.take([128, D], BF16)
            pT_off = CX.off
            pT = CX.take([128, 2, L], BF16)
            wpp = CX.take([128, 2, D], BF16)
            keep = CX.off
            CX.reset()
            hsl = [CX.take([128, 4, 512], F32) for _ in range(2)]
            gat = [CX.take([128, 512], F32) for _ in range(2)]
            pn = [CX.take([128, 512], F32) for _ in range(2)]
            ost = [CX.take([128, 4, 512], F32) for _ in range(2)]
            assert CX.off <= pT_off
            Rhs = Ring(S, hsl, "hs")
            Rg = Ring(S, gat, "g", with_dsem=False)
            Rpn = Ring(S, pn, "pn", with_dsem=False)
            Ros = Ring(S, ost, "os")
            wbuf = [CW.take([128, 16, 512], BF16) for _ in range(2)]
            Rw = Ring(S, wbuf, "w")
            wg_l = w_gate_d[l].rearrange("(kc p) n -> p kc n", p=128)
            next_w = load_w(wg_l[:, :, 0:512], 512, 16)
            Bout = Buf("hout")

            def p5_group(c, tg, wb, Bw):
                csl = slice(c * 512, (c + 1) * 512)
                rows = slice(tg * 512, (tg + 1) * 512)
                hs, Bhs, dhs = Rhs.next()
                dma("sp", hs, hscr_d[s][rows, csl].rearrange("(t p) c -> p t c", p=128), reads=[Bh], writes=[Bhs], dsem=dhs)
                o, Bo, do = Ros.next()
                for k_ in range(4):
                    t = tg * 4 + k_
                    tsl = slice(t * 128, (t + 1) * 128)
                    bk, Bb = next_bank()
                    proj_mm(bk, Bb, lambda kc: actT[:, kc, tsl], Bact[t], wb, Bw, 512, 16)
                    bk2, Bb2 = next_bank()
                    for kc in range(2):
                        S.op("pe", lambda e: e.matmul(bk2, lhsT=pT[:, kc, tsl], rhs=wpp[:, kc, csl], start=(kc == 0), stop=(kc == 1)),
                             reads=[BpT, Bwpp], writes=[Bb2])
                    gt_, Bgt_, _ = Rg.next()
                    S.op("act", lambda e: e.activation(out=gt_, in_=bk, func=AF.Sigmoid), writes=[Bb, Bgt_])
                    pn_, Bpn_, _ = Rpn.next()
                    S.op("dve", lambda e: e.scalar_tensor_tensor(out=pn_, in0=bk2, scalar=rstd_p[:, t:t + 1], in1=G1[:, csl], op0=ALU.mult, op1=ALU.mult),
                         reads=[Brp, Bg1], writes=[Bb2, Bpn_])
                    S.op("pool", lambda e: e.tensor_tensor(out=pn_, in0=pn_, in1=gt_, op=ALU.mult), reads=[Bgt_], writes=[Bpn_])
                    S.op("dve", lambda e: e.tensor_tensor(out=o[:, k_, :], in0=pn_, in1=hs[:, k_, :], op=ALU.add), reads=[Bpn_, Bhs], wacc=[Bo])
                dma("sp", h_dst[rows, csl].rearrange("(t p) c -> p t c", p=128), o, reads=[Bo], wacc=[Bout], dsem=do)

            for c in range(4):
                wb, Bw = next_w
                if c < 3:
                    next_w = load_w(wg_l[:, :, (c + 1) * 512:(c + 2) * 512], 512, 16)
                for tg in range(4):
                    p5_group(c, tg, wb, Bw)

    S.emit()
    return nc, S


def _prep_shared(inp, tables):
    tabA, cos, sin, oh = tables
    f = lambda a: np.ascontiguousarray(np.asarray(a, dtype=np.float32))
    rpb = np.asarray(inp["c_rpb"], np.float32)
    rpbT = np.ascontiguousarray(rpb[:, :, ::-1, :].transpose(0, 1, 3, 2))
    return {
        "w_in": np.ascontiguousarray(np.asarray(inp["w_in"], np.float32)[:, :, IN_PERM]),
        "w_out": f(inp["w_out"]), "w_gate": f(inp["w_ple_gate"]), "w_pp": f(inp["w_ple_proj"]),
        "w_uq": f(inp["b_w_uq"]),
        "w_ukv": np.ascontiguousarray(np.asarray(inp["b_w_ukv"], np.float32)[:, :, UKV_PERM]),
        "g_in": f(inp["norm_in"]), "g_ple": f(inp["ple_norm"]), "g_post": f(inp["ple_post_norm"]),
        "g_aq": f(inp["a_q_norm"]), "g_ak": f(inp["a_k_norm"]), "g_cq": f(inp["c_q_norm"]), "g_ck": f(inp["c_k_norm"]),
        "g_bcq": f(inp["b_cq_norm"]), "g_bckv": f(inp["b_ckv_norm"]), "g_bq": f(inp["b_q_norm"]), "g_bk": f(inp["b_k_norm"]),
        "sink": f(inp["a_sink"]), "rpbT": rpbT,
        "ident": np.eye(128).astype(ml_dtypes.bfloat16),
        "tabA": tabA, "cosT": cos, "sinT": sin, "onehot": oh,
    }


_CACHE = {}


def kernel(**inputs):
    xp = np.asarray(inputs["x_prompt"], np.float32)
    xs = np.asarray(inputs["x_sample"], np.float32)
    pp = np.asarray(inputs["p_prompt"], np.float32)
    psm = np.asarray(inputs["p_sample"], np.float32)
    nB, nS = xp.shape[0], xs.shape[0]
    x_all = np.concatenate([xp, xs], axis=0)
    p_all = np.concatenate([pp, psm], axis=1)
    ntot = nB + nS
    ncores = 8
    slots = [[(c + 8 * k) % ntot if (c + 8 * k) < ntot else (c + 8 * k) % ntot for k in range(NSEQ)] for c in range(ncores)]
    if "nc" not in _CACHE:
        _CACHE["nc"] = build()[0]
        _CACHE["tables"] = _const_tables()
    nc = _CACHE["nc"]
    shared = _prep_shared(inputs, _CACHE["tables"])
    in_maps = []
    for c in range(ncores):
        m = dict(shared)
        m["x"] = np.ascontiguousarray(x_all[slots[c]])
        m["p"] = np.ascontiguousarray(p_all[:, slots[c]])
        in_maps.append(m)
    res = run_bass_kernel_spmd(nc, in_maps, core_ids=list(range(ncores)))
    y_all = np.zeros_like(x_all)
    done = set()
    for c in range(ncores):
        yc = res.results[c]["y"]
        for k, sidx in enumerate(slots[c]):
            if (c + 8 * k) < ntot and sidx not in done:
                y_all[sidx] = yc[k]
                done.add(sidx)
    return (y_all[:nB], y_all[nB:])
```

```python
import types
import numpy as np
from contextlib import ExitStack
import ml_dtypes
import concourse.bass as bass
import concourse.mybir as mybir
from concourse.bass_utils import run_bass_kernel_spmd

F32 = mybir.dt.float32
BF16 = mybir.dt.bfloat16
AF = mybir.ActivationFunctionType
ALU = mybir.AluOpType

L = 2048
D = 2048
NT = 16
DEPTH = 4
NSEQ = 3
EPS = 1e-6
IN_W = 5952

EPOCH = 30000


def _freeze(fn):
    if fn.__closure__ is None:
        return fn
    cells = tuple(types.CellType(c.cell_contents) for c in fn.__closure__)
    return types.FunctionType(fn.__code__, fn.__globals__, fn.__name__, fn.__defaults__, cells)


class Tk:
    __slots__ = ("sem", "val", "q", "dma")

    def __init__(self, sem, val, q, dma=False):
        self.sem = sem
        self.val = val
        self.q = q
        self.dma = dma


class Buf:
    __slots__ = ("name", "w", "r")

    def __init__(self, name=""):
        self.name = name
        self.w = {}
        self.r = {}


class DSem:
    __slots__ = ("h", "count")

    def __init__(self, h):
        self.h = h
        self.count = 0


class Sched:
    QS = ("pe", "act", "dve", "pool", "sp")

    def __init__(self, nc, es):
        self.nc = nc
        self.es = es
        self.qs = {k: [] for k in self.QS}
        self.esem = {}
        self.ecount = {}
        self.waited = {k: {} for k in self.QS}
        self.nsem = 0
        self.dsems = []
        self.dpool = []
        self.pool_i = 0
        for k in self.QS:
            self._new_epoch(k)

    def _alloc_sem(self, name):
        self.nsem += 1
        return self.es.enter_context(self.nc.semaphore(name))

    def _new_epoch(self, k):
        self.esem[k] = self._alloc_sem(f"e{k}{self.nsem}")
        self.ecount[k] = 0

    def dsem(self, name="d", persistent=False):
        if persistent:
            d = DSem(self._alloc_sem(f"{name}{self.nsem}"))
            self.dsems.append(d)
            return d
        if self.pool_i >= len(self.dpool):
            d = DSem(self._alloc_sem(f"dp{self.nsem}"))
            self.dsems.append(d)
            self.dpool.append(d)
        d = self.dpool[self.pool_i]
        self.pool_i += 1
        return d

    def op(self, q, fn, reads=(), writes=(), wacc=(), dsem=None):
        raw = {}
        oth = {}

        def add(d, t):
            k = id(t.sem)
            if k not in d or d[k].val < t.val:
                d[k] = t

        for b in reads:
            for t in b.w.values():
                add(raw, t)
        for b in writes:
            for t in b.w.values():
                add(oth, t)
            for t in b.r.values():
                add(oth, t)
        for b in wacc:
            for t in b.r.values():
                add(oth, t)
        waits = []
        wd = self.waited[q]
        for d, is_raw in ((raw, True), (oth, False)):
            for k, t in d.items():
                if t.q == q and (not t.dma) and dsem is None:
                    if not is_raw or q == "pe":
                        continue
                if wd.get(k, 0) >= t.val:
                    continue
                wd[k] = t.val
                waits.append((t.sem, t.val))
        if dsem is not None:
            dsem.count += 16
            tk = Tk(dsem.h, dsem.count, q, True)
            inc = (dsem.h, 16)
        else:
            if self.ecount[q] >= EPOCH:
                self._new_epoch(q)
            self.ecount[q] += 1
            tk = Tk(self.esem[q], self.ecount[q], q)
            inc = (self.esem[q], 1)
        self.qs[q].append((waits, _freeze(fn), inc))
        k = id(tk.sem)
        for b in writes:
            b.w = {k: tk}
            b.r = {}
        for b in wacc:
            b.w[k] = tk
        for b in reads:
            if k not in b.r or b.r[k].val < tk.val:
                b.r[k] = tk
        return tk

    def barrier(self):
        self.pool_i = 0
        tks = []
        for q in self.QS:
            if self.ecount[q] > 0:
                tks.append((self.esem[q], self.ecount[q], q))
        for d in self.dsems:
            if d.count > 0:
                tks.append((d.h, d.count, None))
        for q in self.QS:
            waits = []
            wd = self.waited[q]
            for (h, v, src) in tks:
                if src == q:
                    continue
                if wd.get(id(h), 0) >= v:
                    continue
                wd[id(h)] = v
                waits.append((h, v))
            if waits:
                self.qs[q].append((waits, None, None))

    def emit(self):
        nc = self.nc
        self.barrier()
        qs = self.qs

        def run(eng, lst):
            for waits, fn, inc in lst:
                for (s, v) in waits:
                    eng.wait_ge(s, v)
                if fn is not None:
                    fn(eng).then_inc(inc[0], inc[1])

        with nc.Block() as block:
            @block.tensor
            def _(e):
                run(e, qs["pe"])

            @block.scalar
            def _(e):
                run(e, qs["act"])

            @block.vector
            def _(e):
                run(e, qs["dve"])

            @block.gpsimd
            def _(e):
                run(e, qs["pool"])

            @block.sync
            def _(e):
                run(e, qs["sp"])


class Ring:
    def __init__(self, S, aps, name, with_dsem=True):
        self.aps = aps
        self.bufs = [Buf(f"{name}{i}") for i in range(len(aps))]
        self.ds = [S.dsem(name) for _ in aps] if with_dsem else [None] * len(aps)
        self.i = 0

    def next(self):
        k = self.i % len(self.aps)
        self.i += 1
        return self.aps[k], self.bufs[k], self.ds[k]


_O = dict(aq=0, ak=768, av=1024, az=1280, bcq=2048, bckv=2560, bkr=3072, bz=3136,
          cq=3904, ck=4416, cv=4928, cz=5440)


def _in_blocks():
    r = lambda a, n: list(range(a, a + n))
    blocks = [
        ("QK", r(_O["aq"], 512)),
        ("QK", r(_O["aq"] + 512, 256) + r(_O["ak"], 256)),
        ("QK", r(_O["cq"], 512)),
        ("QK", r(_O["ck"], 512)),
        ("LAT", r(_O["bcq"], 512)),
        ("LAT", r(_O["bckv"], 512)),
        ("VKR", r(_O["av"], 256) + r(_O["bkr"], 64)),
        ("V", r(_O["cv"], 512)),
        ("Z", r(_O["az"], 512)),
        ("Z", r(_O["az"] + 512, 256) + r(_O["bz"], 256)),
        ("Z", r(_O["bz"] + 256, 512)),
        ("Z", r(_O["cz"], 512)),
    ]
    return blocks


IN_BLOCKS = _in_blocks()
IN_PERM = np.concatenate([np.array(c) for _, c in IN_BLOCKS])
UKV_PERM = np.concatenate([np.arange(h * 256, h * 256 + 128) for h in range(6)] +
                          [np.arange(h * 256 + 128, h * 256 + 256) for h in range(6)])


def _const_tables():
    slopes = 2.0 ** (-8.0 * np.arange(1, 7, dtype=np.float64) / 6)
    kt = np.arange(128)[:, None]
    qo = np.arange(384)[None, :]
    delta = 128 + kt - qo
    tabA = np.zeros((6, 128, 384), np.float32)
    for h in range(6):
        tabA[h] = (np.exp(-slopes[h] * np.abs(delta)) * (np.abs(delta) <= 128)).astype(np.float32)
    half = 32
    inv = 10000.0 ** (-np.arange(half, dtype=np.float32) / half)
    ang = np.arange(L, dtype=np.float32)[:, None] * inv[None, :]
    cos = np.cos(ang).astype(np.float32)
    sin = np.sin(ang).astype(np.float32)
    kc = np.arange(64)[:, None]
    qc = np.arange(64)[None, :]
    cs = np.clip(qc - 8, 0, 48)
    valid = (kc >= cs) & (kc < cs + 16)
    oh = np.zeros((32, 64, 64), np.float32)
    b = 15 + kc - qc
    for bb in range(31):
        oh[bb] = ((b == bb) & valid).astype(np.float32)
    oh[31] = np.where(valid, 0.0, -30000.0)
    return tabA, cos, sin, oh.reshape(32, 4096)


def build(nseq=NSEQ, nlayers=DEPTH, debug=False):
    nc = bass.Bass("TRN2", target_bir_lowering=False)
    es = ExitStack()

    def din(name, shape, dt=F32):
        return nc.dram_tensor(name, list(shape), dt, kind="ExternalInput").ap()

    dbg_kind = "ExternalOutput" if debug else "Internal"

    def dscr(name, shape, dt=F32):
        return nc.dram_tensor(name, list(shape), dt, kind=dbg_kind).ap()

    x_d = din("x", [nseq, L, D])
    p_d = din("p", [DEPTH, nseq, L, 256])
    y_d = nc.dram_tensor("y", [nseq, L, D], F32, kind="ExternalOutput").ap()
    w_in_d = din("w_in", [DEPTH, D, IN_W])
    w_out_d = din("w_out", [DEPTH, D, D])
    w_gate_d = din("w_gate", [DEPTH, D, D])
    w_pp_d = din("w_pp", [DEPTH, 256, D])
    w_uq_d = din("w_uq", [DEPTH, 512, 1152])
    w_ukv_d = din("w_ukv", [DEPTH, 512, 1536])
    g_in_d = din("g_in", [DEPTH, D])
    g_ple_d = din("g_ple", [DEPTH, D])
    g_post_d = din("g_post", [DEPTH, D])
    g_aq_d = din("g_aq", [DEPTH, 128])
    g_ak_d = din("g_ak", [DEPTH, 128])
    g_cq_d = din("g_cq", [DEPTH, 128])
    g_ck_d = din("g_ck", [DEPTH, 128])
    g_bcq_d = din("g_bcq", [DEPTH, 512])
    g_bckv_d = din("g_bckv", [DEPTH, 512])
    g_bq_d = din("g_bq", [DEPTH, 192])
    g_bk_d = din("g_bk", [DEPTH, 192])
    sink_d = din("sink", [DEPTH, 6])
    rpbT_d = din("rpbT", [DEPTH, 4, 31, 15])
    ident_d = din("ident", [128, 128], BF16)
    tabA_d = din("tabA", [6, 128, 384])
    cos_d = din("cosT", [L, 32])
    sin_d = din("sinT", [L, 32])
    oh_d = din("onehot", [32, 4096])

    hscr_d = dscr("hscr", [nseq, L, D])
    featA_d = dscr("featA", [8, 128, L], BF16)
    featC_d = dscr("featC", [8, 128, L], BF16)
    qn_d = dscr("qn", [6, 128, L], BF16)
    qr_d = dscr("qr", [6, 128, L], BF16)
    kn_d = dscr("kn", [6, 128, L], BF16)
    vA_d = dscr("vA", [L, 256], BF16)
    vB_d = dscr("vB", [L, 768], BF16)
    vC_d = dscr("vC", [L, 512], BF16)
    sz_d = dscr("sz", [L, D])
    xd_d = dscr("xd", [4, 15, 4096])
    ctab_d = dscr("ctab", [4, 128, 26 * 64])

    S = Sched(nc, es)

    def sbt(name, shape, dt):
        return nc.alloc_sbuf_tensor(name, list(shape), dt)

    ident = sbt("ident_s", [128, 128], BF16)
    cosT = sbt("cosT_s", [128, NT, 32], F32)
    sinT = sbt("sinT_s", [128, NT, 32], F32)
    G1 = sbt("G1", [128, D], F32)
    g_aq = sbt("g_aq_s", [128, 128], F32)
    g_ak = sbt("g_ak_s", [128, 128], F32)
    g_cq = sbt("g_cq_s", [128, 128], F32)
    g_ck = sbt("g_ck_s", [128, 128], F32)
    g_bq = sbt("g_bq_s", [128, 192], F32)
    g_bk = sbt("g_bk_s", [128, 192], F32)
    g_bcq = sbt("g_bcq_s", [128, 512], F32)
    g_bckv = sbt("g_bckv_s", [128, 512], F32)
    esink = sbt("esink", [128, 8], F32)
    kscale = sbt("kscale", [128, NT, 6], F32)
    ss_kr = sbt("ss_kr", [128, NT], F32)
    rstd_p = sbt("rstd_p", [128, NT], F32)
    stat = sbt("stat", [128, 512], F32)
    vaug = [sbt(f"vaug{i}", [128, NT, 129], BF16) for i in range(2)]
    ARENA_ACT = sbt("arena_act", [128, 16 * L], BF16)
    ARENA_W = sbt("arena_w", [128, 16 * 1024], BF16)
    XBYTES = 80 * 1024
    ARENA_X = sbt("arena_x", [128, XBYTES // 4], F32)

    actT = ARENA_ACT[:, :].rearrange("p (c n) -> p c n", c=16)
    PS = nc.alloc_psum_tensor("ps", [128, 3072], F32)
    PT = nc.alloc_psum_tensor("pt", [128, 2048], BF16)
    bank = [PS[:, i * 512:(i + 1) * 512] for i in range(6)]
    Bbank = [Buf(f"bank{i}") for i in range(6)]
    tbank = [PT[:, i * 1024:(i + 1) * 1024] for i in range(2)]
    Btb = [Buf(f"tb{i}") for i in range(2)]

    class Carver:
        def __init__(self, arena_f32, nbytes):
            self.a = arena_f32
            self.n = nbytes
            self.off = 0

        def reset(self):
            self.off = 0

        def take(self, shape, dt):
            esz = 4 if dt == F32 else 2
            nel = int(np.prod(shape[1:]))
            nb = nel * esz
            nb_al = (nb + 31) // 32 * 32
            assert self.off + nb_al <= self.n, ("arena overflow", self.off, nb_al, self.n)
            v = self.a[:, self.off // 4:(self.off + nb_al) // 4]
            if dt != F32:
                v = v.bitcast(dt)
            v = v[0:shape[0], 0:nel]
            if len(shape) == 3:
                v = v.rearrange("p (a b) -> p a b", a=shape[1])
            self.off += nb_al
            return v

    CX = Carver(ARENA_X, XBYTES)
    ARENA_W32 = ARENA_W[:, :].bitcast(F32)
    CW = Carver(ARENA_W32, 32 * 1024)

    Bconst = Buf("const")
    d_const = S.dsem("const", persistent=True)

    def dma(q, out, in_, reads=(), writes=(), wacc=(), dsem=None):
        assert dsem is not None
        return S.op(q, lambda e: e.dma_start(out=out, in_=in_), reads=reads, writes=writes,
                    wacc=wacc, dsem=dsem)

    dma("sp", ident[:], ident_d, wacc=[Bconst], dsem=d_const)
    dma("sp", cosT[:], cos_d.rearrange("(t p) i -> p t i", p=128), wacc=[Bconst], dsem=d_const)
    dma("sp", sinT[:], sin_d.rearrange("(t p) i -> p t i", p=128), wacc=[Bconst], dsem=d_const)
    for i in range(2):
        S.op("dve", (lambda i: lambda e: e.memset(vaug[i][:, :, 128:129], 1.0))(i), wacc=[Bconst])
    S.barrier()

    stat_i = [0]

    def stat_take(n):
        if stat_i[0] + n > 512:
            stat_i[0] = 0
        a = stat[:, stat_i[0]:stat_i[0] + n]
        stat_i[0] += n
        return a

    def rstd_from_ss(ss_ap, n, count, eps=EPS, extra_scale=None, out=None):
        r = out if out is not None else stat_take(n)
        b = Buf("rstd")
        return r, b

    bank_i = [0]

    def next_bank():
        k = bank_i[0] % 6
        bank_i[0] += 1
        return bank[k], Bbank[k]

    tb_i = [0]

    def next_tb():
        k = tb_i[0] % 2
        tb_i[0] += 1
        return tbank[k], Btb[k]

    tr_i = [0]
    DQ = []

    def dq_new_group():
        DQ.append([])

    def defer(thunk):
        if not DQ:
            DQ.append([])
        DQ[-1].append(thunk)

    def run_deferred(keep=0):
        while len(DQ) > keep:
            for th in DQ.pop(0):
                th()

    _orig_barrier = S.barrier

    def _checked_barrier():
        assert not DQ, "deferred work pending at barrier"
        _orig_barrier()
    S.barrier = _checked_barrier

    def bcast_load(dst, src_row, n, buf, ds):
        dma("sp", dst, src_row.partition_broadcast(128), writes=[buf], dsem=ds)

    Bg1, Bgs = Buf("G1"), Buf("gsm")
    G2, Bg2 = G1, Bg1
    d_g1, d_g2, d_gs = S.dsem("g1", True), S.dsem("g2", True), S.dsem("gs", True)

    ROPE_ENG = "pool"

    def rope(xr, out_bf, t, tmp, Bx, Bout, Btmp, n=1):
        c = cosT[:, t, :].unsqueeze(1).to_broadcast([128, n, 32])
        s_ = sinT[:, t, :].unsqueeze(1).to_broadcast([128, n, 32])
        x1 = xr[:, :, 0:32]
        x2 = xr[:, :, 32:64]
        q = ROPE_ENG
        S.op(q, lambda e: e.tensor_tensor(out=tmp[:, 0], in0=x1, in1=c, op=ALU.mult), reads=[Bx, Bconst], writes=[Btmp])
        S.op(q, lambda e: e.tensor_tensor(out=tmp[:, 1], in0=x2, in1=s_, op=ALU.mult), reads=[Bx], writes=[Btmp])
        S.op(q, lambda e: e.tensor_tensor(out=tmp[:, 2], in0=x1, in1=s_, op=ALU.mult), reads=[Bx], writes=[Btmp])
        S.op(q, lambda e: e.tensor_tensor(out=tmp[:, 3], in0=x2, in1=c, op=ALU.mult), reads=[Bx], writes=[Btmp])
        S.op(q, lambda e: e.tensor_tensor(out=out_bf[:, :, 0:32], in0=tmp[:, 0], in1=tmp[:, 1], op=ALU.subtract), reads=[Btmp], writes=[Bout])
        S.op(q, lambda e: e.tensor_tensor(out=out_bf[:, :, 32:64], in0=tmp[:, 2], in1=tmp[:, 3], op=ALU.add), reads=[Btmp], writes=[Bout])

    def sumsq(ps_ap, junk_ap, ss_ap, Bps, Bjunk, Bss):
        S.op("act", lambda e: e.activation(out=junk_ap, in_=ps_ap, func=AF.Square, accum_out=ss_ap),
             writes=[Bps, Bjunk, Bss])

    def rstd(ss_ap, r_ap, count, Bss, Br, mul=None):
        m2 = 1.0 if mul is None else float(mul) ** 2
        S.op("act", lambda e: e.activation(out=r_ap, in_=ss_ap, func=AF.Sqrt, scale=1.0 / (count * m2), bias=EPS / m2),
             reads=[Bss], writes=[Br])
        S.op("dve", lambda e: e.reciprocal(out=r_ap, in_=r_ap), reads=[Br], writes=[Br])

    for l in range(nlayers):
        S.barrier()
        for (dst, src, n) in ((g_aq, g_aq_d, 128), (g_ak, g_ak_d, 128), (g_cq, g_cq_d, 128), (g_ck, g_ck_d, 128),
                              (g_bq, g_bq_d, 192), (g_bk, g_bk_d, 192), (g_bcq, g_bcq_d, 512), (g_bckv, g_bckv_d, 512)):
            dma("sp", dst[:], src[l, :].partition_broadcast(128), wacc=[Bgs], dsem=d_gs)
        dma("sp", esink[:, 0:6], sink_d[l, :].partition_broadcast(128), wacc=[Bgs], dsem=d_gs)
        S.barrier()
        S.op("act", lambda e: e.activation(out=esink[:, 0:6], in_=esink[:, 0:6], func=AF.Exp), writes=[Bgs])
        CX.reset()
        CW.reset()
        oh_s = CX.take([32, 4096], F32)
        xa_s = CX.take([15, 4096], F32)
        rt_s = CX.take([32, 16], F32)
        xtr = CW.take([128, 26, 64], F32)
        Boh, Bxa, Brt, Bxtr, Bxd = Buf(), Buf(), Buf(), Buf(), Buf()
        d_t1, d_t2 = d_g1, d_g2
        dma("sp", oh_s, oh_d, writes=[Boh], dsem=d_t1)
        for h in range(4):
            S.op("dve", lambda e: e.memset(rt_s[:, 0:15], 1.0), writes=[Brt])
            dma("sp", rt_s[0:31, 0:15], rpbT_d[l, h], writes=[Brt], dsem=d_t2)
            for c in range(8):
                bk, Bb = next_bank()
                S.op("pe", (lambda bk, c: lambda e: e.matmul(bk[0:15, :], lhsT=rt_s[:, 0:15], rhs=oh_s[:, c * 512:(c + 1) * 512],
                                                              start=True, stop=True))(bk, c), reads=[Brt, Boh], writes=[Bb])
                S.op("act", (lambda bk, c: lambda e: e.activation(out=xa_s[:, c * 512:(c + 1) * 512], in_=bk[0:15, :], func=AF.Exp))(bk, c),
                     writes=[Bb, Bxa])
            dma("sp", xd_d[h], xa_s, reads=[Bxa], writes=[Bxd], dsem=d_t1)
            S.op("dve", lambda e: e.memset(xtr, 0.0), writes=[Bxtr])
            src = xd_d[h].rearrange("a (k q) -> k a q", k=64)
            dma("sp", xtr[0:64, 0:15, :], src, reads=[Bxd], writes=[Bxtr], dsem=d_t2)
            dma("sp", xtr[64:128, 1:16, :], src, reads=[Bxd], writes=[Bxtr], dsem=d_t2)
            S.op("dve", lambda e: e.tensor_copy(out=xtr[:, 16:26, :], in_=xtr[:, 3:13, :]), reads=[Bxtr], writes=[Bxtr])
            S.op("dve", lambda e: e.memset(xtr[:, 16, :], 0.0), writes=[Bxtr])
            S.op("dve", lambda e: e.memset(xtr[64:128, 17, :], 0.0), writes=[Bxtr])
            S.op("dve", lambda e: e.memset(xtr[0:64, 25, :], 0.0), writes=[Bxtr])
            dma("sp", ctab_d[h], xtr.rearrange("p a b -> p (a b)"), reads=[Bxtr], writes=[Bxd], dsem=d_t1)
        S.barrier()

        for s in range(nseq):
            h_src = x_d[s] if l == 0 else hscr_d[s]
            h_dst = y_d[s] if l == nlayers - 1 else hscr_d[s]

            S.barrier()
            CX.reset()
            hb = [CX.take([128, D], F32) for _ in range(4)]
            ub = [CX.take([128, D], BF16) for _ in range(2)]
            junk = CX.take([128, D], BF16)
            Bjunk = Buf("junk")
            Rhb = Ring(S, hb, "hb")
            Rub = Ring(S, ub, "ub", with_dsem=False)
            bcast_load(G1[:], g_in_d[l, :], D, Bg1, d_g1)
            Bact = [Buf(f"act{t}") for t in range(NT)]

            def norm_transpose(src_rows, gtile, Bgt, t, pre=None):
                hbt, Bh, dh = Rhb.next()
                dma("sp", hbt, src_rows, writes=[Bh], dsem=dh)
                ss = stat_take(1)
                rr = stat_take(1)
                Bss, Br = Buf(), Buf()
                S.op("act", lambda e: e.activation(out=junk, in_=hbt, func=AF.Square, accum_out=ss),
                     reads=[Bh], writes=[Bjunk, Bss])
                rstd(ss, rr, D, Bss, Br)
                ubt, Bu, _ = Rub.next()
                S.op("dve", lambda e: e.scalar_tensor_tensor(out=ubt, in0=hbt, scalar=rr, in1=gtile[:], op0=ALU.mult, op1=ALU.mult),
                     reads=[Bh, Br, Bgt], writes=[Bu])
                run_deferred()

                def tr_part():
                    for half in range(2):
                        tb, Bt = next_tb()
                        for k in range(8):
                            kc = half * 8 + k
                            S.op("pe", (lambda tb, k, kc: lambda e: e.transpose(out=tb[:, k * 128:(k + 1) * 128], in_=ubt[:, kc * 128:(kc + 1) * 128],
                                                                                identity=ident[:]))(tb, k, kc), reads=[Bu, Bconst], writes=[Bt])
                        dst = actT[:, half * 8:(half + 1) * 8, t * 128:(t + 1) * 128]
                        if half == 0:
                            S.op("act", (lambda tb, dst: lambda e: e.copy(out=dst, in_=tb.rearrange("p (a b) -> p a b", a=8)))(tb, dst),
                                 writes=[Bt], wacc=[Bact[t]])
                        else:
                            S.op("dve", (lambda tb, dst: lambda e: e.tensor_copy(out=dst, in_=tb.rearrange("p (a b) -> p a b", a=8)))(tb, dst),
                                 writes=[Bt], wacc=[Bact[t]])
                dq_new_group()
                defer(tr_part)

            for t in range(NT):
                norm_transpose(h_src[t * 128:(t + 1) * 128, :], G1, Bg1, t)
            run_deferred()

            S.barrier()
            CX.reset()
            CW.reset()
            wbuf = [CW.take([128, 16, 512], BF16) for _ in range(2)]
            Rw = Ring(S, wbuf, "w")
            stageT = CX.take([128, 4, L], BF16)
            stageA = stageT[:, :, 0:1024]
            stageB = stageT[:, :, 1024:2048]
            BstageA, BstageB = Buf("stageA"), Buf("stageB")
            d_stage = S.dsem("stg")
            latT = [CX.take([128, 4, L], BF16) for _ in range(2)]
            Blat = [Buf("lat0"), Buf("lat1")]
            krT = CX.take([128, L], BF16)
            Bkr = Buf("krT")
            xn = [CX.take([128, 512], BF16) for _ in range(4)]
            Rxn = Ring(S, xn, "xn", with_dsem=False)
            zst = [CX.take([128, 512], F32) for _ in range(2)]
            Rz = Ring(S, zst, "z")
            vst = [CX.take([128, 512], BF16) for _ in range(2)]
            Rv = Ring(S, vst, "v")
            xr2 = [CX.take([128, 2, 64], F32) for _ in range(3)]
            ropet2 = [CX.take([128, 4, 2 * 32], F32).rearrange("p a (n c) -> p a n c", n=2) for _ in range(3)]
            xrr4 = CX.take([128, 8, 128], BF16)
            junk = CX.take([128, 512], BF16)
            Rxr = Ring(S, xr2, "xr", with_dsem=False)
            Rrt = Ring(S, ropet2, "rt", with_dsem=False)
            Rxrr = Ring(S, [xrr4[:, 2 * i:2 * i + 2, :] for i in range(4)], "xrr", with_dsem=False)
            S.op("dve", lambda e: e.memset(xrr4[:, :, 64:128], 0.0), writes=Rxrr.bufs)
            Bsz, BvA, BvB, BvC, BfA, BfC = Buf("sz"), Buf("vA"), Buf("vB"), Buf("vC"), Buf("fA"), Buf("fC")
            Bqn, Bqr, Bkn = Buf("qn"), Buf("qr"), Buf("kn")
            Bksc = Buf("kscale")

            def load_w(src_ap, ncols, nk):
                wb, Bw, dw = Rw.next()
                dst = wb[:, 0:nk, 0:ncols]
                dma("pool", dst, src_ap, writes=[Bw], dsem=dw)
                return wb, Bw

            def proj_mm(bk, Bb, lhs_of_kc, Blhs, wb, Bw, ncols, nk):
                for kc in range(nk):
                    lhs = lhs_of_kc(kc)
                    S.op("pe", lambda e: e.matmul(bk[:, 0:ncols], lhsT=lhs, rhs=wb[:, kc, 0:ncols],
                                                  start=(kc == 0), stop=(kc == nk - 1)),
                         reads=[Blhs, Bw], writes=[Bb])

            def transposes_to(srcs, dst_ap, Bsrc, Bdst, wacc=False, later=True):
                if later:
                    defer(lambda: transposes_to(srcs, dst_ap, Bsrc, Bdst, wacc=wacc, later=False))
                    return
                tb, Bt = next_tb()
                n = len(srcs)
                for k, sap in enumerate(srcs):
                    S.op("pe", (lambda k, sap: lambda e: e.transpose(out=tb[:, k * 128:(k + 1) * 128], in_=sap, identity=ident[:]))(k, sap),
                         reads=[Bsrc, Bconst], writes=[Bt])
                if n == 1:
                    src = tb[:, 0:128]
                else:
                    src = tb[:, 0:n * 128].rearrange("p (a b) -> p a b", a=n)
                tr_i[0] += 1
                if tr_i[0] % 2 == 0:
                    fn_ = ("act", lambda e: e.copy(out=dst_ap, in_=src))
                else:
                    fn_ = ("dve", lambda e: e.tensor_copy(out=dst_ap, in_=src))
                if wacc:
                    S.op(fn_[0], fn_[1], writes=[Bt], wacc=[Bdst])
                else:
                    S.op(fn_[0], fn_[1], writes=[Bt, Bdst])

            in_w_l = w_in_d[l].rearrange("(kc p) n -> p kc n", p=128)
            col_start = np.concatenate([[0], np.cumsum([len(c) for _, c in IN_BLOCKS])]).tolist()
            ORDER = [4, 5, 6, 0, 1, 2, 3, 7, 8, 9, 10, 11]
            b0_ = ORDER[0]
            P2st = {"next_w": load_w(in_w_l[:, :, col_start[b0_]:col_start[b0_] + len(IN_BLOCKS[b0_][1])], len(IN_BLOCKS[b0_][1]), 16)}
            KEEP = [1]
            A_thunks = []
            for oi, bi in enumerate(ORDER):
                btype, cols = IN_BLOCKS[bi]
                ncols = len(cols)
                gl = None
                if btype == "QK":
                    if bi == 0:
                        gl = [g_aq] * 4
                    elif bi == 1:
                        gl = [g_aq, g_aq, g_ak, g_ak]
                    elif bi == 2:
                        gl = [g_cq] * 4
                    else:
                        gl = [g_ck] * 4
                ctx = {}

                def p2_start(oi=oi, btype=btype, ctx=ctx):
                    ctx["w"] = P2st["next_w"]
                    if oi + 1 < len(ORDER):
                        nb_ = ORDER[oi + 1]
                        nn = len(IN_BLOCKS[nb_][1])
                        P2st["next_w"] = load_w(in_w_l[:, :, col_start[nb_]:col_start[nb_] + nn], nn, 16)
                    if btype == "VKR":
                        S.op("dve", lambda e: e.memset(krT[64:128, :], 0.0), writes=[Bkr])

                def p2_tile(t, bi=bi, btype=btype, ctx=ctx, ncols=ncols, gl=gl):
                    wb, Bw = ctx["w"]
                    bk, Bb = next_bank()
                    proj_mm(bk, Bb, lambda kc: actT[:, kc, t * 128:(t + 1) * 128], Bact[t], wb, Bw, ncols, 16)
                    run_deferred(keep=KEEP[0])
                    dq_new_group()
                    tsl = slice(t * 128, (t + 1) * 128)
                    tsl8 = slice((t % 8) * 128, (t % 8 + 1) * 128)
                    if btype == "QK":
                        ss = stat_take(4)
                        rr = stat_take(4)
                        Bss, Br = Buf(), Buf()
                        for u in range(4):
                            S.op("act", (lambda u: lambda e: e.activation(out=junk[:, 0:128], in_=bk[:, u * 128:(u + 1) * 128], func=AF.Square,
                                                                         accum_out=ss[:, u:u + 1]))(u), writes=[Bb, Bss, Bjunk])
                        rstd(ss, rr, 128, Bss, Br)
                        xnt, Bxn, _ = Rxn.next()
                        for u in range(4):
                            S.op("dve", (lambda u: lambda e: e.scalar_tensor_tensor(out=xnt[:, u * 128:(u + 1) * 128], in0=bk[:, u * 128:(u + 1) * 128],
                                                                                   scalar=rr[:, u:u + 1], in1=gl[u][:], op0=ALU.mult, op1=ALU.mult))(u),
                                 reads=[Br, Bgs], writes=[Bb, Bxn])
                        transposes_to([xnt[:, u * 128:(u + 1) * 128] for u in range(4)], stageA[:, :, tsl8], Bxn, BstageA, wacc=True)
                    elif btype == "LAT":
                        which = bi - 4
                        gt = g_bcq if which == 0 else g_bckv
                        ss = stat_take(1)
                        rr = stat_take(1)
                        Bss, Br = Buf(), Buf()
                        S.op("act", lambda e: e.activation(out=junk, in_=bk, func=AF.Square, accum_out=ss), writes=[Bb, Bss, Bjunk])
                        rstd(ss, rr, 512, Bss, Br)
                        xnt, Bxn, _ = Rxn.next()
                        S.op("dve", lambda e: e.scalar_tensor_tensor(out=xnt, in0=bk, scalar=rr, in1=gt[:], op0=ALU.mult, op1=ALU.mult),
                             reads=[Br, Bgs], writes=[Bb, Bxn])
                        transposes_to([xnt[:, u * 128:(u + 1) * 128] for u in range(4)], latT[which][:, :, tsl], Bxn, Blat[which], wacc=True)
                    elif btype == "VKR":
                        vt, Bv, dv = Rv.next()
                        S.op("act", lambda e: e.copy(out=vt[:, 0:256], in_=bk[:, 0:256]), writes=[Bb, Bv])
                        dma("sp", vA_d[tsl, :], vt[:, 0:256], reads=[Bv], wacc=[BvA], dsem=dv)
                        S.op("act", lambda e: e.activation(out=junk[:, 0:64], in_=bk[:, 256:320], func=AF.Square, accum_out=ss_kr[:, t:t + 1]),
                             writes=[Bb, Bjunk], wacc=[Bkr])
                        xr, Bxr, _ = Rxr.next()
                        ropet, Bropet, _ = Rrt.next()
                        S.op("dve", lambda e: e.tensor_tensor(out=xr[:, 0, :], in0=bk[:, 256:320], in1=g_bk[:, 128:192], op=ALU.mult),
                             reads=[Bgs], writes=[Bb, Bxr])
                        xrr, Bxrr, _ = Rxrr.next()
                        rope(xr[:, 0:1, :], xrr[:, 0:1, :], t, ropet[:, :, 0:1, :], Bxr, Bxrr, Bropet, n=1)
                        transposes_to([xrr[:, 0, :]], krT[:, tsl], Bxrr, Bkr, wacc=True)
                    elif btype == "V":
                        vt, Bv, dv = Rv.next()
                        S.op("act", lambda e: e.copy(out=vt, in_=bk), writes=[Bb, Bv])
                        dma("sp", vC_d[tsl, :], vt, reads=[Bv], wacc=[BvC], dsem=dv)
                    elif btype == "Z":
                        zt, Bz, dz = Rz.next()
                        S.op("act", lambda e: e.activation(out=zt, in_=bk, func=AF.Silu), writes=[Bb, Bz])
                        zc0 = (bi - 8) * 512
                        dma("sp", sz_d[tsl, zc0:zc0 + 512], zt, reads=[Bz], wacc=[Bsz], dsem=dz)
                def p2_post(half, bi=bi, btype=btype):
                    if btype == "QK":
                        if bi == 0:
                            dst, Bd = featA_d[0:4], BfA
                        elif bi == 1:
                            dst, Bd = featA_d[4:8], BfA
                        elif bi == 2:
                            dst, Bd = featC_d[0:4], BfC
                        else:
                            dst, Bd = featC_d[4:8], BfC
                        hs_ = slice(half * 1024, (half + 1) * 1024)
                        defer((lambda dst, Bd, hs_: lambda: dma("sp", dst.rearrange("u p n -> p u n")[:, :, hs_], stageA, reads=[BstageA], wacc=[Bd], dsem=d_stage))(dst, Bd, hs_))

                def p2_th(t, st_=p2_start, ti_=p2_tile, po_=p2_post):
                    if t == 0:
                        st_()
                    ti_(t)
                    if t % 8 == 7:
                        po_(t // 8)
                A_thunks += [(lambda t, f: lambda: f(t))(t, p2_th) for t in range(NT)]

            uq_l = w_uq_d[l].rearrange("(kc p) n -> p kc n", p=128)
            ukv_l = w_ukv_d[l].rearrange("(kc p) n -> p kc n", p=128)
            stq_n = stageB[:, 0:2, :]
            stq_r = stageB[:, 2:4, :]
            stk = stageB[:, 0:3, :]
            d_stageB = S.dsem("stgB")
            p2b_blocks = ([("q", qb, uq_l[:, :, qb * 384:(qb + 1) * 384]) for qb in range(3)] +
                          [("k", kb, ukv_l[:, :, kb * 384:(kb + 1) * 384]) for kb in range(2)] +
                          [("v", vb, ukv_l[:, :, 768 + vb * 384:768 + (vb + 1) * 384]) for vb in range(2)])
            wbuf2 = [CX.take([128, 4, 384], BF16) for _ in range(2)]
            Rw2 = Ring(S, wbuf2, "w2")

            def load_w2(src_ap):
                wb, Bw, dw = Rw2.next()
                dma("pool", wb, src_ap, writes=[Bw], dsem=dw)
                return wb, Bw

            def p2b_q_tile(qb, t, wb, Bw):
                tsl = slice(t * 128, (t + 1) * 128)
                tsl8 = slice((t % 8) * 128, (t % 8 + 1) * 128)
                bk, Bb = next_bank()
                proj_mm(bk, Bb, lambda kc: latT[0][:, kc, tsl], Blat[0], wb, Bw, 384, 4)
                run_deferred(keep=KEEP[0])
                dq_new_group()
                ss = stat_take(2)
                rr = stat_take(2)
                Bss, Br = Buf(), Buf()
                for hh in range(2):
                    S.op("act", (lambda hh: lambda e: e.activation(out=junk[:, 0:192], in_=bk[:, hh * 192:(hh + 1) * 192], func=AF.Square,
                                                                  accum_out=ss[:, hh:hh + 1]))(hh), writes=[Bb, Bss, Bjunk])
                rstd(ss, rr, 192, Bss, Br)
                xnt, Bxn, _ = Rxn.next()
                for hh in range(2):
                    c0 = hh * 192
                    S.op("dve", (lambda hh, c0: lambda e: e.scalar_tensor_tensor(out=xnt[:, hh * 128:(hh + 1) * 128], in0=bk[:, c0:c0 + 128],
                                                                                scalar=rr[:, hh:hh + 1], in1=g_bq[:, 0:128], op0=ALU.mult, op1=ALU.mult))(hh, c0),
                         reads=[Br, Bgs], writes=[Bb, Bxn])
                transposes_to([xnt[:, hh * 128:(hh + 1) * 128] for hh in range(2)], stq_n[:, :, tsl8], Bxn, BstageB, wacc=True)
                xr, Bxr, _ = Rxr.next()
                ropet, Bropet, _ = Rrt.next()
                for hh in range(2):
                    c0 = hh * 192 + 128
                    S.op("dve", (lambda hh, c0: lambda e: e.scalar_tensor_tensor(out=xr[:, hh, :], in0=bk[:, c0:c0 + 64],
                                                                                scalar=rr[:, hh:hh + 1], in1=g_bq[:, 128:192], op0=ALU.mult, op1=ALU.mult))(hh, c0),
                         reads=[Br, Bgs], writes=[Bb, Bxr])
                xrr, Bxrr, _ = Rxrr.next()
                rope(xr, xrr, t, ropet, Bxr, Bxrr, Bropet, n=2)
                transposes_to([xrr[:, 0, :], xrr[:, 1, :]], stq_r[:, :, tsl8], Bxrr, BstageB, wacc=True)

            def p2b_k_tile(kb, t, wb, Bw):
                tsl = slice(t * 128, (t + 1) * 128)
                tsl8 = slice((t % 8) * 128, (t % 8 + 1) * 128)
                bk, Bb = next_bank()
                proj_mm(bk, Bb, lambda kc: latT[1][:, kc, tsl], Blat[1], wb, Bw, 384, 4)
                run_deferred(keep=KEEP[0])
                dq_new_group()
                ss = stat_take(3)
                Bss = Buf()
                for hh in range(3):
                    S.op("act", (lambda hh: lambda e: e.activation(out=junk[:, 0:128], in_=bk[:, hh * 128:(hh + 1) * 128], func=AF.Square,
                                                                  accum_out=ss[:, hh:hh + 1]))(hh), writes=[Bb, Bss, Bjunk])
                S.op("dve", lambda e: e.tensor_scalar(out=ss, in0=ss, scalar1=ss_kr[:, t:t + 1], scalar2=None, op0=ALU.add),
                     reads=[Bss, Bkr], writes=[Bss])
                rstd(ss, kscale[:, t, kb * 3:kb * 3 + 3], 192, Bss, Bksc, mul=192.0 ** -0.5)
                xnt, Bxn, _ = Rxn.next()
                S.op("dve", lambda e: e.tensor_tensor(out=xnt[:, 0:384].rearrange("p (a b) -> p a b", a=3), in0=bk[:, 0:384].rearrange("p (a b) -> p a b", a=3),
                                                      in1=g_bk[:, 0:128].unsqueeze(1).to_broadcast([128, 3, 128]), op=ALU.mult),
                     reads=[Bgs], writes=[Bb, Bxn])
                transposes_to([xnt[:, hh * 128:(hh + 1) * 128] for hh in range(3)], stk[:, :, tsl8], Bxn, BstageB, wacc=True)

            def p2b_v_tile(vb, t, wb, Bw):
                tsl = slice(t * 128, (t + 1) * 128)
                bk, Bb = next_bank()
                proj_mm(bk, Bb, lambda kc: latT[1][:, kc, tsl], Blat[1], wb, Bw, 384, 4)
                run_deferred(keep=KEEP[0])
                dq_new_group()
                vt, Bv, dv = Rv.next()
                S.op("act", lambda e: e.copy(out=vt[:, 0:384], in_=bk[:, 0:384]), writes=[Bb, Bv])
                dma("sp", vB_d[tsl, vb * 384:(vb + 1) * 384], vt[:, 0:384], reads=[Bv], wacc=[BvB], dsem=dv)

            P2bst = {}
            B_thunks = []
            for pi, (kind, idx, _src) in enumerate(p2b_blocks):
                ctxb = {}

                def p2b_th(t, pi=pi, kind=kind, idx=idx, ctxb=ctxb):
                    if t == 0:
                        if pi == 0:
                            P2bst["next_w"] = load_w2(p2b_blocks[0][2])
                        ctxb["w"] = P2bst["next_w"]
                        if pi + 1 < len(p2b_blocks):
                            P2bst["next_w"] = load_w2(p2b_blocks[pi + 1][2])
                    wb, Bw = ctxb["w"]
                    if kind == "q":
                        p2b_q_tile(idx, t, wb, Bw)
                    elif kind == "k":
                        p2b_k_tile(idx, t, wb, Bw)
                    else:
                        p2b_v_tile(idx, t, wb, Bw)
                    if t % 8 == 7:
                        hs_ = slice((t // 8) * 1024, (t // 8 + 1) * 1024)
                        if kind == "q":
                            defer((lambda idx, hs_: lambda: (dma("sp", qn_d[idx * 2:idx * 2 + 2].rearrange("u p n -> p u n")[:, :, hs_], stq_n, reads=[BstageB], wacc=[Bqn], dsem=d_stageB),
                                                            dma("sp", qr_d[idx * 2:idx * 2 + 2].rearrange("u p n -> p u n")[:, :, hs_], stq_r, reads=[BstageB], wacc=[Bqr], dsem=d_stageB)))(idx, hs_))
                        elif kind == "k":
                            defer((lambda idx, hs_: lambda: dma("sp", kn_d[idx * 3:idx * 3 + 3].rearrange("u p n -> p u n")[:, :, hs_], stk, reads=[BstageB], wacc=[Bkn], dsem=d_stageB))(idx, hs_))
                B_thunks += [(lambda t, f: lambda: f(t))(t, p2b_th) for t in range(NT)]

            n_pre = 3 * NT
            n_mid = 8 * NT
            for th in A_thunks[:n_pre]:
                th()
            KEEP[0] = 2
            ia, ib, k_ = n_pre, 0, 0
            while ia < n_mid or ib < len(B_thunks):
                if ia < n_mid:
                    A_thunks[ia]()
                    ia += 1
                for _ in range(1 + (k_ % 2)):
                    if ib < len(B_thunks):
                        B_thunks[ib]()
                        ib += 1
                k_ += 1
            KEEP[0] = 1
            for th in A_thunks[n_mid:]:
                th()
            run_deferred()

            S.barrier()
            CX.reset()
            _ = CX.take([128, 4, L], BF16)
            _ = [CX.take([128, 4, L], BF16) for _ in range(2)]
            krT2 = CX.take([128, L], BF16)
            CX.reset()
            opnd = [[CX.take([128, L], BF16) for _ in range(3)] for _ in range(2)]
            szt = [CX.take([128, NT, 128], F32) for _ in range(2)]
            et = [CX.take([128, 640], F32) for _ in range(3)]
            assert CX.off <= 48 * 1024
            CX.off = 48 * 1024 + 4 * 1024
            ptile = [CX.take([128, 640], BF16) for _ in range(6)]
            yg = [CX.take([128, 128], BF16) for _ in range(4)]
            Ret = Ring(S, et, "et", with_dsem=False)
            Rpt = Ring(S, ptile, "pt", with_dsem=False)
            Ryg = Ring(S, yg, "yg", with_dsem=False)
            CW.reset()
            tabA_s = CW.take([128, 6, 384], F32)
            ctab_s = [CW.take([128, 26, 64], F32) for _ in range(2)]
            Btab = Buf("tabA")
            d_tab = S.dsem("tab")
            dma("sp", tabA_s, tabA_d.rearrange("h p n -> p h n"), writes=[Btab], dsem=d_tab)
            Bmix = [Buf(f"mix{i}") for i in range(NT)]
            Rq = Ring(S, [opnd[0][0], opnd[1][0]], "q")
            Rk = Ring(S, [opnd[0][1], opnd[1][1]], "k")
            Rqr = Ring(S, [opnd[0][2], opnd[1][2]], "qr")
            Rsz = Ring(S, szt, "sz")
            Rva = Ring(S, [vaug[0], vaug[1]], "va")
            Rct = Ring(S, ctab_s, "ct")
            mixT = actT
            acc_i = [0]

            def fin_dve(acc_ap, Bacc, szslice, Bszt, sink_col=None):
                r = stat_take(1)
                Br = Buf()
                if sink_col is not None:
                    S.op("dve", lambda e: e.tensor_scalar(out=r, in0=acc_ap[:, 128:129], scalar1=esink[:, sink_col:sink_col + 1], scalar2=None, op0=ALU.add),
                         reads=[Bgs], writes=[Bacc, Br])
                    S.op("dve", lambda e: e.reciprocal(out=r, in_=r), reads=[Br], writes=[Br])
                else:
                    S.op("dve", lambda e: e.reciprocal(out=r, in_=acc_ap[:, 128:129]), writes=[Bacc, Br])
                ygt, Byg, _ = Ryg.next()
                S.op("dve", lambda e: e.scalar_tensor_tensor(out=ygt, in0=acc_ap[:, 0:128], scalar=r, in1=szslice, op0=ALU.mult, op1=ALU.mult),
                     reads=[Br, Bszt], writes=[Bacc, Byg])
                return ygt, Byg

            def fin_tr(ygs, chunk, i0):
                tb, Bt = next_tb()
                n = len(ygs)
                for k_, (ygt, Byg) in enumerate(ygs):
                    S.op("pe", (lambda k_, ygt: lambda e: e.transpose(out=tb[:, k_ * 128:(k_ + 1) * 128], in_=ygt, identity=ident[:]))(k_, ygt),
                         reads=[Byg, Bconst], writes=[Bt])
                S.op("act", lambda e: e.copy(out=mixT[:, chunk, i0 * 128:(i0 + n) * 128], in_=tb[:, 0:n * 128]), writes=[Bt], wacc=[Bmix[i0 + k] for k in range(n)])

            def load_head(ring, src, extra_reads=()):
                ap, B, d = ring.next()
                dma("sp", ap, src, reads=list(extra_reads), writes=[B], dsem=d)
                return ap, B

            def load_v(src_cols, Bsrc):
                ap, B, d = Rva.next()
                dma("sp", ap[:, :, 0:128], src_cols.rearrange("(t p) c -> p t c", p=128), reads=[Bsrc], writes=[B], dsem=d)
                return ap, B

            def load_sz(c0):
                ap, B, d = Rsz.next()
                dma("sp", ap, sz_d[:, c0:c0 + 128].rearrange("(t p) c -> p t c", p=128), reads=[Bsz], writes=[B], dsem=d)
                return ap, B

            SC_A = 128.0 ** -0.5
            A_ops = {}

            def a_load(h):
                kvh = h // 3
                if h % 3 == 0:
                    A_ops[("k", kvh)] = load_head(Rk, featA_d[6 + kvh], [BfA])
                    A_ops[("v", kvh)] = load_v(vA_d[:, kvh * 128:(kvh + 1) * 128], BvA)
                A_ops[("q", h)] = load_head(Rq, featA_d[h], [BfA])
                A_ops[("sz", h)] = load_sz(h * 128)

            A_pts = {}

            def a_sA(h, j):
                kT, Bk = A_ops[("k", h // 3)]
                qT, Bq = A_ops[("q", h)]
                qlo, qhi = max(j - 1, 0), min(j + 1, NT - 1)
                nq = (qhi - qlo + 1) * 128
                tc0 = (qlo - (j - 1)) * 128
                sb_, Bsb = bank[j % 2], Bbank[j % 2]
                S.op("pe", lambda e: e.matmul(sb_[:, 0:nq], lhsT=kT[:, j * 128:(j + 1) * 128], rhs=qT[:, qlo * 128:qlo * 128 + nq], start=True, stop=True),
                     reads=[Bk, Bq], writes=[Bsb])
                e_t, Be, _ = Ret.next()
                S.op("act", lambda e: e.activation(out=e_t[:, 0:nq], in_=sb_[:, 0:nq], func=AF.Exp, scale=SC_A), writes=[Bsb, Be])
                p_t, Bp, _ = Rpt.next()
                S.op("pool" if j % 3 == 2 else "dve", lambda e: e.tensor_tensor(out=p_t[:, 0:nq], in0=e_t[:, 0:nq], in1=tabA_s[:, h, tc0:tc0 + nq], op=ALU.mult),
                     reads=[Be, Btab], writes=[Bp])
                A_pts[(h, j)] = (p_t, Bp, qlo)

            A_yg = {}

            def a_sB(h, i):
                va, Bva = A_ops[("v", h // 3)]
                szh, Bszh = A_ops[("sz", h)]
                ab, Bab = bank[4 + acc_i[0] % 2], Bbank[4 + acc_i[0] % 2]
                acc_i[0] += 1
                js = [jj for jj in (i - 1, i, i + 1) if 0 <= jj < NT]
                for n_, jj in enumerate(js):
                    p_t, Bp, qlo = A_pts[(h, jj)]
                    co = (i - qlo) * 128
                    S.op("pe", (lambda p_t, co, jj, n_: lambda e: e.matmul(ab[:, 0:129], lhsT=p_t[:, co:co + 128], rhs=va[:, jj, :],
                                                                          start=(n_ == 0), stop=(n_ == len(js) - 1)))(p_t, co, jj, n_),
                         reads=[Bp, Bva], writes=[Bab])
                A_yg[(h, i)] = fin_dve(ab, Bab, szh[:, i, :], Bszh, sink_col=h)
                A_pts.pop((h, i - 1), None)

            def a_sD(h, i):
                fin_tr([A_yg.pop((h, i))], h, i)

            nA = 6 * NT
            a_load(0)
            for st in range(nA + 4):
                if st < nA:
                    h, j = divmod(st, NT)
                    if j == 4 and h + 1 < 6:
                        a_load(h + 1)
                    a_sA(h, j)
                if 0 <= st - 2 < nA:
                    a_sB(*divmod(st - 2, NT))
                if 0 <= st - 4 < nA:
                    a_sD(*divmod(st - 4, NT))

            B_ops = {}

            def b_load(h):
                B_ops[("q", h)] = load_head(Rq, qn_d[h], [Bqn])
                B_ops[("qr", h)] = load_head(Rqr, qr_d[h], [Bqr])
                B_ops[("k", h)] = load_head(Rk, kn_d[h], [Bkn])
                B_ops[("v", h)] = load_v(vB_d[:, h * 128:(h + 1) * 128], BvB)
                B_ops[("sz", h)] = load_sz(768 + h * 128)

            B_pts = {}

            def b_sA(h, qt, j):
                qT, Bq = B_ops[("q", h)]
                qrT, Bqr_ = B_ops[("qr", h)]
                kT, Bk = B_ops[("k", h)]
                qs = slice(qt * 512, (qt + 1) * 512)
                ks = slice(j * 128, (j + 1) * 128)
                sb_, Bsb = bank[j % 2], Bbank[j % 2]
                S.op("pe", lambda e: e.matmul(sb_, lhsT=kT[:, ks], rhs=qT[:, qs], start=True, stop=False), reads=[Bk, Bq], writes=[Bsb])
                S.op("pe", lambda e: e.matmul(sb_, lhsT=krT2[:, ks], rhs=qrT[:, qs], start=False, stop=True), reads=[Bkr, Bqr_], writes=[Bsb])
                p_t, Bp, _ = Rpt.next()
                S.op("act", lambda e: e.activation(out=p_t[:, 0:512], in_=sb_, func=AF.Exp, scale=kscale[:, j, h:h + 1]),
                     reads=[Bksc], writes=[Bsb, Bp])
                B_pts[(h, qt, j)] = (p_t, Bp)

            B_yg = {}

            def b_sB(h, qt, j):
                va, Bva = B_ops[("v", h)]
                szh, Bszh = B_ops[("sz", h)]
                p_t, Bp = B_pts.pop((h, qt, j))
                a0 = 2 + 2 * (qt % 2)
                for qb in range(4):
                    ab, Bab = bank[a0 + qb // 2], Bbank[a0 + qb // 2]
                    co = (qb % 2) * 130
                    S.op("pe", (lambda ab, qb, co: lambda e: e.matmul(ab[:, co:co + 129], lhsT=p_t[:, qb * 128:(qb + 1) * 128], rhs=va[:, j, :],
                                                                     start=(j == 0 and qb % 2 == 0), stop=(j == NT - 1), skip_group_check=True))(ab, qb, co),
                         reads=[Bp, Bva], writes=[Bab])
                if j == NT - 1:
                    ygs = []
                    for qb in range(4):
                        ab, Bab = bank[a0 + qb // 2], Bbank[a0 + qb // 2]
                        co = (qb % 2) * 130
                        ygs.append(fin_dve(ab[:, co:co + 129], Bab, szh[:, qt * 4 + qb, :], Bszh))
                    B_yg[(h, qt)] = ygs

            def b_sD(h, qt):
                fin_tr(B_yg.pop((h, qt)), 6 + h, qt * 4)

            nB = 6 * 4 * NT
            b_load(0)
            for st in range(nB + 4):
                if st < nB:
                    h, rem = divmod(st, 4 * NT)
                    qt, j = divmod(rem, NT)
                    if rem == 8 and h + 1 < 6:
                        b_load(h + 1)
                    b_sA(h, qt, j)
                if 0 <= st - 1 < nB:
                    h, rem = divmod(st - 1, 4 * NT)
                    b_sB(h, *divmod(rem, NT))
                if 0 <= st - 4 < nB:
                    h, rem = divmod(st - 4, 4 * NT)
                    qt, j = divmod(rem, NT)
                    if j == NT - 1:
                        b_sD(h, qt)

            SC_C = 128.0 ** -0.5
            C_ops = {}

            def c_load(h):
                C_ops[("q", h)] = load_head(Rq, featC_d[h], [BfC])
                C_ops[("k", h)] = load_head(Rk, featC_d[4 + h], [BfC])
                C_ops[("v", h)] = load_v(vC_d[:, h * 128:(h + 1) * 128], BvC)
                C_ops[("sz", h)] = load_sz(1536 + h * 128)
                ct, Bct, dct = Rct.next()
                dma("sp", ct.rearrange("p a b -> p (a b)"), ctab_d[h], writes=[Bct], dsem=dct)
                C_ops[("ct", h)] = (ct, Bct)

            def c_js(i):
                if i <= 1:
                    return [3, 2, 1, 0]
                if i >= NT - 2:
                    return [15, 14, 13, 12]
                return [i + 2, i + 1, i, i - 1, i - 2]

            C_pts = {}

            def c_sA(h, i):
                qT, Bq = C_ops[("q", h)]
                kT, Bk = C_ops[("k", h)]
                ct, Bct = C_ops[("ct", h)]
                js = c_js(i)
                nj = len(js)
                if 2 <= i <= NT - 3:
                    tsl_ = ct[:, 16:26, :]
                else:
                    b0 = 7 - 2 * (js[0] - i)
                    tsl_ = ct[:, b0:b0 + 2 * nj, :]
                sbase = (i % 2) * 1024
                sb_ = PS[:, sbase:sbase + nj * 128]
                Bs_ = [Bbank[(i % 2) * 2], Bbank[(i % 2) * 2 + 1]]
                for n_, jj in enumerate(js):
                    S.op("pe", (lambda n_, jj: lambda e: e.matmul(PS[:, sbase + n_ * 128:sbase + (n_ + 1) * 128], lhsT=kT[:, jj * 128:(jj + 1) * 128],
                                                                 rhs=qT[:, i * 128:(i + 1) * 128], start=True, stop=True))(n_, jj),
                         reads=[Bk, Bq], writes=Bs_)
                e_t, Be, _ = Ret.next()
                S.op("act", lambda e: e.activation(out=e_t[:, 0:nj * 128], in_=sb_, func=AF.Exp, scale=SC_C), writes=Bs_ + [Be])
                p_t, Bp, _ = Rpt.next()
                S.op("pool" if i % 3 == 2 else "dve", lambda e: e.tensor_tensor(out=p_t[:, 0:nj * 128], in0=e_t[:, 0:nj * 128], in1=tsl_.rearrange("p a b -> p (a b)"), op=ALU.mult),
                     reads=[Be, Bct], writes=[Bp])
                C_pts[(h, i)] = (p_t, Bp)

            C_yg = {}

            def c_sB(h, i):
                va, Bva = C_ops[("v", h)]
                szh, Bszh = C_ops[("sz", h)]
                p_t, Bp = C_pts.pop((h, i))
                js = c_js(i)
                nj = len(js)
                ab, Bab = bank[4 + acc_i[0] % 2], Bbank[4 + acc_i[0] % 2]
                acc_i[0] += 1
                for n_, jj in enumerate(js):
                    S.op("pe", (lambda n_, jj: lambda e: e.matmul(ab[:, 0:129], lhsT=p_t[:, n_ * 128:(n_ + 1) * 128], rhs=va[:, jj, :],
                                                                 start=(n_ == 0), stop=(n_ == nj - 1)))(n_, jj),
                         reads=[Bp, Bva], writes=[Bab])
                C_yg[(h, i)] = fin_dve(ab, Bab, szh[:, i, :], Bszh)

            def c_sD(h, i):
                fin_tr([C_yg.pop((h, i))], 12 + h, i)

            nC = 4 * NT
            c_load(0)
            for st in range(nC + 4):
                if st < nC:
                    h, i = divmod(st, NT)
                    if i == 4 and h + 1 < 4:
                        c_load(h + 1)
                    c_sA(h, i)
                if 0 <= st - 1 < nC:
                    c_sB(*divmod(st - 1, NT))
                if 0 <= st - 3 < nC:
                    c_sD(*divmod(st - 3, NT))

            S.barrier()
            CX.reset()
            CW.reset()
            wbuf = [CW.take([128, 16, 512], BF16) for _ in range(2)]
            Rw = Ring(S, wbuf, "w")
            hsl = [CX.take([128, 4, 512], F32) for _ in range(2)]
            Rhs = Ring(S, hsl, "hs")
            ost = [CX.take([128, 4, 512], F32) for _ in range(2)]
            Ros = Ring(S, ost, "os")
            Bh = Buf("hscr")
            wo_l = w_out_d[l].rearrange("(kc p) n -> p kc n", p=128)
            next_w = load_w(wo_l[:, :, 0:512], 512, 16)

            def p4_group(c, tg, wb, Bw):
                csl = slice(c * 512, (c + 1) * 512)
                rows = slice(tg * 512, (tg + 1) * 512)
                hs, Bhs, dhs = Rhs.next()
                dma("sp", hs, h_src[rows, csl].rearrange("(t p) c -> p t c", p=128), writes=[Bhs], dsem=dhs)
                o, Bo, do = Ros.next()
                for k_ in range(4):
                    t = tg * 4 + k_
                    tsl = slice(t * 128, (t + 1) * 128)
                    bk, Bb = next_bank()
                    proj_mm(bk, Bb, lambda kc: mixT[:, kc, tsl], Bmix[t], wb, Bw, 512, 16)
                    S.op("dve", lambda e: e.tensor_tensor(out=o[:, k_, :], in0=bk, in1=hs[:, k_, :], op=ALU.add), reads=[Bhs], writes=[Bb], wacc=[Bo])
                dma("sp", hscr_d[s][rows, csl].rearrange("(t p) c -> p t c", p=128), o, reads=[Bo], wacc=[Bh], dsem=do)

            for c in range(4):
                wb, Bw = next_w
                if c < 3:
                    next_w = load_w(wo_l[:, :, (c + 1) * 512:(c + 2) * 512], 512, 16)
                for tg in range(4):
                    p4_group(c, tg, wb, Bw)

            S.barrier()
            CX.reset()
            CW.reset()
            hb = [CX.take([128, D], F32) for _ in range(4)]
            ub = [CX.take([128, D], BF16) for _ in range(2)]
            junk = CX.take([128, D], BF16)
            pT = CX.take([128, 2, L], BF16)
            wpp = CX.take([128, 2, D], BF16)
            pf = [CX.take([128, 256], F32) for _ in range(2)]
            pb = [CX.take([128, 256], BF16) for _ in range(2)]
            Rhb = Ring(S, hb, "hb")
            Rub = Ring(S, ub, "ub", with_dsem=False)
            Rpf = Ring(S, pf, "pf")
            Rpb = Ring(S, pb, "pb", with_dsem=False)
            Bjunk = Buf("junk")
            BpT, Bwpp = Buf("pT"), Buf("wpp")
            d_wpp = S.dsem("wpp")
            dma("pool", wpp, w_pp_d[l].rearrange("(kc p) n -> p kc n", p=128), writes=[Bwpp], dsem=d_wpp)
            bcast_load(G2[:], g_ple_d[l, :], D, Bg2, d_g2)
            Bact = [Buf(f"act{t}") for t in range(NT)]
            Brp = Buf("rstd_p")
            for t in range(NT):
                tsl = slice(t * 128, (t + 1) * 128)
                norm_transpose(hscr_d[s][tsl, :], G2, Bg2, t)
                pft, Bpf, dpf = Rpf.next()
                dma("sp", pft, p_d[l, s, tsl, :], writes=[Bpf], dsem=dpf)
                pbt, Bpb, _ = Rpb.next()
                S.op("dve", lambda e: e.tensor_copy(out=pbt, in_=pft), reads=[Bpf], writes=[Bpb])
                transposes_to([pbt[:, 0:128], pbt[:, 128:256]], pT[:, :, tsl], Bpb, BpT, wacc=True, later=False)
                ss = stat_take(4)
                Bss = Buf()
                for c in range(4):
                    bk, Bb = next_bank()
                    for kc in range(2):
                        S.op("pe", (lambda bk, kc, c: lambda e: e.matmul(bk, lhsT=pT[:, kc, tsl], rhs=wpp[:, kc, c * 512:(c + 1) * 512],
                                                                         start=(kc == 0), stop=(kc == 1)))(bk, kc, c), reads=[BpT, Bwpp], writes=[Bb])
                    S.op("act", (lambda bk, c: lambda e: e.activation(out=junk[:, 0:512], in_=bk, func=AF.Square, accum_out=ss[:, c:c + 1]))(bk, c),
                         writes=[Bb, Bjunk, Bss])
                sst = stat_take(1)
                S.op("dve", lambda e: e.tensor_reduce(out=sst, in_=ss, axis=mybir.AxisListType.X, op=ALU.add), reads=[Bss], writes=[Bss])
                rstd(sst, rstd_p[:, t:t + 1], D, Bss, Brp)
            run_deferred()

            S.barrier()
            bcast_load(G1[:], g_post_d[l, :], D, Bg1, d_g1)
            CX.reset()
            _ = [CX.take([128, D], F32) for _ in range(4)]
            _ = [CX.take([128, D], BF16) for _ in range(2)]
            _ = CX.take([128, D], BF16)
            pT_off = CX.off
            pT = CX.take([128, 2, L], BF16)
            wpp = CX.take([128, 2, D], BF16)
            keep = CX.off
            CX.reset()
            hsl = [CX.take([128, 4, 512], F32) for _ in range(2)]
            gat = [CX.take([128, 512], F32) for _ in range(2)]
            pn = [CX.take([128, 512], F32) for _ in range(2)]
            ost = [CX.take([128, 4, 512], F32) for _ in range(2)]
            assert CX.off <= pT_off
            Rhs = Ring(S, hsl, "hs")
            Rg = Ring(S, gat, "g", with_dsem=False)
            Rpn = Ring(S, pn, "pn", with_dsem=False)
            Ros = Ring(S, ost, "os")
            wbuf = [CW.take([128, 16, 512], BF16) for _ in range(2)]
            Rw = Ring(S, wbuf, "w")
            wg_l = w_gate_d[l].rearrange("(kc p) n -> p kc n", p=128)
            next_w = load_w(wg_l[:, :, 0:512], 512, 16)
            Bout = Buf("hout")

            def p5_group(c, tg, wb, Bw):
                csl = slice(c * 512, (c + 1) * 512)
                rows = slice(tg * 512, (tg + 1) * 512)
                hs, Bhs, dhs = Rhs.next()
                dma("sp", hs, hscr_d[s][rows, csl].rearrange("(t p) c -> p t c", p=128), reads=[Bh], writes=[Bhs], dsem=dhs)
                o, Bo, do = Ros.next()
                for k_ in range(4):
                    t = tg * 4 + k_
                    tsl = slice(t * 128, (t + 1) * 128)
                    bk, Bb = next_bank()
                    proj_mm(bk, Bb, lambda kc: actT[:, kc, tsl], Bact[t], wb, Bw, 512, 16)
                    bk2, Bb2 = next_bank()
                    for kc in range(2):
                        S.op("pe", lambda e: e.matmul(bk2, lhsT=pT[:, kc, tsl], rhs=wpp[:, kc, csl], start=(kc == 0), stop=(kc == 1)),
                             reads=[BpT, Bwpp], writes=[Bb2])
                    gt_, Bgt_, _ = Rg.next()
                    S.op("act", lambda e: e.activation(out=gt_, in_=bk, func=AF.Sigmoid), writes=[Bb, Bgt_])
                    pn_, Bpn_, _ = Rpn.next()
                    S.op("dve", lambda e: e.scalar_tensor_tensor(out=pn_, in0=bk2, scalar=rstd_p[:, t:t + 1], in1=G1[:, csl], op0=ALU.mult, op1=ALU.mult),
                         reads=[Brp, Bg1], writes=[Bb2, Bpn_])
                    S.op("pool", lambda e: e.tensor_tensor(out=pn_, in0=pn_, in1=gt_, op=ALU.mult), reads=[Bgt_], writes=[Bpn_])
                    S.op("dve", lambda e: e.tensor_tensor(out=o[:, k_, :], in0=pn_, in1=hs[:, k_, :], op=ALU.add), reads=[Bpn_, Bhs], wacc=[Bo])
                dma("sp", h_dst[rows, csl].rearrange("(t p) c -> p t c", p=128), o, reads=[Bo], wacc=[Bout], dsem=do)

            for c in range(4):
                wb, Bw = next_w
                if c < 3:
                    next_w = load_w(wg_l[:, :, (c + 1) * 512:(c + 2) * 512], 512, 16)
                for tg in range(4):
                    p5_group(c, tg, wb, Bw)

    S.emit()
    return nc, S


def _prep_shared(inp, tables):
    tabA, cos, sin, oh = tables
    f = lambda a: np.ascontiguousarray(np.asarray(a, dtype=np.float32))
    rpb = np.asarray(inp["c_rpb"], np.float32)
    rpbT = np.ascontiguousarray(rpb[:, :, ::-1, :].transpose(0, 1, 3, 2))
    return {
        "w_in": np.ascontiguousarray(np.asarray(inp["w_in"], np.float32)[:, :, IN_PERM]),
        "w_out": f(inp["w_out"]), "w_gate": f(inp["w_ple_gate"]), "w_pp": f(inp["w_ple_proj"]),
        "w_uq": f(inp["b_w_uq"]),
        "w_ukv": np.ascontiguousarray(np.asarray(inp["b_w_ukv"], np.float32)[:, :, UKV_PERM]),
        "g_in": f(inp["norm_in"]), "g_ple": f(inp["ple_norm"]), "g_post": f(inp["ple_post_norm"]),
        "g_aq": f(inp["a_q_norm"]), "g_ak": f(inp["a_k_norm"]), "g_cq": f(inp["c_q_norm"]), "g_ck": f(inp["c_k_norm"]),
        "g_bcq": f(inp["b_cq_norm"]), "g_bckv": f(inp["b_ckv_norm"]), "g_bq": f(inp["b_q_norm"]), "g_bk": f(inp["b_k_norm"]),
        "sink": f(inp["a_sink"]), "rpbT": rpbT,
        "ident": np.eye(128).astype(ml_dtypes.bfloat16),
        "tabA": tabA, "cosT": cos, "sinT": sin, "onehot": oh,
    }


_CACHE = {}


def kernel(**inputs):
    xp = np.asarray(inputs["x_prompt"], np.float32)
    xs = np.asarray(inputs["x_sample"], np.float32)
    pp = np.asarray(inputs["p_prompt"], np.float32)
    psm = np.asarray(inputs["p_sample"], np.float32)
    nB, nS = xp.shape[0], xs.shape[0]
    x_all = np.concatenate([xp, xs], axis=0)
    p_all = np.concatenate([pp, psm], axis=1)
    ntot = nB + nS
    ncores = 8
    slots = [[(c + 8 * k) % ntot if (c + 8 * k) < ntot else (c + 8 * k) % ntot for k in range(NSEQ)] for c in range(ncores)]
    if "nc" not in _CACHE:
        _CACHE["nc"] = build()[0]
        _CACHE["tables"] = _const_tables()
    nc = _CACHE["nc"]
    shared = _prep_shared(inputs, _CACHE["tables"])
    in_maps = []
    for c in range(ncores):
        m = dict(shared)
        m["x"] = np.ascontiguousarray(x_all[slots[c]])
        m["p"] = np.ascontiguousarray(p_all[:, slots[c]])
        in_maps.append(m)
    res = run_bass_kernel_spmd(nc, in_maps, core_ids=list(range(ncores)))
    y_all = np.zeros_like(x_all)
    done = set()
    for c in range(ncores):
        yc = res.results[c]["y"]
        for k, sidx in enumerate(slots[c]):
            if (c + 8 * k) < ntot and sidx not in done:
                y_all[sidx] = yc[k]
                done.add(sidx)
    return (y_all[:nB], y_all[nB:])
```

```python
import types
import numpy as np
from contextlib import ExitStack
import ml_dtypes
import concourse.bass as bass
import concourse.mybir as mybir
from concourse.bass_utils import run_bass_kernel_spmd

F32 = mybir.dt.float32
BF16 = mybir.dt.bfloat16
AF = mybir.ActivationFunctionType
ALU = mybir.AluOpType

L = 2048
D = 2048
NT = 16
DEPTH = 4
NSEQ = 3
EPS = 1e-6
IN_W = 5952

EPOCH = 30000


def _freeze(fn):
    if fn.__closure__ is None:
        return fn
    cells = tuple(types.CellType(c.cell_contents) for c in fn.__closure__)
    return types.FunctionType(fn.__code__, fn.__globals__, fn.__name__, fn.__defaults__, cells)


class Tk:
    __slots__ = ("sem", "val", "q", "dma")

    def __init__(self, sem, val, q, dma=False):
        self.sem = sem
        self.val = val
        self.q = q
        self.dma = dma


class Buf:
    __slots__ = ("name", "w", "r")

    def __init__(self, name=""):
        self.name = name
        self.w = {}
        self.r = {}


class DSem:
    __slots__ = ("h", "count")

    def __init__(self, h):
        self.h = h
        self.count = 0


class Sched:
    QS = ("pe", "act", "dve", "pool", "sp")

    def __init__(self, nc, es):
        self.nc = nc
        self.es = es
        self.qs = {k: [] for k in self.QS}
        self.esem = {}
        self.ecount = {}
        self.waited = {k: {} for k in self.QS}
        self.nsem = 0
        self.dsems = []
        self.dpool = []
        self.pool_i = 0
        for k in self.QS:
            self._new_epoch(k)

    def _alloc_sem(self, name):
        self.nsem += 1
        return self.es.enter_context(self.nc.semaphore(name))

    def _new_epoch(self, k):
        self.esem[k] = self._alloc_sem(f"e{k}{self.nsem}")
        self.ecount[k] = 0

    def dsem(self, name="d", persistent=False):
        if persistent:
            d = DSem(self._alloc_sem(f"{name}{self.nsem}"))
            self.dsems.append(d)
            return d
        if self.pool_i >= len(self.dpool):
            d = DSem(self._alloc_sem(f"dp{self.nsem}"))
            self.dsems.append(d)
            self.dpool.append(d)
        d = self.dpool[self.pool_i]
        self.pool_i += 1
        return d

    def op(self, q, fn, reads=(), writes=(), wacc=(), dsem=None):
        raw = {}
        oth = {}

        def add(d, t):
            k = id(t.sem)
            if k not in d or d[k].val < t.val:
                d[k] = t

        for b in reads:
            for t in b.w.values():
                add(raw, t)
        for b in writes:
            for t in b.w.values():
                add(oth, t)
            for t in b.r.values():
                add(oth, t)
        for b in wacc:
            for t in b.r.values():
                add(oth, t)
        waits = []
        wd = self.waited[q]
        for d, is_raw in ((raw, True), (oth, False)):
            for k, t in d.items():
                if t.q == q and (not t.dma) and dsem is None:
                    if not is_raw or q == "pe":
                        continue
                if wd.get(k, 0) >= t.val:
                    continue
                wd[k] = t.val
                waits.append((t.sem, t.val))
        if dsem is not None:
            dsem.count += 16
            tk = Tk(dsem.h, dsem.count, q, True)
            inc = (dsem.h, 16)
        else:
            if self.ecount[q] >= EPOCH:
                self._new_epoch(q)
            self.ecount[q] += 1
            tk = Tk(self.esem[q], self.ecount[q], q)
            inc = (self.esem[q], 1)
        self.qs[q].append((waits, _freeze(fn), inc))
        k = id(tk.sem)
        for b in writes:
            b.w = {k: tk}
            b.r = {}
        for b in wacc:
            b.w[k] = tk
        for b in reads:
            if k not in b.r or b.r[k].val < tk.val:
                b.r[k] = tk
        return tk

    def barrier(self):
        self.pool_i = 0
        tks = []
        for q in self.QS:
            if self.ecount[q] > 0:
                tks.append((self.esem[q], self.ecount[q], q))
        for d in self.dsems:
            if d.count > 0:
                tks.append((d.h, d.count, None))
        for q in self.QS:
            waits = []
            wd = self.waited[q]
            for (h, v, src) in tks:
                if src == q:
                    continue
                if wd.get(id(h), 0) >= v:
                    continue
                wd[id(h)] = v
                waits.append((h, v))
            if waits:
                self.qs[q].append((waits, None, None))

    def emit(self):
        nc = self.nc
        self.barrier()
        qs = self.qs

        def run(eng, lst):
            for waits, fn, inc in lst:
                for (s, v) in waits:
                    eng.wait_ge(s, v)
                if fn is not None:
                    fn(eng).then_inc(inc[0], inc[1])

        with nc.Block() as block:
            @block.tensor
            def _(e):
                run(e, qs["pe"])

            @block.scalar
            def _(e):
                run(e, qs["act"])

            @block.vector
            def _(e):
                run(e, qs["dve"])

            @block.gpsimd
            def _(e):
                run(e, qs["pool"])

            @block.sync
            def _(e):
                run(e, qs["sp"])


class Ring:
    def __init__(self, S, aps, name, with_dsem=True):
        self.aps = aps
        self.bufs = [Buf(f"{name}{i}") for i in range(len(aps))]
        self.ds = [S.dsem(name) for _ in aps] if with_dsem else [None] * len(aps)
        self.i = 0

    def next(self):
        k = self.i % len(self.aps)
        self.i += 1
        return self.aps[k], self.bufs[k], self.ds[k]


_O = dict(aq=0, ak=768, av=1024, az=1280, bcq=2048, bckv=2560, bkr=3072, bz=3136,
          cq=3904, ck=4416, cv=4928, cz=5440)


def _in_blocks():
    r = lambda a, n: list(range(a, a + n))
    blocks = [
        ("QK", r(_O["aq"], 512)),
        ("QK", r(_O["aq"] + 512, 256) + r(_O["ak"], 256)),
        ("QK", r(_O["cq"], 512)),
        ("QK", r(_O["ck"], 512)),
        ("LAT", r(_O["bcq"], 512)),
        ("LAT", r(_O["bckv"], 512)),
        ("VKR", r(_O["av"], 256) + r(_O["bkr"], 64)),
        ("V", r(_O["cv"], 512)),
        ("Z", r(_O["az"], 512)),
        ("Z", r(_O["az"] + 512, 256) + r(_O["bz"], 256)),
        ("Z", r(_O["bz"] + 256, 512)),
        ("Z", r(_O["cz"], 512)),
    ]
    return blocks


IN_BLOCKS = _in_blocks()
IN_PERM = np.concatenate([np.array(c) for _, c in IN_BLOCKS])
UKV_PERM = np.concatenate([np.arange(h * 256, h * 256 + 128) for h in range(6)] +
                          [np.arange(h * 256 + 128, h * 256 + 256) for h in range(6)])


def _const_tables():
    slopes = 2.0 ** (-8.0 * np.arange(1, 7, dtype=np.float64) / 6)
    kt = np.arange(128)[:, None]
    qo = np.arange(384)[None, :]
    delta = 128 + kt - qo
    sc_a = 128.0 ** -0.5
    tabA = np.zeros((6, 2, 128, 384), ml_dtypes.bfloat16)
    for h in range(6):
        b = np.where(np.abs(delta) <= 128, -slopes[h] * np.abs(delta), -30000.0) / sc_a
        hi = b.astype(np.float32).astype(ml_dtypes.bfloat16)
        lo = (b - hi.astype(np.float64)).astype(np.float32).astype(ml_dtypes.bfloat16)
        tabA[h, 0] = hi
        tabA[h, 1] = lo
    half = 32
    inv = 10000.0 ** (-np.arange(half, dtype=np.float32) / half)
    ang = np.arange(L, dtype=np.float32)[:, None] * inv[None, :]
    cos = np.cos(ang).astype(np.float32)
    sin = np.sin(ang).astype(np.float32)
    kc = np.arange(64)[:, None]
    qc = np.arange(64)[None, :]
    cs = np.clip(qc - 8, 0, 48)
    valid = (kc >= cs) & (kc < cs + 16)
    oh = np.zeros((32, 64, 64), np.float32)
    b = 15 + kc - qc
    for bb in range(31):
        oh[bb] = ((b == bb) & valid).astype(np.float32)
    oh[31] = np.where(valid, 0.0, -30000.0)
    return tabA, cos, sin, oh.reshape(32, 4096)


def build(nseq=NSEQ, nlayers=DEPTH, debug=False):
    nc = bass.Bass("TRN2", target_bir_lowering=False)
    es = ExitStack()

    def din(name, shape, dt=F32):
        return nc.dram_tensor(name, list(shape), dt, kind="ExternalInput").ap()

    dbg_kind = "ExternalOutput" if debug else "Internal"

    def dscr(name, shape, dt=F32):
        return nc.dram_tensor(name, list(shape), dt, kind=dbg_kind).ap()

    x_d = din("x", [nseq, L, D])
    p_d = din("p", [DEPTH, nseq, L, 256])
    y_d = nc.dram_tensor("y", [nseq, L, D], F32, kind="ExternalOutput").ap()
    w_in_d = din("w_in", [DEPTH, D, IN_W])
    w_out_d = din("w_out", [DEPTH, D, D])
    w_gate_d = din("w_gate", [DEPTH, D, D])
    w_pp_d = din("w_pp", [DEPTH, 256, D])
    w_uq_d = din("w_uq", [DEPTH, 512, 1152])
    w_ukv_d = din("w_ukv", [DEPTH, 512, 1536])
    g_in_d = din("g_in", [DEPTH, D])
    g_ple_d = din("g_ple", [DEPTH, D])
    g_post_d = din("g_post", [DEPTH, D])
    g_aq_d = din("g_aq", [DEPTH, 128])
    g_ak_d = din("g_ak", [DEPTH, 128])
    g_cq_d = din("g_cq", [DEPTH, 128])
    g_ck_d = din("g_ck", [DEPTH, 128])
    g_bcq_d = din("g_bcq", [DEPTH, 512])
    g_bckv_d = din("g_bckv", [DEPTH, 512])
    g_bq_d = din("g_bq", [DEPTH, 192])
    g_bk_d = din("g_bk", [DEPTH, 192])
    sink_d = din("sink", [DEPTH, 6])
    rpbT_d = din("rpbT", [DEPTH, 4, 31, 15])
    ident_d = din("ident", [128, 128], BF16)
    tabA_d = din("tabA", [6, 2, 128, 384], BF16)
    cos_d = din("cosT", [L, 32])
    sin_d = din("sinT", [L, 32])
    oh_d = din("onehot", [32, 4096])

    hscr_d = dscr("hscr", [nseq, L, D])
    featA_d = dscr("featA", [8, 128, L], BF16)
    featC_d = dscr("featC", [8, 128, L], BF16)
    qn_d = dscr("qn", [6, 128, L], BF16)
    qr_d = dscr("qr", [6, 128, L], BF16)
    kn_d = dscr("kn", [6, 128, L], BF16)
    vA_d = dscr("vA", [L, 256], BF16)
    vB_d = dscr("vB", [L, 768], BF16)
    vC_d = dscr("vC", [L, 512], BF16)
    sz_d = dscr("sz", [L, D])
    xd_d = dscr("xd", [4, 15, 4096])
    ctab_d = dscr("ctab", [4, 128, 26 * 64])

    S = Sched(nc, es)

    def sbt(name, shape, dt):
        return nc.alloc_sbuf_tensor(name, list(shape), dt)

    ident = sbt("ident_s", [128, 128], BF16)
    cosT = sbt("cosT_s", [128, NT, 32], F32)
    sinT = sbt("sinT_s", [128, NT, 32], F32)
    G1 = sbt("G1", [128, D], F32)
    g_aq = sbt("g_aq_s", [128, 128], F32)
    g_ak = sbt("g_ak_s", [128, 128], F32)
    g_cq = sbt("g_cq_s", [128, 128], F32)
    g_ck = sbt("g_ck_s", [128, 128], F32)
    g_bq = sbt("g_bq_s", [128, 192], F32)
    g_bk = sbt("g_bk_s", [128, 192], F32)
    g_bcq = sbt("g_bcq_s", [128, 512], F32)
    g_bckv = sbt("g_bckv_s", [128, 512], F32)
    esink = sbt("esink", [128, 8], F32)
    kscale = sbt("kscale", [128, NT, 6], F32)
    ss_kr = sbt("ss_kr", [128, NT], F32)
    rstd_p = sbt("rstd_p", [128, NT], F32)
    stat = sbt("stat", [128, 512], F32)
    vaug = [sbt(f"vaug{i}", [128, NT, 129], BF16) for i in range(2)]
    ARENA_ACT = sbt("arena_act", [128, 16 * L], BF16)
    ARENA_W = sbt("arena_w", [128, 16 * 1024], BF16)
    XBYTES = 80 * 1024
    ARENA_X = sbt("arena_x", [128, XBYTES // 4], F32)

    actT = ARENA_ACT[:, :].rearrange("p (c n) -> p c n", c=16)
    PS = nc.alloc_psum_tensor("ps", [128, 3072], F32)
    PT = nc.alloc_psum_tensor("pt", [128, 2048], BF16)
    bank = [PS[:, i * 512:(i + 1) * 512] for i in range(6)]
    Bbank = [Buf(f"bank{i}") for i in range(6)]
    tbank = [PT[:, i * 1024:(i + 1) * 1024] for i in range(2)]
    Btb = [Buf(f"tb{i}") for i in range(2)]

    class Carver:
        def __init__(self, arena_f32, nbytes):
            self.a = arena_f32
            self.n = nbytes
            self.off = 0

        def reset(self):
            self.off = 0

        def take(self, shape, dt):
            esz = 4 if dt == F32 else 2
            nel = int(np.prod(shape[1:]))
            nb = nel * esz
            nb_al = (nb + 31) // 32 * 32
            assert self.off + nb_al <= self.n, ("arena overflow", self.off, nb_al, self.n)
            v = self.a[:, self.off // 4:(self.off + nb_al) // 4]
            if dt != F32:
                v = v.bitcast(dt)
            v = v[0:shape[0], 0:nel]
            if len(shape) == 3:
                v = v.rearrange("p (a b) -> p a b", a=shape[1])
            self.off += nb_al
            return v

    CX = Carver(ARENA_X, XBYTES)
    ARENA_W32 = ARENA_W[:, :].bitcast(F32)
    CW = Carver(ARENA_W32, 32 * 1024)

    Bconst = Buf("const")
    d_const = S.dsem("const", persistent=True)

    def dma(q, out, in_, reads=(), writes=(), wacc=(), dsem=None):
        assert dsem is not None
        return S.op(q, lambda e: e.dma_start(out=out, in_=in_), reads=reads, writes=writes,
                    wacc=wacc, dsem=dsem)

    dma("sp", ident[:], ident_d, wacc=[Bconst], dsem=d_const)
    dma("sp", cosT[:], cos_d.rearrange("(t p) i -> p t i", p=128), wacc=[Bconst], dsem=d_const)
    dma("sp", sinT[:], sin_d.rearrange("(t p) i -> p t i", p=128), wacc=[Bconst], dsem=d_const)
    for i in range(2):
        S.op("dve", (lambda i: lambda e: e.memset(vaug[i][:, :, 128:129], 1.0))(i), wacc=[Bconst])
    S.barrier()

    stat_i = [0]

    def stat_take(n):
        if stat_i[0] + n > 512:
            stat_i[0] = 0
        a = stat[:, stat_i[0]:stat_i[0] + n]
        stat_i[0] += n
        return a

    def rstd_from_ss(ss_ap, n, count, eps=EPS, extra_scale=None, out=None):
        r = out if out is not None else stat_take(n)
        b = Buf("rstd")
        return r, b

    bank_i = [0]

    def next_bank():
        k = bank_i[0] % 6
        bank_i[0] += 1
        return bank[k], Bbank[k]

    tb_i = [0]

    def next_tb():
        k = tb_i[0] % 2
        tb_i[0] += 1
        return tbank[k], Btb[k]

    tr_i = [0]
    DQ = []

    def dq_new_group():
        DQ.append([])

    def defer(thunk):
        if not DQ:
            DQ.append([])
        DQ[-1].append(thunk)

    def run_deferred(keep=0):
        while len(DQ) > keep:
            for th in DQ.pop(0):
                th()

    _orig_barrier = S.barrier

    def _checked_barrier():
        assert not DQ, "deferred work pending at barrier"
        _orig_barrier()
    S.barrier = _checked_barrier

    def bcast_load(dst, src_row, n, buf, ds):
        dma("sp", dst, src_row.partition_broadcast(128), writes=[buf], dsem=ds)

    Bg1, Bgs = Buf("G1"), Buf("gsm")
    G2, Bg2 = G1, Bg1
    d_g1, d_g2, d_gs = S.dsem("g1", True), S.dsem("g2", True), S.dsem("gs", True)

    ROPE_ENG = "pool"

    def rope(xr, out_bf, t, tmp, Bx, Bout, Btmp, n=1):
        c = cosT[:, t, :].unsqueeze(1).to_broadcast([128, n, 32])
        s_ = sinT[:, t, :].unsqueeze(1).to_broadcast([128, n, 32])
        x1 = xr[:, :, 0:32]
        x2 = xr[:, :, 32:64]
        q = ROPE_ENG
        S.op(q, lambda e: e.tensor_tensor(out=tmp[:, 0], in0=x1, in1=c, op=ALU.mult), reads=[Bx, Bconst], writes=[Btmp])
        S.op(q, lambda e: e.tensor_tensor(out=tmp[:, 1], in0=x2, in1=s_, op=ALU.mult), reads=[Bx], writes=[Btmp])
        S.op(q, lambda e: e.tensor_tensor(out=tmp[:, 2], in0=x1, in1=s_, op=ALU.mult), reads=[Bx], writes=[Btmp])
        S.op(q, lambda e: e.tensor_tensor(out=tmp[:, 3], in0=x2, in1=c, op=ALU.mult), reads=[Bx], writes=[Btmp])
        S.op(q, lambda e: e.tensor_tensor(out=out_bf[:, :, 0:32], in0=tmp[:, 0], in1=tmp[:, 1], op=ALU.subtract), reads=[Btmp], writes=[Bout])
        S.op(q, lambda e: e.tensor_tensor(out=out_bf[:, :, 32:64], in0=tmp[:, 2], in1=tmp[:, 3], op=ALU.add), reads=[Btmp], writes=[Bout])

    def sumsq(ps_ap, junk_ap, ss_ap, Bps, Bjunk, Bss):
        S.op("act", lambda e: e.activation(out=junk_ap, in_=ps_ap, func=AF.Square, accum_out=ss_ap),
             writes=[Bps, Bjunk, Bss])

    def rstd(ss_ap, r_ap, count, Bss, Br, mul=None):
        m2 = 1.0 if mul is None else float(mul) ** 2
        S.op("act", lambda e: e.activation(out=r_ap, in_=ss_ap, func=AF.Sqrt, scale=1.0 / (count * m2), bias=EPS / m2),
             reads=[Bss], writes=[Br])
        S.op("dve", lambda e: e.reciprocal(out=r_ap, in_=r_ap), reads=[Br], writes=[Br])

    for l in range(nlayers):
        S.barrier()
        for (dst, src, n) in ((g_aq, g_aq_d, 128), (g_ak, g_ak_d, 128), (g_cq, g_cq_d, 128), (g_ck, g_ck_d, 128),
                              (g_bq, g_bq_d, 192), (g_bk, g_bk_d, 192), (g_bcq, g_bcq_d, 512), (g_bckv, g_bckv_d, 512)):
            dma("sp", dst[:], src[l, :].partition_broadcast(128), wacc=[Bgs], dsem=d_gs)
        dma("sp", esink[:, 0:6], sink_d[l, :].partition_broadcast(128), wacc=[Bgs], dsem=d_gs)
        S.barrier()
        S.op("act", lambda e: e.activation(out=esink[:, 0:6], in_=esink[:, 0:6], func=AF.Exp), writes=[Bgs])
        CX.reset()
        CW.reset()
        oh_s = CX.take([32, 4096], F32)
        xa_s = CX.take([15, 4096], F32)
        rt_s = CX.take([32, 16], F32)
        xtr = CW.take([128, 26, 64], F32)
        Boh, Bxa, Brt, Bxtr, Bxd = Buf(), Buf(), Buf(), Buf(), Buf()
        d_t1, d_t2 = d_g1, d_g2
        dma("sp", oh_s, oh_d, writes=[Boh], dsem=d_t1)
        for h in range(4):
            S.op("dve", lambda e: e.memset(rt_s[:, 0:15], 1.0), writes=[Brt])
            dma("sp", rt_s[0:31, 0:15], rpbT_d[l, h], writes=[Brt], dsem=d_t2)
            for c in range(8):
                bk, Bb = next_bank()
                S.op("pe", (lambda bk, c: lambda e: e.matmul(bk[0:15, :], lhsT=rt_s[:, 0:15], rhs=oh_s[:, c * 512:(c + 1) * 512],
                                                              start=True, stop=True))(bk, c), reads=[Brt, Boh], writes=[Bb])
                S.op("act", (lambda bk, c: lambda e: e.activation(out=xa_s[:, c * 512:(c + 1) * 512], in_=bk[0:15, :], func=AF.Exp))(bk, c),
                     writes=[Bb, Bxa])
            dma("sp", xd_d[h], xa_s, reads=[Bxa], writes=[Bxd], dsem=d_t1)
            S.op("dve", lambda e: e.memset(xtr, 0.0), writes=[Bxtr])
            src = xd_d[h].rearrange("a (k q) -> k a q", k=64)
            dma("sp", xtr[0:64, 0:15, :], src, reads=[Bxd], writes=[Bxtr], dsem=d_t2)
            dma("sp", xtr[64:128, 1:16, :], src, reads=[Bxd], writes=[Bxtr], dsem=d_t2)
            S.op("dve", lambda e: e.tensor_copy(out=xtr[:, 16:26, :], in_=xtr[:, 3:13, :]), reads=[Bxtr], writes=[Bxtr])
            S.op("dve", lambda e: e.memset(xtr[:, 16, :], 0.0), writes=[Bxtr])
            S.op("dve", lambda e: e.memset(xtr[64:128, 17, :], 0.0), writes=[Bxtr])
            S.op("dve", lambda e: e.memset(xtr[0:64, 25, :], 0.0), writes=[Bxtr])
            dma("sp", ctab_d[h], xtr.rearrange("p a b -> p (a b)"), reads=[Bxtr], writes=[Bxd], dsem=d_t1)
        S.barrier()

        for s in range(nseq):
            h_src = x_d[s] if l == 0 else hscr_d[s]
            h_dst = y_d[s] if l == nlayers - 1 else hscr_d[s]

            S.barrier()
            CX.reset()
            hb = [CX.take([128, D], F32) for _ in range(4)]
            ub = [CX.take([128, D], BF16) for _ in range(2)]
            junk = CX.take([128, D], BF16)
            Bjunk = Buf("junk")
            Rhb = Ring(S, hb, "hb")
            Rub = Ring(S, ub, "ub", with_dsem=False)
            bcast_load(G1[:], g_in_d[l, :], D, Bg1, d_g1)
            Bact = [Buf(f"act{t}") for t in range(NT)]

            nt_loads = {}

            def nt_load(src_rows, t):
                hbt, Bh, dh = Rhb.next()
                dma("sp", hbt, src_rows, writes=[Bh], dsem=dh)
                nt_loads[t] = (hbt, Bh)

            def norm_transpose(src_rows, gtile, Bgt, t, pre=None):
                if t not in nt_loads:
                    nt_load(src_rows, t)
                hbt, Bh = nt_loads.pop(t)
                ss = stat_take(1)
                rr = stat_take(1)
                Bss, Br = Buf(), Buf()
                S.op("act", lambda e: e.activation(out=junk, in_=hbt, func=AF.Square, accum_out=ss),
                     reads=[Bh], writes=[Bjunk, Bss])
                rstd(ss, rr, D, Bss, Br)
                ubt, Bu, _ = Rub.next()
                S.op("dve", lambda e: e.scalar_tensor_tensor(out=ubt, in0=hbt, scalar=rr, in1=gtile[:], op0=ALU.mult, op1=ALU.mult),
                     reads=[Bh, Br, Bgt], writes=[Bu])
                run_deferred()

                def tr_part():
                    for half in range(2):
                        tb, Bt = next_tb()
                        for k in range(8):
                            kc = half * 8 + k
                            S.op("pe", (lambda tb, k, kc: lambda e: e.transpose(out=tb[:, k * 128:(k + 1) * 128], in_=ubt[:, kc * 128:(kc + 1) * 128],
                                                                                identity=ident[:]))(tb, k, kc), reads=[Bu, Bconst], writes=[Bt])
                        dst = actT[:, half * 8:(half + 1) * 8, t * 128:(t + 1) * 128]
                        if half == 0:
                            S.op("act", (lambda tb, dst: lambda e: e.copy(out=dst, in_=tb.rearrange("p (a b) -> p a b", a=8)))(tb, dst),
                                 writes=[Bt], wacc=[Bact[t]])
                        else:
                            S.op("dve", (lambda tb, dst: lambda e: e.tensor_copy(out=dst, in_=tb.rearrange("p (a b) -> p a b", a=8)))(tb, dst),
                                 writes=[Bt], wacc=[Bact[t]])
                dq_new_group()
                defer(tr_part)

            for t in range(3):
                nt_load(h_src[t * 128:(t + 1) * 128, :], t)
            for t in range(NT):
                if t + 3 < NT:
                    nt_load(h_src[(t + 3) * 128:(t + 4) * 128, :], t + 3)
                norm_transpose(h_src[t * 128:(t + 1) * 128, :], G1, Bg1, t)
            run_deferred()

            S.barrier()
            CX.reset()
            CW.reset()
            wbuf = [CW.take([128, 16, 512], BF16) for _ in range(2)]
            Rw = Ring(S, wbuf, "w")
            stageT = CX.take([128, 4, L], BF16)
            stageA = stageT[:, :, 0:1024]
            stageB = stageT[:, :, 1024:2048]
            BstageA, BstageB = Buf("stageA"), Buf("stageB")
            d_stage = S.dsem("stg")
            latT = [CX.take([128, 4, L], BF16) for _ in range(2)]
            Blat = [Buf("lat0"), Buf("lat1")]
            krT = CX.take([128, L], BF16)
            Bkr = Buf("krT")
            xn = [CX.take([128, 512], BF16) for _ in range(5)]
            Rxn = Ring(S, xn, "xn", with_dsem=False)
            zst = [CX.take([128, 512], F32) for _ in range(2)]
            Rz = Ring(S, zst, "z")
            vst = [CX.take([128, 512], BF16) for _ in range(2)]
            Rv = Ring(S, vst, "v")
            xr2 = [CX.take([128, 2, 64], F32) for _ in range(3)]
            ropet2 = [CX.take([128, 4, 2 * 32], F32).rearrange("p a (n c) -> p a n c", n=2) for _ in range(3)]
            xrr4 = CX.take([128, 8, 128], BF16)
            junk = CX.take([128, 512], BF16)
            Rxr = Ring(S, xr2, "xr", with_dsem=False)
            Rrt = Ring(S, ropet2, "rt", with_dsem=False)
            Rxrr = Ring(S, [xrr4[:, 2 * i:2 * i + 2, :] for i in range(4)], "xrr", with_dsem=False)
            S.op("dve", lambda e: e.memset(xrr4[:, :, 64:128], 0.0), writes=Rxrr.bufs)
            Bsz, BvA, BvB, BvC, BfA, BfC = Buf("sz"), Buf("vA"), Buf("vB"), Buf("vC"), Buf("fA"), Buf("fC")
            Bqn, Bqr, Bkn = Buf("qn"), Buf("qr"), Buf("kn")
            Bksc = Buf("kscale")

            def load_w(src_ap, ncols, nk):
                wb, Bw, dw = Rw.next()
                dst = wb[:, 0:nk, 0:ncols]
                dma("pool", dst, src_ap, writes=[Bw], dsem=dw)
                return wb, Bw

            def proj_mm(bk, Bb, lhs_of_kc, Blhs, wb, Bw, ncols, nk):
                for kc in range(nk):
                    lhs = lhs_of_kc(kc)
                    S.op("pe", lambda e: e.matmul(bk[:, 0:ncols], lhsT=lhs, rhs=wb[:, kc, 0:ncols],
                                                  start=(kc == 0), stop=(kc == nk - 1)),
                         reads=[Blhs, Bw], writes=[Bb])

            def transposes_to(srcs, dst_ap, Bsrc, Bdst, wacc=False, later=True):
                if later:
                    defer(lambda: transposes_to(srcs, dst_ap, Bsrc, Bdst, wacc=wacc, later=False))
                    return
                tb, Bt = next_tb()
                n = len(srcs)
                for k, sap in enumerate(srcs):
                    S.op("pe", (lambda k, sap: lambda e: e.transpose(out=tb[:, k * 128:(k + 1) * 128], in_=sap, identity=ident[:]))(k, sap),
                         reads=[Bsrc, Bconst], writes=[Bt])
                if n == 1:
                    src = tb[:, 0:128]
                else:
                    src = tb[:, 0:n * 128].rearrange("p (a b) -> p a b", a=n)
                tr_i[0] += 1
                if tr_i[0] % 2 == 0:
                    fn_ = ("act", lambda e: e.copy(out=dst_ap, in_=src))
                else:
                    fn_ = ("dve", lambda e: e.tensor_copy(out=dst_ap, in_=src))
                if wacc:
                    S.op(fn_[0], fn_[1], writes=[Bt], wacc=[Bdst])
                else:
                    S.op(fn_[0], fn_[1], writes=[Bt, Bdst])

            in_w_l = w_in_d[l].rearrange("(kc p) n -> p kc n", p=128)
            col_start = np.concatenate([[0], np.cumsum([len(c) for _, c in IN_BLOCKS])]).tolist()
            ORDER = [4, 5, 6, 0, 1, 2, 3, 7, 8, 9, 10, 11]
            b0_ = ORDER[0]
            P2st = {"next_w": load_w(in_w_l[:, :, col_start[b0_]:col_start[b0_] + len(IN_BLOCKS[b0_][1])], len(IN_BLOCKS[b0_][1]), 16)}
            KEEP = [1]
            A_thunks = []
            for oi, bi in enumerate(ORDER):
                btype, cols = IN_BLOCKS[bi]
                ncols = len(cols)
                gl = None
                if btype == "QK":
                    if bi == 0:
                        gl = [g_aq] * 4
                    elif bi == 1:
                        gl = [g_aq, g_aq, g_ak, g_ak]
                    elif bi == 2:
                        gl = [g_cq] * 4
                    else:
                        gl = [g_ck] * 4
                ctx = {}

                def p2_start(oi=oi, btype=btype, ctx=ctx):
                    ctx["w"] = P2st["next_w"]
                    if oi + 1 < len(ORDER):
                        nb_ = ORDER[oi + 1]
                        nn = len(IN_BLOCKS[nb_][1])
                        P2st["next_w"] = load_w(in_w_l[:, :, col_start[nb_]:col_start[nb_] + nn], nn, 16)
                    if btype == "VKR":
                        S.op("dve", lambda e: e.memset(krT[64:128, :], 0.0), writes=[Bkr])

                def p2_tile(t, bi=bi, btype=btype, ctx=ctx, ncols=ncols, gl=gl):
                    wb, Bw = ctx["w"]
                    bk, Bb = next_bank()
                    proj_mm(bk, Bb, lambda kc: actT[:, kc, t * 128:(t + 1) * 128], Bact[t], wb, Bw, ncols, 16)
                    run_deferred(keep=KEEP[0])
                    dq_new_group()
                    tsl = slice(t * 128, (t + 1) * 128)
                    tsl8 = slice((t % 8) * 128, (t % 8 + 1) * 128)
                    if btype == "QK":
                        ss = stat_take(4)
                        rr = stat_take(4)
                        Bss, Br = Buf(), Buf()
                        for u in range(4):
                            S.op("act", (lambda u: lambda e: e.activation(out=junk[:, 0:128], in_=bk[:, u * 128:(u + 1) * 128], func=AF.Square,
                                                                         accum_out=ss[:, u:u + 1]))(u), writes=[Bb, Bss, Bjunk])
                        rstd(ss, rr, 128, Bss, Br)
                        xnt, Bxn, _ = Rxn.next()
                        for u in range(4):
                            S.op("dve", (lambda u: lambda e: e.scalar_tensor_tensor(out=xnt[:, u * 128:(u + 1) * 128], in0=bk[:, u * 128:(u + 1) * 128],
                                                                                   scalar=rr[:, u:u + 1], in1=gl[u][:], op0=ALU.mult, op1=ALU.mult))(u),
                                 reads=[Br, Bgs], writes=[Bb, Bxn])
                        transposes_to([xnt[:, u * 128:(u + 1) * 128] for u in range(4)], stageA[:, :, tsl8], Bxn, BstageA, wacc=True)
                    elif btype == "LAT":
                        which = bi - 4
                        gt = g_bcq if which == 0 else g_bckv
                        ss = stat_take(1)
                        rr = stat_take(1)
                        Bss, Br = Buf(), Buf()
                        S.op("act", lambda e: e.activation(out=junk, in_=bk, func=AF.Square, accum_out=ss), writes=[Bb, Bss, Bjunk])
                        rstd(ss, rr, 512, Bss, Br)
                        xnt, Bxn, _ = Rxn.next()
                        S.op("dve", lambda e: e.scalar_tensor_tensor(out=xnt, in0=bk, scalar=rr, in1=gt[:], op0=ALU.mult, op1=ALU.mult),
                             reads=[Br, Bgs], writes=[Bb, Bxn])
                        transposes_to([xnt[:, u * 128:(u + 1) * 128] for u in range(4)], latT[which][:, :, tsl], Bxn, Blat[which], wacc=True)
                    elif btype == "VKR":
                        vt, Bv, dv = Rv.next()
                        S.op("act", lambda e: e.copy(out=vt[:, 0:256], in_=bk[:, 0:256]), writes=[Bb, Bv])
                        dma("sp", vA_d[tsl, :], vt[:, 0:256], reads=[Bv], wacc=[BvA], dsem=dv)
                        S.op("act", lambda e: e.activation(out=junk[:, 0:64], in_=bk[:, 256:320], func=AF.Square, accum_out=ss_kr[:, t:t + 1]),
                             writes=[Bb, Bjunk], wacc=[Bkr])
                        xr, Bxr, _ = Rxr.next()
                        ropet, Bropet, _ = Rrt.next()
                        S.op("dve", lambda e: e.tensor_tensor(out=xr[:, 0, :], in0=bk[:, 256:320], in1=g_bk[:, 128:192], op=ALU.mult),
                             reads=[Bgs], writes=[Bb, Bxr])
                        xrr, Bxrr, _ = Rxrr.next()
                        rope(xr[:, 0:1, :], xrr[:, 0:1, :], t, ropet[:, :, 0:1, :], Bxr, Bxrr, Bropet, n=1)
                        transposes_to([xrr[:, 0, :]], krT[:, tsl], Bxrr, Bkr, wacc=True)
                    elif btype == "V":
                        vt, Bv, dv = Rv.next()
                        S.op("act", lambda e: e.copy(out=vt, in_=bk), writes=[Bb, Bv])
                        dma("sp", vC_d[tsl, :], vt, reads=[Bv], wacc=[BvC], dsem=dv)
                    elif btype == "Z":
                        zt, Bz, dz = Rz.next()
                        S.op("act", lambda e: e.activation(out=zt, in_=bk, func=AF.Silu), writes=[Bb, Bz])
                        zc0 = (bi - 8) * 512
                        dma("sp", sz_d[tsl, zc0:zc0 + 512], zt, reads=[Bz], wacc=[Bsz], dsem=dz)
                def p2_post(half, bi=bi, btype=btype):
                    if btype == "QK":
                        if bi == 0:
                            dst, Bd = featA_d[0:4], BfA
                        elif bi == 1:
                            dst, Bd = featA_d[4:8], BfA
                        elif bi == 2:
                            dst, Bd = featC_d[0:4], BfC
                        else:
                            dst, Bd = featC_d[4:8], BfC
                        hs_ = slice(half * 1024, (half + 1) * 1024)
                        defer((lambda dst, Bd, hs_: lambda: dma("sp", dst.rearrange("u p n -> p u n")[:, :, hs_], stageA, reads=[BstageA], wacc=[Bd], dsem=d_stage))(dst, Bd, hs_))

                def p2_th(t, st_=p2_start, ti_=p2_tile, po_=p2_post):
                    if t == 0:
                        st_()
                    ti_(t)
                    if t % 8 == 7:
                        po_(t // 8)
                A_thunks += [(lambda t, f: lambda: f(t))(t, p2_th) for t in range(NT)]

            uq_l = w_uq_d[l].rearrange("(kc p) n -> p kc n", p=128)
            ukv_l = w_ukv_d[l].rearrange("(kc p) n -> p kc n", p=128)
            stq_n = stageB[:, 0:2, :]
            stq_r = stageB[:, 2:4, :]
            stk = stageB[:, 0:3, :]
            d_stageB = S.dsem("stgB")
            p2b_blocks = ([("q", qb, uq_l[:, :, qb * 384:(qb + 1) * 384]) for qb in range(3)] +
                          [("k", kb, ukv_l[:, :, kb * 384:(kb + 1) * 384]) for kb in range(2)] +
                          [("v", vb, ukv_l[:, :, 768 + vb * 384:768 + (vb + 1) * 384]) for vb in range(2)])
            wbuf2 = [CX.take([128, 4, 384], BF16) for _ in range(2)]
            Rw2 = Ring(S, wbuf2, "w2")

            def load_w2(src_ap):
                wb, Bw, dw = Rw2.next()
                dma("pool", wb, src_ap, writes=[Bw], dsem=dw)
                return wb, Bw

            def p2b_q_tile(qb, t, wb, Bw):
                tsl = slice(t * 128, (t + 1) * 128)
                tsl8 = slice((t % 8) * 128, (t % 8 + 1) * 128)
                bk, Bb = next_bank()
                proj_mm(bk, Bb, lambda kc: latT[0][:, kc, tsl], Blat[0], wb, Bw, 384, 4)
                run_deferred(keep=KEEP[0])
                dq_new_group()
                ss = stat_take(2)
                rr = stat_take(2)
                Bss, Br = Buf(), Buf()
                for hh in range(2):
                    S.op("act", (lambda hh: lambda e: e.activation(out=junk[:, 0:192], in_=bk[:, hh * 192:(hh + 1) * 192], func=AF.Square,
                                                                  accum_out=ss[:, hh:hh + 1]))(hh), writes=[Bb, Bss, Bjunk])
                rstd(ss, rr, 192, Bss, Br)
                xnt, Bxn, _ = Rxn.next()
                for hh in range(2):
                    c0 = hh * 192
                    S.op("dve", (lambda hh, c0: lambda e: e.scalar_tensor_tensor(out=xnt[:, hh * 128:(hh + 1) * 128], in0=bk[:, c0:c0 + 128],
                                                                                scalar=rr[:, hh:hh + 1], in1=g_bq[:, 0:128], op0=ALU.mult, op1=ALU.mult))(hh, c0),
                         reads=[Br, Bgs], writes=[Bb, Bxn])
                transposes_to([xnt[:, hh * 128:(hh + 1) * 128] for hh in range(2)], stq_n[:, :, tsl8], Bxn, BstageB, wacc=True)
                xr, Bxr, _ = Rxr.next()
                ropet, Bropet, _ = Rrt.next()
                for hh in range(2):
                    c0 = hh * 192 + 128
                    S.op("dve", (lambda hh, c0: lambda e: e.scalar_tensor_tensor(out=xr[:, hh, :], in0=bk[:, c0:c0 + 64],
                                                                                scalar=rr[:, hh:hh + 1], in1=g_bq[:, 128:192], op0=ALU.mult, op1=ALU.mult))(hh, c0),
                         reads=[Br, Bgs], writes=[Bb, Bxr])
                xrr, Bxrr, _ = Rxrr.next()
                rope(xr, xrr, t, ropet, Bxr, Bxrr, Bropet, n=2)
                transposes_to([xrr[:, 0, :], xrr[:, 1, :]], stq_r[:, :, tsl8], Bxrr, BstageB, wacc=True)

            def p2b_k_tile(kb, t, wb, Bw):
                tsl = slice(t * 128, (t + 1) * 128)
                tsl8 = slice((t % 8) * 128, (t % 8 + 1) * 128)
                bk, Bb = next_bank()
                proj_mm(bk, Bb, lambda kc: latT[1][:, kc, tsl], Blat[1], wb, Bw, 384, 4)
                run_deferred(keep=KEEP[0])
                dq_new_group()
                ss = stat_take(3)
                Bss = Buf()
                for hh in range(3):
                    S.op("act", (lambda hh: lambda e: e.activation(out=junk[:, 0:128], in_=bk[:, hh * 128:(hh + 1) * 128], func=AF.Square,
                                                                  accum_out=ss[:, hh:hh + 1]))(hh), writes=[Bb, Bss, Bjunk])
                S.op("dve", lambda e: e.tensor_scalar(out=ss, in0=ss, scalar1=ss_kr[:, t:t + 1], scalar2=None, op0=ALU.add),
                     reads=[Bss, Bkr], writes=[Bss])
                rstd(ss, kscale[:, t, kb * 3:kb * 3 + 3], 192, Bss, Bksc, mul=192.0 ** -0.5)
                xnt, Bxn, _ = Rxn.next()
                S.op("dve", lambda e: e.tensor_tensor(out=xnt[:, 0:384].rearrange("p (a b) -> p a b", a=3), in0=bk[:, 0:384].rearrange("p (a b) -> p a b", a=3),
                                                      in1=g_bk[:, 0:128].unsqueeze(1).to_broadcast([128, 3, 128]), op=ALU.mult),
                     reads=[Bgs], writes=[Bb, Bxn])
                transposes_to([xnt[:, hh * 128:(hh + 1) * 128] for hh in range(3)], stk[:, :, tsl8], Bxn, BstageB, wacc=True)

            def p2b_v_tile(vb, t, wb, Bw):
                tsl = slice(t * 128, (t + 1) * 128)
                bk, Bb = next_bank()
                proj_mm(bk, Bb, lambda kc: latT[1][:, kc, tsl], Blat[1], wb, Bw, 384, 4)
                run_deferred(keep=KEEP[0])
                dq_new_group()
                vt, Bv, dv = Rv.next()
                S.op("act", lambda e: e.copy(out=vt[:, 0:384], in_=bk[:, 0:384]), writes=[Bb, Bv])
                dma("sp", vB_d[tsl, vb * 384:(vb + 1) * 384], vt[:, 0:384], reads=[Bv], wacc=[BvB], dsem=dv)

            P2bst = {}
            B_thunks = []
            for pi, (kind, idx, _src) in enumerate(p2b_blocks):
                ctxb = {}

                def p2b_th(t, pi=pi, kind=kind, idx=idx, ctxb=ctxb):
                    if t == 0:
                        if pi == 0:
                            P2bst["next_w"] = load_w2(p2b_blocks[0][2])
                        ctxb["w"] = P2bst["next_w"]
                        if pi + 1 < len(p2b_blocks):
                            P2bst["next_w"] = load_w2(p2b_blocks[pi + 1][2])
                    wb, Bw = ctxb["w"]
                    if kind == "q":
                        p2b_q_tile(idx, t, wb, Bw)
                    elif kind == "k":
                        p2b_k_tile(idx, t, wb, Bw)
                    else:
                        p2b_v_tile(idx, t, wb, Bw)
                    if t % 8 == 7:
                        hs_ = slice((t // 8) * 1024, (t // 8 + 1) * 1024)
                        if kind == "q":
                            defer((lambda idx, hs_: lambda: (dma("sp", qn_d[idx * 2:idx * 2 + 2].rearrange("u p n -> p u n")[:, :, hs_], stq_n, reads=[BstageB], wacc=[Bqn], dsem=d_stageB),
                                                            dma("sp", qr_d[idx * 2:idx * 2 + 2].rearrange("u p n -> p u n")[:, :, hs_], stq_r, reads=[BstageB], wacc=[Bqr], dsem=d_stageB)))(idx, hs_))
                        elif kind == "k":
                            defer((lambda idx, hs_: lambda: dma("sp", kn_d[idx * 3:idx * 3 + 3].rearrange("u p n -> p u n")[:, :, hs_], stk, reads=[BstageB], wacc=[Bkn], dsem=d_stageB))(idx, hs_))
                B_thunks += [(lambda t, f: lambda: f(t))(t, p2b_th) for t in range(NT)]

            n_pre = 3 * NT
            n_mid = 8 * NT
            for th in A_thunks[:n_pre]:
                th()
            KEEP[0] = 3
            ia, ib, k_ = n_pre, 0, 0
            while ia < n_mid or ib < len(B_thunks):
                if ia < n_mid:
                    A_thunks[ia]()
                    ia += 1
                for _ in range(1 + (k_ % 2)):
                    if ib < len(B_thunks):
                        B_thunks[ib]()
                        ib += 1
                k_ += 1
            KEEP[0] = 1
            for th in A_thunks[n_mid:]:
                th()
            run_deferred()

            S.barrier()
            CX.reset()
            _ = CX.take([128, 4, L], BF16)
            _ = [CX.take([128, 4, L], BF16) for _ in range(2)]
            krT2 = CX.take([128, L], BF16)
            CX.reset()
            opnd = [[CX.take([128, L], BF16) for _ in range(3)] for _ in range(2)]
            szt = [CX.take([128, NT, 128], F32) for _ in range(2)]
            et = [CX.take([128, 640], F32) for _ in range(3)]
            assert CX.off <= 48 * 1024
            CX.off = 48 * 1024 + 4 * 1024
            ptile = [CX.take([128, 640], BF16) for _ in range(6)]
            yg = [CX.take([128, 128], BF16) for _ in range(4)]
            Ret = Ring(S, et, "et", with_dsem=False)
            Rpt = Ring(S, ptile, "pt", with_dsem=False)
            Ryg = Ring(S, yg, "yg", with_dsem=False)
            CW.reset()
            tabA_s = CW.take([128, 12, 384], BF16)
            ctab_s = [CW.take([128, 26, 64], F32) for _ in range(2)]
            Btab = Buf("tabA")
            d_tab = S.dsem("tab")
            dma("sp", tabA_s, tabA_d.rearrange("h c p n -> p (h c) n"), writes=[Btab], dsem=d_tab)
            Bmix = [Buf(f"mix{i}") for i in range(NT)]
            Rq = Ring(S, [opnd[0][0], opnd[1][0]], "q")
            Rk = Ring(S, [opnd[0][1], opnd[1][1]], "k")
            Rqr = Ring(S, [opnd[0][2], opnd[1][2]], "qr")
            Rsz = Ring(S, szt, "sz")
            Rva = Ring(S, [vaug[0], vaug[1]], "va")
            Rct = Ring(S, ctab_s, "ct")
            mixT = actT
            acc_i = [0]

            def fin_dve(acc_ap, Bacc, szslice, Bszt, sink_col=None):
                r = stat_take(1)
                Br = Buf()
                if sink_col is not None:
                    S.op("dve", lambda e: e.tensor_scalar(out=r, in0=acc_ap[:, 128:129], scalar1=esink[:, sink_col:sink_col + 1], scalar2=None, op0=ALU.add),
                         reads=[Bgs], writes=[Bacc, Br])
                    S.op("dve", lambda e: e.reciprocal(out=r, in_=r), reads=[Br], writes=[Br])
                else:
                    S.op("dve", lambda e: e.reciprocal(out=r, in_=acc_ap[:, 128:129]), writes=[Bacc, Br])
                ygt, Byg, _ = Ryg.next()
                S.op("dve", lambda e: e.scalar_tensor_tensor(out=ygt, in0=acc_ap[:, 0:128], scalar=r, in1=szslice, op0=ALU.mult, op1=ALU.mult),
                     reads=[Br, Bszt], writes=[Bacc, Byg])
                return ygt, Byg

            def fin_tr(ygs, chunk, i0):
                tb, Bt = next_tb()
                n = len(ygs)
                for k_, (ygt, Byg) in enumerate(ygs):
                    S.op("pe", (lambda k_, ygt: lambda e: e.transpose(out=tb[:, k_ * 128:(k_ + 1) * 128], in_=ygt, identity=ident[:]))(k_, ygt),
                         reads=[Byg, Bconst], writes=[Bt])
                S.op("act", lambda e: e.copy(out=mixT[:, chunk, i0 * 128:(i0 + n) * 128], in_=tb[:, 0:n * 128]), writes=[Bt], wacc=[Bmix[i0 + k] for k in range(n)])

            def load_head(ring, src, extra_reads=()):
                ap, B, d = ring.next()
                dma("sp", ap, src, reads=list(extra_reads), writes=[B], dsem=d)
                return ap, B

            def load_v(src_cols, Bsrc):
                ap, B, d = Rva.next()
                dma("sp", ap[:, :, 0:128], src_cols.rearrange("(t p) c -> p t c", p=128), reads=[Bsrc], writes=[B], dsem=d)
                return ap, B

            def load_sz(c0):
                ap, B, d = Rsz.next()
                dma("sp", ap, sz_d[:, c0:c0 + 128].rearrange("(t p) c -> p t c", p=128), reads=[Bsz], writes=[B], dsem=d)
                return ap, B

            SC_A = 128.0 ** -0.5
            A_ops = {}

            def a_load(h):
                kvh = h // 3
                if h % 3 == 0:
                    A_ops[("k", kvh)] = load_head(Rk, featA_d[6 + kvh], [BfA])
                    A_ops[("v", kvh)] = load_v(vA_d[:, kvh * 128:(kvh + 1) * 128], BvA)
                A_ops[("q", h)] = load_head(Rq, featA_d[h], [BfA])
                A_ops[("sz", h)] = load_sz(h * 128)

            A_pts = {}

            def a_sA(h, j):
                kT, Bk = A_ops[("k", h // 3)]
                qT, Bq = A_ops[("q", h)]
                qlo, qhi = max(j - 1, 0), min(j + 1, NT - 1)
                nq = (qhi - qlo + 1) * 128
                tc0 = (qlo - (j - 1)) * 128
                sb_, Bsb = bank[j % 2], Bbank[j % 2]
                S.op("pe", lambda e: e.matmul(sb_[:, 0:nq], lhsT=ident[:], rhs=tabA_s[:, 2 * h, tc0:tc0 + nq], start=True, stop=False),
                     reads=[Btab, Bconst], writes=[Bsb])
                S.op("pe", lambda e: e.matmul(sb_[:, 0:nq], lhsT=ident[:], rhs=tabA_s[:, 2 * h + 1, tc0:tc0 + nq], start=False, stop=False),
                     reads=[Btab, Bconst], writes=[Bsb])
                S.op("pe", lambda e: e.matmul(sb_[:, 0:nq], lhsT=kT[:, j * 128:(j + 1) * 128], rhs=qT[:, qlo * 128:qlo * 128 + nq], start=False, stop=True),
                     reads=[Bk, Bq], writes=[Bsb])
                p_t, Bp, _ = Rpt.next()
                S.op("act", lambda e: e.activation(out=p_t[:, 0:nq], in_=sb_[:, 0:nq], func=AF.Exp, scale=SC_A), writes=[Bsb, Bp])
                A_pts[(h, j)] = (p_t, Bp, qlo)

            A_yg = {}

            def a_sB(h, i):
                va, Bva = A_ops[("v", h // 3)]
                szh, Bszh = A_ops[("sz", h)]
                ab, Bab = bank[4 + acc_i[0] % 2], Bbank[4 + acc_i[0] % 2]
                acc_i[0] += 1
                js = [jj for jj in (i - 1, i, i + 1) if 0 <= jj < NT]
                for n_, jj in enumerate(js):
                    p_t, Bp, qlo = A_pts[(h, jj)]
                    co = (i - qlo) * 128
                    S.op("pe", (lambda p_t, co, jj, n_: lambda e: e.matmul(ab[:, 0:129], lhsT=p_t[:, co:co + 128], rhs=va[:, jj, :],
                                                                          start=(n_ == 0), stop=(n_ == len(js) - 1)))(p_t, co, jj, n_),
                         reads=[Bp, Bva], writes=[Bab])
                A_yg[(h, i)] = fin_dve(ab, Bab, szh[:, i, :], Bszh, sink_col=h)
                A_pts.pop((h, i - 1), None)

            def a_sD(h, i):
                fin_tr([A_yg.pop((h, i))], h, i)

            nA = 6 * NT
            a_load(0)
            for st in range(nA + 4):
                if st < nA:
                    h, j = divmod(st, NT)
                    if j == 4 and h + 1 < 6:
                        a_load(h + 1)
                    a_sA(h, j)
                if 0 <= st - 2 < nA:
                    a_sB(*divmod(st - 2, NT))
                if 0 <= st - 4 < nA:
                    a_sD(*divmod(st - 4, NT))

            B_ops = {}

            def b_load(h):
                B_ops[("q", h)] = load_head(Rq, qn_d[h], [Bqn])
                B_ops[("qr", h)] = load_head(Rqr, qr_d[h], [Bqr])
                B_ops[("k", h)] = load_head(Rk, kn_d[h], [Bkn])
                B_ops[("v", h)] = load_v(vB_d[:, h * 128:(h + 1) * 128], BvB)
                B_ops[("sz", h)] = load_sz(768 + h * 128)

            B_pts = {}

            def b_sA(h, qt, j):
                qT, Bq = B_ops[("q", h)]
                qrT, Bqr_ = B_ops[("qr", h)]
                kT, Bk = B_ops[("k", h)]
                qs = slice(qt * 512, (qt + 1) * 512)
                ks = slice(j * 128, (j + 1) * 128)
                sb_, Bsb = bank[j % 2], Bbank[j % 2]
                S.op("pe", lambda e: e.matmul(sb_, lhsT=kT[:, ks], rhs=qT[:, qs], start=True, stop=False), reads=[Bk, Bq], writes=[Bsb])
                S.op("pe", lambda e: e.matmul(sb_, lhsT=krT2[:, ks], rhs=qrT[:, qs], start=False, stop=True), reads=[Bkr, Bqr_], writes=[Bsb])
                p_t, Bp, _ = Rpt.next()
                S.op("act", lambda e: e.activation(out=p_t[:, 0:512], in_=sb_, func=AF.Exp, scale=kscale[:, j, h:h + 1]),
                     reads=[Bksc], writes=[Bsb, Bp])
                B_pts[(h, qt, j)] = (p_t, Bp)

            B_yg = {}

            def b_sB(h, qt, j):
                va, Bva = B_ops[("v", h)]
                szh, Bszh = B_ops[("sz", h)]
                p_t, Bp = B_pts.pop((h, qt, j))
                a0 = 2 + 2 * (qt % 2)
                for qb in range(4):
                    ab, Bab = bank[a0 + qb // 2], Bbank[a0 + qb // 2]
                    co = (qb % 2) * 130
                    S.op("pe", (lambda ab, qb, co: lambda e: e.matmul(ab[:, co:co + 129], lhsT=p_t[:, qb * 128:(qb + 1) * 128], rhs=va[:, j, :],
                                                                     start=(j == 0 and qb % 2 == 0), stop=(j == NT - 1), skip_group_check=True))(ab, qb, co),
                         reads=[Bp, Bva], writes=[Bab])
                if j == NT - 1:
                    ygs = []
                    for qb in range(4):
                        ab, Bab = bank[a0 + qb // 2], Bbank[a0 + qb // 2]
                        co = (qb % 2) * 130
                        ygs.append(fin_dve(ab[:, co:co + 129], Bab, szh[:, qt * 4 + qb, :], Bszh))
                    B_yg[(h, qt)] = ygs

            def b_sD(h, qt):
                fin_tr(B_yg.pop((h, qt)), 6 + h, qt * 4)

            nB = 6 * 4 * NT
            b_load(0)
            for st in range(nB + 4):
                if st < nB:
                    h, rem = divmod(st, 4 * NT)
                    qt, j = divmod(rem, NT)
                    if rem == 8 and h + 1 < 6:
                        b_load(h + 1)
                    b_sA(h, qt, j)
                if 0 <= st - 1 < nB:
                    h, rem = divmod(st - 1, 4 * NT)
                    b_sB(h, *divmod(rem, NT))
                if 0 <= st - 4 < nB:
                    h, rem = divmod(st - 4, 4 * NT)
                    qt, j = divmod(rem, NT)
                    if j == NT - 1:
                        b_sD(h, qt)

            SC_C = 128.0 ** -0.5
            C_ops = {}

            def c_load(h):
                C_ops[("q", h)] = load_head(Rq, featC_d[h], [BfC])
                C_ops[("k", h)] = load_head(Rk, featC_d[4 + h], [BfC])
                C_ops[("v", h)] = load_v(vC_d[:, h * 128:(h + 1) * 128], BvC)
                C_ops[("sz", h)] = load_sz(1536 + h * 128)
                ct, Bct, dct = Rct.next()
                dma("sp", ct.rearrange("p a b -> p (a b)"), ctab_d[h], writes=[Bct], dsem=dct)
                C_ops[("ct", h)] = (ct, Bct)

            def c_js(i):
                if i <= 1:
                    return [3, 2, 1, 0]
                if i >= NT - 2:
                    return [15, 14, 13, 12]
                return [i + 2, i + 1, i, i - 1, i - 2]

            C_pts = {}

            def c_sA(h, i):
                qT, Bq = C_ops[("q", h)]
                kT, Bk = C_ops[("k", h)]
                ct, Bct = C_ops[("ct", h)]
                js = c_js(i)
                nj = len(js)
                if 2 <= i <= NT - 3:
                    tsl_ = ct[:, 16:26, :]
                else:
                    b0 = 7 - 2 * (js[0] - i)
                    tsl_ = ct[:, b0:b0 + 2 * nj, :]
                sbase = (i % 2) * 1024
                sb_ = PS[:, sbase:sbase + nj * 128]
                Bs_ = [Bbank[(i % 2) * 2], Bbank[(i % 2) * 2 + 1]]
                for n_, jj in enumerate(js):
                    S.op("pe", (lambda n_, jj: lambda e: e.matmul(PS[:, sbase + n_ * 128:sbase + (n_ + 1) * 128], lhsT=kT[:, jj * 128:(jj + 1) * 128],
                                                                 rhs=qT[:, i * 128:(i + 1) * 128], start=True, stop=True))(n_, jj),
                         reads=[Bk, Bq], writes=Bs_)
                e_t, Be, _ = Ret.next()
                S.op("act", lambda e: e.activation(out=e_t[:, 0:nj * 128], in_=sb_, func=AF.Exp, scale=SC_C), writes=Bs_ + [Be])
                p_t, Bp, _ = Rpt.next()
                S.op("pool" if i % 3 == 2 else "dve", lambda e: e.tensor_tensor(out=p_t[:, 0:nj * 128], in0=e_t[:, 0:nj * 128], in1=tsl_.rearrange("p a b -> p (a b)"), op=ALU.mult),
                     reads=[Be, Bct], writes=[Bp])
                C_pts[(h, i)] = (p_t, Bp)

            C_yg = {}

            def c_sB(h, i):
                va, Bva = C_ops[("v", h)]
                szh, Bszh = C_ops[("sz", h)]
                p_t, Bp = C_pts.pop((h, i))
                js = c_js(i)
                nj = len(js)
                ab, Bab = bank[4 + acc_i[0] % 2], Bbank[4 + acc_i[0] % 2]
                acc_i[0] += 1
                for n_, jj in enumerate(js):
                    S.op("pe", (lambda n_, jj: lambda e: e.matmul(ab[:, 0:129], lhsT=p_t[:, n_ * 128:(n_ + 1) * 128], rhs=va[:, jj, :],
                                                                 start=(n_ == 0), stop=(n_ == nj - 1)))(n_, jj),
                         reads=[Bp, Bva], writes=[Bab])
                C_yg[(h, i)] = fin_dve(ab, Bab, szh[:, i, :], Bszh)

            def c_sD(h, i):
                fin_tr([C_yg.pop((h, i))], 12 + h, i)

            nC = 4 * NT
            c_load(0)
            for st in range(nC + 4):
                if st < nC:
                    h, i = divmod(st, NT)
                    if i == 4 and h + 1 < 4:
                        c_load(h + 1)
                    c_sA(h, i)
                if 0 <= st - 1 < nC:
                    c_sB(*divmod(st - 1, NT))
                if 0 <= st - 3 < nC:
                    c_sD(*divmod(st - 3, NT))

            S.barrier()
            CX.reset()
            CW.reset()
            wbuf = [CW.take([128, 16, 512], BF16) for _ in range(2)]
            Rw = Ring(S, wbuf, "w")
            hsl = [CX.take([128, 4, 512], F32) for _ in range(2)]
            Rhs = Ring(S, hsl, "hs")
            ost = [CX.take([128, 4, 512], F32) for _ in range(2)]
            Ros = Ring(S, ost, "os")
            Bh = Buf("hscr")
            wo_l = w_out_d[l].rearrange("(kc p) n -> p kc n", p=128)
            next_w = load_w(wo_l[:, :, 0:512], 512, 16)

            def p4_group(c, tg, wb, Bw):
                csl = slice(c * 512, (c + 1) * 512)
                rows = slice(tg * 512, (tg + 1) * 512)
                hs, Bhs, dhs = Rhs.next()
                dma("sp", hs, h_src[rows, csl].rearrange("(t p) c -> p t c", p=128), writes=[Bhs], dsem=dhs)
                o, Bo, do = Ros.next()
                for k_ in range(4):
                    t = tg * 4 + k_
                    tsl = slice(t * 128, (t + 1) * 128)
                    bk, Bb = next_bank()
                    proj_mm(bk, Bb, lambda kc: mixT[:, kc, tsl], Bmix[t], wb, Bw, 512, 16)
                    S.op("dve", lambda e: e.tensor_tensor(out=o[:, k_, :], in0=bk, in1=hs[:, k_, :], op=ALU.add), reads=[Bhs], writes=[Bb], wacc=[Bo])
                dma("sp", hscr_d[s][rows, csl].rearrange("(t p) c -> p t c", p=128), o, reads=[Bo], wacc=[Bh], dsem=do)

            for c in range(4):
                wb, Bw = next_w
                if c < 3:
                    next_w = load_w(wo_l[:, :, (c + 1) * 512:(c + 2) * 512], 512, 16)
                for tg in range(4):
                    p4_group(c, tg, wb, Bw)

            S.barrier()
            CX.reset()
            CW.reset()
            hb = [CX.take([128, D], F32) for _ in range(4)]
            ub = [CX.take([128, D], BF16) for _ in range(2)]
            junk = CX.take([128, D], BF16)
            pT = CX.take([128, 2, L], BF16)
            wpp = CX.take([128, 2, D], BF16)
            pf = [CX.take([128, 256], F32) for _ in range(4)]
            pb = [CX.take([128, 256], BF16) for _ in range(4)]
            Rhb = Ring(S, hb, "hb")
            Rub = Ring(S, ub, "ub", with_dsem=False)
            Rpf = Ring(S, pf, "pf")
            Rpb = Ring(S, pb, "pb", with_dsem=False)
            Bjunk = Buf("junk")
            BpT, Bwpp = Buf("pT"), Buf("wpp")
            d_wpp = S.dsem("wpp")
            dma("pool", wpp, w_pp_d[l].rearrange("(kc p) n -> p kc n", p=128), writes=[Bwpp], dsem=d_wpp)
            bcast_load(G2[:], g_ple_d[l, :], D, Bg2, d_g2)
            Bact = [Buf(f"act{t}") for t in range(NT)]
            Brp = Buf("rstd_p")
            hrows = lambda t: hscr_d[s][t * 128:(t + 1) * 128, :]
            for t in range(3):
                nt_load(hrows(t), t)
            BpTt = [Buf(f"pT{t}") for t in range(NT)]
            for t in range(NT):
                tsl = slice(t * 128, (t + 1) * 128)
                pft, Bpf, dpf = Rpf.next()
                dma("sp", pft, p_d[l, s, tsl, :], writes=[Bpf], dsem=dpf)
                pbt, Bpb, _ = Rpb.next()
                S.op("dve", lambda e: e.tensor_copy(out=pbt, in_=pft), reads=[Bpf], writes=[Bpb])
                transposes_to([pbt[:, 0:128], pbt[:, 128:256]], pT[:, :, tsl], Bpb, BpTt[t], wacc=True, later=False)
            for t in range(NT):
                tsl = slice(t * 128, (t + 1) * 128)
                ss = stat_take(4)
                Bss = Buf()
                for c in range(4):
                    bk, Bb = next_bank()
                    for kc in range(2):
                        S.op("pe", lambda e: e.matmul(bk, lhsT=pT[:, kc, tsl], rhs=wpp[:, kc, c * 512:(c + 1) * 512], start=(kc == 0), stop=(kc == 1)),
                             reads=[BpTt[t], Bwpp], writes=[Bb])
                    S.op("act", lambda e: e.activation(out=junk[:, 0:512], in_=bk, func=AF.Square, accum_out=ss[:, c:c + 1]),
                         writes=[Bb, Bjunk], wacc=[Bss])
                sst = stat_take(1)
                Bsst = Buf()
                S.op("dve", lambda e: e.tensor_reduce(out=sst, in_=ss, axis=mybir.AxisListType.X, op=ALU.add), reads=[Bss], writes=[Bsst])
                rstd(sst, rstd_p[:, t:t + 1], D, Bsst, Brp)
            BpT = Buf("pT")
            for t in range(NT):
                for tk_ in BpTt[t].w.values():
                    kk_ = id(tk_.sem)
                    if kk_ not in BpT.w or BpT.w[kk_].val < tk_.val:
                        BpT.w[kk_] = tk_
            for t in range(NT):
                if t + 3 < NT:
                    nt_load(hrows(t + 3), t + 3)
                norm_transpose(hrows(t), G2, Bg2, t)
            run_deferred()

            S.barrier()
            bcast_load(G1[:], g_post_d[l, :], D, Bg1, d_g1)
            CX.reset()
            _ = [CX.take([128, D], F32) for _ in range(4)]
            _ = [CX.take([128, D], BF16) for _ in range(2)]
            _ = CX.take([128, D], BF16)
            pT_off = CX.off
            pT = CX.take([128, 2, L], BF16)
            wpp = CX.take([128, 2, D], BF16)
            keep = CX.off
            CX.reset()
            hsl = [CX.take([128, 4, 512], F32) for _ in range(2)]
            gat = [CX.take([128, 512], F32) for _ in range(2)]
            pn = [CX.take([128, 512], F32) for _ in range(2)]
            ost = [CX.take([128, 4, 512], F32) for _ in range(2)]
            assert CX.off <= pT_off
            Rhs = Ring(S, hsl, "hs")
            Rg = Ring(S, gat, "g", with_dsem=False)
            Rpn = Ring(S, pn, "pn", with_dsem=False)
            Ros = Ring(S, ost, "os")
            wbuf = [CW.take([128, 16, 512], BF16) for _ in range(2)]
            Rw = Ring(S, wbuf, "w")
            wg_l = w_gate_d[l].rearrange("(kc p) n -> p kc n", p=128)
            next_w = load_w(wg_l[:, :, 0:512], 512, 16)
            Bout = Buf("hout")

            def p5_group(c, tg, wb, Bw):
                csl = slice(c * 512, (c + 1) * 512)
                rows = slice(tg * 512, (tg + 1) * 512)
                hs, Bhs, dhs = Rhs.next()
                dma("sp", hs, hscr_d[s][rows, csl].rearrange("(t p) c -> p t c", p=128), reads=[Bh], writes=[Bhs], dsem=dhs)
                o, Bo, do = Ros.next()
                for k_ in range(4):
                    t = tg * 4 + k_
                    tsl = slice(t * 128, (t + 1) * 128)
                    bk, Bb = next_bank()
                    proj_mm(bk, Bb, lambda kc: actT[:, kc, tsl], Bact[t], wb, Bw, 512, 16)
                    bk2, Bb2 = next_bank()
                    for kc in range(2):
                        S.op("pe", lambda e: e.matmul(bk2, lhsT=pT[:, kc, tsl], rhs=wpp[:, kc, csl], start=(kc == 0), stop=(kc == 1)),
                             reads=[BpT, Bwpp], writes=[Bb2])
                    gt_, Bgt_, _ = Rg.next()
                    S.op("act", lambda e: e.activation(out=gt_, in_=bk, func=AF.Sigmoid), writes=[Bb, Bgt_])
                    pn_, Bpn_, _ = Rpn.next()
                    S.op("dve", lambda e: e.scalar_tensor_tensor(out=pn_, in0=bk2, scalar=rstd_p[:, t:t + 1], in1=G1[:, csl], op0=ALU.mult, op1=ALU.mult),
                         reads=[Brp, Bg1], writes=[Bb2, Bpn_])
                    S.op("pool", lambda e: e.tensor_tensor(out=pn_, in0=pn_, in1=gt_, op=ALU.mult), reads=[Bgt_], writes=[Bpn_])
                    S.op("dve", lambda e: e.tensor_tensor(out=o[:, k_, :], in0=pn_, in1=hs[:, k_, :], op=ALU.add), reads=[Bpn_, Bhs], wacc=[Bo])
                dma("sp", h_dst[rows, csl].rearrange("(t p) c -> p t c", p=128), o, reads=[Bo], wacc=[Bout], dsem=do)

            for c in range(4):
                wb, Bw = next_w
                if c < 3:
                    next_w = load_w(wg_l[:, :, (c + 1) * 512:(c + 2) * 512], 512, 16)
                for tg in range(4):
                    p5_group(c, tg, wb, Bw)

    S.emit()
    return nc, S


def _prep_shared(inp, tables):
    tabA, cos, sin, oh = tables
    f = lambda a: np.ascontiguousarray(np.asarray(a, dtype=np.float32))
    rpb = np.asarray(inp["c_rpb"], np.float32)
    rpbT = np.ascontiguousarray(rpb[:, :, ::-1, :].transpose(0, 1, 3, 2))
    return {
        "w_in": np.ascontiguousarray(np.asarray(inp["w_in"], np.float32)[:, :, IN_PERM]),
        "w_out": f(inp["w_out"]), "w_gate": f(inp["w_ple_gate"]), "w_pp": f(inp["w_ple_proj"]),
        "w_uq": f(inp["b_w_uq"]),
        "w_ukv": np.ascontiguousarray(np.asarray(inp["b_w_ukv"], np.float32)[:, :, UKV_PERM]),
        "g_in": f(inp["norm_in"]), "g_ple": f(inp["ple_norm"]), "g_post": f(inp["ple_post_norm"]),
        "g_aq": f(inp["a_q_norm"]), "g_ak": f(inp["a_k_norm"]), "g_cq": f(inp["c_q_norm"]), "g_ck": f(inp["c_k_norm"]),
        "g_bcq": f(inp["b_cq_norm"]), "g_bckv": f(inp["b_ckv_norm"]), "g_bq": f(inp["b_q_norm"]), "g_bk": f(inp["b_k_norm"]),
        "sink": f(inp["a_sink"]), "rpbT": rpbT,
        "ident": np.eye(128).astype(ml_dtypes.bfloat16),
        "tabA": tabA, "cosT": cos, "sinT": sin, "onehot": oh,
    }


_CACHE = {}


def kernel(**inputs):
    xp = np.asarray(inputs["x_prompt"], np.float32)
    xs = np.asarray(inputs["x_sample"], np.float32)
    pp = np.asarray(inputs["p_prompt"], np.float32)
    psm = np.asarray(inputs["p_sample"], np.float32)
    nB, nS = xp.shape[0], xs.shape[0]
    x_all = np.concatenate([xp, xs], axis=0)
    p_all = np.concatenate([pp, psm], axis=1)
    ntot = nB + nS
    ncores = 8
    slots = [[(c + 8 * k) % ntot if (c + 8 * k) < ntot else (c + 8 * k) % ntot for k in range(NSEQ)] for c in range(ncores)]
    if "nc" not in _CACHE:
        _CACHE["nc"] = build()[0]
        _CACHE["tables"] = _const_tables()
    nc = _CACHE["nc"]
    shared = _prep_shared(inputs, _CACHE["tables"])
    in_maps = []
    for c in range(ncores):
        m = dict(shared)
        m["x"] = np.ascontiguousarray(x_all[slots[c]])
        m["p"] = np.ascontiguousarray(p_all[:, slots[c]])
        in_maps.append(m)
    res = run_bass_kernel_spmd(nc, in_maps, core_ids=list(range(ncores)))
    y_all = np.zeros_like(x_all)
    done = set()
    for c in range(ncores):
        yc = res.results[c]["y"]
        for k, sidx in enumerate(slots[c]):
            if (c + 8 * k) < ntot and sidx not in done:
                y_all[sidx] = yc[k]
                done.add(sidx)
    return (y_all[:nB], y_all[nB:])
```

```python
import types
import numpy as np
from contextlib import ExitStack
import ml_dtypes
import concourse.bass as bass
import concourse.mybir as mybir
from concourse.bass_utils import run_bass_kernel_spmd

F32 = mybir.dt.float32
BF16 = mybir.dt.bfloat16
AF = mybir.ActivationFunctionType
ALU = mybir.AluOpType

L = 2048
D = 2048
NT = 16
DEPTH = 4
NSEQ = 3
EPS = 1e-6
IN_W = 5952

EPOCH = 30000


def _freeze(fn):
    if fn.__closure__ is None:
        return fn
    cells = tuple(types.CellType(c.cell_contents) for c in fn.__closure__)
    return types.FunctionType(fn.__code__, fn.__globals__, fn.__name__, fn.__defaults__, cells)


class Tk:
    __slots__ = ("sem", "val", "q", "dma")

    def __init__(self, sem, val, q, dma=False):
        self.sem = sem
        self.val = val
        self.q = q
        self.dma = dma


class Buf:
    __slots__ = ("name", "w", "r")

    def __init__(self, name=""):
        self.name = name
        self.w = {}
        self.r = {}


class DSem:
    __slots__ = ("h", "count")

    def __init__(self, h):
        self.h = h
        self.count = 0


class Sched:
    QS = ("pe", "act", "dve", "pool", "sp")

    def __init__(self, nc, es):
        self.nc = nc
        self.es = es
        self.qs = {k: [] for k in self.QS}
        self.esem = {}
        self.ecount = {}
        self.waited = {k: {} for k in self.QS}
        self.nsem = 0
        self.dsems = []
        self.dpool = []
        self.pool_i = 0
        for k in self.QS:
            self._new_epoch(k)

    def _alloc_sem(self, name):
        self.nsem += 1
        return self.es.enter_context(self.nc.semaphore(name))

    def _new_epoch(self, k):
        self.esem[k] = self._alloc_sem(f"e{k}{self.nsem}")
        self.ecount[k] = 0

    def dsem(self, name="d", persistent=False):
        if persistent:
            d = DSem(self._alloc_sem(f"{name}{self.nsem}"))
            self.dsems.append(d)
            return d
        if self.pool_i >= len(self.dpool):
            d = DSem(self._alloc_sem(f"dp{self.nsem}"))
            self.dsems.append(d)
            self.dpool.append(d)
        d = self.dpool[self.pool_i]
        self.pool_i += 1
        return d

    def op(self, q, fn, reads=(), writes=(), wacc=(), dsem=None):
        raw = {}
        oth = {}

        def add(d, t):
            k = id(t.sem)
            if k not in d or d[k].val < t.val:
                d[k] = t

        for b in reads:
            for t in b.w.values():
                add(raw, t)
        for b in writes:
            for t in b.w.values():
                add(oth, t)
            for t in b.r.values():
                add(oth, t)
        for b in wacc:
            for t in b.r.values():
                add(oth, t)
        waits = []
        wd = self.waited[q]
        for d, is_raw in ((raw, True), (oth, False)):
            for k, t in d.items():
                if t.q == q and (not t.dma) and dsem is None:
                    if not is_raw or q == "pe":
                        continue
                if wd.get(k, 0) >= t.val:
                    continue
                wd[k] = t.val
                waits.append((t.sem, t.val))
        if dsem is not None:
            dsem.count += 16
            tk = Tk(dsem.h, dsem.count, q, True)
            inc = (dsem.h, 16)
        else:
            if self.ecount[q] >= EPOCH:
                self._new_epoch(q)
            self.ecount[q] += 1
            tk = Tk(self.esem[q], self.ecount[q], q)
            inc = (self.esem[q], 1)
        self.qs[q].append((waits, _freeze(fn), inc))
        k = id(tk.sem)
        for b in writes:
            b.w = {k: tk}
            b.r = {}
        for b in wacc:
            b.w[k] = tk
        for b in reads:
            if k not in b.r or b.r[k].val < tk.val:
                b.r[k] = tk
        return tk

    def barrier(self):
        self.pool_i = 0
        tks = []
        for q in self.QS:
            if self.ecount[q] > 0:
                tks.append((self.esem[q], self.ecount[q], q))
        for d in self.dsems:
            if d.count > 0:
                tks.append((d.h, d.count, None))
        for q in self.QS:
            waits = []
            wd = self.waited[q]
            for (h, v, src) in tks:
                if src == q:
                    continue
                if wd.get(id(h), 0) >= v:
                    continue
                wd[id(h)] = v
                waits.append((h, v))
            if waits:
                self.qs[q].append((waits, None, None))

    def emit(self):
        nc = self.nc
        self.barrier()
        qs = self.qs

        def run(eng, lst):
            for waits, fn, inc in lst:
                for (s, v) in waits:
                    eng.wait_ge(s, v)
                if fn is not None:
                    fn(eng).then_inc(inc[0], inc[1])

        with nc.Block() as block:
            @block.tensor
            def _(e):
                run(e, qs["pe"])

            @block.scalar
            def _(e):
                run(e, qs["act"])

            @block.vector
            def _(e):
                run(e, qs["dve"])

            @block.gpsimd
            def _(e):
                run(e, qs["pool"])

            @block.sync
            def _(e):
                run(e, qs["sp"])


class Ring:
    def __init__(self, S, aps, name, with_dsem=True):
        self.aps = aps
        self.bufs = [Buf(f"{name}{i}") for i in range(len(aps))]
        self.ds = [S.dsem(name) for _ in aps] if with_dsem else [None] * len(aps)
        self.i = 0

    def next(self):
        k = self.i % len(self.aps)
        self.i += 1
        return self.aps[k], self.bufs[k], self.ds[k]


_O = dict(aq=0, ak=768, av=1024, az=1280, bcq=2048, bckv=2560, bkr=3072, bz=3136,
          cq=3904, ck=4416, cv=4928, cz=5440)


def _in_blocks():
    r = lambda a, n: list(range(a, a + n))
    blocks = [
        ("QK", r(_O["aq"], 512)),
        ("QK", r(_O["aq"] + 512, 256) + r(_O["ak"], 256)),
        ("QK", r(_O["cq"], 512)),
        ("QK", r(_O["ck"], 512)),
        ("LAT", r(_O["bcq"], 512)),
        ("LAT", r(_O["bckv"], 512)),
        ("VKR", r(_O["av"], 256) + r(_O["bkr"], 64)),
        ("V", r(_O["cv"], 512)),
        ("Z", r(_O["az"], 512)),
        ("Z", r(_O["az"] + 512, 256) + r(_O["bz"], 256)),
        ("Z", r(_O["bz"] + 256, 512)),
        ("Z", r(_O["cz"], 512)),
    ]
    return blocks


IN_BLOCKS = _in_blocks()
IN_PERM = np.concatenate([np.array(c) for _, c in IN_BLOCKS])
UKV_PERM = np.concatenate([np.arange(h * 256, h * 256 + 128) for h in range(6)] +
                          [np.arange(h * 256 + 128, h * 256 + 256) for h in range(6)])


def _const_tables():
    slopes = 2.0 ** (-8.0 * np.arange(1, 7, dtype=np.float64) / 6)
    kt = np.arange(128)[:, None]
    qo = np.arange(384)[None, :]
    delta = 128 + kt - qo
    tabA = np.zeros((6, 128, 384), np.float32)
    for h in range(6):
        tabA[h] = (np.exp(-slopes[h] * np.abs(delta)) * (np.abs(delta) <= 128)).astype(np.float32)
    half = 32
    inv = 10000.0 ** (-np.arange(half, dtype=np.float32) / half)
    ang = np.arange(L, dtype=np.float32)[:, None] * inv[None, :]
    cos = np.cos(ang).astype(np.float32)
    sin = np.sin(ang).astype(np.float32)
    kc = np.arange(64)[:, None]
    qc = np.arange(64)[None, :]
    cs = np.clip(qc - 8, 0, 48)
    valid = (kc >= cs) & (kc < cs + 16)
    oh = np.zeros((32, 64, 64), np.float32)
    b = 15 + kc - qc
    for bb in range(31):
        oh[bb] = ((b == bb) & valid).astype(np.float32)
    oh[31] = np.where(valid, 0.0, -30000.0)
    return tabA, cos, sin, oh.reshape(32, 4096)


def build(nseq=NSEQ, nlayers=DEPTH, debug=False):
    nc = bass.Bass("TRN2", target_bir_lowering=False)
    es = ExitStack()

    def din(name, shape, dt=F32):
        return nc.dram_tensor(name, list(shape), dt, kind="ExternalInput").ap()

    dbg_kind = "ExternalOutput" if debug else "Internal"

    def dscr(name, shape, dt=F32):
        return nc.dram_tensor(name, list(shape), dt, kind=dbg_kind).ap()

    x_d = din("x", [nseq, L, D])
    p_d = din("p", [DEPTH, nseq, L, 256])
    y_d = nc.dram_tensor("y", [nseq, L, D], F32, kind="ExternalOutput").ap()
    w_in_d = din("w_in", [DEPTH, D, IN_W])
    w_out_d = din("w_out", [DEPTH, D, D])
    w_gate_d = din("w_gate", [DEPTH, D, D])
    w_pp_d = din("w_pp", [DEPTH, 256, D])
    w_uq_d = din("w_uq", [DEPTH, 512, 1152])
    w_ukv_d = din("w_ukv", [DEPTH, 512, 1536])
    g_in_d = din("g_in", [DEPTH, D])
    g_ple_d = din("g_ple", [DEPTH, D])
    g_post_d = din("g_post", [DEPTH, D])
    g_aq_d = din("g_aq", [DEPTH, 128])
    g_ak_d = din("g_ak", [DEPTH, 128])
    g_cq_d = din("g_cq", [DEPTH, 128])
    g_ck_d = din("g_ck", [DEPTH, 128])
    g_bcq_d = din("g_bcq", [DEPTH, 512])
    g_bckv_d = din("g_bckv", [DEPTH, 512])
    g_bq_d = din("g_bq", [DEPTH, 192])
    g_bk_d = din("g_bk", [DEPTH, 192])
    sink_d = din("sink", [DEPTH, 6])
    rpbT_d = din("rpbT", [DEPTH, 4, 31, 15])
    ident_d = din("ident", [128, 128], BF16)
    tabA_d = din("tabA", [6, 128, 384])
    cos_d = din("cosT", [L, 32])
    sin_d = din("sinT", [L, 32])
    oh_d = din("onehot", [32, 4096])

    hscr_d = dscr("hscr", [nseq, L, D])
    featA_d = dscr("featA", [8, 128, L], BF16)
    featC_d = dscr("featC", [8, 128, L], BF16)
    qn_d = dscr("qn", [6, 128, L], BF16)
    qr_d = dscr("qr", [6, 128, L], BF16)
    kn_d = dscr("kn", [6, 128, L], BF16)
    vA_d = dscr("vA", [L, 256], BF16)
    vB_d = dscr("vB", [L, 768], BF16)
    vC_d = dscr("vC", [L, 512], BF16)
    sz_d = dscr("sz", [L, D])
    xd_d = dscr("xd", [4, 15, 4096])
    ctab_d = dscr("ctab", [4, 128, 26 * 64])

    S = Sched(nc, es)

    def sbt(name, shape, dt):
        return nc.alloc_sbuf_tensor(name, list(shape), dt)

    ident = sbt("ident_s", [128, 128], BF16)
    cosT = sbt("cosT_s", [128, NT, 32], F32)
    sinT = sbt("sinT_s", [128, NT, 32], F32)
    G1 = sbt("G1", [128, D], F32)
    g_aq = sbt("g_aq_s", [128, 128], F32)
    g_ak = sbt("g_ak_s", [128, 128], F32)
    g_cq = sbt("g_cq_s", [128, 128], F32)
    g_ck = sbt("g_ck_s", [128, 128], F32)
    g_bq = sbt("g_bq_s", [128, 192], F32)
    g_bk = sbt("g_bk_s", [128, 192], F32)
    g_bcq = sbt("g_bcq_s", [128, 512], F32)
    g_bckv = sbt("g_bckv_s", [128, 512], F32)
    esink = sbt("esink", [128, 8], F32)
    kscale = sbt("kscale", [128, NT, 6], F32)
    ss_kr = sbt("ss_kr", [128, NT], F32)
    rstd_p = sbt("rstd_p", [128, NT], F32)
    stat = sbt("stat", [128, 512], F32)
    vaug = [sbt(f"vaug{i}", [128, NT, 129], BF16) for i in range(2)]
    ARENA_ACT = sbt("arena_act", [128, 16 * L], BF16)
    ARENA_W = sbt("arena_w", [128, 16 * 1024], BF16)
    XBYTES = 80 * 1024
    ARENA_X = sbt("arena_x", [128, XBYTES // 4], F32)

    actT = ARENA_ACT[:, :].rearrange("p (c n) -> p c n", c=16)
    PS = nc.alloc_psum_tensor("ps", [128, 3072], F32)
    PT = nc.alloc_psum_tensor("pt", [128, 2048], BF16)
    bank = [PS[:, i * 512:(i + 1) * 512] for i in range(6)]
    Bbank = [Buf(f"bank{i}") for i in range(6)]
    tbank = [PT[:, i * 1024:(i + 1) * 1024] for i in range(2)]
    Btb = [Buf(f"tb{i}") for i in range(2)]

    class Carver:
        def __init__(self, arena_f32, nbytes):
            self.a = arena_f32
            self.n = nbytes
            self.off = 0

        def reset(self):
            self.off = 0

        def take(self, shape, dt):
            esz = 4 if dt == F32 else 2
            nel = int(np.prod(shape[1:]))
            nb = nel * esz
            nb_al = (nb + 31) // 32 * 32
            assert self.off + nb_al <= self.n, ("arena overflow", self.off, nb_al, self.n)
            v = self.a[:, self.off // 4:(self.off + nb_al) // 4]
            if dt != F32:
                v = v.bitcast(dt)
            v = v[0:shape[0], 0:nel]
            if len(shape) == 3:
                v = v.rearrange("p (a b) -> p a b", a=shape[1])
            self.off += nb_al
            return v

    CX = Carver(ARENA_X, XBYTES)
    ARENA_W32 = ARENA_W[:, :].bitcast(F32)
    CW = Carver(ARENA_W32, 32 * 1024)

    Bconst = Buf("const")
    d_const = S.dsem("const", persistent=True)

    def dma(q, out, in_, reads=(), writes=(), wacc=(), dsem=None):
        assert dsem is not None
        return S.op(q, lambda e: e.dma_start(out=out, in_=in_), reads=reads, writes=writes,
                    wacc=wacc, dsem=dsem)

    dma("sp", ident[:], ident_d, wacc=[Bconst], dsem=d_const)
    dma("sp", cosT[:], cos_d.rearrange("(t p) i -> p t i", p=128), wacc=[Bconst], dsem=d_const)
    dma("sp", sinT[:], sin_d.rearrange("(t p) i -> p t i", p=128), wacc=[Bconst], dsem=d_const)
    for i in range(2):
        S.op("dve", (lambda i: lambda e: e.memset(vaug[i][:, :, 128:129], 1.0))(i), wacc=[Bconst])
    S.barrier()

    stat_i = [0]

    def stat_take(n):
        if stat_i[0] + n > 512:
            stat_i[0] = 0
        a = stat[:, stat_i[0]:stat_i[0] + n]
        stat_i[0] += n
        return a

    def rstd_from_ss(ss_ap, n, count, eps=EPS, extra_scale=None, out=None):
        r = out if out is not None else stat_take(n)
        b = Buf("rstd")
        return r, b

    bank_i = [0]

    def next_bank():
        k = bank_i[0] % 6
        bank_i[0] += 1
        return bank[k], Bbank[k]

    tb_i = [0]

    def next_tb():
        k = tb_i[0] % 2
        tb_i[0] += 1
        return tbank[k], Btb[k]

    tr_i = [0]
    DQ = []

    def dq_new_group():
        DQ.append([])

    def defer(thunk):
        if not DQ:
            DQ.append([])
        DQ[-1].append(thunk)

    def run_deferred(keep=0):
        while len(DQ) > keep:
            for th in DQ.pop(0):
                th()

    _orig_barrier = S.barrier

    def _checked_barrier():
        assert not DQ, "deferred work pending at barrier"
        _orig_barrier()
    S.barrier = _checked_barrier

    def bcast_load(dst, src_row, n, buf, ds):
        dma("sp", dst, src_row.partition_broadcast(128), writes=[buf], dsem=ds)

    Bg1, Bgs = Buf("G1"), Buf("gsm")
    G2, Bg2 = G1, Bg1
    d_g1, d_g2, d_gs = S.dsem("g1", True), S.dsem("g2", True), S.dsem("gs", True)

    ROPE_ENG = "pool"

    def rope(xr, out_bf, t, tmp, Bx, Bout, Btmp, n=1):
        c = cosT[:, t, :].unsqueeze(1).to_broadcast([128, n, 32])
        s_ = sinT[:, t, :].unsqueeze(1).to_broadcast([128, n, 32])
        x1 = xr[:, :, 0:32]
        x2 = xr[:, :, 32:64]
        q = ROPE_ENG
        S.op(q, lambda e: e.tensor_tensor(out=tmp[:, 0], in0=x1, in1=c, op=ALU.mult), reads=[Bx, Bconst], writes=[Btmp])
        S.op(q, lambda e: e.tensor_tensor(out=tmp[:, 1], in0=x2, in1=s_, op=ALU.mult), reads=[Bx], writes=[Btmp])
        S.op(q, lambda e: e.tensor_tensor(out=tmp[:, 2], in0=x1, in1=s_, op=ALU.mult), reads=[Bx], writes=[Btmp])
        S.op(q, lambda e: e.tensor_tensor(out=tmp[:, 3], in0=x2, in1=c, op=ALU.mult), reads=[Bx], writes=[Btmp])
        S.op(q, lambda e: e.tensor_tensor(out=out_bf[:, :, 0:32], in0=tmp[:, 0], in1=tmp[:, 1], op=ALU.subtract), reads=[Btmp], writes=[Bout])
        S.op(q, lambda e: e.tensor_tensor(out=out_bf[:, :, 32:64], in0=tmp[:, 2], in1=tmp[:, 3], op=ALU.add), reads=[Btmp], writes=[Bout])

    def sumsq(ps_ap, junk_ap, ss_ap, Bps, Bjunk, Bss):
        S.op("act", lambda e: e.activation(out=junk_ap, in_=ps_ap, func=AF.Square, accum_out=ss_ap),
             writes=[Bps, Bjunk, Bss])

    def rstd(ss_ap, r_ap, count, Bss, Br, mul=None):
        m2 = 1.0 if mul is None else float(mul) ** 2
        S.op("act", lambda e: e.activation(out=r_ap, in_=ss_ap, func=AF.Sqrt, scale=1.0 / (count * m2), bias=EPS / m2),
             reads=[Bss], writes=[Br])
        S.op("dve", lambda e: e.reciprocal(out=r_ap, in_=r_ap), reads=[Br], writes=[Br])

    for l in range(nlayers):
        S.barrier()
        for (dst, src, n) in ((g_aq, g_aq_d, 128), (g_ak, g_ak_d, 128), (g_cq, g_cq_d, 128), (g_ck, g_ck_d, 128),
                              (g_bq, g_bq_d, 192), (g_bk, g_bk_d, 192), (g_bcq, g_bcq_d, 512), (g_bckv, g_bckv_d, 512)):
            dma("sp", dst[:], src[l, :].partition_broadcast(128), wacc=[Bgs], dsem=d_gs)
        dma("sp", esink[:, 0:6], sink_d[l, :].partition_broadcast(128), wacc=[Bgs], dsem=d_gs)
        S.barrier()
        S.op("act", lambda e: e.activation(out=esink[:, 0:6], in_=esink[:, 0:6], func=AF.Exp), writes=[Bgs])
        CX.reset()
        CW.reset()
        oh_s = CX.take([32, 4096], F32)
        xa_s = CX.take([15, 4096], F32)
        rt_s = CX.take([32, 16], F32)
        xtr = CW.take([128, 26, 64], F32)
        Boh, Bxa, Brt, Bxtr, Bxd = Buf(), Buf(), Buf(), Buf(), Buf()
        d_t1, d_t2 = d_g1, d_g2
        dma("sp", oh_s, oh_d, writes=[Boh], dsem=d_t1)
        for h in range(4):
            S.op("dve", lambda e: e.memset(rt_s[:, 0:15], 1.0), writes=[Brt])
            dma("sp", rt_s[0:31, 0:15], rpbT_d[l, h], writes=[Brt], dsem=d_t2)
            for c in range(8):
                bk, Bb = next_bank()
                S.op("pe", (lambda bk, c: lambda e: e.matmul(bk[0:15, :], lhsT=rt_s[:, 0:15], rhs=oh_s[:, c * 512:(c + 1) * 512],
                                                              start=True, stop=True))(bk, c), reads=[Brt, Boh], writes=[Bb])
                S.op("act", (lambda bk, c: lambda e: e.activation(out=xa_s[:, c * 512:(c + 1) * 512], in_=bk[0:15, :], func=AF.Exp))(bk, c),
                     writes=[Bb, Bxa])
            dma("sp", xd_d[h], xa_s, reads=[Bxa], writes=[Bxd], dsem=d_t1)
            S.op("dve", lambda e: e.memset(xtr, 0.0), writes=[Bxtr])
            src = xd_d[h].rearrange("a (k q) -> k a q", k=64)
            dma("sp", xtr[0:64, 0:15, :], src, reads=[Bxd], writes=[Bxtr], dsem=d_t2)
            dma("sp", xtr[64:128, 1:16, :], src, reads=[Bxd], writes=[Bxtr], dsem=d_t2)
            S.op("dve", lambda e: e.tensor_copy(out=xtr[:, 16:26, :], in_=xtr[:, 3:13, :]), reads=[Bxtr], writes=[Bxtr])
            S.op("dve", lambda e: e.memset(xtr[:, 16, :], 0.0), writes=[Bxtr])
            S.op("dve", lambda e: e.memset(xtr[64:128, 17, :], 0.0), writes=[Bxtr])
            S.op("dve", lambda e: e.memset(xtr[0:64, 25, :], 0.0), writes=[Bxtr])
            dma("sp", ctab_d[h], xtr.rearrange("p a b -> p (a b)"), reads=[Bxtr], writes=[Bxd], dsem=d_t1)
        S.barrier()

        for s in range(nseq):
            h_src = x_d[s] if l == 0 else hscr_d[s]
            h_dst = y_d[s] if l == nlayers - 1 else hscr_d[s]

            S.barrier()
            CX.reset()
            hb = [CX.take([128, D], F32) for _ in range(4)]
            ub = [CX.take([128, D], BF16) for _ in range(2)]
            junk = CX.take([128, D], BF16)
            Bjunk = Buf("junk")
            Rhb = Ring(S, hb, "hb")
            Rub = Ring(S, ub, "ub", with_dsem=False)
            bcast_load(G1[:], g_in_d[l, :], D, Bg1, d_g1)
            Bact = [Buf(f"act{t}") for t in range(NT)]

            nt_loads = {}

            def nt_load(src_rows, t):
                hbt, Bh, dh = Rhb.next()
                dma("sp", hbt, src_rows, writes=[Bh], dsem=dh)
                nt_loads[t] = (hbt, Bh)

            def norm_transpose(src_rows, gtile, Bgt, t, pre=None):
                if t not in nt_loads:
                    nt_load(src_rows, t)
                hbt, Bh = nt_loads.pop(t)
                ss = stat_take(1)
                rr = stat_take(1)
                Bss, Br = Buf(), Buf()
                S.op("act", lambda e: e.activation(out=junk, in_=hbt, func=AF.Square, accum_out=ss),
                     reads=[Bh], writes=[Bjunk, Bss])
                rstd(ss, rr, D, Bss, Br)
                ubt, Bu, _ = Rub.next()
                S.op("dve", lambda e: e.scalar_tensor_tensor(out=ubt, in0=hbt, scalar=rr, in1=gtile[:], op0=ALU.mult, op1=ALU.mult),
                     reads=[Bh, Br, Bgt], writes=[Bu])
                run_deferred()

                def tr_part():
                    for half in range(2):
                        tb, Bt = next_tb()
                        for k in range(8):
                            kc = half * 8 + k
                            S.op("pe", (lambda tb, k, kc: lambda e: e.transpose(out=tb[:, k * 128:(k + 1) * 128], in_=ubt[:, kc * 128:(kc + 1) * 128],
                                                                                identity=ident[:]))(tb, k, kc), reads=[Bu, Bconst], writes=[Bt])
                        dst = actT[:, half * 8:(half + 1) * 8, t * 128:(t + 1) * 128]
                        if half == 0:
                            S.op("act", (lambda tb, dst: lambda e: e.copy(out=dst, in_=tb.rearrange("p (a b) -> p a b", a=8)))(tb, dst),
                                 writes=[Bt], wacc=[Bact[t]])
                        else:
                            S.op("dve", (lambda tb, dst: lambda e: e.tensor_copy(out=dst, in_=tb.rearrange("p (a b) -> p a b", a=8)))(tb, dst),
                                 writes=[Bt], wacc=[Bact[t]])
                dq_new_group()
                defer(tr_part)

            for t in range(3):
                nt_load(h_src[t * 128:(t + 1) * 128, :], t)
            for t in range(NT):
                if t + 3 < NT:
                    nt_load(h_src[(t + 3) * 128:(t + 4) * 128, :], t + 3)
                norm_transpose(h_src[t * 128:(t + 1) * 128, :], G1, Bg1, t)
            run_deferred()

            S.barrier()
            CX.reset()
            CW.reset()
            wbuf = [CW.take([128, 16, 512], BF16) for _ in range(2)]
            Rw = Ring(S, wbuf, "w")
            stageT = CX.take([128, 4, L], BF16)
            stageA = stageT[:, :, 0:1024]
            stageB = stageT[:, :, 1024:2048]
            BstageA, BstageB = Buf("stageA"), Buf("stageB")
            d_stage = S.dsem("stg")
            latT = [CX.take([128, 4, L], BF16) for _ in range(2)]
            Blat = [Buf("lat0"), Buf("lat1")]
            krT = CX.take([128, L], BF16)
            Bkr = Buf("krT")
            xn = [CX.take([128, 512], BF16) for _ in range(5)]
            Rxn = Ring(S, xn, "xn", with_dsem=False)
            zst = [CX.take([128, 512], F32) for _ in range(2)]
            Rz = Ring(S, zst, "z")
            vst = [CX.take([128, 512], BF16) for _ in range(2)]
            Rv = Ring(S, vst, "v")
            xr2 = [CX.take([128, 2, 64], F32) for _ in range(3)]
            ropet2 = [CX.take([128, 4, 2 * 32], F32).rearrange("p a (n c) -> p a n c", n=2) for _ in range(3)]
            xrr4 = CX.take([128, 8, 128], BF16)
            junk = CX.take([128, 512], BF16)
            Rxr = Ring(S, xr2, "xr", with_dsem=False)
            Rrt = Ring(S, ropet2, "rt", with_dsem=False)
            Rxrr = Ring(S, [xrr4[:, 2 * i:2 * i + 2, :] for i in range(4)], "xrr", with_dsem=False)
            S.op("dve", lambda e: e.memset(xrr4[:, :, 64:128], 0.0), writes=Rxrr.bufs)
            Bsz, BvA, BvB, BvC, BfA, BfC = Buf("sz"), Buf("vA"), Buf("vB"), Buf("vC"), Buf("fA"), Buf("fC")
            Bqn, Bqr, Bkn = Buf("qn"), Buf("qr"), Buf("kn")
            Bksc = Buf("kscale")

            def load_w(src_ap, ncols, nk):
                wb, Bw, dw = Rw.next()
                dst = wb[:, 0:nk, 0:ncols]
                dma("pool", dst, src_ap, writes=[Bw], dsem=dw)
                return wb, Bw

            def proj_mm(bk, Bb, lhs_of_kc, Blhs, wb, Bw, ncols, nk):
                for kc in range(nk):
                    lhs = lhs_of_kc(kc)
                    S.op("pe", lambda e: e.matmul(bk[:, 0:ncols], lhsT=lhs, rhs=wb[:, kc, 0:ncols],
                                                  start=(kc == 0), stop=(kc == nk - 1)),
                         reads=[Blhs, Bw], writes=[Bb])

            def transposes_to(srcs, dst_ap, Bsrc, Bdst, wacc=False, later=True):
                if later:
                    defer(lambda: transposes_to(srcs, dst_ap, Bsrc, Bdst, wacc=wacc, later=False))
                    return
                tb, Bt = next_tb()
                n = len(srcs)
                for k, sap in enumerate(srcs):
                    S.op("pe", (lambda k, sap: lambda e: e.transpose(out=tb[:, k * 128:(k + 1) * 128], in_=sap, identity=ident[:]))(k, sap),
                         reads=[Bsrc, Bconst], writes=[Bt])
                if n == 1:
                    src = tb[:, 0:128]
                else:
                    src = tb[:, 0:n * 128].rearrange("p (a b) -> p a b", a=n)
                tr_i[0] += 1
                if tr_i[0] % 2 == 0:
                    fn_ = ("act", lambda e: e.copy(out=dst_ap, in_=src))
                else:
                    fn_ = ("dve", lambda e: e.tensor_copy(out=dst_ap, in_=src))
                if wacc:
                    S.op(fn_[0], fn_[1], writes=[Bt], wacc=[Bdst])
                else:
                    S.op(fn_[0], fn_[1], writes=[Bt, Bdst])

            in_w_l = w_in_d[l].rearrange("(kc p) n -> p kc n", p=128)
            col_start = np.concatenate([[0], np.cumsum([len(c) for _, c in IN_BLOCKS])]).tolist()
            ORDER = [4, 5, 6, 0, 1, 2, 3, 7, 8, 9, 10, 11]
            b0_ = ORDER[0]
            P2st = {"next_w": load_w(in_w_l[:, :, col_start[b0_]:col_start[b0_] + len(IN_BLOCKS[b0_][1])], len(IN_BLOCKS[b0_][1]), 16)}
            KEEP = [1]
            A_thunks = []
            for oi, bi in enumerate(ORDER):
                btype, cols = IN_BLOCKS[bi]
                ncols = len(cols)
                gl = None
                if btype == "QK":
                    if bi == 0:
                        gl = [g_aq] * 4
                    elif bi == 1:
                        gl = [g_aq, g_aq, g_ak, g_ak]
                    elif bi == 2:
                        gl = [g_cq] * 4
                    else:
                        gl = [g_ck] * 4
                ctx = {}

                def p2_start(oi=oi, btype=btype, ctx=ctx):
                    ctx["w"] = P2st["next_w"]
                    if oi + 1 < len(ORDER):
                        nb_ = ORDER[oi + 1]
                        nn = len(IN_BLOCKS[nb_][1])
                        P2st["next_w"] = load_w(in_w_l[:, :, col_start[nb_]:col_start[nb_] + nn], nn, 16)
                    if btype == "VKR":
                        S.op("dve", lambda e: e.memset(krT[64:128, :], 0.0), writes=[Bkr])

                def p2_tile(t, bi=bi, btype=btype, ctx=ctx, ncols=ncols, gl=gl):
                    wb, Bw = ctx["w"]
                    bk, Bb = next_bank()
                    proj_mm(bk, Bb, lambda kc: actT[:, kc, t * 128:(t + 1) * 128], Bact[t], wb, Bw, ncols, 16)
                    run_deferred(keep=KEEP[0])
                    dq_new_group()
                    tsl = slice(t * 128, (t + 1) * 128)
                    tsl8 = slice((t % 8) * 128, (t % 8 + 1) * 128)
                    if btype == "QK":
                        ss = stat_take(4)
                        rr = stat_take(4)
                        Bss, Br = Buf(), Buf()
                        for u in range(4):
                            S.op("act", (lambda u: lambda e: e.activation(out=junk[:, 0:128], in_=bk[:, u * 128:(u + 1) * 128], func=AF.Square,
                                                                         accum_out=ss[:, u:u + 1]))(u), writes=[Bb, Bss, Bjunk])
                        rstd(ss, rr, 128, Bss, Br)
                        xnt, Bxn, _ = Rxn.next()
                        for u in range(4):
                            S.op("dve", (lambda u: lambda e: e.scalar_tensor_tensor(out=xnt[:, u * 128:(u + 1) * 128], in0=bk[:, u * 128:(u + 1) * 128],
                                                                                   scalar=rr[:, u:u + 1], in1=gl[u][:], op0=ALU.mult, op1=ALU.mult))(u),
                                 reads=[Br, Bgs], writes=[Bb, Bxn])
                        transposes_to([xnt[:, u * 128:(u + 1) * 128] for u in range(4)], stageA[:, :, tsl8], Bxn, BstageA, wacc=True)
                    elif btype == "LAT":
                        which = bi - 4
                        gt = g_bcq if which == 0 else g_bckv
                        ss = stat_take(1)
                        rr = stat_take(1)
                        Bss, Br = Buf(), Buf()
                        S.op("act", lambda e: e.activation(out=junk, in_=bk, func=AF.Square, accum_out=ss), writes=[Bb, Bss, Bjunk])
                        rstd(ss, rr, 512, Bss, Br)
                        xnt, Bxn, _ = Rxn.next()
                        S.op("dve", lambda e: e.scalar_tensor_tensor(out=xnt, in0=bk, scalar=rr, in1=gt[:], op0=ALU.mult, op1=ALU.mult),
                             reads=[Br, Bgs], writes=[Bb, Bxn])
                        transposes_to([xnt[:, u * 128:(u + 1) * 128] for u in range(4)], latT[which][:, :, tsl], Bxn, Blat[which], wacc=True)
                    elif btype == "VKR":
                        vt, Bv, dv = Rv.next()
                        S.op("act", lambda e: e.copy(out=vt[:, 0:256], in_=bk[:, 0:256]), writes=[Bb, Bv])
                        dma("sp", vA_d[tsl, :], vt[:, 0:256], reads=[Bv], wacc=[BvA], dsem=dv)
                        S.op("act", lambda e: e.activation(out=junk[:, 0:64], in_=bk[:, 256:320], func=AF.Square, accum_out=ss_kr[:, t:t + 1]),
                             writes=[Bb, Bjunk], wacc=[Bkr])
                        xr, Bxr, _ = Rxr.next()
                        ropet, Bropet, _ = Rrt.next()
                        S.op("dve", lambda e: e.tensor_tensor(out=xr[:, 0, :], in0=bk[:, 256:320], in1=g_bk[:, 128:192], op=ALU.mult),
                             reads=[Bgs], writes=[Bb, Bxr])
                        xrr, Bxrr, _ = Rxrr.next()
                        rope(xr[:, 0:1, :], xrr[:, 0:1, :], t, ropet[:, :, 0:1, :], Bxr, Bxrr, Bropet, n=1)
                        transposes_to([xrr[:, 0, :]], krT[:, tsl], Bxrr, Bkr, wacc=True)
                    elif btype == "V":
                        vt, Bv, dv = Rv.next()
                        S.op("act", lambda e: e.copy(out=vt, in_=bk), writes=[Bb, Bv])
                        dma("sp", vC_d[tsl, :], vt, reads=[Bv], wacc=[BvC], dsem=dv)
                    elif btype == "Z":
                        zt, Bz, dz = Rz.next()
                        S.op("act", lambda e: e.activation(out=zt, in_=bk, func=AF.Silu), writes=[Bb, Bz])
                        zc0 = (bi - 8) * 512
                        dma("sp", sz_d[tsl, zc0:zc0 + 512], zt, reads=[Bz], wacc=[Bsz], dsem=dz)
                def p2_post(half, bi=bi, btype=btype):
                    if btype == "QK":
                        if bi == 0:
                            dst, Bd = featA_d[0:4], BfA
                        elif bi == 1:
                            dst, Bd = featA_d[4:8], BfA
                        elif bi == 2:
                            dst, Bd = featC_d[0:4], BfC
                        else:
                            dst, Bd = featC_d[4:8], BfC
                        hs_ = slice(half * 1024, (half + 1) * 1024)
                        defer((lambda dst, Bd, hs_: lambda: dma("sp", dst.rearrange("u p n -> p u n")[:, :, hs_], stageA, reads=[BstageA], wacc=[Bd], dsem=d_stage))(dst, Bd, hs_))

                def p2_th(t, st_=p2_start, ti_=p2_tile, po_=p2_post):
                    if t == 0:
                        st_()
                    ti_(t)
                    if t % 8 == 7:
                        po_(t // 8)
                A_thunks += [(lambda t, f: lambda: f(t))(t, p2_th) for t in range(NT)]

            uq_l = w_uq_d[l].rearrange("(kc p) n -> p kc n", p=128)
            ukv_l = w_ukv_d[l].rearrange("(kc p) n -> p kc n", p=128)
            stq_n = stageB[:, 0:2, :]
            stq_r = stageB[:, 2:4, :]
            stk = stageB[:, 0:3, :]
            d_stageB = S.dsem("stgB")
            p2b_blocks = ([("q", qb, uq_l[:, :, qb * 384:(qb + 1) * 384]) for qb in range(3)] +
                          [("k", kb, ukv_l[:, :, kb * 384:(kb + 1) * 384]) for kb in range(2)] +
                          [("v", vb, ukv_l[:, :, 768 + vb * 384:768 + (vb + 1) * 384]) for vb in range(2)])
            wbuf2 = [CX.take([128, 4, 384], BF16) for _ in range(2)]
            Rw2 = Ring(S, wbuf2, "w2")

            def load_w2(src_ap):
                wb, Bw, dw = Rw2.next()
                dma("pool", wb, src_ap, writes=[Bw], dsem=dw)
                return wb, Bw

            def p2b_q_tile(qb, t, wb, Bw):
                tsl = slice(t * 128, (t + 1) * 128)
                tsl8 = slice((t % 8) * 128, (t % 8 + 1) * 128)
                bk, Bb = next_bank()
                proj_mm(bk, Bb, lambda kc: latT[0][:, kc, tsl], Blat[0], wb, Bw, 384, 4)
                run_deferred(keep=KEEP[0])
                dq_new_group()
                ss = stat_take(2)
                rr = stat_take(2)
                Bss, Br = Buf(), Buf()
                for hh in range(2):
                    S.op("act", (lambda hh: lambda e: e.activation(out=junk[:, 0:192], in_=bk[:, hh * 192:(hh + 1) * 192], func=AF.Square,
                                                                  accum_out=ss[:, hh:hh + 1]))(hh), writes=[Bb, Bss, Bjunk])
                rstd(ss, rr, 192, Bss, Br)
                xnt, Bxn, _ = Rxn.next()
                for hh in range(2):
                    c0 = hh * 192
                    S.op("dve", (lambda hh, c0: lambda e: e.scalar_tensor_tensor(out=xnt[:, hh * 128:(hh + 1) * 128], in0=bk[:, c0:c0 + 128],
                                                                                scalar=rr[:, hh:hh + 1], in1=g_bq[:, 0:128], op0=ALU.mult, op1=ALU.mult))(hh, c0),
                         reads=[Br, Bgs], writes=[Bb, Bxn])
                transposes_to([xnt[:, hh * 128:(hh + 1) * 128] for hh in range(2)], stq_n[:, :, tsl8], Bxn, BstageB, wacc=True)
                xr, Bxr, _ = Rxr.next()
                ropet, Bropet, _ = Rrt.next()
                for hh in range(2):
                    c0 = hh * 192 + 128
                    S.op("dve", (lambda hh, c0: lambda e: e.scalar_tensor_tensor(out=xr[:, hh, :], in0=bk[:, c0:c0 + 64],
                                                                                scalar=rr[:, hh:hh + 1], in1=g_bq[:, 128:192], op0=ALU.mult, op1=ALU.mult))(hh, c0),
                         reads=[Br, Bgs], writes=[Bb, Bxr])
                xrr, Bxrr, _ = Rxrr.next()
                rope(xr, xrr, t, ropet, Bxr, Bxrr, Bropet, n=2)
                transposes_to([xrr[:, 0, :], xrr[:, 1, :]], stq_r[:, :, tsl8], Bxrr, BstageB, wacc=True)

            def p2b_k_tile(kb, t, wb, Bw):
                tsl = slice(t * 128, (t + 1) * 128)
                tsl8 = slice((t % 8) * 128, (t % 8 + 1) * 128)
                bk, Bb = next_bank()
                proj_mm(bk, Bb, lambda kc: latT[1][:, kc, tsl], Blat[1], wb, Bw, 384, 4)
                run_deferred(keep=KEEP[0])
                dq_new_group()
                ss = stat_take(3)
                Bss = Buf()
                for hh in range(3):
                    S.op("act", (lambda hh: lambda e: e.activation(out=junk[:, 0:128], in_=bk[:, hh * 128:(hh + 1) * 128], func=AF.Square,
                                                                  accum_out=ss[:, hh:hh + 1]))(hh), writes=[Bb, Bss, Bjunk])
                S.op("dve", lambda e: e.tensor_scalar(out=ss, in0=ss, scalar1=ss_kr[:, t:t + 1], scalar2=None, op0=ALU.add),
                     reads=[Bss, Bkr], writes=[Bss])
                rstd(ss, kscale[:, t, kb * 3:kb * 3 + 3], 192, Bss, Bksc, mul=192.0 ** -0.5)
                xnt, Bxn, _ = Rxn.next()
                S.op("dve", lambda e: e.tensor_tensor(out=xnt[:, 0:384].rearrange("p (a b) -> p a b", a=3), in0=bk[:, 0:384].rearrange("p (a b) -> p a b", a=3),
                                                      in1=g_bk[:, 0:128].unsqueeze(1).to_broadcast([128, 3, 128]), op=ALU.mult),
                     reads=[Bgs], writes=[Bb, Bxn])
                transposes_to([xnt[:, hh * 128:(hh + 1) * 128] for hh in range(3)], stk[:, :, tsl8], Bxn, BstageB, wacc=True)

            def p2b_v_tile(vb, t, wb, Bw):
                tsl = slice(t * 128, (t + 1) * 128)
                bk, Bb = next_bank()
                proj_mm(bk, Bb, lambda kc: latT[1][:, kc, tsl], Blat[1], wb, Bw, 384, 4)
                run_deferred(keep=KEEP[0])
                dq_new_group()
                vt, Bv, dv = Rv.next()
                S.op("act", lambda e: e.copy(out=vt[:, 0:384], in_=bk[:, 0:384]), writes=[Bb, Bv])
                dma("sp", vB_d[tsl, vb * 384:(vb + 1) * 384], vt[:, 0:384], reads=[Bv], wacc=[BvB], dsem=dv)

            P2bst = {}
            B_thunks = []
            for pi, (kind, idx, _src) in enumerate(p2b_blocks):
                ctxb = {}

                def p2b_th(t, pi=pi, kind=kind, idx=idx, ctxb=ctxb):
                    if t == 0:
                        if pi == 0:
                            P2bst["next_w"] = load_w2(p2b_blocks[0][2])
                        ctxb["w"] = P2bst["next_w"]
                        if pi + 1 < len(p2b_blocks):
                            P2bst["next_w"] = load_w2(p2b_blocks[pi + 1][2])
                    wb, Bw = ctxb["w"]
                    if kind == "q":
                        p2b_q_tile(idx, t, wb, Bw)
                    elif kind == "k":
                        p2b_k_tile(idx, t, wb, Bw)
                    else:
                        p2b_v_tile(idx, t, wb, Bw)
                    if t % 8 == 7:
                        hs_ = slice((t // 8) * 1024, (t // 8 + 1) * 1024)
                        if kind == "q":
                            defer((lambda idx, hs_: lambda: (dma("sp", qn_d[idx * 2:idx * 2 + 2].rearrange("u p n -> p u n")[:, :, hs_], stq_n, reads=[BstageB], wacc=[Bqn], dsem=d_stageB),
                                                            dma("sp", qr_d[idx * 2:idx * 2 + 2].rearrange("u p n -> p u n")[:, :, hs_], stq_r, reads=[BstageB], wacc=[Bqr], dsem=d_stageB)))(idx, hs_))
                        elif kind == "k":
                            defer((lambda idx, hs_: lambda: dma("sp", kn_d[idx * 3:idx * 3 + 3].rearrange("u p n -> p u n")[:, :, hs_], stk, reads=[BstageB], wacc=[Bkn], dsem=d_stageB))(idx, hs_))
                B_thunks += [(lambda t, f: lambda: f(t))(t, p2b_th) for t in range(NT)]

            n_pre = 3 * NT
            n_mid = 8 * NT
            for th in A_thunks[:n_pre]:
                th()
            KEEP[0] = 3
            ia, ib, k_ = n_pre, 0, 0
            while ia < n_mid or ib < len(B_thunks):
                if ia < n_mid:
                    A_thunks[ia]()
                    ia += 1
                for _ in range(1 + (k_ % 2)):
                    if ib < len(B_thunks):
                        B_thunks[ib]()
                        ib += 1
                k_ += 1
            KEEP[0] = 1
            for th in A_thunks[n_mid:]:
                th()
            run_deferred()

            S.barrier()
            CX.reset()
            _ = CX.take([128, 4, L], BF16)
            _ = [CX.take([128, 4, L], BF16) for _ in range(2)]
            krT2 = CX.take([128, L], BF16)
            CX.reset()
            opnd = [[CX.take([128, L], BF16) for _ in range(3)] for _ in range(2)]
            szt = [CX.take([128, NT, 128], F32) for _ in range(2)]
            et = [CX.take([128, 640], F32) for _ in range(3)]
            assert CX.off <= 48 * 1024
            CX.off = 48 * 1024 + 4 * 1024
            ptile = [CX.take([128, 640], BF16) for _ in range(6)]
            yg = [CX.take([128, 128], BF16) for _ in range(4)]
            Ret = Ring(S, et, "et", with_dsem=False)
            Rpt = Ring(S, ptile, "pt", with_dsem=False)
            Ryg = Ring(S, yg, "yg", with_dsem=False)
            CW.reset()
            tabA_s = CW.take([128, 6, 384], F32)
            ctab_s = [CW.take([128, 26, 64], F32) for _ in range(2)]
            Btab = Buf("tabA")
            d_tab = S.dsem("tab")
            dma("sp", tabA_s, tabA_d.rearrange("h p n -> p h n"), writes=[Btab], dsem=d_tab)
            Bmix = [Buf(f"mix{i}") for i in range(NT)]
            Rq = Ring(S, [opnd[0][0], opnd[1][0]], "q")
            Rk = Ring(S, [opnd[0][1], opnd[1][1]], "k")
            Rqr = Ring(S, [opnd[0][2], opnd[1][2]], "qr")
            Rsz = Ring(S, szt, "sz")
            Rva = Ring(S, [vaug[0], vaug[1]], "va")
            Rct = Ring(S, ctab_s, "ct")
            mixT = actT
            acc_i = [0]

            def fin_dve(acc_ap, Bacc, szslice, Bszt, sink_col=None):
                r = stat_take(1)
                Br = Buf()
                if sink_col is not None:
                    S.op("dve", lambda e: e.tensor_scalar(out=r, in0=acc_ap[:, 128:129], scalar1=esink[:, sink_col:sink_col + 1], scalar2=None, op0=ALU.add),
                         reads=[Bgs], writes=[Bacc, Br])
                    S.op("dve", lambda e: e.reciprocal(out=r, in_=r), reads=[Br], writes=[Br])
                else:
                    S.op("dve", lambda e: e.reciprocal(out=r, in_=acc_ap[:, 128:129]), writes=[Bacc, Br])
                ygt, Byg, _ = Ryg.next()
                S.op("dve", lambda e: e.scalar_tensor_tensor(out=ygt, in0=acc_ap[:, 0:128], scalar=r, in1=szslice, op0=ALU.mult, op1=ALU.mult),
                     reads=[Br, Bszt], writes=[Bacc, Byg])
                return ygt, Byg

            def fin_tr(ygs, chunk, i0):
                tb, Bt = next_tb()
                n = len(ygs)
                for k_, (ygt, Byg) in enumerate(ygs):
                    S.op("pe", (lambda k_, ygt: lambda e: e.transpose(out=tb[:, k_ * 128:(k_ + 1) * 128], in_=ygt, identity=ident[:]))(k_, ygt),
                         reads=[Byg, Bconst], writes=[Bt])
                S.op("act", lambda e: e.copy(out=mixT[:, chunk, i0 * 128:(i0 + n) * 128], in_=tb[:, 0:n * 128]), writes=[Bt], wacc=[Bmix[i0 + k] for k in range(n)])

            def load_head(ring, src, extra_reads=()):
                ap, B, d = ring.next()
                dma("sp", ap, src, reads=list(extra_reads), writes=[B], dsem=d)
                return ap, B

            def load_v(src_cols, Bsrc):
                ap, B, d = Rva.next()
                dma("sp", ap[:, :, 0:128], src_cols.rearrange("(t p) c -> p t c", p=128), reads=[Bsrc], writes=[B], dsem=d)
                return ap, B

            def load_sz(c0):
                ap, B, d = Rsz.next()
                dma("sp", ap, sz_d[:, c0:c0 + 128].rearrange("(t p) c -> p t c", p=128), reads=[Bsz], writes=[B], dsem=d)
                return ap, B

            SC_A = 128.0 ** -0.5
            A_ops = {}

            def a_load(h):
                kvh = h // 3
                if h % 3 == 0:
                    A_ops[("k", kvh)] = load_head(Rk, featA_d[6 + kvh], [BfA])
                    A_ops[("v", kvh)] = load_v(vA_d[:, kvh * 128:(kvh + 1) * 128], BvA)
                A_ops[("q", h)] = load_head(Rq, featA_d[h], [BfA])
                A_ops[("sz", h)] = load_sz(h * 128)

            A_pts = {}

            def a_sA(h, j):
                kT, Bk = A_ops[("k", h // 3)]
                qT, Bq = A_ops[("q", h)]
                qlo, qhi = max(j - 1, 0), min(j + 1, NT - 1)
                nq = (qhi - qlo + 1) * 128
                tc0 = (qlo - (j - 1)) * 128
                sb_, Bsb = bank[j % 2], Bbank[j % 2]
                S.op("pe", lambda e: e.matmul(sb_[:, 0:nq], lhsT=kT[:, j * 128:(j + 1) * 128], rhs=qT[:, qlo * 128:qlo * 128 + nq], start=True, stop=True),
                     reads=[Bk, Bq], writes=[Bsb])
                e_t, Be, _ = Ret.next()
                S.op("act", lambda e: e.activation(out=e_t[:, 0:nq], in_=sb_[:, 0:nq], func=AF.Exp, scale=SC_A), writes=[Bsb, Be])
                p_t, Bp, _ = Rpt.next()
                S.op("pool" if j % 3 == 2 else "dve", lambda e: e.tensor_tensor(out=p_t[:, 0:nq], in0=e_t[:, 0:nq], in1=tabA_s[:, h, tc0:tc0 + nq], op=ALU.mult),
                     reads=[Be, Btab], writes=[Bp])
                A_pts[(h, j)] = (p_t, Bp, qlo)

            A_yg = {}

            def a_sB(h, i):
                va, Bva = A_ops[("v", h // 3)]
                szh, Bszh = A_ops[("sz", h)]
                ab, Bab = bank[4 + acc_i[0] % 2], Bbank[4 + acc_i[0] % 2]
                acc_i[0] += 1
                js = [jj for jj in (i - 1, i, i + 1) if 0 <= jj < NT]
                for n_, jj in enumerate(js):
                    p_t, Bp, qlo = A_pts[(h, jj)]
                    co = (i - qlo) * 128
                    S.op("pe", (lambda p_t, co, jj, n_: lambda e: e.matmul(ab[:, 0:129], lhsT=p_t[:, co:co + 128], rhs=va[:, jj, :],
                                                                          start=(n_ == 0), stop=(n_ == len(js) - 1)))(p_t, co, jj, n_),
                         reads=[Bp, Bva], writes=[Bab])
                A_yg[(h, i)] = fin_dve(ab, Bab, szh[:, i, :], Bszh, sink_col=h)
                A_pts.pop((h, i - 1), None)

            def a_sD(h, i):
                fin_tr([A_yg.pop((h, i))], h, i)

            nA = 6 * NT
            a_load(0)
            for st in range(nA + 4):
                if st < nA:
                    h, j = divmod(st, NT)
                    if j == 4 and h + 1 < 6:
                        a_load(h + 1)
                    a_sA(h, j)
                if 0 <= st - 2 < nA:
                    a_sB(*divmod(st - 2, NT))
                if 0 <= st - 4 < nA:
                    a_sD(*divmod(st - 4, NT))

            B_ops = {}

            def b_load(h):
                B_ops[("q", h)] = load_head(Rq, qn_d[h], [Bqn])
                B_ops[("qr", h)] = load_head(Rqr, qr_d[h], [Bqr])
                B_ops[("k", h)] = load_head(Rk, kn_d[h], [Bkn])
                B_ops[("v", h)] = load_v(vB_d[:, h * 128:(h + 1) * 128], BvB)
                B_ops[("sz", h)] = load_sz(768 + h * 128)

            B_pts = {}

            def b_sA(h, qt, j):
                qT, Bq = B_ops[("q", h)]
                qrT, Bqr_ = B_ops[("qr", h)]
                kT, Bk = B_ops[("k", h)]
                qs = slice(qt * 512, (qt + 1) * 512)
                ks = slice(j * 128, (j + 1) * 128)
                sb_, Bsb = bank[j % 2], Bbank[j % 2]
                S.op("pe", lambda e: e.matmul(sb_, lhsT=kT[:, ks], rhs=qT[:, qs], start=True, stop=False), reads=[Bk, Bq], writes=[Bsb])
                S.op("pe", lambda e: e.matmul(sb_, lhsT=krT2[:, ks], rhs=qrT[:, qs], start=False, stop=True), reads=[Bkr, Bqr_], writes=[Bsb])
                p_t, Bp, _ = Rpt.next()
                S.op("act", lambda e: e.activation(out=p_t[:, 0:512], in_=sb_, func=AF.Exp, scale=kscale[:, j, h:h + 1]),
                     reads=[Bksc], writes=[Bsb, Bp])
                B_pts[(h, qt, j)] = (p_t, Bp)

            B_yg = {}

            def b_sB(h, qt, j):
                va, Bva = B_ops[("v", h)]
                szh, Bszh = B_ops[("sz", h)]
                p_t, Bp = B_pts.pop((h, qt, j))
                a0 = 2 + 2 * (qt % 2)
                for qb in range(4):
                    ab, Bab = bank[a0 + qb // 2], Bbank[a0 + qb // 2]
                    co = (qb % 2) * 130
                    S.op("pe", (lambda ab, qb, co: lambda e: e.matmul(ab[:, co:co + 129], lhsT=p_t[:, qb * 128:(qb + 1) * 128], rhs=va[:, j, :],
                                                                     start=(j == 0 and qb % 2 == 0), stop=(j == NT - 1), skip_group_check=True))(ab, qb, co),
                         reads=[Bp, Bva], writes=[Bab])
                if j == NT - 1:
                    ygs = []
                    for qb in range(4):
                        ab, Bab = bank[a0 + qb // 2], Bbank[a0 + qb // 2]
                        co = (qb % 2) * 130
                        ygs.append(fin_dve(ab[:, co:co + 129], Bab, szh[:, qt * 4 + qb, :], Bszh))
                    B_yg[(h, qt)] = ygs

            def b_sD(h, qt):
                fin_tr(B_yg.pop((h, qt)), 6 + h, qt * 4)

            nB = 6 * 4 * NT
            b_load(0)
            for st in range(nB + 4):
                if st < nB:
                    h, rem = divmod(st, 4 * NT)
                    qt, j = divmod(rem, NT)
                    if rem == 8 and h + 1 < 6:
                        b_load(h + 1)
                    b_sA(h, qt, j)
                if 0 <= st - 1 < nB:
                    h, rem = divmod(st - 1, 4 * NT)
                    b_sB(h, *divmod(rem, NT))
                if 0 <= st - 4 < nB:
                    h, rem = divmod(st - 4, 4 * NT)
                    qt, j = divmod(rem, NT)
                    if j == NT - 1:
                        b_sD(h, qt)

            SC_C = 128.0 ** -0.5
            C_ops = {}

            def c_load(h):
                C_ops[("q", h)] = load_head(Rq, featC_d[h], [BfC])
                C_ops[("k", h)] = load_head(Rk, featC_d[4 + h], [BfC])
                C_ops[("v", h)] = load_v(vC_d[:, h * 128:(h + 1) * 128], BvC)
                C_ops[("sz", h)] = load_sz(1536 + h * 128)
                ct, Bct, dct = Rct.next()
                dma("sp", ct.rearrange("p a b -> p (a b)"), ctab_d[h], writes=[Bct], dsem=dct)
                C_ops[("ct", h)] = (ct, Bct)

            def c_js(i):
                if i <= 1:
                    return [3, 2, 1, 0]
                if i >= NT - 2:
                    return [15, 14, 13, 12]
                return [i + 2, i + 1, i, i - 1, i - 2]

            C_pts = {}

            def c_sA(h, i):
                qT, Bq = C_ops[("q", h)]
                kT, Bk = C_ops[("k", h)]
                ct, Bct = C_ops[("ct", h)]
                js = c_js(i)
                nj = len(js)
                if 2 <= i <= NT - 3:
                    tsl_ = ct[:, 16:26, :]
                else:
                    b0 = 7 - 2 * (js[0] - i)
                    tsl_ = ct[:, b0:b0 + 2 * nj, :]
                sbase = (i % 2) * 1024
                sb_ = PS[:, sbase:sbase + nj * 128]
                Bs_ = [Bbank[(i % 2) * 2], Bbank[(i % 2) * 2 + 1]]
                for n_, jj in enumerate(js):
                    S.op("pe", (lambda n_, jj: lambda e: e.matmul(PS[:, sbase + n_ * 128:sbase + (n_ + 1) * 128], lhsT=kT[:, jj * 128:(jj + 1) * 128],
                                                                 rhs=qT[:, i * 128:(i + 1) * 128], start=True, stop=True))(n_, jj),
                         reads=[Bk, Bq], writes=Bs_)
                e_t, Be, _ = Ret.next()
                S.op("act", lambda e: e.activation(out=e_t[:, 0:nj * 128], in_=sb_, func=AF.Exp, scale=SC_C), writes=Bs_ + [Be])
                p_t, Bp, _ = Rpt.next()
                S.op("pool" if i % 3 == 2 else "dve", lambda e: e.tensor_tensor(out=p_t[:, 0:nj * 128], in0=e_t[:, 0:nj * 128], in1=tsl_.rearrange("p a b -> p (a b)"), op=ALU.mult),
                     reads=[Be, Bct], writes=[Bp])
                C_pts[(h, i)] = (p_t, Bp)

            C_yg = {}

            def c_sB(h, i):
                va, Bva = C_ops[("v", h)]
                szh, Bszh = C_ops[("sz", h)]
                p_t, Bp = C_pts.pop((h, i))
                js = c_js(i)
                nj = len(js)
                ab, Bab = bank[4 + acc_i[0] % 2], Bbank[4 + acc_i[0] % 2]
                acc_i[0] += 1
                for n_, jj in enumerate(js):
                    S.op("pe", (lambda n_, jj: lambda e: e.matmul(ab[:, 0:129], lhsT=p_t[:, n_ * 128:(n_ + 1) * 128], rhs=va[:, jj, :],
                                                                 start=(n_ == 0), stop=(n_ == nj - 1)))(n_, jj),
                         reads=[Bp, Bva], writes=[Bab])
                C_yg[(h, i)] = fin_dve(ab, Bab, szh[:, i, :], Bszh)

            def c_sD(h, i):
                fin_tr([C_yg.pop((h, i))], 12 + h, i)

            nC = 4 * NT
            c_load(0)
            for st in range(nC + 4):
                if st < nC:
                    h, i = divmod(st, NT)
                    if i == 4 and h + 1 < 4:
                        c_load(h + 1)
                    c_sA(h, i)
                if 0 <= st - 1 < nC:
                    c_sB(*divmod(st - 1, NT))
                if 0 <= st - 3 < nC:
                    c_sD(*divmod(st - 3, NT))

            S.barrier()
            CX.reset()
            CW.reset()
            wbuf = [CW.take([128, 16, 512], BF16) for _ in range(2)]
            Rw = Ring(S, wbuf, "w")
            hsl = [CX.take([128, 4, 512], F32) for _ in range(2)]
            Rhs = Ring(S, hsl, "hs")
            ost = [CX.take([128, 4, 512], F32) for _ in range(2)]
            Ros = Ring(S, ost, "os")
            Bh = Buf("hscr")
            wo_l = w_out_d[l].rearrange("(kc p) n -> p kc n", p=128)
            next_w = load_w(wo_l[:, :, 0:512], 512, 16)

            def p4_group(c, tg, wb, Bw):
                csl = slice(c * 512, (c + 1) * 512)
                rows = slice(tg * 512, (tg + 1) * 512)
                hs, Bhs, dhs = Rhs.next()
                dma("sp", hs, h_src[rows, csl].rearrange("(t p) c -> p t c", p=128), writes=[Bhs], dsem=dhs)
                o, Bo, do = Ros.next()
                for k_ in range(4):
                    t = tg * 4 + k_
                    tsl = slice(t * 128, (t + 1) * 128)
                    bk, Bb = next_bank()
                    proj_mm(bk, Bb, lambda kc: mixT[:, kc, tsl], Bmix[t], wb, Bw, 512, 16)
                    S.op("dve", lambda e: e.tensor_tensor(out=o[:, k_, :], in0=bk, in1=hs[:, k_, :], op=ALU.add), reads=[Bhs], writes=[Bb], wacc=[Bo])
                dma("sp", hscr_d[s][rows, csl].rearrange("(t p) c -> p t c", p=128), o, reads=[Bo], wacc=[Bh], dsem=do)

            assert CX.off <= 40 * 1024
            CX.off = 40 * 1024
            junk = CX.take([128, D], BF16)
            pT = CX.take([128, 2, L], BF16)
            wpp = CX.take([128, 2, D], BF16)
            pf = [CX.take([128, 256], F32) for _ in range(4)]
            pb = [CX.take([128, 256], BF16) for _ in range(4)]
            Rpf = Ring(S, pf, "pf")
            Rpb = Ring(S, pb, "pb", with_dsem=False)
            Bjunk = Buf("junk")
            Bwpp = Buf("wpp")
            d_wpp = S.dsem("wpp")
            dma("pool", wpp, w_pp_d[l].rearrange("(kc p) n -> p kc n", p=128), writes=[Bwpp], dsem=d_wpp)
            Brp = Buf("rstd_p")
            BpTt = [Buf(f"pT{t}") for t in range(NT)]

            p_loaded = {}

            def ple_load(t):
                pft, Bpf, dpf = Rpf.next()
                dma("sp", pft, p_d[l, s, t * 128:(t + 1) * 128, :], writes=[Bpf], dsem=dpf)
                pbt, Bpb, _ = Rpb.next()
                S.op("dve", lambda e: e.tensor_copy(out=pbt, in_=pft), reads=[Bpf], writes=[Bpb])
                p_loaded[t] = (pbt, Bpb)

            def ple_stage1(t):
                tsl = slice(t * 128, (t + 1) * 128)
                pbt, Bpb = p_loaded.pop(t)
                transposes_to([pbt[:, 0:128], pbt[:, 128:256]], pT[:, :, tsl], Bpb, BpTt[t], wacc=True, later=False)

            def ple_stage2(t):
                tsl = slice(t * 128, (t + 1) * 128)
                ss = stat_take(4)
                Bss = Buf()
                for c in range(4):
                    bk, Bb = next_bank()
                    for kc in range(2):
                        S.op("pe", lambda e: e.matmul(bk, lhsT=pT[:, kc, tsl], rhs=wpp[:, kc, c * 512:(c + 1) * 512], start=(kc == 0), stop=(kc == 1)),
                             reads=[BpTt[t], Bwpp], writes=[Bb])
                    S.op("act", lambda e: e.activation(out=junk[:, 0:512], in_=bk, func=AF.Square, accum_out=ss[:, c:c + 1]),
                         writes=[Bb, Bjunk], wacc=[Bss])
                sst = stat_take(1)
                Bsst = Buf()
                S.op("dve", lambda e: e.tensor_reduce(out=sst, in_=ss, axis=mybir.AxisListType.X, op=ALU.add), reads=[Bss], writes=[Bsst])
                rstd(sst, rstd_p[:, t:t + 1], D, Bsst, Brp)

            g_ = 0
            ple_load(0)
            ple_load(1)
            for c in range(4):
                wb, Bw = next_w
                if c < 3:
                    next_w = load_w(wo_l[:, :, (c + 1) * 512:(c + 2) * 512], 512, 16)
                for tg in range(4):
                    if g_ + 2 < NT:
                        ple_load(g_ + 2)
                    p4_group(c, tg, wb, Bw)
                    ple_stage1(g_)
                    if g_ >= 1:
                        ple_stage2(g_ - 1)
                    g_ += 1
            ple_stage2(NT - 1)
            BpT = Buf("pT")
            for t in range(NT):
                for tk_ in BpTt[t].w.values():
                    kk_ = id(tk_.sem)
                    if kk_ not in BpT.w or BpT.w[kk_].val < tk_.val:
                        BpT.w[kk_] = tk_

            S.barrier()
            CX.reset()
            hb = [CX.take([128, D], F32) for _ in range(4)]
            ub = [CX.take([128, D], BF16) for _ in range(2)]
            assert CX.off == 40 * 1024
            Rhb = Ring(S, hb, "hb")
            Rub = Ring(S, ub, "ub", with_dsem=False)
            bcast_load(G2[:], g_ple_d[l, :], D, Bg2, d_g2)
            Bact = [Buf(f"act{t}") for t in range(NT)]
            hrows = lambda t: hscr_d[s][t * 128:(t + 1) * 128, :]
            for t in range(3):
                nt_load(hrows(t), t)
            for t in range(NT):
                if t + 3 < NT:
                    nt_load(hrows(t + 3), t + 3)
                norm_transpose(hrows(t), G2, Bg2, t)
            run_deferred()

            S.barrier()
            CW.reset()
            bcast_load(G1[:], g_post_d[l, :], D, Bg1, d_g1)
            CX.reset()
            _ = [CX.take([128, D], F32) for _ in range(4)]
            _ = [CX.take([128, D], BF16) for _ in range(2)]
            _ = CX.take([128, D], BF16)
            pT_off = CX.off
            pT = CX.take([128, 2, L], BF16)
            wpp = CX.take([128, 2, D], BF16)
            keep = CX.off
            CX.reset()
            hsl = [CX.take([128, 4, 512], F32) for _ in range(2)]
            gat = [CX.take([128, 512], F32) for _ in range(2)]
            pn = [CX.take([128, 512], F32) for _ in range(2)]
            ost = [CX.take([128, 4, 512], F32) for _ in range(2)]
            assert CX.off <= pT_off
            Rhs = Ring(S, hsl, "hs")
            Rg = Ring(S, gat, "g", with_dsem=False)
            Rpn = Ring(S, pn, "pn", with_dsem=False)
            Ros = Ring(S, ost, "os")
            wbuf = [CW.take([128, 16, 512], BF16) for _ in range(2)]
            Rw = Ring(S, wbuf, "w")
            wg_l = w_gate_d[l].rearrange("(kc p) n -> p kc n", p=128)
            next_w = load_w(wg_l[:, :, 0:512], 512, 16)
            Bout = Buf("hout")

            def p5_group(c, tg, wb, Bw):
                csl = slice(c * 512, (c + 1) * 512)
                rows = slice(tg * 512, (tg + 1) * 512)
                hs, Bhs, dhs = Rhs.next()
                dma("sp", hs, hscr_d[s][rows, csl].rearrange("(t p) c -> p t c", p=128), reads=[Bh], writes=[Bhs], dsem=dhs)
                o, Bo, do = Ros.next()
                for k_ in range(4):
                    t = tg * 4 + k_
                    tsl = slice(t * 128, (t + 1) * 128)
                    bk, Bb = next_bank()
                    proj_mm(bk, Bb, lambda kc: actT[:, kc, tsl], Bact[t], wb, Bw, 512, 16)
                    bk2, Bb2 = next_bank()
                    for kc in range(2):
                        S.op("pe", lambda e: e.matmul(bk2, lhsT=pT[:, kc, tsl], rhs=wpp[:, kc, csl], start=(kc == 0), stop=(kc == 1)),
                             reads=[BpT, Bwpp], writes=[Bb2])
                    gt_, Bgt_, _ = Rg.next()
                    S.op("act", lambda e: e.activation(out=gt_, in_=bk, func=AF.Sigmoid), writes=[Bb, Bgt_])
                    pn_, Bpn_, _ = Rpn.next()
                    S.op("dve", lambda e: e.scalar_tensor_tensor(out=pn_, in0=bk2, scalar=rstd_p[:, t:t + 1], in1=G1[:, csl], op0=ALU.mult, op1=ALU.mult),
                         reads=[Brp, Bg1], writes=[Bb2, Bpn_])
                    S.op("pool", lambda e: e.tensor_tensor(out=pn_, in0=pn_, in1=gt_, op=ALU.mult), reads=[Bgt_], writes=[Bpn_])
                    S.op("dve", lambda e: e.tensor_tensor(out=o[:, k_, :], in0=pn_, in1=hs[:, k_, :], op=ALU.add), reads=[Bpn_, Bhs], wacc=[Bo])
                dma("sp", h_dst[rows, csl].rearrange("(t p) c -> p t c", p=128), o, reads=[Bo], wacc=[Bout], dsem=do)

            for c in range(4):
                wb, Bw = next_w
                if c < 3:
                    next_w = load_w(wg_l[:, :, (c + 1) * 512:(c + 2) * 512], 512, 16)
                for tg in range(4):
                    p5_group(c, tg, wb, Bw)

    S.emit()
    return nc, S


def _prep_shared(inp, tables):
    tabA, cos, sin, oh = tables
    f = lambda a: np.ascontiguousarray(np.asarray(a, dtype=np.float32))
    rpb = np.asarray(inp["c_rpb"], np.float32)
    rpbT = np.ascontiguousarray(rpb[:, :, ::-1, :].transpose(0, 1, 3, 2))
    return {
        "w_in": np.ascontiguousarray(np.asarray(inp["w_in"], np.float32)[:, :, IN_PERM]),
        "w_out": f(inp["w_out"]), "w_gate": f(inp["w_ple_gate"]), "w_pp": f(inp["w_ple_proj"]),
        "w_uq": f(inp["b_w_uq"]),
        "w_ukv": np.ascontiguousarray(np.asarray(inp["b_w_ukv"], np.float32)[:, :, UKV_PERM]),
        "g_in": f(inp["norm_in"]), "g_ple": f(inp["ple_norm"]), "g_post": f(inp["ple_post_norm"]),
        "g_aq": f(inp["a_q_norm"]), "g_ak": f(inp["a_k_norm"]), "g_cq": f(inp["c_q_norm"]), "g_ck": f(inp["c_k_norm"]),
        "g_bcq": f(inp["b_cq_norm"]), "g_bckv": f(inp["b_ckv_norm"]), "g_bq": f(inp["b_q_norm"]), "g_bk": f(inp["b_k_norm"]),
        "sink": f(inp["a_sink"]), "rpbT": rpbT,
        "ident": np.eye(128).astype(ml_dtypes.bfloat16),
        "tabA": tabA, "cosT": cos, "sinT": sin, "onehot": oh,
    }


_CACHE = {}


def kernel(**inputs):
    xp = np.asarray(inputs["x_prompt"], np.float32)
    xs = np.asarray(inputs["x_sample"], np.float32)
    pp = np.asarray(inputs["p_prompt"], np.float32)
    psm = np.asarray(inputs["p_sample"], np.float32)
    nB, nS = xp.shape[0], xs.shape[0]
    x_all = np.concatenate([xp, xs], axis=0)
    p_all = np.concatenate([pp, psm], axis=1)
    ntot = nB + nS
    ncores = 8
    slots = [[(c + 8 * k) % ntot if (c + 8 * k) < ntot else (c + 8 * k) % ntot for k in range(NSEQ)] for c in range(ncores)]
    if "nc" not in _CACHE:
        _CACHE["nc"] = build()[0]
        _CACHE["tables"] = _const_tables()
    nc = _CACHE["nc"]
    shared = _prep_shared(inputs, _CACHE["tables"])
    in_maps = []
    for c in range(ncores):
        m = dict(shared)
        m["x"] = np.ascontiguousarray(x_all[slots[c]])
        m["p"] = np.ascontiguousarray(p_all[:, slots[c]])
        in_maps.append(m)
    res = run_bass_kernel_spmd(nc, in_maps, core_ids=list(range(ncores)))
    y_all = np.zeros_like(x_all)
    done = set()
    for c in range(ncores):
        yc = res.results[c]["y"]
        for k, sidx in enumerate(slots[c]):
            if (c + 8 * k) < ntot and sidx not in done:
                y_all[sidx] = yc[k]
                done.add(sidx)
    return (y_all[:nB], y_all[nB:])
```

```python
import types
import numpy as np
from contextlib import ExitStack
import ml_dtypes
import concourse.bass as bass
import concourse.mybir as mybir
from concourse.bass_utils import run_bass_kernel_spmd

F32 = mybir.dt.float32
BF16 = mybir.dt.bfloat16
AF = mybir.ActivationFunctionType
ALU = mybir.AluOpType

L = 2048
D = 2048
NT = 16
DEPTH = 4
NSEQ = 3
EPS = 1e-6
IN_W = 5952

EPOCH = 30000


def _freeze(fn):
    if fn.__closure__ is None:
        return fn
    cells = tuple(types.CellType(c.cell_contents) for c in fn.__closure__)
    return types.FunctionType(fn.__code__, fn.__globals__, fn.__name__, fn.__defaults__, cells)


class Tk:
    __slots__ = ("sem", "val", "q", "dma")

    def __init__(self, sem, val, q, dma=False):
        self.sem = sem
        self.val = val
        self.q = q
        self.dma = dma


class Buf:
    __slots__ = ("name", "w", "r")

    def __init__(self, name=""):
        self.name = name
        self.w = {}
        self.r = {}


class DSem:
    __slots__ = ("h", "count")

    def __init__(self, h):
        self.h = h
        self.count = 0


class Sched:
    QS = ("pe", "act", "dve", "pool", "sp")

    def __init__(self, nc, es):
        self.nc = nc
        self.es = es
        self.qs = {k: [] for k in self.QS}
        self.esem = {}
        self.ecount = {}
        self.waited = {k: {} for k in self.QS}
        self.nsem = 0
        self.dsems = []
        self.dpool = []
        self.pool_i = 0
        for k in self.QS:
            self._new_epoch(k)

    def _alloc_sem(self, name):
        self.nsem += 1
        return self.es.enter_context(self.nc.semaphore(name))

    def _new_epoch(self, k):
        self.esem[k] = self._alloc_sem(f"e{k}{self.nsem}")
        self.ecount[k] = 0

    def dsem(self, name="d", persistent=False):
        if persistent:
            d = DSem(self._alloc_sem(f"{name}{self.nsem}"))
            self.dsems.append(d)
            return d
        if self.pool_i >= len(self.dpool):
            d = DSem(self._alloc_sem(f"dp{self.nsem}"))
            self.dsems.append(d)
            self.dpool.append(d)
        d = self.dpool[self.pool_i]
        self.pool_i += 1
        return d

    def op(self, q, fn, reads=(), writes=(), wacc=(), dsem=None):
        raw = {}
        oth = {}

        def add(d, t):
            k = id(t.sem)
            if k not in d or d[k].val < t.val:
                d[k] = t

        for b in reads:
            for t in b.w.values():
                add(raw, t)
        for b in writes:
            for t in b.w.values():
                add(oth, t)
            for t in b.r.values():
                add(oth, t)
        for b in wacc:
            for t in b.r.values():
                add(oth, t)
        waits = []
        wd = self.waited[q]
        for d, is_raw in ((raw, True), (oth, False)):
            for k, t in d.items():
                if t.q == q and (not t.dma) and dsem is None:
                    if not is_raw or q == "pe":
                        continue
                if wd.get(k, 0) >= t.val:
                    continue
                wd[k] = t.val
                waits.append((t.sem, t.val))
        if dsem is not None:
            dsem.count += 16
            tk = Tk(dsem.h, dsem.count, q, True)
            inc = (dsem.h, 16)
        else:
            if self.ecount[q] >= EPOCH:
                self._new_epoch(q)
            self.ecount[q] += 1
            tk = Tk(self.esem[q], self.ecount[q], q)
            inc = (self.esem[q], 1)
        self.qs[q].append((waits, _freeze(fn), inc))
        k = id(tk.sem)
        for b in writes:
            b.w = {k: tk}
            b.r = {}
        for b in wacc:
            b.w[k] = tk
        for b in reads:
            if k not in b.r or b.r[k].val < tk.val:
                b.r[k] = tk
        return tk

    def barrier(self):
        self.pool_i = 0
        tks = []
        for q in self.QS:
            if self.ecount[q] > 0:
                tks.append((self.esem[q], self.ecount[q], q))
        for d in self.dsems:
            if d.count > 0:
                tks.append((d.h, d.count, None))
        for q in self.QS:
            waits = []
            wd = self.waited[q]
            for (h, v, src) in tks:
                if src == q:
                    continue
                if wd.get(id(h), 0) >= v:
                    continue
                wd[id(h)] = v
                waits.append((h, v))
            if waits:
                self.qs[q].append((waits, None, None))

    def emit(self):
        nc = self.nc
        self.barrier()
        qs = self.qs

        def run(eng, lst):
            for waits, fn, inc in lst:
                for (s, v) in waits:
                    eng.wait_ge(s, v)
                if fn is not None:
                    fn(eng).then_inc(inc[0], inc[1])

        with nc.Block() as block:
            @block.tensor
            def _(e):
                run(e, qs["pe"])

            @block.scalar
            def _(e):
                run(e, qs["act"])

            @block.vector
            def _(e):
                run(e, qs["dve"])

            @block.gpsimd
            def _(e):
                run(e, qs["pool"])

            @block.sync
            def _(e):
                run(e, qs["sp"])


class Ring:
    def __init__(self, S, aps, name, with_dsem=True):
        self.aps = aps
        self.bufs = [Buf(f"{name}{i}") for i in range(len(aps))]
        self.ds = [S.dsem(name) for _ in aps] if with_dsem else [None] * len(aps)
        self.i = 0

    def next(self):
        k = self.i % len(self.aps)
        self.i += 1
        return self.aps[k], self.bufs[k], self.ds[k]


_O = dict(aq=0, ak=768, av=1024, az=1280, bcq=2048, bckv=2560, bkr=3072, bz=3136,
          cq=3904, ck=4416, cv=4928, cz=5440)


def _in_blocks():
    r = lambda a, n: list(range(a, a + n))
    blocks = [
        ("QK", r(_O["aq"], 512)),
        ("QK", r(_O["aq"] + 512, 256) + r(_O["ak"], 256)),
        ("QK", r(_O["cq"], 512)),
        ("QK", r(_O["ck"], 512)),
        ("LAT", r(_O["bcq"], 512)),
        ("LAT", r(_O["bckv"], 512)),
        ("VKR", r(_O["av"], 256) + r(_O["bkr"], 64)),
        ("V", r(_O["cv"], 512)),
        ("Z", r(_O["az"], 512)),
        ("Z", r(_O["az"] + 512, 256) + r(_O["bz"], 256)),
        ("Z", r(_O["bz"] + 256, 512)),
        ("Z", r(_O["cz"], 512)),
    ]
    return blocks


IN_BLOCKS = _in_blocks()
IN_PERM = np.concatenate([np.array(c) for _, c in IN_BLOCKS])
UKV_PERM = np.concatenate([np.arange(h * 256, h * 256 + 128) for h in range(6)] +
                          [np.arange(h * 256 + 128, h * 256 + 256) for h in range(6)])


def _const_tables():
    slopes = 2.0 ** (-8.0 * np.arange(1, 7, dtype=np.float64) / 6)
    kt = np.arange(128)[:, None]
    qo = np.arange(384)[None, :]
    delta = 128 + kt - qo
    tabA = np.zeros((6, 128, 384), np.float32)
    for h in range(6):
        tabA[h] = (np.exp(-slopes[h] * np.abs(delta)) * (np.abs(delta) <= 128)).astype(np.float32)
    half = 32
    inv = 10000.0 ** (-np.arange(half, dtype=np.float32) / half)
    ang = np.arange(L, dtype=np.float32)[:, None] * inv[None, :]
    cos = np.cos(ang).astype(np.float32)
    sin = np.sin(ang).astype(np.float32)
    kc = np.arange(64)[:, None]
    qc = np.arange(64)[None, :]
    cs = np.clip(qc - 8, 0, 48)
    valid = (kc >= cs) & (kc < cs + 16)
    oh = np.zeros((32, 64, 64), np.float32)
    b = 15 + kc - qc
    for bb in range(31):
        oh[bb] = ((b == bb) & valid).astype(np.float32)
    oh[31] = np.where(valid, 0.0, -30000.0)
    return tabA, cos, sin, oh.reshape(32, 4096)


def build(nseq=NSEQ, nlayers=DEPTH, debug=False):
    nc = bass.Bass("TRN2", target_bir_lowering=False)
    es = ExitStack()

    def din(name, shape, dt=F32):
        return nc.dram_tensor(name, list(shape), dt, kind="ExternalInput").ap()

    dbg_kind = "ExternalOutput" if debug else "Internal"

    def dscr(name, shape, dt=F32):
        return nc.dram_tensor(name, list(shape), dt, kind=dbg_kind).ap()

    x_d = din("x", [nseq, L, D])
    p_d = din("p", [DEPTH, nseq, L, 256])
    y_d = nc.dram_tensor("y", [nseq, L, D], F32, kind="ExternalOutput").ap()
    w_in_d = din("w_in", [DEPTH, D, IN_W])
    w_out_d = din("w_out", [DEPTH, D, D])
    w_gate_d = din("w_gate", [DEPTH, D, D])
    w_pp_d = din("w_pp", [DEPTH, 256, D])
    w_uq_d = din("w_uq", [DEPTH, 512, 1152])
    w_ukv_d = din("w_ukv", [DEPTH, 512, 1536])
    g_in_d = din("g_in", [DEPTH, D])
    g_ple_d = din("g_ple", [DEPTH, D])
    g_post_d = din("g_post", [DEPTH, D])
    g_aq_d = din("g_aq", [DEPTH, 128])
    g_ak_d = din("g_ak", [DEPTH, 128])
    g_cq_d = din("g_cq", [DEPTH, 128])
    g_ck_d = din("g_ck", [DEPTH, 128])
    g_bcq_d = din("g_bcq", [DEPTH, 512])
    g_bckv_d = din("g_bckv", [DEPTH, 512])
    g_bq_d = din("g_bq", [DEPTH, 192])
    g_bk_d = din("g_bk", [DEPTH, 192])
    sink_d = din("sink", [DEPTH, 6])
    rpbT_d = din("rpbT", [DEPTH, 4, 31, 15])
    ident_d = din("ident", [128, 128], BF16)
    tabA_d = din("tabA", [6, 128, 384])
    cos_d = din("cosT", [L, 32])
    sin_d = din("sinT", [L, 32])
    oh_d = din("onehot", [32, 4096])

    hscr_d = dscr("hscr", [nseq, L, D])
    featA_d = dscr("featA", [8, 128, L], BF16)
    featC_d = dscr("featC", [8, 128, L], BF16)
    qn_d = dscr("qn", [6, 128, L], BF16)
    qr_d = dscr("qr", [6, 128, L], BF16)
    kn_d = dscr("kn", [6, 128, L], BF16)
    vA_d = dscr("vA", [L, 256], BF16)
    vB_d = dscr("vB", [L, 768], BF16)
    vC_d = dscr("vC", [L, 512], BF16)
    sz_d = dscr("sz", [L, D])
    xd_d = dscr("xd", [4, 15, 4096])
    ctab_d = dscr("ctab", [4, 128, 26 * 64])

    S = Sched(nc, es)

    def sbt(name, shape, dt):
        return nc.alloc_sbuf_tensor(name, list(shape), dt)

    ident = sbt("ident_s", [128, 128], BF16)
    cosT = sbt("cosT_s", [128, NT, 32], F32)
    sinT = sbt("sinT_s", [128, NT, 32], F32)
    G1 = sbt("G1", [128, D], F32)
    g_aq = sbt("g_aq_s", [128, 128], F32)
    g_ak = sbt("g_ak_s", [128, 128], F32)
    g_cq = sbt("g_cq_s", [128, 128], F32)
    g_ck = sbt("g_ck_s", [128, 128], F32)
    g_bq = sbt("g_bq_s", [128, 192], F32)
    g_bk = sbt("g_bk_s", [128, 192], F32)
    g_bcq = sbt("g_bcq_s", [128, 512], F32)
    g_bckv = sbt("g_bckv_s", [128, 512], F32)
    esink = sbt("esink", [128, 8], F32)
    kscale = sbt("kscale", [128, NT, 6], F32)
    ss_kr = sbt("ss_kr", [128, NT], F32)
    rstd_p = sbt("rstd_p", [128, NT], F32)
    stat = sbt("stat", [128, 512], F32)
    vaug = [sbt(f"vaug{i}", [128, NT, 129], BF16) for i in range(2)]
    ARENA_ACT = sbt("arena_act", [128, 16 * L], BF16)
    ARENA_W = sbt("arena_w", [128, 16 * 1024], BF16)
    XBYTES = 80 * 1024
    ARENA_X = sbt("arena_x", [128, XBYTES // 4], F32)

    actT = ARENA_ACT[:, :].rearrange("p (c n) -> p c n", c=16)
    PS = nc.alloc_psum_tensor("ps", [128, 3072], F32)
    PT = nc.alloc_psum_tensor("pt", [128, 2048], BF16)
    bank = [PS[:, i * 512:(i + 1) * 512] for i in range(6)]
    Bbank = [Buf(f"bank{i}") for i in range(6)]
    tbank = [PT[:, i * 1024:(i + 1) * 1024] for i in range(2)]
    Btb = [Buf(f"tb{i}") for i in range(2)]

    class Carver:
        def __init__(self, arena_f32, nbytes):
            self.a = arena_f32
            self.n = nbytes
            self.off = 0

        def reset(self):
            self.off = 0

        def take(self, shape, dt):
            esz = 4 if dt == F32 else 2
            nel = int(np.prod(shape[1:]))
            nb = nel * esz
            nb_al = (nb + 31) // 32 * 32
            assert self.off + nb_al <= self.n, ("arena overflow", self.off, nb_al, self.n)
            v = self.a[:, self.off // 4:(self.off + nb_al) // 4]
            if dt != F32:
                v = v.bitcast(dt)
            v = v[0:shape[0], 0:nel]
            if len(shape) == 3:
                v = v.rearrange("p (a b) -> p a b", a=shape[1])
            self.off += nb_al
            return v

    CX = Carver(ARENA_X, XBYTES)
    ARENA_W32 = ARENA_W[:, :].bitcast(F32)
    CW = Carver(ARENA_W32, 32 * 1024)

    Bconst = Buf("const")
    d_const = S.dsem("const", persistent=True)

    def dma(q, out, in_, reads=(), writes=(), wacc=(), dsem=None):
        assert dsem is not None
        return S.op(q, lambda e: e.dma_start(out=out, in_=in_), reads=reads, writes=writes,
                    wacc=wacc, dsem=dsem)

    dma("sp", ident[:], ident_d, wacc=[Bconst], dsem=d_const)
    dma("sp", cosT[:], cos_d.rearrange("(t p) i -> p t i", p=128), wacc=[Bconst], dsem=d_const)
    dma("sp", sinT[:], sin_d.rearrange("(t p) i -> p t i", p=128), wacc=[Bconst], dsem=d_const)
    for i in range(2):
        S.op("dve", (lambda i: lambda e: e.memset(vaug[i][:, :, 128:129], 1.0))(i), wacc=[Bconst])
    S.barrier()

    stat_i = [0]

    def stat_take(n):
        if stat_i[0] + n > 512:
            stat_i[0] = 0
        a = stat[:, stat_i[0]:stat_i[0] + n]
        stat_i[0] += n
        return a

    def rstd_from_ss(ss_ap, n, count, eps=EPS, extra_scale=None, out=None):
        r = out if out is not None else stat_take(n)
        b = Buf("rstd")
        return r, b

    bank_i = [0]

    def next_bank():
        k = bank_i[0] % 6
        bank_i[0] += 1
        return bank[k], Bbank[k]

    tb_i = [0]

    def next_tb():
        k = tb_i[0] % 2
        tb_i[0] += 1
        return tbank[k], Btb[k]

    tr_i = [0]
    DQ = []

    def dq_new_group():
        DQ.append([])

    def defer(thunk):
        if not DQ:
            DQ.append([])
        DQ[-1].append(thunk)

    def run_deferred(keep=0):
        while len(DQ) > keep:
            for th in DQ.pop(0):
                th()

    _orig_barrier = S.barrier

    def _checked_barrier():
        assert not DQ, "deferred work pending at barrier"
        _orig_barrier()
    S.barrier = _checked_barrier

    def bcast_load(dst, src_row, n, buf, ds):
        dma("sp", dst, src_row.partition_broadcast(128), writes=[buf], dsem=ds)

    Bg1, Bgs = Buf("G1"), Buf("gsm")
    G2, Bg2 = G1, Bg1
    d_g1, d_g2, d_gs = S.dsem("g1", True), S.dsem("g2", True), S.dsem("gs", True)

    ROPE_ENG = "pool"

    def rope(xr, out_bf, t, tmp, Bx, Bout, Btmp, n=1):
        c = cosT[:, t, :].unsqueeze(1).to_broadcast([128, n, 32])
        s_ = sinT[:, t, :].unsqueeze(1).to_broadcast([128, n, 32])
        x1 = xr[:, :, 0:32]
        x2 = xr[:, :, 32:64]
        q = ROPE_ENG
        S.op(q, lambda e: e.tensor_tensor(out=tmp[:, 0], in0=x1, in1=c, op=ALU.mult), reads=[Bx, Bconst], writes=[Btmp])
        S.op(q, lambda e: e.tensor_tensor(out=tmp[:, 1], in0=x2, in1=s_, op=ALU.mult), reads=[Bx], writes=[Btmp])
        S.op(q, lambda e: e.tensor_tensor(out=tmp[:, 2], in0=x1, in1=s_, op=ALU.mult), reads=[Bx], writes=[Btmp])
        S.op(q, lambda e: e.tensor_tensor(out=tmp[:, 3], in0=x2, in1=c, op=ALU.mult), reads=[Bx], writes=[Btmp])
        S.op(q, lambda e: e.tensor_tensor(out=out_bf[:, :, 0:32], in0=tmp[:, 0], in1=tmp[:, 1], op=ALU.subtract), reads=[Btmp], writes=[Bout])
        S.op(q, lambda e: e.tensor_tensor(out=out_bf[:, :, 32:64], in0=tmp[:, 2], in1=tmp[:, 3], op=ALU.add), reads=[Btmp], writes=[Bout])

    def sumsq(ps_ap, junk_ap, ss_ap, Bps, Bjunk, Bss):
        S.op("act", lambda e: e.activation(out=junk_ap, in_=ps_ap, func=AF.Square, accum_out=ss_ap),
             writes=[Bps, Bjunk, Bss])

    def rstd(ss_ap, r_ap, count, Bss, Br, mul=None):
        m2 = 1.0 if mul is None else float(mul) ** 2
        S.op("act", lambda e: e.activation(out=r_ap, in_=ss_ap, func=AF.Sqrt, scale=1.0 / (count * m2), bias=EPS / m2),
             reads=[Bss], writes=[Br])
        S.op("dve", lambda e: e.reciprocal(out=r_ap, in_=r_ap), reads=[Br], writes=[Br])

    for l in range(nlayers):
        S.barrier()
        for (dst, src, n) in ((g_aq, g_aq_d, 128), (g_ak, g_ak_d, 128), (g_cq, g_cq_d, 128), (g_ck, g_ck_d, 128),
                              (g_bq, g_bq_d, 192), (g_bk, g_bk_d, 192), (g_bcq, g_bcq_d, 512), (g_bckv, g_bckv_d, 512)):
            dma("sp", dst[:], src[l, :].partition_broadcast(128), wacc=[Bgs], dsem=d_gs)
        dma("sp", esink[:, 0:6], sink_d[l, :].partition_broadcast(128), wacc=[Bgs], dsem=d_gs)
        S.barrier()
        S.op("act", lambda e: e.activation(out=esink[:, 0:6], in_=esink[:, 0:6], func=AF.Exp), writes=[Bgs])
        CX.reset()
        CW.reset()
        oh_s = CX.take([32, 4096], F32)
        xa_s = CX.take([15, 4096], F32)
        rt_s = CX.take([32, 16], F32)
        xtr = CW.take([128, 26, 64], F32)
        Boh, Bxa, Brt, Bxtr, Bxd = Buf(), Buf(), Buf(), Buf(), Buf()
        d_t1, d_t2 = d_g1, d_g2
        dma("sp", oh_s, oh_d, writes=[Boh], dsem=d_t1)
        for h in range(4):
            S.op("dve", lambda e: e.memset(rt_s[:, 0:15], 1.0), writes=[Brt])
            dma("sp", rt_s[0:31, 0:15], rpbT_d[l, h], writes=[Brt], dsem=d_t2)
            for c in range(8):
                bk, Bb = next_bank()
                S.op("pe", (lambda bk, c: lambda e: e.matmul(bk[0:15, :], lhsT=rt_s[:, 0:15], rhs=oh_s[:, c * 512:(c + 1) * 512],
                                                              start=True, stop=True))(bk, c), reads=[Brt, Boh], writes=[Bb])
                S.op("act", (lambda bk, c: lambda e: e.activation(out=xa_s[:, c * 512:(c + 1) * 512], in_=bk[0:15, :], func=AF.Exp))(bk, c),
                     writes=[Bb, Bxa])
            dma("sp", xd_d[h], xa_s, reads=[Bxa], writes=[Bxd], dsem=d_t1)
            S.op("dve", lambda e: e.memset(xtr, 0.0), writes=[Bxtr])
            src = xd_d[h].rearrange("a (k q) -> k a q", k=64)
            dma("sp", xtr[0:64, 0:15, :], src, reads=[Bxd], writes=[Bxtr], dsem=d_t2)
            dma("sp", xtr[64:128, 1:16, :], src, reads=[Bxd], writes=[Bxtr], dsem=d_t2)
            S.op("dve", lambda e: e.tensor_copy(out=xtr[:, 16:26, :], in_=xtr[:, 3:13, :]), reads=[Bxtr], writes=[Bxtr])
            S.op("dve", lambda e: e.memset(xtr[:, 16, :], 0.0), writes=[Bxtr])
            S.op("dve", lambda e: e.memset(xtr[64:128, 17, :], 0.0), writes=[Bxtr])
            S.op("dve", lambda e: e.memset(xtr[0:64, 25, :], 0.0), writes=[Bxtr])
            dma("sp", ctab_d[h], xtr.rearrange("p a b -> p (a b)"), reads=[Bxtr], writes=[Bxd], dsem=d_t1)
        S.barrier()

        for s in range(nseq):
            h_src = x_d[s] if l == 0 else hscr_d[s]
            h_dst = y_d[s] if l == nlayers - 1 else hscr_d[s]

            S.barrier()
            CX.reset()
            hb = [CX.take([128, D], F32) for _ in range(4)]
            ub = [CX.take([128, D], BF16) for _ in range(2)]
            junk = CX.take([128, D], BF16)
            Bjunk = Buf("junk")
            Rhb = Ring(S, hb, "hb")
            Rub = Ring(S, ub, "ub", with_dsem=False)
            bcast_load(G1[:], g_in_d[l, :], D, Bg1, d_g1)
            Bact = [Buf(f"act{t}") for t in range(NT)]

            nt_loads = {}

            def nt_load(src_rows, t):
                hbt, Bh, dh = Rhb.next()
                dma("sp", hbt, src_rows, writes=[Bh], dsem=dh)
                nt_loads[t] = (hbt, Bh)

            def norm_transpose(src_rows, gtile, Bgt, t, pre=None):
                if t not in nt_loads:
                    nt_load(src_rows, t)
                hbt, Bh = nt_loads.pop(t)
                ss = stat_take(1)
                rr = stat_take(1)
                Bss, Br = Buf(), Buf()
                S.op("act", lambda e: e.activation(out=junk, in_=hbt, func=AF.Square, accum_out=ss),
                     reads=[Bh], writes=[Bjunk, Bss])
                rstd(ss, rr, D, Bss, Br)
                ubt, Bu, _ = Rub.next()
                S.op("dve", lambda e: e.scalar_tensor_tensor(out=ubt, in0=hbt, scalar=rr, in1=gtile[:], op0=ALU.mult, op1=ALU.mult),
                     reads=[Bh, Br, Bgt], writes=[Bu])
                run_deferred()

                def tr_part():
                    for half in range(2):
                        tb, Bt = next_tb()
                        for k in range(8):
                            kc = half * 8 + k
                            S.op("pe", (lambda tb, k, kc: lambda e: e.transpose(out=tb[:, k * 128:(k + 1) * 128], in_=ubt[:, kc * 128:(kc + 1) * 128],
                                                                                identity=ident[:]))(tb, k, kc), reads=[Bu, Bconst], writes=[Bt])
                        dst = actT[:, half * 8:(half + 1) * 8, t * 128:(t + 1) * 128]
                        if half == 0:
                            S.op("act", (lambda tb, dst: lambda e: e.copy(out=dst, in_=tb.rearrange("p (a b) -> p a b", a=8)))(tb, dst),
                                 writes=[Bt], wacc=[Bact[t]])
                        else:
                            S.op("dve", (lambda tb, dst: lambda e: e.tensor_copy(out=dst, in_=tb.rearrange("p (a b) -> p a b", a=8)))(tb, dst),
                                 writes=[Bt], wacc=[Bact[t]])
                dq_new_group()
                defer(tr_part)

            for t in range(3):
                nt_load(h_src[t * 128:(t + 1) * 128, :], t)
            for t in range(NT):
                if t + 3 < NT:
                    nt_load(h_src[(t + 3) * 128:(t + 4) * 128, :], t + 3)
                norm_transpose(h_src[t * 128:(t + 1) * 128, :], G1, Bg1, t)
            run_deferred()

            S.barrier()
            CX.reset()
            CW.reset()
            wbuf = [CW.take([128, 16, 512], BF16) for _ in range(2)]
            Rw = Ring(S, wbuf, "w")
            stageT = CX.take([128, 4, L], BF16)
            stageA = stageT[:, :, 0:1024]
            stageB = stageT[:, :, 1024:2048]
            BstageA, BstageB = Buf("stageA"), Buf("stageB")
            d_stage = S.dsem("stg")
            latT = [CX.take([128, 4, L], BF16) for _ in range(2)]
            Blat = [Buf("lat0"), Buf("lat1")]
            krT = CX.take([128, L], BF16)
            Bkr = Buf("krT")
            xn = [CX.take([128, 512], BF16) for _ in range(5)]
            Rxn = Ring(S, xn, "xn", with_dsem=False)
            zst = [CX.take([128, 512], F32) for _ in range(2)]
            Rz = Ring(S, zst, "z")
            vst = [CX.take([128, 512], BF16) for _ in range(2)]
            Rv = Ring(S, vst, "v")
            xr2 = [CX.take([128, 2, 64], F32) for _ in range(3)]
            ropet2 = [CX.take([128, 4, 2 * 32], F32).rearrange("p a (n c) -> p a n c", n=2) for _ in range(3)]
            xrr4 = CX.take([128, 8, 128], BF16)
            junk = CX.take([128, 512], BF16)
            Rxr = Ring(S, xr2, "xr", with_dsem=False)
            Rrt = Ring(S, ropet2, "rt", with_dsem=False)
            Rxrr = Ring(S, [xrr4[:, 2 * i:2 * i + 2, :] for i in range(4)], "xrr", with_dsem=False)
            S.op("dve", lambda e: e.memset(xrr4[:, :, 64:128], 0.0), writes=Rxrr.bufs)
            Bsz, BvA, BvB, BvC, BfA, BfC = Buf("sz"), Buf("vA"), Buf("vB"), Buf("vC"), Buf("fA"), Buf("fC")
            Bqn, Bqr, Bkn = Buf("qn"), Buf("qr"), Buf("kn")
            Bksc = Buf("kscale")

            def load_w(src_ap, ncols, nk):
                wb, Bw, dw = Rw.next()
                dst = wb[:, 0:nk, 0:ncols]
                dma("pool", dst, src_ap, writes=[Bw], dsem=dw)
                return wb, Bw

            def proj_mm(bk, Bb, lhs_of_kc, Blhs, wb, Bw, ncols, nk):
                for kc in range(nk):
                    lhs = lhs_of_kc(kc)
                    S.op("pe", lambda e: e.matmul(bk[:, 0:ncols], lhsT=lhs, rhs=wb[:, kc, 0:ncols],
                                                  start=(kc == 0), stop=(kc == nk - 1)),
                         reads=[Blhs, Bw], writes=[Bb])

            def transposes_to(srcs, dst_ap, Bsrc, Bdst, wacc=False, later=True):
                if later:
                    defer(lambda: transposes_to(srcs, dst_ap, Bsrc, Bdst, wacc=wacc, later=False))
                    return
                tb, Bt = next_tb()
                n = len(srcs)
                for k, sap in enumerate(srcs):
                    S.op("pe", (lambda k, sap: lambda e: e.transpose(out=tb[:, k * 128:(k + 1) * 128], in_=sap, identity=ident[:]))(k, sap),
                         reads=[Bsrc, Bconst], writes=[Bt])
                if n == 1:
                    src = tb[:, 0:128]
                else:
                    src = tb[:, 0:n * 128].rearrange("p (a b) -> p a b", a=n)
                tr_i[0] += 1
                if tr_i[0] % 2 == 0:
                    fn_ = ("act", lambda e: e.copy(out=dst_ap, in_=src))
                else:
                    fn_ = ("dve", lambda e: e.tensor_copy(out=dst_ap, in_=src))
                if wacc:
                    S.op(fn_[0], fn_[1], writes=[Bt], wacc=[Bdst])
                else:
                    S.op(fn_[0], fn_[1], writes=[Bt, Bdst])

            in_w_l = w_in_d[l].rearrange("(kc p) n -> p kc n", p=128)
            col_start = np.concatenate([[0], np.cumsum([len(c) for _, c in IN_BLOCKS])]).tolist()
            ORDER = [4, 5, 6, 0, 1, 2, 3, 7, 8, 9, 10, 11]
            b0_ = ORDER[0]
            P2st = {"next_w": load_w(in_w_l[:, :, col_start[b0_]:col_start[b0_] + len(IN_BLOCKS[b0_][1])], len(IN_BLOCKS[b0_][1]), 16)}
            KEEP = [1]
            A_thunks = []
            for oi, bi in enumerate(ORDER):
                btype, cols = IN_BLOCKS[bi]
                ncols = len(cols)
                gl = None
                if btype == "QK":
                    if bi == 0:
                        gl = [g_aq] * 4
                    elif bi == 1:
                        gl = [g_aq, g_aq, g_ak, g_ak]
                    elif bi == 2:
                        gl = [g_cq] * 4
                    else:
                        gl = [g_ck] * 4
                ctx = {}

                def p2_start(oi=oi, btype=btype, ctx=ctx):
                    ctx["w"] = P2st["next_w"]
                    if oi + 1 < len(ORDER):
                        nb_ = ORDER[oi + 1]
                        nn = len(IN_BLOCKS[nb_][1])
                        P2st["next_w"] = load_w(in_w_l[:, :, col_start[nb_]:col_start[nb_] + nn], nn, 16)
                    if btype == "VKR":
                        S.op("dve", lambda e: e.memset(krT[64:128, :], 0.0), writes=[Bkr])

                def p2_tile(t, bi=bi, btype=btype, ctx=ctx, ncols=ncols, gl=gl):
                    wb, Bw = ctx["w"]
                    bk, Bb = next_bank()
                    proj_mm(bk, Bb, lambda kc: actT[:, kc, t * 128:(t + 1) * 128], Bact[t], wb, Bw, ncols, 16)
                    run_deferred(keep=KEEP[0])
                    dq_new_group()
                    tsl = slice(t * 128, (t + 1) * 128)
                    tsl8 = slice((t % 8) * 128, (t % 8 + 1) * 128)
                    if btype == "QK":
                        ss = stat_take(4)
                        rr = stat_take(4)
                        Bss, Br = Buf(), Buf()
                        for u in range(4):
                            S.op("act", (lambda u: lambda e: e.activation(out=junk[:, 0:128], in_=bk[:, u * 128:(u + 1) * 128], func=AF.Square,
                                                                         accum_out=ss[:, u:u + 1]))(u), writes=[Bb, Bss, Bjunk])
                        rstd(ss, rr, 128, Bss, Br)
                        xnt, Bxn, _ = Rxn.next()
                        for u in range(4):
                            S.op("dve", (lambda u: lambda e: e.scalar_tensor_tensor(out=xnt[:, u * 128:(u + 1) * 128], in0=bk[:, u * 128:(u + 1) * 128],
                                                                                   scalar=rr[:, u:u + 1], in1=gl[u][:], op0=ALU.mult, op1=ALU.mult))(u),
                                 reads=[Br, Bgs], writes=[Bb, Bxn])
                        transposes_to([xnt[:, u * 128:(u + 1) * 128] for u in range(4)], stageA[:, :, tsl8], Bxn, BstageA, wacc=True)
                    elif btype == "LAT":
                        which = bi - 4
                        gt = g_bcq if which == 0 else g_bckv
                        ss = stat_take(1)
                        rr = stat_take(1)
                        Bss, Br = Buf(), Buf()
                        S.op("act", lambda e: e.activation(out=junk, in_=bk, func=AF.Square, accum_out=ss), writes=[Bb, Bss, Bjunk])
                        rstd(ss, rr, 512, Bss, Br)
                        xnt, Bxn, _ = Rxn.next()
                        S.op("dve", lambda e: e.scalar_tensor_tensor(out=xnt, in0=bk, scalar=rr, in1=gt[:], op0=ALU.mult, op1=ALU.mult),
                             reads=[Br, Bgs], writes=[Bb, Bxn])
                        transposes_to([xnt[:, u * 128:(u + 1) * 128] for u in range(4)], latT[which][:, :, tsl], Bxn, Blat[which], wacc=True)
                    elif btype == "VKR":
                        vt, Bv, dv = Rv.next()
                        S.op("act", lambda e: e.copy(out=vt[:, 0:256], in_=bk[:, 0:256]), writes=[Bb, Bv])
                        dma("sp", vA_d[tsl, :], vt[:, 0:256], reads=[Bv], wacc=[BvA], dsem=dv)
                        S.op("act", lambda e: e.activation(out=junk[:, 0:64], in_=bk[:, 256:320], func=AF.Square, accum_out=ss_kr[:, t:t + 1]),
                             writes=[Bb, Bjunk], wacc=[Bkr])
                        xr, Bxr, _ = Rxr.next()
                        ropet, Bropet, _ = Rrt.next()
                        S.op("dve", lambda e: e.tensor_tensor(out=xr[:, 0, :], in0=bk[:, 256:320], in1=g_bk[:, 128:192], op=ALU.mult),
                             reads=[Bgs], writes=[Bb, Bxr])
                        xrr, Bxrr, _ = Rxrr.next()
                        rope(xr[:, 0:1, :], xrr[:, 0:1, :], t, ropet[:, :, 0:1, :], Bxr, Bxrr, Bropet, n=1)
                        transposes_to([xrr[:, 0, :]], krT[:, tsl], Bxrr, Bkr, wacc=True)
                    elif btype == "V":
                        vt, Bv, dv = Rv.next()
                        S.op("act", lambda e: e.copy(out=vt, in_=bk), writes=[Bb, Bv])
                        dma("sp", vC_d[tsl, :], vt, reads=[Bv], wacc=[BvC], dsem=dv)
                    elif btype == "Z":
                        zt, Bz, dz = Rz.next()
                        S.op("act", lambda e: e.activation(out=zt, in_=bk, func=AF.Silu), writes=[Bb, Bz])
                        zc0 = (bi - 8) * 512
                        dma("sp", sz_d[tsl, zc0:zc0 + 512], zt, reads=[Bz], wacc=[Bsz], dsem=dz)
                def p2_post(half, bi=bi, btype=btype):
                    if btype == "QK":
                        if bi == 0:
                            dst, Bd = featA_d[0:4], BfA
                        elif bi == 1:
                            dst, Bd = featA_d[4:8], BfA
                        elif bi == 2:
                            dst, Bd = featC_d[0:4], BfC
                        else:
                            dst, Bd = featC_d[4:8], BfC
                        hs_ = slice(half * 1024, (half + 1) * 1024)
                        defer((lambda dst, Bd, hs_: lambda: dma("sp", dst.rearrange("u p n -> p u n")[:, :, hs_], stageA, reads=[BstageA], wacc=[Bd], dsem=d_stage))(dst, Bd, hs_))

                def p2_th(t, st_=p2_start, ti_=p2_tile, po_=p2_post):
                    if t == 0:
                        st_()
                    ti_(t)
                    if t % 8 == 7:
                        po_(t // 8)
                A_thunks += [(lambda t, f: lambda: f(t))(t, p2_th) for t in range(NT)]

            uq_l = w_uq_d[l].rearrange("(kc p) n -> p kc n", p=128)
            ukv_l = w_ukv_d[l].rearrange("(kc p) n -> p kc n", p=128)
            stq_n = stageB[:, 0:2, :]
            stq_r = stageB[:, 2:4, :]
            stk = stageB[:, 0:3, :]
            d_stageB = S.dsem("stgB")
            p2b_blocks = ([("q", qb, uq_l[:, :, qb * 384:(qb + 1) * 384]) for qb in range(3)] +
                          [("k", kb, ukv_l[:, :, kb * 384:(kb + 1) * 384]) for kb in range(2)] +
                          [("v", vb, ukv_l[:, :, 768 + vb * 384:768 + (vb + 1) * 384]) for vb in range(2)])
            wbuf2 = [CX.take([128, 4, 384], BF16) for _ in range(2)]
            Rw2 = Ring(S, wbuf2, "w2")

            def load_w2(src_ap):
                wb, Bw, dw = Rw2.next()
                dma("pool", wb, src_ap, writes=[Bw], dsem=dw)
                return wb, Bw

            def p2b_q_tile(qb, t, wb, Bw):
                tsl = slice(t * 128, (t + 1) * 128)
                tsl8 = slice((t % 8) * 128, (t % 8 + 1) * 128)
                bk, Bb = next_bank()
                proj_mm(bk, Bb, lambda kc: latT[0][:, kc, tsl], Blat[0], wb, Bw, 384, 4)
                run_deferred(keep=KEEP[0])
                dq_new_group()
                ss = stat_take(2)
                rr = stat_take(2)
                Bss, Br = Buf(), Buf()
                for hh in range(2):
                    S.op("act", (lambda hh: lambda e: e.activation(out=junk[:, 0:192], in_=bk[:, hh * 192:(hh + 1) * 192], func=AF.Square,
                                                                  accum_out=ss[:, hh:hh + 1]))(hh), writes=[Bb, Bss, Bjunk])
                rstd(ss, rr, 192, Bss, Br)
                xnt, Bxn, _ = Rxn.next()
                for hh in range(2):
                    c0 = hh * 192
                    S.op("dve", (lambda hh, c0: lambda e: e.scalar_tensor_tensor(out=xnt[:, hh * 128:(hh + 1) * 128], in0=bk[:, c0:c0 + 128],
                                                                                scalar=rr[:, hh:hh + 1], in1=g_bq[:, 0:128], op0=ALU.mult, op1=ALU.mult))(hh, c0),
                         reads=[Br, Bgs], writes=[Bb, Bxn])
                transposes_to([xnt[:, hh * 128:(hh + 1) * 128] for hh in range(2)], stq_n[:, :, tsl8], Bxn, BstageB, wacc=True)
                xr, Bxr, _ = Rxr.next()
                ropet, Bropet, _ = Rrt.next()
                for hh in range(2):
                    c0 = hh * 192 + 128
                    S.op("dve", (lambda hh, c0: lambda e: e.scalar_tensor_tensor(out=xr[:, hh, :], in0=bk[:, c0:c0 + 64],
                                                                                scalar=rr[:, hh:hh + 1], in1=g_bq[:, 128:192], op0=ALU.mult, op1=ALU.mult))(hh, c0),
                         reads=[Br, Bgs], writes=[Bb, Bxr])
                xrr, Bxrr, _ = Rxrr.next()
                rope(xr, xrr, t, ropet, Bxr, Bxrr, Bropet, n=2)
                transposes_to([xrr[:, 0, :], xrr[:, 1, :]], stq_r[:, :, tsl8], Bxrr, BstageB, wacc=True)

            def p2b_k_tile(kb, t, wb, Bw):
                tsl = slice(t * 128, (t + 1) * 128)
                tsl8 = slice((t % 8) * 128, (t % 8 + 1) * 128)
                bk, Bb = next_bank()
                proj_mm(bk, Bb, lambda kc: latT[1][:, kc, tsl], Blat[1], wb, Bw, 384, 4)
                run_deferred(keep=KEEP[0])
                dq_new_group()
                ss = stat_take(3)
                Bss = Buf()
                for hh in range(3):
                    S.op("act", (lambda hh: lambda e: e.activation(out=junk[:, 0:128], in_=bk[:, hh * 128:(hh + 1) * 128], func=AF.Square,
                                                                  accum_out=ss[:, hh:hh + 1]))(hh), writes=[Bb, Bss, Bjunk])
                S.op("dve", lambda e: e.tensor_scalar(out=ss, in0=ss, scalar1=ss_kr[:, t:t + 1], scalar2=None, op0=ALU.add),
                     reads=[Bss, Bkr], writes=[Bss])
                rstd(ss, kscale[:, t, kb * 3:kb * 3 + 3], 192, Bss, Bksc, mul=192.0 ** -0.5)
                xnt, Bxn, _ = Rxn.next()
                S.op("dve", lambda e: e.tensor_tensor(out=xnt[:, 0:384].rearrange("p (a b) -> p a b", a=3), in0=bk[:, 0:384].rearrange("p (a b) -> p a b", a=3),
                                                      in1=g_bk[:, 0:128].unsqueeze(1).to_broadcast([128, 3, 128]), op=ALU.mult),
                     reads=[Bgs], writes=[Bb, Bxn])
                transposes_to([xnt[:, hh * 128:(hh + 1) * 128] for hh in range(3)], stk[:, :, tsl8], Bxn, BstageB, wacc=True)

            def p2b_v_tile(vb, t, wb, Bw):
                tsl = slice(t * 128, (t + 1) * 128)
                bk, Bb = next_bank()
                proj_mm(bk, Bb, lambda kc: latT[1][:, kc, tsl], Blat[1], wb, Bw, 384, 4)
                run_deferred(keep=KEEP[0])
                dq_new_group()
                vt, Bv, dv = Rv.next()
                S.op("act", lambda e: e.copy(out=vt[:, 0:384], in_=bk[:, 0:384]), writes=[Bb, Bv])
                dma("sp", vB_d[tsl, vb * 384:(vb + 1) * 384], vt[:, 0:384], reads=[Bv], wacc=[BvB], dsem=dv)

            P2bst = {}
            B_thunks = []
            for pi, (kind, idx, _src) in enumerate(p2b_blocks):
                ctxb = {}

                def p2b_th(t, pi=pi, kind=kind, idx=idx, ctxb=ctxb):
                    if t == 0:
                        if pi == 0:
                            P2bst["next_w"] = load_w2(p2b_blocks[0][2])
                        ctxb["w"] = P2bst["next_w"]
                        if pi + 1 < len(p2b_blocks):
                            P2bst["next_w"] = load_w2(p2b_blocks[pi + 1][2])
                    wb, Bw = ctxb["w"]
                    if kind == "q":
                        p2b_q_tile(idx, t, wb, Bw)
                    elif kind == "k":
                        p2b_k_tile(idx, t, wb, Bw)
                    else:
                        p2b_v_tile(idx, t, wb, Bw)
                    if t % 8 == 7:
                        hs_ = slice((t // 8) * 1024, (t // 8 + 1) * 1024)
                        if kind == "q":
                            defer((lambda idx, hs_: lambda: (dma("sp", qn_d[idx * 2:idx * 2 + 2].rearrange("u p n -> p u n")[:, :, hs_], stq_n, reads=[BstageB], wacc=[Bqn], dsem=d_stageB),
                                                            dma("sp", qr_d[idx * 2:idx * 2 + 2].rearrange("u p n -> p u n")[:, :, hs_], stq_r, reads=[BstageB], wacc=[Bqr], dsem=d_stageB)))(idx, hs_))
                        elif kind == "k":
                            defer((lambda idx, hs_: lambda: dma("sp", kn_d[idx * 3:idx * 3 + 3].rearrange("u p n -> p u n")[:, :, hs_], stk, reads=[BstageB], wacc=[Bkn], dsem=d_stageB))(idx, hs_))
                B_thunks += [(lambda t, f: lambda: f(t))(t, p2b_th) for t in range(NT)]

            n_pre = 3 * NT
            n_mid = 8 * NT
            for th in A_thunks[:n_pre]:
                th()
            KEEP[0] = 3
            ia, ib, k_ = n_pre, 0, 0
            while ia < n_mid or ib < len(B_thunks):
                if ia < n_mid:
                    A_thunks[ia]()
                    ia += 1
                for _ in range(1 + (k_ % 2)):
                    if ib < len(B_thunks):
                        B_thunks[ib]()
                        ib += 1
                k_ += 1
            KEEP[0] = 1
            for th in A_thunks[n_mid:]:
                th()
            run_deferred()

            S.barrier()
            CX.reset()
            _ = CX.take([128, 4, L], BF16)
            _ = [CX.take([128, 4, L], BF16) for _ in range(2)]
            krT2 = CX.take([128, L], BF16)
            CX.reset()
            opnd = [[CX.take([128, L], BF16) for _ in range(3)] for _ in range(2)]
            szt = [CX.take([128, NT, 128], F32) for _ in range(2)]
            et = [CX.take([128, 640], F32) for _ in range(3)]
            assert CX.off <= 48 * 1024
            CX.off = 48 * 1024 + 4 * 1024
            ptile = [CX.take([128, 640], BF16) for _ in range(6)]
            yg = [CX.take([128, 128], BF16) for _ in range(4)]
            Ret = Ring(S, et, "et", with_dsem=False)
            Rpt = Ring(S, ptile, "pt", with_dsem=False)
            Ryg = Ring(S, yg, "yg", with_dsem=False)
            CW.reset()
            tabA_s = CW.take([128, 6, 384], F32)
            ctab_s = [CW.take([128, 26, 64], F32) for _ in range(2)]
            Btab = Buf("tabA")
            d_tab = S.dsem("tab")
            dma("sp", tabA_s, tabA_d.rearrange("h p n -> p h n"), writes=[Btab], dsem=d_tab)
            Bmix = [Buf(f"mix{i}") for i in range(NT)]
            Rq = Ring(S, [opnd[0][0], opnd[1][0]], "q")
            Rk = Ring(S, [opnd[0][1], opnd[1][1]], "k")
            Rqr = Ring(S, [opnd[0][2], opnd[1][2]], "qr")
            Rsz = Ring(S, szt, "sz")
            Rva = Ring(S, [vaug[0], vaug[1]], "va")
            Rct = Ring(S, ctab_s, "ct")
            mixT = actT
            acc_i = [0]

            def fin_dve(acc_ap, Bacc, szslice, Bszt, sink_col=None):
                r = stat_take(1)
                Br = Buf()
                if sink_col is not None:
                    S.op("dve", lambda e: e.tensor_scalar(out=r, in0=acc_ap[:, 128:129], scalar1=esink[:, sink_col:sink_col + 1], scalar2=None, op0=ALU.add),
                         reads=[Bgs], writes=[Bacc, Br])
                    S.op("dve", lambda e: e.reciprocal(out=r, in_=r), reads=[Br], writes=[Br])
                else:
                    S.op("dve", lambda e: e.reciprocal(out=r, in_=acc_ap[:, 128:129]), writes=[Bacc, Br])
                ygt, Byg, _ = Ryg.next()
                S.op("dve", lambda e: e.scalar_tensor_tensor(out=ygt, in0=acc_ap[:, 0:128], scalar=r, in1=szslice, op0=ALU.mult, op1=ALU.mult),
                     reads=[Br, Bszt], writes=[Bacc, Byg])
                return ygt, Byg

            def fin_tr(ygs, chunk, i0):
                tb, Bt = next_tb()
                n = len(ygs)
                for k_, (ygt, Byg) in enumerate(ygs):
                    S.op("pe", (lambda k_, ygt: lambda e: e.transpose(out=tb[:, k_ * 128:(k_ + 1) * 128], in_=ygt, identity=ident[:]))(k_, ygt),
                         reads=[Byg, Bconst], writes=[Bt])
                S.op("act", lambda e: e.copy(out=mixT[:, chunk, i0 * 128:(i0 + n) * 128], in_=tb[:, 0:n * 128]), writes=[Bt], wacc=[Bmix[i0 + k] for k in range(n)])

            def load_head(ring, src, extra_reads=()):
                ap, B, d = ring.next()
                dma("sp", ap, src, reads=list(extra_reads), writes=[B], dsem=d)
                return ap, B

            def load_v(src_cols, Bsrc):
                ap, B, d = Rva.next()
                dma("sp", ap[:, :, 0:128], src_cols.rearrange("(t p) c -> p t c", p=128), reads=[Bsrc], writes=[B], dsem=d)
                return ap, B

            def load_sz(c0):
                ap, B, d = Rsz.next()
                dma("sp", ap, sz_d[:, c0:c0 + 128].rearrange("(t p) c -> p t c", p=128), reads=[Bsz], writes=[B], dsem=d)
                return ap, B

            SC_A = 128.0 ** -0.5
            A_ops = {}

            def a_load(h):
                kvh = h // 3
                if h % 3 == 0:
                    A_ops[("k", kvh)] = load_head(Rk, featA_d[6 + kvh], [BfA])
                    A_ops[("v", kvh)] = load_v(vA_d[:, kvh * 128:(kvh + 1) * 128], BvA)
                A_ops[("q", h)] = load_head(Rq, featA_d[h], [BfA])
                A_ops[("sz", h)] = load_sz(h * 128)

            A_pts = {}

            def a_sA(h, j):
                kT, Bk = A_ops[("k", h // 3)]
                qT, Bq = A_ops[("q", h)]
                qlo, qhi = max(j - 1, 0), min(j + 1, NT - 1)
                nq = (qhi - qlo + 1) * 128
                tc0 = (qlo - (j - 1)) * 128
                sb_, Bsb = bank[j % 2], Bbank[j % 2]
                S.op("pe", lambda e: e.matmul(sb_[:, 0:nq], lhsT=kT[:, j * 128:(j + 1) * 128], rhs=qT[:, qlo * 128:qlo * 128 + nq], start=True, stop=True),
                     reads=[Bk, Bq], writes=[Bsb])
                e_t, Be, _ = Ret.next()
                S.op("act", lambda e: e.activation(out=e_t[:, 0:nq], in_=sb_[:, 0:nq], func=AF.Exp, scale=SC_A), writes=[Bsb, Be])
                p_t, Bp, _ = Rpt.next()
                S.op("dve", lambda e: e.tensor_tensor(out=p_t[:, 0:nq], in0=e_t[:, 0:nq], in1=tabA_s[:, h, tc0:tc0 + nq], op=ALU.mult),
                     reads=[Be, Btab], writes=[Bp])
                A_pts[(h, j)] = (p_t, Bp, qlo)

            A_yg = {}

            def a_sB(h, i):
                va, Bva = A_ops[("v", h // 3)]
                szh, Bszh = A_ops[("sz", h)]
                ab, Bab = bank[4 + acc_i[0] % 2], Bbank[4 + acc_i[0] % 2]
                acc_i[0] += 1
                js = [jj for jj in (i - 1, i, i + 1) if 0 <= jj < NT]
                for n_, jj in enumerate(js):
                    p_t, Bp, qlo = A_pts[(h, jj)]
                    co = (i - qlo) * 128
                    S.op("pe", (lambda p_t, co, jj, n_: lambda e: e.matmul(ab[:, 0:129], lhsT=p_t[:, co:co + 128], rhs=va[:, jj, :],
                                                                          start=(n_ == 0), stop=(n_ == len(js) - 1)))(p_t, co, jj, n_),
                         reads=[Bp, Bva], writes=[Bab])
                A_yg[(h, i)] = fin_dve(ab, Bab, szh[:, i, :], Bszh, sink_col=h)
                A_pts.pop((h, i - 1), None)

            def a_sD(h, i):
                fin_tr([A_yg.pop((h, i))], h, i)

            nA = 6 * NT
            a_load(0)
            for st in range(nA + 4):
                if st < nA:
                    h, j = divmod(st, NT)
                    if j == 4 and h + 1 < 6:
                        a_load(h + 1)
                    a_sA(h, j)
                if 0 <= st - 2 < nA:
                    a_sB(*divmod(st - 2, NT))
                if 0 <= st - 4 < nA:
                    a_sD(*divmod(st - 4, NT))

            B_ops = {}

            def b_load(h):
                B_ops[("q", h)] = load_head(Rq, qn_d[h], [Bqn])
                B_ops[("qr", h)] = load_head(Rqr, qr_d[h], [Bqr])
                B_ops[("k", h)] = load_head(Rk, kn_d[h], [Bkn])
                B_ops[("v", h)] = load_v(vB_d[:, h * 128:(h + 1) * 128], BvB)
                B_ops[("sz", h)] = load_sz(768 + h * 128)

            B_pts = {}

            def b_sA(h, qt, j):
                qT, Bq = B_ops[("q", h)]
                qrT, Bqr_ = B_ops[("qr", h)]
                kT, Bk = B_ops[("k", h)]
                qs = slice(qt * 512, (qt + 1) * 512)
                ks = slice(j * 128, (j + 1) * 128)
                sb_, Bsb = bank[j % 2], Bbank[j % 2]
                S.op("pe", lambda e: e.matmul(sb_, lhsT=kT[:, ks], rhs=qT[:, qs], start=True, stop=False), reads=[Bk, Bq], writes=[Bsb])
                S.op("pe", lambda e: e.matmul(sb_, lhsT=krT2[:, ks], rhs=qrT[:, qs], start=False, stop=True), reads=[Bkr, Bqr_], writes=[Bsb])
                p_t, Bp, _ = Rpt.next()
                S.op("act", lambda e: e.activation(out=p_t[:, 0:512], in_=sb_, func=AF.Exp, scale=kscale[:, j, h:h + 1]),
                     reads=[Bksc], writes=[Bsb, Bp])
                B_pts[(h, qt, j)] = (p_t, Bp)

            B_yg = {}

            def b_sB(h, qt, j):
                va, Bva = B_ops[("v", h)]
                szh, Bszh = B_ops[("sz", h)]
                p_t, Bp = B_pts.pop((h, qt, j))
                a0 = 2 + 2 * (qt % 2)
                for qb in range(4):
                    ab, Bab = bank[a0 + qb // 2], Bbank[a0 + qb // 2]
                    co = (qb % 2) * 130
                    S.op("pe", (lambda ab, qb, co: lambda e: e.matmul(ab[:, co:co + 129], lhsT=p_t[:, qb * 128:(qb + 1) * 128], rhs=va[:, j, :],
                                                                     start=(j == 0 and qb % 2 == 0), stop=(j == NT - 1), skip_group_check=True))(ab, qb, co),
                         reads=[Bp, Bva], writes=[Bab])
                if j == NT - 1:
                    ygs = []
                    for qb in range(4):
                        ab, Bab = bank[a0 + qb // 2], Bbank[a0 + qb // 2]
                        co = (qb % 2) * 130
                        ygs.append(fin_dve(ab[:, co:co + 129], Bab, szh[:, qt * 4 + qb, :], Bszh))
                    B_yg[(h, qt)] = ygs

            def b_sD(h, qt):
                fin_tr(B_yg.pop((h, qt)), 6 + h, qt * 4)

            nB = 6 * 4 * NT
            b_load(0)
            for st in range(nB + 4):
                if st < nB:
                    h, rem = divmod(st, 4 * NT)
                    qt, j = divmod(rem, NT)
                    if rem == 8 and h + 1 < 6:
                        b_load(h + 1)
                    b_sA(h, qt, j)
                if 0 <= st - 1 < nB:
                    h, rem = divmod(st - 1, 4 * NT)
                    b_sB(h, *divmod(rem, NT))
                if 0 <= st - 4 < nB:
                    h, rem = divmod(st - 4, 4 * NT)
                    qt, j = divmod(rem, NT)
                    if j == NT - 1:
                        b_sD(h, qt)

            SC_C = 128.0 ** -0.5
            C_ops = {}

            def c_load(h):
                C_ops[("q", h)] = load_head(Rq, featC_d[h], [BfC])
                C_ops[("k", h)] = load_head(Rk, featC_d[4 + h], [BfC])
                C_ops[("v", h)] = load_v(vC_d[:, h * 128:(h + 1) * 128], BvC)
                C_ops[("sz", h)] = load_sz(1536 + h * 128)
                ct, Bct, dct = Rct.next()
                dma("sp", ct.rearrange("p a b -> p (a b)"), ctab_d[h], writes=[Bct], dsem=dct)
                C_ops[("ct", h)] = (ct, Bct)

            def c_js(i):
                if i <= 1:
                    return [3, 2, 1, 0]
                if i >= NT - 2:
                    return [15, 14, 13, 12]
                return [i + 2, i + 1, i, i - 1, i - 2]

            C_pts = {}

            def c_sA(h, i):
                qT, Bq = C_ops[("q", h)]
                kT, Bk = C_ops[("k", h)]
                ct, Bct = C_ops[("ct", h)]
                js = c_js(i)
                nj = len(js)
                if 2 <= i <= NT - 3:
                    tsl_ = ct[:, 16:26, :]
                else:
                    b0 = 7 - 2 * (js[0] - i)
                    tsl_ = ct[:, b0:b0 + 2 * nj, :]
                sbase = (i % 2) * 1024
                sb_ = PS[:, sbase:sbase + nj * 128]
                Bs_ = [Bbank[(i % 2) * 2], Bbank[(i % 2) * 2 + 1]]
                for n_, jj in enumerate(js):
                    S.op("pe", (lambda n_, jj: lambda e: e.matmul(PS[:, sbase + n_ * 128:sbase + (n_ + 1) * 128], lhsT=kT[:, jj * 128:(jj + 1) * 128],
                                                                 rhs=qT[:, i * 128:(i + 1) * 128], start=True, stop=True))(n_, jj),
                         reads=[Bk, Bq], writes=Bs_)
                e_t, Be, _ = Ret.next()
                S.op("act", lambda e: e.activation(out=e_t[:, 0:nj * 128], in_=sb_, func=AF.Exp, scale=SC_C), writes=Bs_ + [Be])
                p_t, Bp, _ = Rpt.next()
                S.op("dve", lambda e: e.tensor_tensor(out=p_t[:, 0:nj * 128], in0=e_t[:, 0:nj * 128], in1=tsl_.rearrange("p a b -> p (a b)"), op=ALU.mult),
                     reads=[Be, Bct], writes=[Bp])
                C_pts[(h, i)] = (p_t, Bp)

            C_yg = {}

            def c_sB(h, i):
                va, Bva = C_ops[("v", h)]
                szh, Bszh = C_ops[("sz", h)]
                p_t, Bp = C_pts.pop((h, i))
                js = c_js(i)
                nj = len(js)
                ab, Bab = bank[4 + acc_i[0] % 2], Bbank[4 + acc_i[0] % 2]
                acc_i[0] += 1
                for n_, jj in enumerate(js):
                    S.op("pe", (lambda n_, jj: lambda e: e.matmul(ab[:, 0:129], lhsT=p_t[:, n_ * 128:(n_ + 1) * 128], rhs=va[:, jj, :],
                                                                 start=(n_ == 0), stop=(n_ == nj - 1)))(n_, jj),
                         reads=[Bp, Bva], writes=[Bab])
                C_yg[(h, i)] = fin_dve(ab, Bab, szh[:, i, :], Bszh)

            def c_sD(h, i):
                fin_tr([C_yg.pop((h, i))], 12 + h, i)

            nC = 4 * NT
            c_load(0)
            for st in range(nC + 4):
                if st < nC:
                    h, i = divmod(st, NT)
                    if i == 4 and h + 1 < 4:
                        c_load(h + 1)
                    c_sA(h, i)
                if 0 <= st - 1 < nC:
                    c_sB(*divmod(st - 1, NT))
                if 0 <= st - 3 < nC:
                    c_sD(*divmod(st - 3, NT))

            S.barrier()
            CX.reset()
            CW.reset()
            wbuf = [CW.take([128, 16, 512], BF16) for _ in range(2)]
            Rw = Ring(S, wbuf, "w")
            hsl = [CX.take([128, 4, 512], F32) for _ in range(2)]
            Rhs = Ring(S, hsl, "hs")
            ost = [CX.take([128, 4, 512], F32) for _ in range(2)]
            Ros = Ring(S, ost, "os")
            Bh = Buf("hscr")
            wo_l = w_out_d[l].rearrange("(kc p) n -> p kc n", p=128)
            next_w = load_w(wo_l[:, :, 0:512], 512, 16)

            def p4_group(c, tg, wb, Bw):
                csl = slice(c * 512, (c + 1) * 512)
                rows = slice(tg * 512, (tg + 1) * 512)
                hs, Bhs, dhs = Rhs.next()
                dma("sp", hs, h_src[rows, csl].rearrange("(t p) c -> p t c", p=128), writes=[Bhs], dsem=dhs)
                o, Bo, do = Ros.next()
                for k_ in range(4):
                    t = tg * 4 + k_
                    tsl = slice(t * 128, (t + 1) * 128)
                    bk, Bb = next_bank()
                    proj_mm(bk, Bb, lambda kc: mixT[:, kc, tsl], Bmix[t], wb, Bw, 512, 16)
                    S.op("dve", lambda e: e.tensor_tensor(out=o[:, k_, :], in0=bk, in1=hs[:, k_, :], op=ALU.add), reads=[Bhs], writes=[Bb], wacc=[Bo])
                dma("sp", hscr_d[s][rows, csl].rearrange("(t p) c -> p t c", p=128), o, reads=[Bo], wacc=[Bh], dsem=do)

            assert CX.off <= 40 * 1024
            CX.off = 40 * 1024
            junk = CX.take([128, D], BF16)
            pT = CX.take([128, 2, L], BF16)
            wpp = CX.take([128, 2, D], BF16)
            pf = [CX.take([128, 256], F32) for _ in range(4)]
            pb = [CX.take([128, 256], BF16) for _ in range(4)]
            Rpf = Ring(S, pf, "pf")
            Rpb = Ring(S, pb, "pb", with_dsem=False)
            Bjunk = Buf("junk")
            Bwpp = Buf("wpp")
            d_wpp = S.dsem("wpp")
            dma("pool", wpp, w_pp_d[l].rearrange("(kc p) n -> p kc n", p=128), writes=[Bwpp], dsem=d_wpp)
            Brp = Buf("rstd_p")
            BpTt = [Buf(f"pT{t}") for t in range(NT)]

            p_loaded = {}

            def ple_load(t):
                pft, Bpf, dpf = Rpf.next()
                dma("sp", pft, p_d[l, s, t * 128:(t + 1) * 128, :], writes=[Bpf], dsem=dpf)
                pbt, Bpb, _ = Rpb.next()
                S.op("dve", lambda e: e.tensor_copy(out=pbt, in_=pft), reads=[Bpf], writes=[Bpb])
                p_loaded[t] = (pbt, Bpb)

            def ple_stage1(t):
                tsl = slice(t * 128, (t + 1) * 128)
                pbt, Bpb = p_loaded.pop(t)
                transposes_to([pbt[:, 0:128], pbt[:, 128:256]], pT[:, :, tsl], Bpb, BpTt[t], wacc=True, later=False)

            def ple_stage2(t):
                tsl = slice(t * 128, (t + 1) * 128)
                ss = stat_take(4)
                Bss = Buf()
                for c in range(4):
                    bk, Bb = next_bank()
                    for kc in range(2):
                        S.op("pe", lambda e: e.matmul(bk, lhsT=pT[:, kc, tsl], rhs=wpp[:, kc, c * 512:(c + 1) * 512], start=(kc == 0), stop=(kc == 1)),
                             reads=[BpTt[t], Bwpp], writes=[Bb])
                    S.op("act", lambda e: e.activation(out=junk[:, 0:512], in_=bk, func=AF.Square, accum_out=ss[:, c:c + 1]),
                         writes=[Bb, Bjunk], wacc=[Bss])
                sst = stat_take(1)
                Bsst = Buf()
                S.op("dve", lambda e: e.tensor_reduce(out=sst, in_=ss, axis=mybir.AxisListType.X, op=ALU.add), reads=[Bss], writes=[Bsst])
                rstd(sst, rstd_p[:, t:t + 1], D, Bsst, Brp)

            g_ = 0
            ple_load(0)
            ple_load(1)
            for c in range(4):
                wb, Bw = next_w
                if c < 3:
                    next_w = load_w(wo_l[:, :, (c + 1) * 512:(c + 2) * 512], 512, 16)
                for tg in range(4):
                    if g_ + 2 < NT:
                        ple_load(g_ + 2)
                    p4_group(c, tg, wb, Bw)
                    ple_stage1(g_)
                    if g_ >= 1:
                        ple_stage2(g_ - 1)
                    g_ += 1
            ple_stage2(NT - 1)
            BpT = Buf("pT")
            for t in range(NT):
                for tk_ in BpTt[t].w.values():
                    kk_ = id(tk_.sem)
                    if kk_ not in BpT.w or BpT.w[kk_].val < tk_.val:
                        BpT.w[kk_] = tk_

            S.barrier()
            CX.reset()
            hb = [CX.take([128, D], F32) for _ in range(4)]
            ub = [CX.take([128, D], BF16) for _ in range(2)]
            assert CX.off == 40 * 1024
            Rhb = Ring(S, hb, "hb")
            Rub = Ring(S, ub, "ub", with_dsem=False)
            bcast_load(G2[:], g_ple_d[l, :], D, Bg2, d_g2)
            Bact = [Buf(f"act{t}") for t in range(NT)]
            hrows = lambda t: hscr_d[s][t * 128:(t + 1) * 128, :]
            for t in range(3):
                nt_load(hrows(t), t)
            for t in range(NT):
                if t + 3 < NT:
                    nt_load(hrows(t + 3), t + 3)
                norm_transpose(hrows(t), G2, Bg2, t)
            run_deferred()

            S.barrier()
            CW.reset()
            bcast_load(G1[:], g_post_d[l, :], D, Bg1, d_g1)
            CX.reset()
            _ = [CX.take([128, D], F32) for _ in range(4)]
            _ = [CX.take([128, D], BF16) for _ in range(2)]
            _ = CX.take([128, D], BF16)
            pT_off = CX.off
            pT = CX.take([128, 2, L], BF16)
            wpp = CX.take([128, 2, D], BF16)
            keep = CX.off
            CX.reset()
            hsl = [CX.take([128, 4, 512], F32) for _ in range(2)]
            gat = [CX.take([128, 512], F32) for _ in range(2)]
            pn = [CX.take([128, 512], F32) for _ in range(2)]
            ost = [CX.take([128, 4, 512], F32) for _ in range(2)]
            assert CX.off <= pT_off
            Rhs = Ring(S, hsl, "hs")
            Rg = Ring(S, gat, "g", with_dsem=False)
            Rpn = Ring(S, pn, "pn", with_dsem=False)
            Ros = Ring(S, ost, "os")
            wbuf = [CW.take([128, 16, 512], BF16) for _ in range(2)]
            Rw = Ring(S, wbuf, "w")
            wg_l = w_gate_d[l].rearrange("(kc p) n -> p kc n", p=128)
            next_w = load_w(wg_l[:, :, 0:512], 512, 16)
            Bout = Buf("hout")

            def p5_group(c, tg, wb, Bw):
                csl = slice(c * 512, (c + 1) * 512)
                rows = slice(tg * 512, (tg + 1) * 512)
                hs, Bhs, dhs = Rhs.next()
                dma("sp", hs, hscr_d[s][rows, csl].rearrange("(t p) c -> p t c", p=128), reads=[Bh], writes=[Bhs], dsem=dhs)
                o, Bo, do = Ros.next()
                for k_ in range(4):
                    t = tg * 4 + k_
                    tsl = slice(t * 128, (t + 1) * 128)
                    bk, Bb = next_bank()
                    proj_mm(bk, Bb, lambda kc: actT[:, kc, tsl], Bact[t], wb, Bw, 512, 16)
                    bk2, Bb2 = next_bank()
                    for kc in range(2):
                        S.op("pe", lambda e: e.matmul(bk2, lhsT=pT[:, kc, tsl], rhs=wpp[:, kc, csl], start=(kc == 0), stop=(kc == 1)),
                             reads=[BpT, Bwpp], writes=[Bb2])
                    gt_, Bgt_, _ = Rg.next()
                    S.op("act", lambda e: e.activation(out=gt_, in_=bk, func=AF.Sigmoid), writes=[Bb, Bgt_])
                    pn_, Bpn_, _ = Rpn.next()
                    S.op("dve", lambda e: e.scalar_tensor_tensor(out=pn_, in0=bk2, scalar=rstd_p[:, t:t + 1], in1=G1[:, csl], op0=ALU.mult, op1=ALU.mult),
                         reads=[Brp, Bg1], writes=[Bb2, Bpn_])
                    S.op("pool", lambda e: e.tensor_tensor(out=pn_, in0=pn_, in1=gt_, op=ALU.mult), reads=[Bgt_], writes=[Bpn_])
                    S.op("dve", lambda e: e.tensor_tensor(out=o[:, k_, :], in0=pn_, in1=hs[:, k_, :], op=ALU.add), reads=[Bpn_, Bhs], wacc=[Bo])
                dma("sp", h_dst[rows, csl].rearrange("(t p) c -> p t c", p=128), o, reads=[Bo], wacc=[Bout], dsem=do)

            for c in range(4):
                wb, Bw = next_w
                if c < 3:
                    next_w = load_w(wg_l[:, :, (c + 1) * 512:(c + 2) * 512], 512, 16)
                for tg in range(4):
                    p5_group(c, tg, wb, Bw)

    S.emit()
    return nc, S


def _prep_shared(inp, tables):
    tabA, cos, sin, oh = tables
    f = lambda a: np.ascontiguousarray(np.asarray(a, dtype=np.float32))
    rpb = np.asarray(inp["c_rpb"], np.float32)
    rpbT = np.ascontiguousarray(rpb[:, :, ::-1, :].transpose(0, 1, 3, 2))
    return {
        "w_in": np.ascontiguousarray(np.asarray(inp["w_in"], np.float32)[:, :, IN_PERM]),
        "w_out": f(inp["w_out"]), "w_gate": f(inp["w_ple_gate"]), "w_pp": f(inp["w_ple_proj"]),
        "w_uq": f(inp["b_w_uq"]),
        "w_ukv": np.ascontiguousarray(np.asarray(inp["b_w_ukv"], np.float32)[:, :, UKV_PERM]),
        "g_in": f(inp["norm_in"]), "g_ple": f(inp["ple_norm"]), "g_post": f(inp["ple_post_norm"]),
        "g_aq": f(inp["a_q_norm"]), "g_ak": f(inp["a_k_norm"]), "g_cq": f(inp["c_q_norm"]), "g_ck": f(inp["c_k_norm"]),
        "g_bcq": f(inp["b_cq_norm"]), "g_bckv": f(inp["b_ckv_norm"]), "g_bq": f(inp["b_q_norm"]), "g_bk": f(inp["b_k_norm"]),
        "sink": f(inp["a_sink"]), "rpbT": rpbT,
        "ident": np.eye(128).astype(ml_dtypes.bfloat16),
        "tabA": tabA, "cosT": cos, "sinT": sin, "onehot": oh,
    }


_CACHE = {}


def kernel(**inputs):
    xp = np.asarray(inputs["x_prompt"], np.float32)
    xs = np.asarray(inputs["x_sample"], np.float32)
    pp = np.asarray(inputs["p_prompt"], np.float32)
    psm = np.asarray(inputs["p_sample"], np.float32)
    nB, nS = xp.shape[0], xs.shape[0]
    x_all = np.concatenate([xp, xs], axis=0)
    p_all = np.concatenate([pp, psm], axis=1)
    ntot = nB + nS
    ncores = 8
    slots = [[(c + 8 * k) % ntot if (c + 8 * k) < ntot else (c + 8 * k) % ntot for k in range(NSEQ)] for c in range(ncores)]
    if "nc" not in _CACHE:
        _CACHE["nc"] = build()[0]
        _CACHE["tables"] = _const_tables()
    nc = _CACHE["nc"]
    shared = _prep_shared(inputs, _CACHE["tables"])
    in_maps = []
    for c in range(ncores):
        m = dict(shared)
        m["x"] = np.ascontiguousarray(x_all[slots[c]])
        m["p"] = np.ascontiguousarray(p_all[:, slots[c]])
        in_maps.append(m)
    res = run_bass_kernel_spmd(nc, in_maps, core_ids=list(range(ncores)))
    y_all = np.zeros_like(x_all)
    done = set()
    for c in range(ncores):
        yc = res.results[c]["y"]
        for k, sidx in enumerate(slots[c]):
            if (c + 8 * k) < ntot and sidx not in done:
                y_all[sidx] = yc[k]
                done.add(sidx)
    return (y_all[:nB], y_all[nB:])
```
